# Optimizing a Trainium2 kernel written in Bass

```python
import math
import jax, jax.numpy as jnp
from jax import lax
import numpy as np

D_MODEL = 1024
BATCH = 32
SEQ = 256
DEPTH = 2
DEC_BATCH = 8
DEC_SEQ = 1024
PAST_LEN = 512

GRID_W = 64
NA_HEADS = 8
HEAD_DIM = 64
NA_WIDTH = NA_HEADS * HEAD_DIM
NA_MAX_ROWS = 8
NA_COLS = 16
NA_BAND = 2 * NA_COLS
S5_GROUPS = 16
S5_GROUP_CH = 16
S5_WIDTH = S5_GROUPS * S5_GROUP_CH
S5_STATE = 64
LRU_WIDTH = 256
LRU_BLOCKS = 4
LRU_BLOCK = LRU_WIDTH // LRU_BLOCKS
LRU_C = 8.0
LRU_CONV = 4
FFN_DIM = 2816
FFN_CONV = 3
ATTN_BLOCK = 128
N_BRANCH = 3
IN_SPLITS = (NA_WIDTH, 2 * NA_WIDTH, 3 * NA_WIDTH, 3 * NA_WIDTH + S5_WIDTH,
             3 * NA_WIDTH + S5_WIDTH + LRU_WIDTH, 3 * NA_WIDTH + S5_WIDTH + 2 * LRU_WIDTH)
IN_WIDTH = 3 * NA_WIDTH + S5_WIDTH + 2 * LRU_WIDTH + N_BRANCH * D_MODEL
EPS = 1e-6
NEG_INF = -1e30

kernel_name = "hybrid_flow_na_s5_rglru_step"


def rms_norm(x, g):
    x32 = x.astype(jnp.float32)
    y = x32 * lax.rsqrt(jnp.mean(x32 * x32, axis=-1, keepdims=True) + EPS)
    return y.astype(x.dtype) * g


def dw_conv(x, w, b, pad_left, pad_right):
    ch = x.shape[-1]
    y = lax.conv_general_dilated(x, w[:, None, :].astype(x.dtype), window_strides=(1,),
                                 padding=[(pad_left, pad_right)],
                                 dimension_numbers=('NWC', 'WIO', 'NWC'), feature_group_count=ch)
    return y + b


def linear_scan(a, b, h0, reverse):
    if reverse:
        a, b = jnp.flip(a, axis=1), jnp.flip(b, axis=1)

    def combine(left, right):
        a1, b1 = left
        a2, b2 = right
        return a1 * a2, a2 * b1 + b2

    a_cum, h_cum = lax.associative_scan(combine, (a, b), axis=1)
    h = h_cum + a_cum * h0[:, None]
    h_last = h[:, -1]
    if reverse:
        h = jnp.flip(h, axis=1)
    return h, h_last


def context_attention(q, k, v):
    bsz, length, heads, dh = q.shape
    nb = length // ATTN_BLOCK
    qb = (q * HEAD_DIM ** -0.5).reshape(bsz, nb, ATTN_BLOCK, heads, dh).swapaxes(0, 1)

    def block(qi):
        s = jnp.einsum('bqhd,bkhd->bhqk', qi, k).astype(jnp.float32)
        p = jax.nn.softmax(s, axis=-1).astype(v.dtype)
        return jnp.einsum('bhqk,bkhd->bqhd', p, v)

    out = lax.map(block, qb)
    return out.swapaxes(0, 1).reshape(bsz, length, heads * dh)


def neighbourhood_attention(q, k, v, k_ctx, v_ctx, rpb):
    bsz, length, heads, dh = q.shape
    rows = length // GRID_W
    wr = min(NA_MAX_ROWS, rows)
    ncb = GRID_W // NA_COLS
    r = jnp.arange(rows)
    row_idx = jnp.clip(r - wr // 2, 0, rows - wr)[:, None] + jnp.arange(wr)
    band0 = jnp.clip(jnp.arange(ncb) * NA_COLS - NA_COLS // 2, 0, GRID_W - NA_BAND)
    col_idx = band0[:, None] + jnp.arange(NA_BAND)
    qcol = jnp.arange(GRID_W).reshape(ncb, NA_COLS)
    win0 = jnp.clip(qcol - NA_COLS // 2, 0, GRID_W - NA_COLS)
    kc = col_idx[:, None, :]
    in_win = (kc >= win0[..., None]) & (kc < win0[..., None] + NA_COLS)
    dy = row_idx - r[:, None] + NA_MAX_ROWS - 1
    dx = jnp.clip(kc - qcol[..., None] + NA_COLS - 1, 0, 2 * NA_COLS - 2)
    bias = rpb[:, dy[:, None, None, :, None], dx[None, :, :, None, :]].astype(jnp.float32)

    k_grid = k.reshape(bsz, rows, GRID_W, heads, dh)
    v_grid = v.reshape(bsz, rows, GRID_W, heads, dh)
    gi_r = row_idx[:, :, None, None]
    gi_c = col_idx[None, None]
    kg = k_grid[:, gi_r, gi_c]
    vg = v_grid[:, gi_r, gi_c]
    qg = (q * HEAD_DIM ** -0.5).reshape(bsz, rows, ncb, NA_COLS, heads, dh)

    s_loc = jnp.einsum('brjqhd,brwjkhd->bhrjqwk', qg, kg).astype(jnp.float32) + bias
    s_loc = jnp.where(in_win[:, :, None, :], s_loc, NEG_INF)
    s_ctx = jnp.einsum('brjqhd,bchd->bhrjqc', qg, k_ctx).astype(jnp.float32)
    n_loc = wr * NA_BAND
    s = jnp.concatenate([s_loc.reshape(bsz, heads, rows, ncb, NA_COLS, n_loc), s_ctx], axis=-1)
    p = jax.nn.softmax(s, axis=-1).astype(v.dtype)
    p_loc = p[..., :n_loc].reshape(bsz, heads, rows, ncb, NA_COLS, wr, NA_BAND)
    out = (jnp.einsum('bhrjqwk,brwjkhd->brjqhd', p_loc, vg)
           + jnp.einsum('bhrjqc,bchd->brjqhd', p[..., n_loc:], v_ctx))
    return out.reshape(bsz, length, heads * dh)


def s5_mixer(u, lp, h0):
    bsz, length, _ = u.shape
    f32 = jnp.float32
    u32 = u.astype(f32).reshape(bsz, length, S5_GROUPS, S5_GROUP_CH)
    lam = lax.complex(lp['s5_lam_re'].astype(f32), lp['s5_lam_im'].astype(f32))
    step = jnp.exp(lp['s5_log_step'].astype(f32))[..., None]
    lam_bar = jnp.exp(lam * step)
    b_mat = lax.complex(lp['s5_b_re'].astype(f32), lp['s5_b_im'].astype(f32))
    b_bar = ((lam_bar - 1.0) / lam)[..., None] * b_mat[None]
    c_mat = lax.complex(lp['s5_c_re'].astype(f32), lp['s5_c_im'].astype(f32))
    bu = jnp.einsum('blgp,dgnp->dblgn', u32.astype(jnp.complex64), b_bar)
    y = lp['s5_d'].astype(f32).reshape(S5_GROUPS, S5_GROUP_CH) * u32
    finals = []
    for d, reverse in enumerate((False, True)):
        a = jnp.broadcast_to(lam_bar[d], bu[d].shape)
        h, h_last = linear_scan(a, bu[d], h0[:, d], reverse)
        y = y + jnp.einsum('blgn,gpn->blgp', h, c_mat[d]).real
        finals.append(h_last)
    y = jax.nn.gelu(y.reshape(bsz, length, S5_WIDTH))
    y = y * jax.nn.sigmoid(y @ lp['s5_w_glu'].astype(f32))
    return y.astype(u.dtype), jnp.stack(finals, axis=1)


def rglru_mixer(xr, gate, lp, h0):
    bsz, length, _ = xr.shape
    f32 = jnp.float32
    pad_l = LRU_CONV // 2
    xc = dw_conv(xr, lp['lru_conv_w'], lp['lru_conv_b'], pad_l, LRU_CONV - 1 - pad_l).astype(f32)
    xb = xc.reshape(bsz, length, LRU_BLOCKS, LRU_BLOCK)

    def block_diag_gate(w, b):
        y = jnp.einsum('blhi,dhij->dblhj', xb, w.astype(f32)).reshape(2, bsz, length, LRU_WIDTH)
        return jax.nn.sigmoid(y + b.astype(f32)[:, None, None, :])

    r = block_diag_gate(lp['lru_w_a'], lp['lru_b_a'])
    i = block_diag_gate(lp['lru_w_x'], lp['lru_b_x'])
    log_a = -LRU_C * r * jax.nn.softplus(-lp['lru_lam'].astype(f32))[:, None, None, :]
    a = jnp.exp(log_a)
    b = jnp.sqrt(-jnp.expm1(2.0 * log_a)) * (i * xc[None])
    h_f, last_f = linear_scan(a[0], b[0], h0[:, 0].astype(f32), False)
    h_b, last_b = linear_scan(a[1], b[1], h0[:, 1].astype(f32), True)
    y = (h_f + h_b) * jax.nn.gelu(gate.astype(f32))
    return y.astype(xr.dtype), jnp.stack([last_f, last_b], axis=1)


def token_mixer(h, lp, prefix):
    bsz, length, _ = h.shape
    z = h @ lp['w_in']
    q, k, v, u, xr, xg, g = jnp.split(z, IN_SPLITS, axis=-1)
    heads = (bsz, length, NA_HEADS, HEAD_DIM)
    q, k, v = q.reshape(heads), k.reshape(heads), v.reshape(heads)
    if prefix is None:
        attn = context_attention(q, k, v)
        s5_h0 = jnp.zeros((bsz, 2, S5_GROUPS, S5_STATE), jnp.complex64)
        lru_h0 = jnp.zeros((bsz, 2, LRU_WIDTH), jnp.float32)
    else:
        k_ctx, v_ctx, s5_h0, lru_h0 = prefix
        attn = neighbourhood_attention(q, k, v, k_ctx, v_ctx, lp['rpb'])
    s5_y, s5_last = s5_mixer(u, lp, s5_h0)
    lru_y, lru_last = rglru_mixer(xr, xg, lp, lru_h0)
    gates = jax.nn.sigmoid(g.reshape(bsz, length, N_BRANCH, D_MODEL).astype(jnp.float32)).astype(h.dtype)
    merged = (gates[:, :, 0] * (attn @ lp['w_br_attn'])
              + gates[:, :, 1] * (s5_y @ lp['w_br_s5'])
              + gates[:, :, 2] * (lru_y @ lp['w_br_lru']))
    return merged @ lp['w_out'], (k, v, s5_last, lru_last)


def conv_ffn(h, lp):
    a, b = jnp.split(h @ lp['ffn_w_up'], 2, axis=-1)
    a = dw_conv(a, lp['ffn_conv_w'], lp['ffn_conv_b'], FFN_CONV // 2, FFN_CONV - 1 - FFN_CONV // 2)
    return (jax.nn.gelu(a) * b) @ lp['ffn_w_down']


def layer(x, cond, lp, prefix):
    mod = jax.nn.silu(cond) @ lp['w_ada'] + lp['b_ada']
    sh1, sc1, gt1, sh2, sc2, gt2 = jnp.split(mod[:, None, :], 6, axis=-1)
    h = rms_norm(x, lp['g1']) * (1 + sc1) + sh1
    m, ctx_state = token_mixer(h, lp, prefix)
    x = x + gt1 * m
    h = rms_norm(x, lp['g2']) * (1 + sc2) + sh2
    x = x + gt2 * conv_ffn(h, lp)
    return x, ctx_state


def setup_inputs(seed: int = 0) -> dict:
    key = jax.random.key(seed)
    keys = iter(jax.random.split(key, 64))
    f32 = jnp.float32

    def nrm(shape, scale=1.0):
        return scale * jax.random.normal(next(keys), shape, f32)

    def unif(shape, lo, hi):
        return jax.random.uniform(next(keys), shape, f32, lo, hi)

    L = DEPTH
    n_idx = jnp.arange(S5_STATE, dtype=f32)
    lru_s = unif((L, 2, LRU_WIDTH), 0.9, 0.999) ** (1.0 / LRU_C)
    return {
        'x_prompt': nrm((BATCH, SEQ, D_MODEL)),
        'x_sample': nrm((DEC_BATCH, DEC_SEQ, D_MODEL)),
        'cache_k': nrm((DEC_BATCH, L, PAST_LEN, NA_HEADS, HEAD_DIM)),
        'cache_v': nrm((DEC_BATCH, L, PAST_LEN, NA_HEADS, HEAD_DIM)),
        'state_s5_re': nrm((DEC_BATCH, L, 2, S5_GROUPS, S5_STATE), 0.1),
        'state_s5_im': nrm((DEC_BATCH, L, 2, S5_GROUPS, S5_STATE), 0.1),
        'state_lru': nrm((DEC_BATCH, L, 2, LRU_WIDTH), 0.5),
        'c': nrm((DEC_BATCH, D_MODEL)),
        'c_ctx': nrm((D_MODEL,)),
        'w_ada': nrm((L, D_MODEL, 6 * D_MODEL), 0.5 * D_MODEL ** -0.5),
        'b_ada': nrm((L, 6 * D_MODEL), 0.01),
        'g_norm1': 1.0 + nrm((L, D_MODEL), 0.01),
        'g_norm2': 1.0 + nrm((L, D_MODEL), 0.01),
        'w_in': nrm((L, D_MODEL, IN_WIDTH), D_MODEL ** -0.5),
        'rpb': nrm((L, NA_HEADS, 2 * NA_MAX_ROWS - 1, 2 * NA_COLS - 1), 0.1),
        's5_lam_re': -0.5 + nrm((L, 2, S5_GROUPS, S5_STATE), 0.01),
        's5_lam_im': math.pi * n_idx + nrm((L, 2, S5_GROUPS, S5_STATE), 0.01),
        's5_log_step': jnp.log(unif((L, 2, S5_GROUPS), 0.001, 0.1)),
        's5_b_re': nrm((L, S5_GROUPS, S5_STATE, S5_GROUP_CH), (2 * S5_GROUP_CH) ** -0.5),
        's5_b_im': nrm((L, S5_GROUPS, S5_STATE, S5_GROUP_CH), (2 * S5_GROUP_CH) ** -0.5),
        's5_c_re': nrm((L, 2, S5_GROUPS, S5_GROUP_CH, S5_STATE), S5_STATE ** -0.5),
        's5_c_im': nrm((L, 2, S5_GROUPS, S5_GROUP_CH, S5_STATE), S5_STATE ** -0.5),
        's5_d': nrm((L, S5_WIDTH)),
        's5_w_glu': nrm((L, S5_WIDTH, S5_WIDTH), S5_WIDTH ** -0.5),
        'lru_conv_w': nrm((L, LRU_CONV, LRU_WIDTH), LRU_CONV ** -0.5),
        'lru_conv_b': nrm((L, LRU_WIDTH), 0.01),
        'lru_w_a': nrm((L, 2, LRU_BLOCKS, LRU_BLOCK, LRU_BLOCK), LRU_BLOCK ** -0.5),
        'lru_b_a': nrm((L, 2, LRU_WIDTH), 0.01),
        'lru_w_x': nrm((L, 2, LRU_BLOCKS, LRU_BLOCK, LRU_BLOCK), LRU_BLOCK ** -0.5),
        'lru_b_x': nrm((L, 2, LRU_WIDTH), 0.01),
        'lru_lam': jnp.log(lru_s) - jnp.log1p(-lru_s),
        'w_br_attn': nrm((L, NA_WIDTH, D_MODEL), NA_WIDTH ** -0.5),
        'w_br_s5': nrm((L, S5_WIDTH, D_MODEL), S5_WIDTH ** -0.5),
        'w_br_lru': nrm((L, LRU_WIDTH, D_MODEL), LRU_WIDTH ** -0.5),
        'w_out': nrm((L, D_MODEL, D_MODEL), D_MODEL ** -0.5),
        'ffn_w_up': nrm((L, D_MODEL, 2 * FFN_DIM), D_MODEL ** -0.5),
        'ffn_conv_w': nrm((L, FFN_CONV, FFN_DIM), FFN_CONV ** -0.5),
        'ffn_conv_b': nrm((L, FFN_DIM), 0.01),
        'ffn_w_down': nrm((L, FFN_DIM, D_MODEL), FFN_DIM ** -0.5),
        'g_final': 1.0 + nrm((D_MODEL,), 0.01),
    }


def reference(x_prompt, x_sample, cache_k, cache_v, state_s5_re, state_s5_im, state_lru, c, c_ctx,
              w_ada, b_ada, g_norm1, g_norm2, w_in, rpb,
              s5_lam_re, s5_lam_im, s5_log_step, s5_b_re, s5_b_im, s5_c_re, s5_c_im, s5_d, s5_w_glu,
              lru_conv_w, lru_conv_b, lru_w_a, lru_b_a, lru_w_x, lru_b_x, lru_lam,
              w_br_attn, w_br_s5, w_br_lru, w_out,
              ffn_w_up, ffn_conv_w, ffn_conv_b, ffn_w_down, g_final):
    f32 = jnp.float32
    yp, ys = x_prompt, x_sample
    ks, vs, s5r, s5i, lrus = [], [], [], [], []
    for l in range(DEPTH):
        lp = {
            'w_ada': w_ada[l], 'b_ada': b_ada[l], 'g1': g_norm1[l], 'g2': g_norm2[l],
            'w_in': w_in[l], 'rpb': rpb[l],
            's5_lam_re': s5_lam_re[l], 's5_lam_im': s5_lam_im[l], 's5_log_step': s5_log_step[l],
            's5_b_re': s5_b_re[l], 's5_b_im': s5_b_im[l], 's5_c_re': s5_c_re[l], 's5_c_im': s5_c_im[l],
            's5_d': s5_d[l], 's5_w_glu': s5_w_glu[l],
            'lru_conv_w': lru_conv_w[l], 'lru_conv_b': lru_conv_b[l],
            'lru_w_a': lru_w_a[l], 'lru_b_a': lru_b_a[l], 'lru_w_x': lru_w_x[l], 'lru_b_x': lru_b_x[l],
            'lru_lam': lru_lam[l],
            'w_br_attn': w_br_attn[l], 'w_br_s5': w_br_s5[l], 'w_br_lru': w_br_lru[l], 'w_out': w_out[l],
            'ffn_w_up': ffn_w_up[l], 'ffn_conv_w': ffn_conv_w[l], 'ffn_conv_b': ffn_conv_b[l],
            'ffn_w_down': ffn_w_down[l],
        }
        yp, (k_l, v_l, s5_last, lru_last) = layer(yp, c_ctx[None], lp, None)
        ks.append(k_l)
        vs.append(v_l)
        s5r.append(s5_last.real)
        s5i.append(s5_last.imag)
        lrus.append(lru_last)
        prefix = (cache_k[:, l], cache_v[:, l],
                  lax.complex(state_s5_re[:, l].astype(f32), state_s5_im[:, l].astype(f32)),
                  state_lru[:, l])
        ys, _ = layer(ys, c, lp, prefix)
    y_prompt = rms_norm(yp, g_final)
    y_sample = rms_norm(ys, g_final)
    return (y_prompt, y_sample, jnp.stack(ks, axis=1), jnp.stack(vs, axis=1),
            jnp.stack(s5r, axis=1), jnp.stack(s5i, axis=1), jnp.stack(lrus, axis=1))
```

```python
import math
import types as _types
import numpy as np
from contextlib import ExitStack
import concourse.bass as bass
import concourse.mybir as mybir
from concourse.bass_utils import run_bass_kernel_spmd

F32 = mybir.dt.float32
BF16 = mybir.dt.bfloat16
I32 = mybir.dt.int32
AF = mybir.ActivationFunctionType
ALU = mybir.AluOpType

NCORES = 8
D = 1024
TG = 1024
DEPTH = 2
NEG = -30000.0
EPS = 1e-6
IN_W = 5376
PAGE = 512


class Buf:
    __slots__ = ("name", "lw", "rd", "dsem", "dcnt", "excl")

    def __init__(self, name, excl=False):
        self.name = name
        self.excl = excl
        self.lw = None
        self.rd = []
        self.dsem = None
        self.dcnt = 0


class Reg:
    __slots__ = ("ap", "bufs", "tag")

    def __init__(self, ap, bufs, tag=None):
        self.ap = ap
        self.bufs = bufs
        self.tag = tag

    def __getitem__(self, k):
        return self.ap[k]


def _freeze(f):
    if getattr(f, "__closure__", None) is None:
        return f
    cells = []
    for c in f.__closure__:
        try:
            cells.append(_types.CellType(c.cell_contents))
        except ValueError:
            cells.append(c)
    return _types.FunctionType(f.__code__, f.__globals__, f.__name__, f.__defaults__, tuple(cells))


class Eng:
    def __init__(self, name):
        self.key = name
        self.cnt = 0
        self.seen = {}
        self.prog = []


class FW:
    def __init__(self, nc, stack):
        self.nc = nc
        self.stack = stack
        self.sems = {}
        self.E = {}
        for n in ("pe", "act", "dve", "pool", "sp"):
            self.sems[n] = stack.enter_context(nc.semaphore("s_" + n))
            self.E[n] = Eng(n)
        self.ndsem = 0
        self.dsem_free = []
        self.final = []

    def _expand(self, lst):
        out = []
        for r in lst:
            if isinstance(r, Buf):
                out.append(r)
            else:
                out.extend(r.bufs)
        return out

    def _need(self, eng, dep, same_ok):
        key, val, clock = dep
        if same_ok and key == eng.key:
            return
        if eng.seen.get(key, 0) >= val:
            return
        eng.prog.append(("wait", key, val))
        eng.seen[key] = val
        for k, v in clock.items():
            if eng.seen.get(k, 0) < v:
                eng.seen[k] = v

    def _deps(self, eng, reads, writes):
        for b in reads:
            if b.lw is not None:
                self._need(eng, b.lw, False)
            if b.excl:
                for r in b.rd:
                    self._need(eng, r, True)
        for b in writes:
            if b.lw is not None:
                self._need(eng, b.lw, True)
            for r in b.rd:
                self._need(eng, r, True)

    def _commit(self, dep, reads, writes):
        for b in reads:
            b.rd.append(dep)
        for b in writes:
            b.lw = dep
            b.rd = []

    def op(self, en, fn, reads=(), writes=()):
        eng = self.E[en]
        reads = self._expand(reads)
        writes = self._expand(writes)
        self._deps(eng, reads, writes)
        eng.cnt += 1
        eng.prog.append(("ins", _freeze(fn), eng.key, 1))
        clock = dict(eng.seen)
        clock[eng.key] = eng.cnt
        dep = (eng.key, eng.cnt, clock)
        self._commit(dep, reads, writes)
        return dep

    def mm(self, fns, reads=(), writes=()):
        eng = self.E["pe"]
        reads = self._expand(reads)
        writes = self._expand(writes)
        self._deps(eng, reads, writes)
        for f in fns[:-1]:
            eng.prog.append(("ins", _freeze(f), None, 0))
        eng.cnt += 1
        eng.prog.append(("ins", _freeze(fns[-1]), eng.key, 1))
        clock = dict(eng.seen)
        clock[eng.key] = eng.cnt
        dep = (eng.key, eng.cnt, clock)
        self._commit(dep, reads, writes)
        return dep

    def dma(self, qn, fn, dbuf, reads=(), writes=()):
        eng = self.E[qn]
        reads = self._expand(reads)
        writes = self._expand(writes)
        self._deps(eng, reads, writes)
        if dbuf.dsem is None:
            self.ndsem += 1
            dbuf.dsem = self.stack.enter_context(self.nc.semaphore(f"d{self.ndsem}"))
        if dbuf.dcnt:
            self._need(eng, ("D%d" % id(dbuf), dbuf.dcnt, {}), False)
        dbuf.dcnt += 16
        key = "D%d" % id(dbuf)
        self.sems[key] = dbuf.dsem
        eng.prog.append(("ins", _freeze(fn), key, 16))
        dep = (key, dbuf.dcnt, dict(eng.seen))
        self._commit(dep, reads, writes)
        return dep

    def emit(self):
        nc = self.nc
        sems = self.sems
        for d in self.final:
            self._need(self.E["sp"], d, False)

        def replay(eng, h):
            for it in eng.prog:
                if it[0] == "wait":
                    h.wait_ge(sems[it[1]], it[2])
                else:
                    ins = it[1](h)
                    if it[2] is not None:
                        ins.then_inc(sems[it[2]], it[3])

        with nc.Block() as block:
            @block.sync
            def _(h):
                replay(self.E["sp"], h)

            @block.scalar
            def _(h):
                replay(self.E["act"], h)

            @block.vector
            def _(h):
                replay(self.E["dve"], h)

            @block.gpsimd
            def _(h):
                replay(self.E["pool"], h)

            @block.tensor
            def _(h):
                replay(self.E["pe"], h)


class Arena:
    def __init__(self, fw, nbytes):
        self.fw = fw
        self.nbytes = nbytes
        self.t = fw.stack.enter_context(fw.nc.sbuf_tensor("arena", [128, nbytes // 2], BF16))
        self.pages = [Buf(f"pg{i}") for i in range((nbytes + PAGE - 1) // PAGE)]
        self.top = 0
        self.peak = 0

    def alloc(self, shape, dt, align=64):
        esz = 4 if dt in (F32, I32) else 2
        n = 1
        for s in shape[1:]:
            n *= s
        nb = n * esz
        off = (self.top + align - 1) // align * align
        assert off + nb <= self.nbytes, f"arena overflow: need {off + nb} have {self.nbytes}"
        self.top = off + nb
        self.peak = max(self.peak, self.top)
        ap = self.t[0:shape[0], off // 2:(off + nb) // 2]
        if esz == 4:
            ap = ap.bitcast(dt)
        if len(shape) > 2:
            names = " ".join(f"d{i}" for i in range(1, len(shape)))
            kw = {f"d{i}": shape[i] for i in range(1, len(shape))}
            ap = ap.rearrange(f"p ({names}) -> p {names}", **kw)
        pages = self.pages[off // PAGE:(off + nb - 1) // PAGE + 1]
        return Reg(ap, pages)

    def mark(self):
        return self.top

    def release(self, m):
        self.top = m


def na_valid(kr, qr):
    w0 = min(max(qr - 4, 0), 8)
    return w0 <= kr < w0 + 8


class _Stop(Exception):
    pass


class Builder:
    def __init__(self, debug=False, stage=None):
        self.debug = debug
        self.stage = stage
        self.dbg_outs = []
        self.nc = bass.Bass("TRN2", target_bir_lowering=False)
        self.stack = ExitStack()
        self.dram = {}

    def din(self, name, shape):
        t = self.nc.dram_tensor(name, list(shape), F32, kind="ExternalInput")
        self.dram[name] = t
        return t.ap()

    def dout(self, name, shape):
        t = self.nc.dram_tensor(name, list(shape), F32, kind="ExternalOutput")
        self.dram[name] = t
        return t.ap()

    def build(self):
        nc = self.nc
        with self.stack as st:
            fw = self.fw = FW(nc, st)
            I = self.I = {}
            O = self.O = {}
            I["xp"] = self.din("xp", [TG, D])
            I["xs"] = self.din("xs", [TG, D])
            I["ck"] = self.din("ck", [DEPTH, 512, 512])
            I["cv"] = self.din("cv", [DEPTH, 512, 512])
            I["s5r"] = self.din("s5r", [DEPTH, 2, 1024])
            I["s5i"] = self.din("s5i", [DEPTH, 2, 1024])
            I["slru"] = self.din("slru", [DEPTH, 2, 256])
            I["cvec"] = self.din("cvec", [2, D])
            wshapes = {
                "w_ada": [2, D, 6 * D], "b_ada": [2, 6 * D], "g_norm1": [2, D], "g_norm2": [2, D],
                "w_in": [2, D, IN_W], "rpb": [2, 8, 15, 31],
                "s5_lam_re": [2, 2, 16, 64], "s5_lam_im": [2, 2, 16, 64], "s5_log_step": [2, 2, 16],
                "s5_b_re": [2, 16, 64, 16], "s5_b_im": [2, 16, 64, 16],
                "s5_c_re": [2, 2, 16, 16, 64], "s5_c_im": [2, 2, 16, 16, 64],
                "s5_d": [2, 256], "s5_w_glu": [2, 256, 256],
                "lru_conv_w": [2, 4, 256], "lru_conv_b": [2, 256],
                "lru_w_a": [2, 2, 4, 64, 64], "lru_b_a": [2, 2, 256],
                "lru_w_x": [2, 2, 4, 64, 64], "lru_b_x": [2, 2, 256], "lru_lam": [2, 2, 256],
                "w_br_attn": [2, 512, D], "w_br_s5": [2, 256, D], "w_br_lru": [2, 256, D],
                "w_out": [2, D, D], "ffn_w_up": [2, D, 5632], "ffn_conv_w": [2, 3, 2816],
                "ffn_conv_b": [2, 2816], "ffn_w_down": [2, 2816, D], "g_final": [D],
            }
            self.wshapes = wshapes
            for k, s in wshapes.items():
                I[k] = self.din(k, s)
            O["yp"] = self.dout("yp", [TG, D])
            O["ys"] = self.dout("ys", [TG, D])
            O["nk"] = self.dout("nk", [4, DEPTH, 256, 512])
            O["nv"] = self.dout("nv", [4, DEPTH, 256, 512])
            O["ns5r"] = self.dout("ns5r", [4, DEPTH, 2, 1024])
            O["ns5i"] = self.dout("ns5i", [4, DEPTH, 2, 1024])
            O["nlru"] = self.dout("nlru", [4, DEPTH, 2, 256])

            self.ar = Arena(fw, 207 * 1024)
            ps_t = st.enter_context(nc.psum_tensor("psum", [128, 8, 512], F32))
            self.banks = [Reg(ps_t[:, i, :], [Buf(f"bank{i}", excl=True)], i) for i in range(8)]
            self.bank_i = 0
            self.held = set()
            try:
                self.setup()
                self.mod_init()
                self.chk("setup")
                for l in range(DEPTH):
                    if l == 0:
                        self.prep_bg = False
                        for _ in self.layer_prep_gen(0):
                            pass
                        self.bg = self.mod_gen(0, look=4)
                        self.drain()
                    self.chk(f"prep{l}")
                    for g in range(2):
                        self.group_layer(l, g)
                        self.chk(f"gl{l}{g}")
                for g in range(2):
                    self.final_norm(g)
            except _Stop:
                pass
            fw.emit()
        return nc

    def chk(self, name):
        if self.stage == name:
            raise _Stop()

    def dbg(self, name, reg, ap, shape):
        t = self.nc.dram_tensor("dbg_" + name, list(shape), F32, kind="ExternalOutput").ap()
        d = self.fw.dma("sp", lambda h: h.dma_start(out=t, in_=ap), Buf("dbg_" + name), reads=[reg])
        self.fw.final.append(d)
        self.dbg_outs.append("dbg_" + name)

    def bank(self, hold=False):
        while True:
            i = self.bank_i % 8
            self.bank_i += 1
            if i not in self.held:
                break
        if hold:
            self.held.add(i)
        return self.banks[i]

    def unhold(self, bk):
        self.held.discard(bk.tag)

    def col(self, reg, i):
        return reg.ap[:, i:i + 1]

    def setup(self):
        fw, ar, nc, I = self.fw, self.ar, self.nc, self.I
        self.x = ar.alloc([128, 2, 8, TG], F32)
        self.ident_bf = ar.alloc([128, 128], BF16, align=PAGE)
        self.ident_f = ar.alloc([128, 128], F32)
        self.ones_bf = ar.alloc([128, 128], BF16)
        self.epsc = ar.alloc([128, 1], F32)
        self.gcols = ar.alloc([128, 5, 8], F32)
        self.cTb = ar.alloc([128, 8, 2], BF16)
        self.badaT = ar.alloc([128, 2, 48], F32)
        self.mod = ar.alloc([128, 2, 48, 2], F32, align=PAGE)
        self.A1 = ar.alloc([128, 2, 8, 2], F32)
        self.A2 = ar.alloc([128, 2, 8, 2], F32)
        self.s5cols = ar.alloc([128, 16, 8], F32, align=PAGE)
        self.lrucols = ar.alloc([128, 2, 16], F32)
        self.s5d = ar.alloc([128, 2], F32)
        self.ffcols = ar.alloc([128, 22, 4], F32)
        self.lrust0 = ar.alloc([128, 2, 2], F32)
        self.lruw = ar.alloc([128, 8, 128], BF16)
        self.wglu = ar.alloc([128, 2, 256], BF16)
        self.s5w = ar.alloc([128, 16, 4, 128], BF16, align=PAGE)
        self.outst = ar.alloc([128, 4, 2, 8, 2], F32, align=PAGE)
        self.lrust = ar.alloc([128, 4, 2, 2], F32)
        self.slots = [ar.alloc([128, 2048], BF16, align=PAGE) for _ in range(6)]
        self.slot_sem = [Buf(f"slotsem{i}") for i in range(6)]
        self.slot_i = 0
        self.stg = [ar.alloc([128, 512], F32, align=PAGE) for _ in range(2)]
        self.stg_sem = [Buf("stg0"), Buf("stg1")]
        self.stg_i = 0
        self.small_sem = Buf("small")
        self.h = ar.alloc([128, 8, TG], BF16, align=PAGE)
        self.bg = None
        fw.op("pool", lambda h: h.memset(self.epsc.ap, EPS), writes=[self.epsc])
        self.scr_mark = ar.mark()

        idb, idf, ones = self.ident_bf, self.ident_f, self.ones_bf
        fw.op("pool", lambda h: h.memset(idf.ap, 1.0), writes=[idf])
        fw.op("pool", lambda h: h.affine_select(out=idf.ap, in_=idf.ap, pattern=[[-1, 128]], compare_op=ALU.is_equal,
                                                fill=0.0, base=0, channel_multiplier=1), reads=[idf], writes=[idf])
        fw.op("dve", lambda h: h.tensor_copy(out=idb.ap, in_=idf.ap), reads=[idf], writes=[idb])
        fw.op("dve", lambda h: h.memset(ones.ap, 1.0), writes=[ones])

        for g, name in enumerate(("xp", "xs")):
            for tt in range(8):
                for half in range(2):
                    s = self.stg_i % 2
                    self.stg_i += 1
                    stg = self.stg[s]
                    src = I[name][tt * 128:(tt + 1) * 128, half * 512:(half + 1) * 512]
                    fw.dma("sp", lambda h, stg=stg, src=src: h.dma_start(out=stg.ap, in_=src), self.stg_sem[s], writes=[stg])
                    bk = self.bank()
                    fw.mm([lambda h, bk=bk, stg=stg, j=j: h.transpose(out=bk.ap[:, j * 128:(j + 1) * 128], in_=stg.ap[:, j * 128:(j + 1) * 128],
                                                                      identity=idf.ap) for j in range(4)], reads=[stg, idf], writes=[bk])
                    dstv = self.x.ap[:, g, half * 4:half * 4 + 4, tt * 128:(tt + 1) * 128]
                    srcv = bk.ap.rearrange("p (j t) -> p j t", j=4)
                    if half:
                        fw.op("act", lambda h, dstv=dstv, srcv=srcv: h.activation(out=dstv, in_=srcv, func=AF.Copy), reads=[bk], writes=[self.x])
                    else:
                        fw.op("dve", lambda h, dstv=dstv, srcv=srcv: h.tensor_copy(out=dstv, in_=srcv), reads=[bk], writes=[self.x])

    def small_load(self, dst_ap, src_ap, dst_reg):
        return self.fw.dma("sp", lambda h: h.dma_start(out=dst_ap, in_=src_ap, allow_slow_non_contiguous=True),
                           self.small_sem, writes=[dst_reg])

    def wslice(self, parts):
        s = self.slot_i % 6
        self.slot_i += 1
        slot = self.slots[s]
        for (off, kc, ncols, src) in parts:
            dst = slot.ap[:, off:off + kc * ncols].rearrange("p (k n) -> p k n", k=kc)
            sv = src.rearrange("(k p) n -> p k n", p=128)
            self.fw.dma("pool", lambda h, dst=dst, sv=sv: h.dma_start(out=dst, in_=sv), self.slot_sem[s], writes=[slot])
        return slot

    def mod_init(self):
        fw, ar, I = self.fw, self.ar, self.I
        m = ar.mark()
        cT = ar.alloc([128, 8, 2], F32)
        for cc in range(2):
            self.small_load(cT.ap[:, :, cc], I["cvec"][cc].rearrange("(k p) -> p k", p=128), cT)
            self.small_load(self.badaT.ap[:, cc, :], I["b_ada"][cc].rearrange("(t p) -> p t", p=128), self.badaT)
            self.small_load(self.gcols.ap[:, cc, :], I["g_norm1"][cc].rearrange("(k p) -> p k", p=128), self.gcols)
            self.small_load(self.gcols.ap[:, 2 + cc, :], I["g_norm2"][cc].rearrange("(k p) -> p k", p=128), self.gcols)
        self.small_load(self.gcols.ap[:, 4, :], I["g_final"].rearrange("(k p) -> p k", p=128), self.gcols)
        fw.op("act", lambda h: h.activation(out=self.cTb.ap, in_=cT.ap, func=AF.Silu), reads=[cT], writes=[self.cTb])
        ar.release(m)

    def mod_gen(self, l, look=2):
        fw, I = self.fw, self.I
        cTb, badaT = self.cTb, self.badaT
        pend = []
        nxt = 0
        for s in range(24):
            while nxt < 24 and nxt <= s + look:
                pend.append(self.wslice([(0, 8, 256, I["w_ada"][l][:, nxt * 256:(nxt + 1) * 256])]))
                nxt += 1
            slot = pend.pop(0)
            sv = slot.ap.rearrange("p (k n) -> p k n", k=8)
            bk = self.bank()
            for t in range(2):
                fw.mm([lambda h, bk=bk, sv=sv, t=t, kc=kc: h.matmul(bk.ap[:, t * 2:t * 2 + 2], lhsT=sv[:, kc, t * 128:(t + 1) * 128],
                                                                    rhs=cTb.ap[:, kc, :], start=(kc == 0), stop=(kc == 7)) for kc in range(8)],
                      reads=[slot, cTb], writes=[bk])
            fw.op("dve", lambda h, bk=bk, l=l, s=s: h.tensor_tensor(out=self.mod.ap[:, l, 2 * s:2 * s + 2, :],
                                                                   in0=bk.ap[:, 0:4].rearrange("p (t c) -> p t c", t=2),
                                                                   in1=badaT.ap[:, l, 2 * s:2 * s + 2].unsqueeze(2).to_broadcast([128, 2, 2]),
                                                                   op=ALU.add), reads=[bk, badaT], writes=[self.mod])
            yield
        for (A, comp, gi) in ((self.A1, 1, 0), (self.A2, 4, 2)):
            fw.op("dve", lambda h, A=A, comp=comp, gi=gi, l=l: h.scalar_tensor_tensor(
                out=A.ap[:, l], in0=self.mod.ap[:, l, comp * 8:comp * 8 + 8, :], scalar=1.0,
                in1=self.gcols.ap[:, gi + l, :].unsqueeze(2).to_broadcast([128, 8, 2]), op0=ALU.add, op1=ALU.mult),
                reads=[self.mod, self.gcols], writes=[A])

    def tick(self):
        if self.bg is not None:
            try:
                next(self.bg)
            except StopIteration:
                self.bg = None

    def drain(self):
        while self.bg is not None:
            self.tick()

    def rms_rstd(self, g, rstd):
        fw, ar = self.fw, self.ar
        m = ar.mark()
        sq = [ar.alloc([128, 512], BF16) for _ in range(2)]
        tmp = ar.alloc([128, 512], F32)
        for tb in range(2):
            bk = self.bank()
            for kc in range(8):
                q = sq[kc % 2]
                fw.op("act", lambda h, q=q, kc=kc, tb=tb: h.activation(out=q.ap, in_=self.x.ap[:, g, kc, tb * 512:(tb + 1) * 512], func=AF.Square),
                      reads=[self.x], writes=[q])
                fw.mm([lambda h, bk=bk, q=q, kc=kc: h.matmul(bk.ap, lhsT=self.ones_bf.ap, rhs=q.ap, start=(kc == 0), stop=(kc == 7))],
                      reads=[q, self.ones_bf], writes=[bk])
            fw.op("act", lambda h, bk=bk: h.activation(out=tmp.ap, in_=bk.ap, func=AF.Sqrt, scale=1.0 / D, bias=self.epsc.ap[:, 0:1]),
                  reads=[bk, self.epsc], writes=[tmp])
            fw.op("dve", lambda h, tb=tb: h.reciprocal(out=rstd.ap[:, tb * 512:(tb + 1) * 512], in_=tmp.ap), reads=[tmp], writes=[rstd])
        ar.release(m)

    def norm_mod(self, l, g, A, shcomp):
        fw, ar = self.fw, self.ar
        m = ar.mark()
        rstd = ar.alloc([128, TG], F32)
        self.rms_rstd(g, rstd)
        tmps = [ar.alloc([128, TG], F32) for _ in range(2)]
        for kc in range(8):
            t = tmps[kc % 2]
            fw.op("dve", lambda h, t=t, kc=kc: h.scalar_tensor_tensor(out=t.ap, in0=self.x.ap[:, g, kc, :], scalar=A.ap[:, l, kc, g:g + 1],
                                                                      in1=rstd.ap, op0=ALU.mult, op1=ALU.mult),
                  reads=[self.x, A, rstd], writes=[t])
            fw.op("act", lambda h, t=t, kc=kc: h.activation(out=self.h.ap[:, kc, :], in_=t.ap, func=AF.Identity,
                                                            bias=self.mod.ap[:, l, shcomp * 8 + kc, g:g + 1], scale=1.0),
                  reads=[t, self.mod], writes=[self.h])
        ar.release(m)

    def proj_fm(self, src_cols_fn, ntiles, rhs, rhs_regs, kcs, consume):
        fw = self.fw
        for sp in range(0, ntiles, 2):
            nt = min(2, ntiles - sp)
            slot = self.wslice(src_cols_fn(sp * 128, nt * 128))
            sv = slot.ap[:, 0:kcs * nt * 128].rearrange("p (k n) -> p k n", k=kcs)
            for t in range(nt):
                for tb in range(2):
                    bk = self.bank()
                    fw.mm([lambda h, bk=bk, sv=sv, t=t, tb=tb, kc=kc: h.matmul(bk.ap, lhsT=sv[:, kc, t * 128:(t + 1) * 128], rhs=rhs(kc, tb),
                                                                                start=(kc == 0), stop=(kc == kcs - 1)) for kc in range(kcs)],
                          reads=[slot] + rhs_regs, writes=[bk])
                    consume(sp + t, tb, bk)

    def win_cols(self, l, base):
        return lambda c0, n: [(0, 8, n, self.I["w_in"][l][:, base + c0:base + c0 + n])]

    def h_rhs(self, kc, tb):
        return self.h.ap[:, kc, tb * 512:(tb + 1) * 512]

    def tt(self, en, out, in0, in1, op, R, W):
        self.fw.op(en, lambda h: h.tensor_tensor(out=out, in0=in0, in1=in1, op=op), reads=R, writes=W)

    def ts(self, en, out, in0, s1, s2, op0, op1, R, W):
        if s2 is None:
            self.fw.op(en, lambda h: h.tensor_scalar(out=out, in0=in0, scalar1=s1, scalar2=None, op0=op0), reads=R, writes=W)
        else:
            self.fw.op(en, lambda h: h.tensor_scalar(out=out, in0=in0, scalar1=s1, scalar2=s2, op0=op0, op1=op1), reads=R, writes=W)

    def stt(self, out, in0, sc, in1, op0, op1, R, W, en="dve"):
        self.fw.op(en, lambda h: h.scalar_tensor_tensor(out=out, in0=in0, scalar=sc, in1=in1, op0=op0, op1=op1), reads=R, writes=W)

    def act(self, out, in_, func, R, W, bias=None, scale=None):
        kw = {}
        if bias is not None:
            kw["bias"] = bias
        if scale is not None:
            kw["scale"] = scale
        self.fw.op("act", lambda h: h.activation(out=out, in_=in_, func=func, **kw), reads=R, writes=W)

    def cp(self, en, out, in_, R, W):
        if en == "act":
            self.fw.op("act", lambda h: h.activation(out=out, in_=in_, func=AF.Copy), reads=R, writes=W)
        else:
            self.fw.op(en, lambda h: h.tensor_copy(out=out, in_=in_), reads=R, writes=W)

    def sincos(self, y, n, cos_out, sin_out, Wc, Ws):
        ar = self.ar
        MAGIC = 12582912.0
        m = ar.mark()
        kf = ar.alloc([128, n], F32)
        fc = ar.alloc([128, n], F32)
        self.ts("dve", kf.ap, y.ap, 0.25, MAGIC, ALU.add, ALU.add, [y], [kf])
        self.ts("dve", kf.ap, kf.ap, MAGIC, None, ALU.subtract, None, [kf], [kf])
        self.stt(fc.ap, y.ap, 0.25, kf.ap, ALU.add, ALU.subtract, [y, kf], [fc])
        self.act(cos_out, fc.ap, AF.Sin, [fc], [Wc], scale=2.0 * math.pi)
        self.ts("dve", kf.ap, y.ap, MAGIC, None, ALU.add, None, [y], [kf])
        self.ts("dve", kf.ap, kf.ap, MAGIC, None, ALU.subtract, None, [kf], [kf])
        self.tt("dve", y.ap, y.ap, kf.ap, ALU.subtract, [y, kf], [y])
        self.act(sin_out, y.ap, AF.Sin, [y], [Ws], scale=2.0 * math.pi)
        ar.release(m)

    _ssem_i = 0

    def sload(self, dst_ap, src_ap, dst_reg, q="sp"):
        if not hasattr(self, "ssems"):
            self.ssems = [Buf(f"ss{i}") for i in range(4)]
        s = self.ssems[Builder._ssem_i % 4]
        Builder._ssem_i += 1
        if q == "sp":
            return self.fw.dma("sp", lambda h: h.dma_start(out=dst_ap, in_=src_ap, allow_slow_non_contiguous=True), s, writes=[dst_reg])
        return self.fw.dma("pool", lambda h: h.dma_start(out=dst_ap, in_=src_ap, allow_slow_non_contiguous=True), s, writes=[dst_reg])

    def layer_prep_gen(self, l):
        fw, ar, I = self.fw, self.ar, self.I
        m = ar.mark()
        c = self.s5cols
        lam_r = ar.alloc([128, 16], F32)
        lam_i = ar.alloc([128, 16], F32)
        stp = ar.alloc([128, 16], F32)
        t1 = ar.alloc([128, 16], F32)
        t2 = ar.alloc([128, 16], F32)
        t3 = ar.alloc([128, 16], F32)
        yv = ar.alloc([128, 16], F32)
        cs = ar.alloc([128, 16], F32)
        sn = ar.alloc([128, 16], F32)
        nat = ar.alloc([128, 4, 8, 16], F32)
        Cn = [ar.alloc([128, 4, 2, 64], F32) for _ in range(2)]
        ct2 = ar.alloc([128, 128], F32)
        lam = ar.alloc([128, 2, 2], F32)
        xx = ar.alloc([128, 2, 2], F32)
        pp = ar.alloc([128, 2, 2], F32)
        msk = ar.alloc([128, 8, 8], F32)
        Br = ar.alloc([128, 8, 16], F32)
        Bi = ar.alloc([128, 8, 16], F32)
        bb = [ar.alloc([128, 2, 8, 16], F32) for _ in range(2)]
        tA = ar.alloc([128, 2, 8, 16], F32)
        self.sload(lam_r.ap, I["s5_lam_re"][l].rearrange("d (j gl) n -> (gl n) (d j)", gl=2), lam_r)
        self.sload(lam_i.ap, I["s5_lam_im"][l].rearrange("d (j gl) n -> (gl n) (d j)", gl=2), lam_i)
        for gl in range(2):
            src = bass.AP(I["s5_log_step"].tensor, l * 32 + gl, [[0, 64], [2, 16]])
            self.sload(stp.ap[64 * gl:64 * gl + 64, :], src, stp)
        self.sload(c.ap[:, :, 6], I["s5r"][l].rearrange("d (j q) -> q (d j)", q=128), c)
        self.sload(c.ap[:, :, 7], I["s5i"][l].rearrange("d (j q) -> q (d j)", q=128), c)
        yield
        self.act(stp.ap, stp.ap, AF.Exp, [stp], [stp])
        self.tt("dve", t1.ap, lam_r.ap, stp.ap, ALU.mult, [lam_r, stp], [t1])
        self.tt("dve", t2.ap, lam_i.ap, stp.ap, ALU.mult, [lam_i, stp], [t2])
        self.act(c.ap[:, :, 1], t1.ap, AF.Exp, [t1], [c])
        self.ts("dve", yv.ap, t2.ap, 1.0 / (2.0 * math.pi), None, ALU.mult, None, [t2], [yv])
        self.sincos(yv, 16, cs.ap, sn.ap, cs, sn)
        self.cp("dve", c.ap[:, :, 0], yv.ap, [yv], [c])
        self.tt("dve", c.ap[:, :, 2], c.ap[:, :, 1], cs.ap, ALU.mult, [c, cs], [c])
        self.tt("dve", c.ap[:, :, 3], c.ap[:, :, 1], sn.ap, ALU.mult, [c, sn], [c])
        yield
        self.ts("dve", t1.ap, c.ap[:, :, 2], -1.0, None, ALU.add, None, [c], [t1])
        self.tt("dve", t2.ap, lam_r.ap, lam_r.ap, ALU.mult, [lam_r], [t2])
        self.tt("dve", t3.ap, lam_i.ap, lam_i.ap, ALU.mult, [lam_i], [t3])
        self.tt("dve", t2.ap, t2.ap, t3.ap, ALU.add, [t2, t3], [t2])
        fw.op("dve", lambda h: h.reciprocal(out=t2.ap, in_=t2.ap), reads=[t2], writes=[t2])
        self.tt("dve", t3.ap, t1.ap, lam_r.ap, ALU.mult, [t1, lam_r], [t3])
        self.tt("dve", yv.ap, c.ap[:, :, 3], lam_i.ap, ALU.mult, [c, lam_i], [yv])
        self.tt("dve", t3.ap, t3.ap, yv.ap, ALU.add, [t3, yv], [t3])
        self.tt("dve", c.ap[:, :, 4], t3.ap, t2.ap, ALU.mult, [t3, t2], [c])
        self.tt("dve", t3.ap, c.ap[:, :, 3], lam_r.ap, ALU.mult, [c, lam_r], [t3])
        self.tt("dve", yv.ap, t1.ap, lam_i.ap, ALU.mult, [t1, lam_i], [yv])
        self.tt("dve", t3.ap, t3.ap, yv.ap, ALU.subtract, [t3, yv], [t3])
        self.tt("dve", c.ap[:, :, 5], t3.ap, t2.ap, ALU.mult, [t3, t2], [c])

        yield
        fw.op("pool", lambda h: h.memset(msk.ap, 1.0), writes=[msk])
        for gl in range(2):
            fw.op("pool", lambda h, gl=gl: h.affine_select(out=msk.ap[64 * gl:64 * gl + 64], in_=msk.ap[64 * gl:64 * gl + 64],
                                                           pattern=[[0, 2], [-2, 4], [1, 8]], compare_op=ALU.is_equal, fill=0.0,
                                                           base=-gl, channel_multiplier=0), reads=[msk], writes=[msk])
        self.sload(Br.ap, I["s5_b_re"][l].rearrange("(j gl) n p -> (gl n) j p", gl=2), Br)
        self.sload(Bi.ap, I["s5_b_im"][l].rearrange("(j gl) n p -> (gl n) j p", gl=2), Bi)
        kr = c.ap[:, :, 4].rearrange("p (d j) -> p d j", d=2).unsqueeze(3).to_broadcast([128, 2, 8, 16])
        ki = c.ap[:, :, 5].rearrange("p (d j) -> p d j", d=2).unsqueeze(3).to_broadcast([128, 2, 8, 16])
        Brb = Br.ap.unsqueeze(1).to_broadcast([128, 2, 8, 16])
        Bib = Bi.ap.unsqueeze(1).to_broadcast([128, 2, 8, 16])
        self.tt("dve", bb[0].ap, Brb, kr, ALU.mult, [Br, c], [bb[0]])
        self.tt("dve", tA.ap, Bib, ki, ALU.mult, [Bi, c], [tA])
        self.tt("dve", bb[0].ap, bb[0].ap, tA.ap, ALU.subtract, [bb[0], tA], [bb[0]])
        self.tt("dve", bb[1].ap, Bib, kr, ALU.mult, [Bi, c], [bb[1]])
        self.tt("dve", tA.ap, Brb, ki, ALU.mult, [Br, c], [tA])
        self.tt("dve", bb[1].ap, bb[1].ap, tA.ap, ALU.add, [bb[1], tA], [bb[1]])
        yield
        for ri in range(2):
            for q4 in range(4):
                d, j0 = q4 // 2, (q4 % 2) * 4
                for jj in range(4):
                    j = j0 + jj
                    self.tt("dve", nat.ap[:, jj], bb[ri].ap[:, d, j].unsqueeze(1).to_broadcast([128, 8, 16]),
                            msk.ap[:, j].unsqueeze(2).to_broadcast([128, 8, 16]), ALU.mult, [bb[ri], msk], [nat])
                yield
                bk = self.bank()
                fw.mm([lambda h, bk=bk, jj=jj: h.transpose(out=bk.ap[:, jj * 128:(jj + 1) * 128],
                                                           in_=nat.ap[:, jj].rearrange("p a b -> p (a b)"), identity=self.ident_f.ap)
                       for jj in range(4)], reads=[nat, self.ident_f], writes=[bk])
                self.cp("act", self.s5w.ap[:, d * 8 + j0:d * 8 + j0 + 4, ri, :], bk.ap.rearrange("p (a b) -> p a b", a=4), [bk], [self.s5w])
        for hf in range(2):
            self.sload(Cn[0].ap[:, :, hf, :], I["s5_c_re"][l].rearrange("d g p n -> (d g p) n").rearrange("(q r) n -> r q n", r=128), Cn[0])
            self.sload(Cn[1].ap[:, :, hf, :], I["s5_c_im"][l].rearrange("d g p n -> (d g p) n").rearrange("(q r) n -> r q n", r=128), Cn[1])
        for ri in range(2):
            for q in range(4):
                d, ut = q // 2, q % 2
                bk = self.bank()
                fw.mm([lambda h, bk=bk, ri=ri, q=q: h.matmul(bk.ap[:, 0:128], lhsT=Cn[ri].ap[:, q].rearrange("p a b -> p (a b)"),
                                                             rhs=self.ident_f.ap, start=True, stop=True)],
                      reads=[Cn[ri], self.ident_f], writes=[bk])
                if ri == 0:
                    self.cp("act", ct2.ap, bk.ap[:, 0:128], [bk], [ct2])
                else:
                    self.act(ct2.ap, bk.ap[:, 0:128], AF.Copy, [bk], [ct2], scale=-1.0)
                j0 = ut * 4
                self.tt("dve", self.s5w.ap[:, d * 8 + j0:d * 8 + j0 + 4, 2 + ri, :].rearrange("p j (g q) -> p j g q", g=8),
                        ct2.ap.rearrange("p (g q) -> p g q", g=8).unsqueeze(1).to_broadcast([128, 4, 8, 16]),
                        msk.ap[:, j0:j0 + 4].unsqueeze(3).to_broadcast([128, 4, 8, 16]), ALU.mult, [ct2, msk], [self.s5w])
                yield
        self.sload(self.s5d.ap, I["s5_d"][l].rearrange("(t p) -> p t", p=128), self.s5d)
        yield
        lc = self.lrucols
        for k in range(4):
            self.sload(lc.ap[:, :, k], I["lru_conv_w"][l, k].rearrange("(t p) -> p t", p=128), lc)
        self.sload(lc.ap[:, :, 4], I["lru_conv_b"][l].rearrange("(t p) -> p t", p=128), lc)
        for d in range(2):
            self.sload(lc.ap[:, :, 5 + d], I["lru_b_a"][l, d].rearrange("(t p) -> p t", p=128), lc)
            self.sload(lc.ap[:, :, 7 + d], I["lru_b_x"][l, d].rearrange("(t p) -> p t", p=128), lc)
        for d in range(2):
            self.sload(lam.ap[:, :, d], I["lru_lam"][l, d].rearrange("(t p) -> p t", p=128), lam)
        self.act(xx.ap, lam.ap, AF.Exp, [lam], [xx], scale=-1.0)
        self.ts("dve", pp.ap, xx.ap, -0.25, 1.0 / 3.0, ALU.mult, ALU.add, [xx], [pp])
        self.tt("dve", pp.ap, pp.ap, xx.ap, ALU.mult, [pp, xx], [pp])
        self.ts("dve", pp.ap, pp.ap, -1.0, 0.5, ALU.mult, ALU.add, [pp], [pp])
        self.tt("dve", pp.ap, pp.ap, xx.ap, ALU.mult, [pp, xx], [pp])
        self.ts("dve", pp.ap, pp.ap, -1.0, 1.0, ALU.mult, ALU.add, [pp], [pp])
        self.tt("dve", pp.ap, pp.ap, xx.ap, ALU.mult, [pp, xx], [pp])
        self.ts("dve", lc.ap[:, :, 9:11], pp.ap, -8.0, None, ALU.mult, None, [pp], [lc])
        self.ts("dve", lc.ap[:, :, 11:13], pp.ap, 8.0, None, ALU.mult, None, [pp], [lc])
        self.ts("dve", lc.ap[:, :, 13:15], pp.ap, -16.0, None, ALU.mult, None, [pp], [lc])
        yield
        fw.op("pool", lambda h: h.memset(self.lruw.ap, 0.0), writes=[self.lruw])
        for gi, nm in enumerate(("lru_w_a", "lru_w_x")):
            for d in range(2):
                for t in range(2):
                    for b2 in range(2):
                        idx = gi * 4 + d * 2 + t
                        self.sload(self.lruw.ap[64 * b2:64 * b2 + 64, idx, 64 * b2:64 * b2 + 64], I[nm][l, d, 2 * t + b2], self.lruw, q="pool")
        for d in range(2):
            self.sload(self.lrust0.ap[:, d, :], I["slru"][l, d].rearrange("(t p) -> p t", p=128), self.lrust0)
        self.sload(self.wglu.ap, I["s5_w_glu"][l].rearrange("(k p) n -> p k n", p=128), self.wglu, q="pool")
        yield
        if not self.prep_bg:
            ar.release(m)

    def ffn_prep(self, l):
        I = self.I
        for k in range(3):
            self.sload(self.ffcols.ap[:, :, k], I["ffn_conv_w"][l, k].rearrange("(t p) -> p t", p=128), self.ffcols)
        self.sload(self.ffcols.ap[:, :, 3], I["ffn_conv_b"][l].rearrange("(t p) -> p t", p=128), self.ffcols)

    def group_layer(self, l, g):
        fw, ar, I, O = self.fw, self.ar, self.I, self.O
        if g == 0:
            self.ffn_prep(l)
        self.norm_mod(l, g, self.A1, 0)
        self.chk(f"norm{l}{g}")
        m0 = ar.mark()
        attnT = ar.alloc([128, 4, TG], BF16)
        m1 = ar.mark()
        self.attention(l, g, attnT)
        self.chk(f"attn{l}{g}")
        ar.release(m1)
        s5y = ar.alloc([128, 2, TG], BF16)
        m1 = ar.mark()
        self.s5(l, g, s5y)
        self.chk(f"s5{l}{g}")
        ar.release(m1)
        if g == 0:
            self.store_s5_states(l)
        lruy = ar.alloc([128, 2, TG], BF16)
        m1 = ar.mark()
        self.lru(l, g, lruy)
        self.chk(f"lru{l}{g}")
        ar.release(m1)
        if g == 1 and l + 1 < DEPTH:
            self.prep_bg = True
            self.bg = self.layer_prep_gen(l + 1)
            self.tick()
        self.merge(l, g, attnT, s5y, lruy)
        self.drain()
        self.chk(f"merge{l}{g}")
        ar.release(m0)
        self.norm_mod(l, g, self.A2, 3)
        self.ffn(l, g)
        ar.release(m0)

    def attention(self, l, g, attnT):
        fw, ar, I, O = self.fw, self.ar, self.I, self.O
        q_sb = ar.alloc([128, 4, TG], BF16)
        k_sb = ar.alloc([128, 4, TG], BF16)
        v_aug = ar.alloc([128, 8, 8, 66], BF16)
        fw.op("pool", lambda h: h.memset(v_aug.ap[:, :, :, 64:65], 1.0), writes=[v_aug])

        def q_cons(ft, tb, bk):
            self.act(q_sb.ap[:, ft, tb * 512:(tb + 1) * 512], bk.ap, AF.Copy, [bk], [q_sb], scale=0.125)

        def k_cons(ft, tb, bk):
            self.cp("dve", k_sb.ap[:, ft, tb * 512:(tb + 1) * 512], bk.ap, [bk], [k_sb])

        self.proj_fm(self.win_cols(l, 0), 4, self.h_rhs, [self.h], 8, q_cons)
        self.proj_fm(self.win_cols(l, 512), 4, self.h_rhs, [self.h], 8, k_cons)
        self.chk(f"attq{l}{g}")
        for which in ((1, 2) if g == 0 else (2,)):
            for half in range(2):
                slot = self.wslice([(0, 8, 256, I["w_in"][l][:, which * 512 + half * 256: which * 512 + half * 256 + 256])])
                sv = slot.ap.rearrange("p (k n) -> p k n", k=8)
                for tt in range(8):
                    bk = self.bank()
                    fw.mm([lambda h, bk=bk, sv=sv, tt=tt, kc=kc: h.matmul(bk.ap[:, 0:256], lhsT=self.h.ap[:, kc, tt * 128:(tt + 1) * 128], rhs=sv[:, kc, :],
                                                                          start=(kc == 0), stop=(kc == 7)) for kc in range(8)],
                          reads=[slot, self.h], writes=[bk])
                    if which == 2:
                        self.cp("act", v_aug.ap[:, tt, half * 4:half * 4 + 4, 0:64], bk.ap[:, 0:256].rearrange("p (a b) -> p a b", a=4), [bk], [v_aug])
                    if g == 0:
                        s = self.stg_i % 2
                        self.stg_i += 1
                        stg = self.stg[s]
                        self.cp("dve", stg.ap[:, 0:256], bk.ap[:, 0:256], [bk], [stg])
                        dst = O["nk" if which == 1 else "nv"][tt // 2, l, (tt % 2) * 128:(tt % 2) * 128 + 128, half * 256:half * 256 + 256]
                        d = fw.dma("sp", lambda h, stg=stg, dst=dst: h.dma_start(out=dst, in_=stg.ap[:, 0:256]), self.stg_sem[s], reads=[stg])
                        fw.final.append(d)
                    self.chk(f"kv1{l}{g}")
            if which == 1:
                self.chk(f"kvK{l}{g}")

        self.chk(f"attkv{l}{g}")
        pT = [ar.alloc([128, 512], BF16) for _ in range(2)]
        pi = [0]
        atok = ar.alloc([128, 8, 128], BF16)
        rec = ar.alloc([128, 8], F32)

        def transposes(hp, tts):
            bk = self.bank()
            bkb = bk.ap.bitcast(BF16)
            fw.mm([lambda h, bkb=bkb, i=i, tt=tt: h.transpose(out=bkb[:, i * 128:(i + 1) * 128], in_=atok.ap[:, tt, :], identity=self.ident_bf.ap)
                   for i, tt in enumerate(tts)], reads=[atok, self.ident_bf], writes=[bk])
            n = len(tts)
            self.cp("dve", attnT.ap[:, hp, tts[0] * 128:(tts[0] + n) * 128], bkb[:, 0:n * 128], [bk], [attnT])

        if g == 0:
            UP = [(sq, hp, e) for sq in range(4) for hp in range(4) for e in range(2)]

            def issue_score_p(k):
                sq, hp, e = UP[k]
                pb = 64 * e
                sb = self.bank()
                fw.mm([lambda h, sb=sb, kt=kt, pb=pb, sq=sq, hp=hp: h.matmul(sb.ap[:, kt * 256:(kt + 1) * 256],
                                                                            lhsT=k_sb.ap[pb:pb + 64, hp, sq * 256 + kt * 128:sq * 256 + kt * 128 + 128],
                                                                            rhs=q_sb.ap[pb:pb + 64, hp, sq * 256:sq * 256 + 256], start=True, stop=True) for kt in range(2)],
                      reads=[k_sb, q_sb], writes=[sb])
                return sb

            nxt = issue_score_p(0)
            ob = ov = None
            for k, (sq, hp, e) in enumerate(UP):
                hh = 2 * hp + e
                if e == 0:
                    ob = self.bank(hold=True)
                    ov = ob.ap[:, 0:260].rearrange("p (q e c) -> p q e c", q=2, e=2)
                sb = nxt
                p = pT[k % 2]
                self.act(p.ap, sb.ap, AF.Exp, [sb], [p])
                if k + 1 < len(UP):
                    nxt = issue_score_p(k + 1)
                fns = []
                for qt in range(2):
                    for kt in range(2):
                        fns.append(lambda h, qt=qt, kt=kt, e=e, hh=hh, p=p, ov=ov, sq=sq, st_=(e == 0 and qt == 0 and kt == 0): h.matmul(
                            ov[:, qt, e, :], lhsT=p.ap[:, kt * 256 + qt * 128:kt * 256 + qt * 128 + 128], rhs=v_aug.ap[:, 2 * sq + kt, hh, 0:65],
                            start=st_, stop=(e == 1 and qt == 1 and kt == 1)))
                fw.mm(fns, reads=[p, v_aug], writes=[ob])
                if e == 1:
                    fw.op("dve", lambda h, ov=ov: h.reciprocal(out=rec.ap[:, 0:4].rearrange("p (q e) -> p q e", q=2), in_=ov[:, :, :, 64]), reads=[ob], writes=[rec])
                    self.tt("dve", atok.ap[:, 2 * sq:2 * sq + 2, :].rearrange("p q (e c) -> p q e c", e=2), ov[:, :, :, 0:64],
                            rec.ap[:, 0:4].rearrange("p (q e) -> p q e", q=2).unsqueeze(3).to_broadcast([128, 2, 2, 64]), ALU.mult, [ob, rec], [atok])
                    self.unhold(ob)
                    transposes(hp, [2 * sq, 2 * sq + 1])
        else:
            self.na_attention(l, attnT, q_sb, k_sb, v_aug, pT, atok, rec, transposes)

    def na_attention(self, l, attnT, q_sb, k_sb, v_aug, pT, atok, rec, transposes):
        fw, ar, I = self.fw, self.ar, self.I
        kctxT = ar.alloc([128, 4, 512], BF16)
        vctx = ar.alloc([128, 4, 8, 66], BF16)
        fw.op("pool", lambda h: h.memset(vctx.ap[:, :, :, 64:65], 1.0), writes=[vctx])
        cvsem = Buf("cvsem")
        for tt in range(4):
            fw.dma("pool", lambda h, tt=tt: h.dma_start(out=vctx.ap[:, tt, :, 0:64], in_=I["cv"][l][tt * 128:(tt + 1) * 128, :].rearrange("p (a b) -> p a b", a=8)),
                   cvsem, writes=[vctx])
        mk = ar.mark()
        cktok = ar.alloc([128, 4, 512], BF16)
        cksem = Buf("cksem")
        fw.dma("pool", lambda h: h.dma_start(out=cktok.ap, in_=I["ck"][l].rearrange("(t p) f -> p t f", p=128)), cksem, writes=[cktok])
        for hp in range(4):
            bk = self.bank()
            bkb = bk.ap.bitcast(BF16)
            fw.mm([lambda h, bkb=bkb, tt=tt, hp=hp: h.transpose(out=bkb[:, tt * 128:(tt + 1) * 128], in_=cktok.ap[:, tt, hp * 128:(hp + 1) * 128],
                                                               identity=self.ident_bf.ap) for tt in range(4)], reads=[cktok, self.ident_bf], writes=[bk])
            self.cp("dve", kctxT.ap[:, hp, :], bkb[:, 0:512], [bk], [kctxT])
        ar.release(mk)
        LT = ar.alloc([128, 8, 18, 64], BF16)
        mk = ar.mark()
        rp = ar.alloc([128, 2, 32], F32)
        fw.op("pool", lambda h: h.memset(rp.ap, 0.0), writes=[rp])
        for e in range(2):
            self.sload(rp.ap[0:120, e, 0:31], I["rpb"][l].rearrange("h a x -> (h a) x"), rp)
        Rs = ar.alloc([64, 8, 15], F32)
        RE = ar.alloc([64, 8, 18], F32)
        BB = ar.alloc([64, 2, 127], F32)
        msk = ar.alloc([128, 64], F32)
        m2 = ar.alloc([128, 64], F32)
        bk = self.bank()
        fw.mm([lambda h, bk=bk: h.matmul(bk.ap[0:64, 0:120], lhsT=rp.ap[0:120].rearrange("p a b -> p (a b)"),
                                         rhs=self.ident_f.ap[0:120, 0:120], start=True, stop=True)], reads=[rp, self.ident_f], writes=[bk])
        fw.op("pool", lambda h: h.memset(Rs.ap, 0.0), writes=[Rs])
        for e in range(2):
            self.cp("dve", Rs.ap[32 * e:32 * e + 31].rearrange("p a b -> p (a b)"), bk.ap[32 * e:32 * e + 31, 0:120], [bk], [Rs])
        fw.op("pool", lambda h: h.memset(RE.ap, 0.0), writes=[RE])
        for e in range(2):
            self.cp("dve", RE.ap[32 * e:32 * e + 31, :, e + 1:e + 16], Rs.ap[32 * e:32 * e + 31, :, ::-1], [Rs], [RE])
        fw.op("pool", lambda h: h.memset(BB.ap, 0.0), writes=[BB])
        for e in range(2):
            fw.op("pool", lambda h, e=e: h.memset(BB.ap[32 * e:32 * e + 32, e, :], 1.0), writes=[BB])
            fw.op("pool", lambda h, e=e: h.affine_select(out=BB.ap[32 * e:32 * e + 32, e, :], in_=BB.ap[32 * e:32 * e + 32, e, :], pattern=[[1, 127]],
                                                          compare_op=ALU.is_equal, fill=0.0, base=-48, channel_multiplier=-1), reads=[BB], writes=[BB])
        fw.op("pool", lambda h: h.memset(msk.ap, 0.0), writes=[msk])
        fw.op("pool", lambda h: h.memset(m2.ap, 0.0), writes=[m2])
        for hf in range(2):
            sl = slice(64 * hf, 64 * hf + 64)
            fw.op("pool", lambda h, sl=sl: h.affine_select(out=msk.ap[sl], in_=msk.ap[sl], pattern=[[-1, 64]], compare_op=ALU.is_ge, fill=NEG,
                                                            base=8, channel_multiplier=1), reads=[msk], writes=[msk])
            fw.op("pool", lambda h, sl=sl: h.affine_select(out=msk.ap[sl], in_=msk.ap[sl], pattern=[[0, 64]], compare_op=ALU.is_ge, fill=0.0,
                                                            base=47, channel_multiplier=-1), reads=[msk], writes=[msk])
            fw.op("pool", lambda h, sl=sl: h.affine_select(out=m2.ap[sl], in_=m2.ap[sl], pattern=[[1, 64]], compare_op=ALU.is_ge, fill=NEG,
                                                            base=7, channel_multiplier=-1), reads=[m2], writes=[m2])
            fw.op("pool", lambda h, sl=sl: h.affine_select(out=m2.ap[sl], in_=m2.ap[sl], pattern=[[0, 64]], compare_op=ALU.is_ge, fill=0.0,
                                                            base=-16, channel_multiplier=1), reads=[m2], writes=[m2])
        self.tt("pool", msk.ap, msk.ap, m2.ap, ALU.add, [msk, m2], [msk])
        REf = RE.ap.rearrange("p a b -> p (a b)")
        for q0 in range(0, 64, 3):
            nq = min(3, 64 - q0)
            bk = self.bank()
            for i in range(nq):
                qc = q0 + i
                fw.mm([lambda h, bk=bk, i=i, qc=qc, e=e: h.matmul(bk.ap[64 * e:64 * e + 64, i * 144:(i + 1) * 144], lhsT=BB.ap[:, e, 63 - qc:63 - qc + 64], rhs=REf,
                                                                  start=True, stop=True) for e in range(2)], reads=[BB, RE], writes=[bk])
            outv = LT.ap[:, :, :, q0:q0 + nq].rearrange("p h d q -> p q (h d)")
            self.tt("dve", outv, bk.ap[:, 0:nq * 144].rearrange("p (q n) -> p q n", q=nq),
                    msk.ap[:, q0:q0 + nq].unsqueeze(2).to_broadcast([128, nq, 144]), ALU.add, [bk, msk], [LT])
        ar.release(mk)

        U = []
        for hp in range(4):
            for e in range(2):
                for c in range(2):
                    grp = []
                    for mt in range(8):
                        js = [j for j in range(4 * c, 4 * c + 4) if any(na_valid(2 * mt + ee, 2 * j + r) for ee in range(2) for r in range(2))]
                        if js:
                            grp.append(("loc", mt, js[0], js[-1]))
                    for kt in range(4):
                        grp.append(("ctx", kt, 4 * c, 4 * c + 3))
                    for ui, (kind, mt, ja, jb) in enumerate(grp):
                        U.append((hp, e, c, kind, mt, ja, jb, ui == 0, ui == len(grp) - 1))

        def issue_score(k):
            hp, e, c, kind, mt, ja, jb, gfirst, glast = U[k]
            pb = 64 * e
            hh = 2 * hp + e
            nq = 128 * (jb - ja + 1)
            sb = self.bank()
            if kind == "loc":
                d0 = 2 * ja - 2 * mt + 8
                d1 = 2 * jb + 1 - 2 * mt + 8
                assert 0 <= d0 and d1 < 18, (mt, ja, jb)
                ltv = LT.ap[:, hh, d0:d1 + 1, :].rearrange("p a b -> p (a b)")
                fw.mm([lambda h, sb=sb, mt=mt, ja=ja, nq=nq, pb=pb, hp=hp: h.matmul(sb.ap[:, 0:nq], lhsT=k_sb.ap[pb:pb + 64, hp, mt * 128:(mt + 1) * 128],
                                                                                   rhs=q_sb.ap[pb:pb + 64, hp, ja * 128:ja * 128 + nq], start=True, stop=False),
                       lambda h, sb=sb, ltv=ltv, nq=nq: h.matmul(sb.ap[:, 0:nq], lhsT=self.ident_bf.ap, rhs=ltv, start=False, stop=True)],
                      reads=[k_sb, q_sb, LT, self.ident_bf], writes=[sb])
            else:
                fw.mm([lambda h, sb=sb, mt=mt, ja=ja, nq=nq, pb=pb, hp=hp: h.matmul(sb.ap[:, 0:nq], lhsT=kctxT.ap[pb:pb + 64, hp, mt * 128:(mt + 1) * 128],
                                                                                   rhs=q_sb.ap[pb:pb + 64, hp, ja * 128:ja * 128 + nq], start=True, stop=True)],
                      reads=[kctxT, q_sb], writes=[sb])
            return sb

        nxt = issue_score(0)
        ob = ov = None
        first = True
        for k, (hp, e, c, kind, mt, ja, jb, gfirst, glast) in enumerate(U):
            hh = 2 * hp + e
            nq = 128 * (jb - ja + 1)
            if gfirst:
                ob = self.bank(hold=True)
                ov = ob.ap[:, 0:260].rearrange("p (q c) -> p q c", q=4)
                first = True
            sb = nxt
            p = pT[k % 2]
            self.act(p.ap[:, 0:nq], sb.ap[:, 0:nq], AF.Exp, [sb], [p])
            if k + 1 < len(U):
                nxt = issue_score(k + 1)
            fns = []
            for j in range(ja, jb + 1):
                if kind == "loc":
                    val = [[na_valid(2 * mt + ee, 2 * j + r) for r in range(2)] for ee in range(2)]
                    if not any(val[0]) and not any(val[1]):
                        continue
                    for ee in range(2):
                        for r in range(2):
                            if not val[ee][r]:
                                c0 = (j - ja) * 128 + r * 64
                                fw.op("dve", lambda h, p=p, ee=ee, c0=c0: h.memset(p.ap[64 * ee:64 * ee + 64, c0:c0 + 64], 0.0), reads=[p], writes=[p])
                    rhs = v_aug.ap[:, mt, hh, 0:65]
                else:
                    rhs = vctx.ap[:, mt, hh, 0:65]
                fns.append(lambda h, j=j, ja=ja, p=p, rhs=rhs, st_=first, ov=ov, c=c: h.matmul(ov[:, j - 4 * c, :], lhsT=p.ap[:, (j - ja) * 128:(j - ja + 1) * 128],
                                                                                            rhs=rhs, start=st_, stop=False))
                first = False
            fw.mm(fns, reads=[p, v_aug, vctx], writes=[ob])
            if glast:
                fw.op("dve", lambda h, ov=ov: h.reciprocal(out=rec.ap[:, 0:4], in_=ov[:, :, 64]), reads=[ob], writes=[rec])
                self.tt("dve", atok.ap[:, 4 * c:4 * c + 4, 64 * e:64 * e + 64], ov[:, :, 0:64],
                        rec.ap[:, 0:4].unsqueeze(2).to_broadcast([128, 4, 64]), ALU.mult, [ob, rec], [atok])
                self.unhold(ob)
                if e == 1 and c == 1:
                    transposes(hp, [0, 1, 2, 3])
                    transposes(hp, [4, 5, 6, 7])

    def s5(self, l, g, s5y):
        fw, ar, I, O = self.fw, self.ar, self.I, self.O
        c = self.s5cols
        u_sb = ar.alloc([128, 2, TG], BF16)

        def u_cons(ft, tb, bk):
            self.cp("act", u_sb.ap[:, ft, tb * 512:(tb + 1) * 512], bk.ap, [bk], [u_sb])

        self.proj_fm(self.win_cols(l, 1536), 2, self.h_rhs, [self.h], 8, u_cons)
        Ec = ar.alloc([128, 16, 256], F32)
        Es = ar.alloc([128, 16, 256], F32)
        mk = ar.mark()
        io_i = ar.alloc([128, 256], I32)
        io_f = ar.alloc([128, 256], F32)
        fw.op("pool", lambda h: h.iota(io_i.ap, pattern=[[1, 256]], base=0, channel_multiplier=0), writes=[io_i])
        self.cp("dve", io_f.ap, io_i.ap, [io_i], [io_f])
        yv = ar.alloc([128, 2, 256], F32)
        for q in range(8):
            self.tt("dve", yv.ap, c.ap[:, 2 * q:2 * q + 2, 0].unsqueeze(2).to_broadcast([128, 2, 256]),
                    io_f.ap.unsqueeze(1).to_broadcast([128, 2, 256]), ALU.mult, [c, io_f], [yv])
            yflat = Reg(yv.ap.rearrange("p a b -> p (a b)"), yv.bufs)
            self.sincos(yflat, 512, Ec.ap[:, 2 * q:2 * q + 2, :].rearrange("p a b -> p (a b)"),
                        Es.ap[:, 2 * q:2 * q + 2, :].rearrange("p a b -> p (a b)"), Ec, Es)
        ar.release(mk)
        Kc = ar.alloc([128, 16, 4], F32)
        if g == 1:
            e255c = Ec.ap[:, :, 255]
            e255s = Es.ap[:, :, 255]
            self.tt("dve", Kc.ap[:, :, 0], c.ap[:, :, 2], e255c, ALU.mult, [c, Ec], [Kc])
            self.tt("dve", Kc.ap[:, :, 3], c.ap[:, :, 3], e255s, ALU.mult, [c, Es], [Kc])
            self.tt("dve", Kc.ap[:, :, 0], Kc.ap[:, :, 0], Kc.ap[:, :, 3], ALU.subtract, [Kc], [Kc])
            self.tt("dve", Kc.ap[:, :, 1], c.ap[:, :, 2], e255s, ALU.mult, [c, Es], [Kc])
            self.tt("dve", Kc.ap[:, :, 3], c.ap[:, :, 3], e255c, ALU.mult, [c, Ec], [Kc])
            self.tt("dve", Kc.ap[:, :, 1], Kc.ap[:, :, 1], Kc.ap[:, :, 3], ALU.add, [Kc], [Kc])
            self.ts("dve", Kc.ap[:, :, 2], Kc.ap[:, :, 1], -1.0, None, ALU.mult, None, [Kc], [Kc])
        ygb = ar.alloc([128, 2, TG], BF16)
        bpr = ar.alloc([128, 512], F32)
        bpi = ar.alloc([128, 512], F32)
        grs = [ar.alloc([128, 512], F32) for _ in range(2)]
        gis = [ar.alloc([128, 512], F32) for _ in range(2)]
        t1 = ar.alloc([128, 512], F32)
        t2 = ar.alloc([128, 512], F32)
        p1 = ar.alloc([128, 512], F32)
        p2 = ar.alloc([128, 512], F32)
        unit = [0]
        wr = ar.alloc([128, 512], BF16)
        wi = ar.alloc([128, 512], BF16)
        sm = ar.alloc([128, 16], F32)

        def v2(ap):
            return ap.rearrange("p (s t) -> p s t", s=2)

        def seg(ap, s2, rev):
            v = ap[:, s2 * 256:(s2 + 1) * 256]
            return v[:, ::-1] if rev else v

        units = []
        for ut in range(2):
            for d in range(2):
                for jj in range(4):
                    for ti, tb in enumerate([1, 0] if d == 1 else [0, 1]):
                        units.append((ut, d, jj, ti, tb))

        def issue_bu(k):
            ut, d, jj, ti, tb = units[k]
            dj = d * 8 + ut * 4 + jj
            tsl = slice(tb * 512, (tb + 1) * 512)
            br = self.bank()
            bi = self.bank()
            fw.mm([lambda h, br=br, dj=dj, tsl=tsl, ut=ut: h.matmul(br.ap, lhsT=self.s5w.ap[:, dj, 0, :], rhs=u_sb.ap[:, ut, tsl], start=True, stop=True)],
                  reads=[self.s5w, u_sb], writes=[br])
            fw.mm([lambda h, bi=bi, dj=dj, tsl=tsl, ut=ut: h.matmul(bi.ap, lhsT=self.s5w.ap[:, dj, 1, :], rhs=u_sb.ap[:, ut, tsl], start=True, stop=True)],
                  reads=[self.s5w, u_sb], writes=[bi])
            return br, bi

        ybanks = None
        yfirst = None
        tbk = self.bank(hold=True)
        deferred = []
        prev_last = None
        nxt = issue_bu(0)
        for k, (ut, d, jj, ti, tb) in enumerate(units):
            rev = (d == 1)
            j = ut * 4 + jj
            dj = d * 8 + j
            if d == 0 and jj == 0 and ti == 0:
                ybanks = [self.bank(hold=True), self.bank(hold=True)]
                yfirst = [True, True]
            Ecv = Ec.ap[:, dj, :]
            Esv = Es.ap[:, dj, :]
            if rev:
                Ecv = Ecv[:, ::-1]
                Esv = Esv[:, ::-1]
            Ec2 = Ecv.unsqueeze(1).to_broadcast([128, 2, 256])
            Es2 = Esv.unsqueeze(1).to_broadcast([128, 2, 256])
            rb = c.ap[:, dj, 1:2].to_broadcast([128, 256])
            gr, gi = grs[k % 2], gis[k % 2]
            br, bi = nxt
            self.tt("dve", v2(t2.ap), v2(bi.ap), Es2, ALU.mult, [bi, Es], [t2])
            self.tt("dve", v2(tbk.ap), v2(br.ap), Ec2, ALU.mult, [br, Ec], [tbk])
            self.tt("dve", bpr.ap, tbk.ap, t2.ap, ALU.add, [tbk, t2], [bpr])
            self.tt("dve", v2(t2.ap), v2(br.ap), Es2, ALU.mult, [br, Es], [t2])
            self.tt("dve", v2(tbk.ap), v2(bi.ap), Ec2, ALU.mult, [bi, Ec], [tbk])
            self.tt("dve", bpi.ap, tbk.ap, t2.ap, ALU.subtract, [tbk, t2], [bpi])
            if k + 1 < len(units):
                nxt = issue_bu(k + 1)
            for fn in deferred:
                fn()
            deferred = []
            segs = [1, 0] if rev else [0, 1]
            for sj, s2 in enumerate(segs):
                s = tb * 2 + s2
                si = ti * 2 + sj
                if g == 1:
                    f_r = seg(bpr.ap, s2, rev)[:, 0:1]
                    f_i = seg(bpi.ap, s2, rev)[:, 0:1]
                    if si == 0:
                        hpr, hpi = c.ap[:, dj, 6:7], c.ap[:, dj, 7:8]
                        self.stt(f_r, hpr, c.ap[:, dj, 2:3], f_r, ALU.mult, ALU.add, [c, bpr], [bpr])
                        self.stt(f_i, hpi, c.ap[:, dj, 2:3], f_i, ALU.mult, ALU.add, [c, bpi], [bpi])
                        self.ts("dve", sm.ap[:, 0:1], hpi, c.ap[:, dj, 3:4], None, ALU.mult, None, [c], [sm])
                        self.stt(f_i, hpr, c.ap[:, dj, 3:4], f_i, ALU.mult, ALU.add, [c, bpi], [bpi])
                        self.tt("dve", f_r, f_r, sm.ap[:, 0:1], ALU.subtract, [bpr, sm], [bpr])
                    else:
                        pgr, pgi = prev_last
                        self.stt(f_r, pgr[0], Kc.ap[:, dj, 0:1], f_r, ALU.mult, ALU.add, [pgr[1], Kc, bpr], [bpr])
                        self.stt(f_i, pgi[0], Kc.ap[:, dj, 0:1], f_i, ALU.mult, ALU.add, [pgi[1], Kc, bpi], [bpi])
                        self.stt(f_r, pgi[0], Kc.ap[:, dj, 2:3], f_r, ALU.mult, ALU.add, [pgi[1], Kc, bpr], [bpr])
                        self.stt(f_i, pgr[0], Kc.ap[:, dj, 1:2], f_i, ALU.mult, ALU.add, [pgr[1], Kc, bpi], [bpi])
                for (src, dst) in ((bpr, gr), (bpi, gi)):
                    fw.op("dve", lambda h, src=src, dst=dst, s2=s2, rev=rev, rb=rb: h.tensor_tensor_scan(
                        out=seg(dst.ap, s2, rev), data0=rb, data1=seg(src.ap, s2, rev), initial=0.0, op0=ALU.mult, op1=ALU.add),
                        reads=[src, c], writes=[dst])
                g_r = seg(gr.ap, s2, rev)[:, 255:256]
                g_i = seg(gi.ap, s2, rev)[:, 255:256]
                prev_last = ((g_r, gr), (g_i, gi))
                if g == 0:
                    def state_ops(dj=dj, g_r=g_r, g_i=g_i, gr=gr, gi=gi, s=s, d=d, j=j):
                        e_c = Ec.ap[:, dj, 255:256]
                        e_s = Es.ap[:, dj, 255:256]
                        o_r, o_i = self.outst.ap[:, s, d, j, 0:1], self.outst.ap[:, s, d, j, 1:2]
                        self.tt("dve", sm.ap[:, 1:2], g_i, e_s, ALU.mult, [gi, Es], [sm])
                        self.tt("dve", sm.ap[:, 2:3], g_i, e_c, ALU.mult, [gi, Ec], [sm])
                        self.stt(o_r, g_r, e_c, sm.ap[:, 1:2], ALU.mult, ALU.subtract, [gr, Ec, sm], [self.outst])
                        self.stt(o_i, g_r, e_s, sm.ap[:, 2:3], ALU.mult, ALU.add, [gr, Es, sm], [self.outst])
                    deferred.append(state_ops)
            self.tt("pool", v2(p1.ap), v2(gr.ap), Ec2, ALU.mult, [gr, Ec], [p1])
            self.tt("pool", v2(p2.ap), v2(gi.ap), Es2, ALU.mult, [gi, Es], [p2])
            self.tt("pool", wr.ap, p1.ap, p2.ap, ALU.subtract, [p1, p2], [wr])
            self.tt("pool", v2(p1.ap), v2(gi.ap), Ec2, ALU.mult, [gi, Ec], [p1])
            self.tt("pool", v2(p2.ap), v2(gr.ap), Es2, ALU.mult, [gr, Es], [p2])
            self.tt("pool", wi.ap, p1.ap, p2.ap, ALU.add, [p1, p2], [wi])
            yb = ybanks[tb]
            last = (d == 1 and jj == 3)
            fw.mm([lambda h, yb=yb, dj=dj, st_=yfirst[tb]: h.matmul(yb.ap, lhsT=self.s5w.ap[:, dj, 2, :], rhs=wr.ap, start=st_, stop=False),
                   lambda h, yb=yb, dj=dj, last=last: h.matmul(yb.ap, lhsT=self.s5w.ap[:, dj, 3, :], rhs=wi.ap, start=False, stop=last)],
                  reads=[self.s5w, wr, wi], writes=[yb])
            yfirst[tb] = False
            if d == 1 and jj == 3 and ti == 1:
                for tb2 in range(2):
                    yb = ybanks[tb2]
                    sl = slice(tb2 * 512, (tb2 + 1) * 512)
                    self.stt(t1.ap, u_sb.ap[:, ut, sl], self.s5d.ap[:, ut:ut + 1], yb.ap, ALU.mult, ALU.add, [u_sb, self.s5d, yb], [t1])
                    self.act(ygb.ap[:, ut, sl], t1.ap, AF.Gelu_apprx_tanh, [t1], [ygb])
                    self.unhold(yb)
        for fn in deferred:
            fn()
        self.unhold(tbk)
        for ot in range(2):
            for tb in range(2):
                sl = slice(tb * 512, (tb + 1) * 512)
                bk = self.bank()
                fw.mm([lambda h, bk=bk, kc=kc, ot=ot, sl=sl: h.matmul(bk.ap, lhsT=self.wglu.ap[:, kc, ot * 128:(ot + 1) * 128], rhs=ygb.ap[:, kc, sl],
                                                                      start=(kc == 0), stop=(kc == 1)) for kc in range(2)], reads=[self.wglu, ygb], writes=[bk])
                self.act(t1.ap, bk.ap, AF.Sigmoid, [bk], [t1])
                self.tt("dve", s5y.ap[:, ot, sl], ygb.ap[:, ot, sl], t1.ap, ALU.mult, [ygb, t1], [s5y])

    def store_s5_states(self, l):
        fw, ar, O = self.fw, self.ar, self.O
        mk = ar.mark()
        tr = ar.alloc([128, 128], F32)
        bk = self.bank()
        fw.mm([lambda h: h.transpose(out=bk.ap[:, 0:128], in_=self.outst.ap.rearrange("p s d j r -> p (s d j r)"), identity=self.ident_f.ap)],
              reads=[self.outst, self.ident_f], writes=[bk])
        self.cp("dve", tr.ap, bk.ap[:, 0:128], [bk], [tr])
        sem = Buf("s5out")
        for ri, nm in enumerate(("ns5r", "ns5i")):
            for s in range(4):
                for d in range(2):
                    r0 = ((s * 2 + d) * 8) * 2 + ri
                    src = tr.ap[r0:r0 + 15:2, :]
                    dst = O[nm][s, l, d, :].rearrange("(j q) -> j q", q=128)
                    dd = fw.dma("sp", lambda h, src=src, dst=dst: h.dma_start(out=dst, in_=src), sem, reads=[tr])
                    fw.final.append(dd)
        ar.release(mk)

    def lru(self, l, g, lruy):
        fw, ar, I, O = self.fw, self.ar, self.I, self.O
        lc = self.lrucols
        xr = ar.alloc([128, 2, TG], F32)
        gg = ar.alloc([128, 2, TG], F32)
        xc = ar.alloc([128, 2, TG], F32)
        xcb = ar.alloc([128, 2, TG], BF16)

        def xr_cons(ft, tb, bk):
            self.cp("act", xr.ap[:, ft, tb * 512:(tb + 1) * 512], bk.ap, [bk], [xr])

        def xg_cons(ft, tb, bk):
            self.act(gg.ap[:, ft, tb * 512:(tb + 1) * 512], bk.ap, AF.Gelu_apprx_tanh, [bk], [gg])

        self.proj_fm(self.win_cols(l, 1792), 2, self.h_rhs, [self.h], 8, xr_cons)
        self.proj_fm(self.win_cols(l, 2048), 2, self.h_rhs, [self.h], 8, xg_cons)
        nseq, L = (4, 256) if g == 0 else (1, 1024)
        for t in range(2):
            xv = xr.ap[:, t, :].rearrange("p (s q) -> p s q", s=nseq)
            cv = xc.ap[:, t, :].rearrange("p (s q) -> p s q", s=nseq)
            self.ts("dve", cv, xv, lc.ap[:, t, 2:3], lc.ap[:, t, 4:5], ALU.mult, ALU.add, [xr, lc], [xc])
            for k in (0, 1, 3):
                sh = k - 2
                lo, hi = max(0, -sh), L - max(0, sh)
                self.stt(cv[:, :, lo:hi], xv[:, :, lo + sh:hi + sh], lc.ap[:, t, k:k + 1], cv[:, :, lo:hi], ALU.mult, ALU.add, [xr, lc, xc], [xc])
            self.cp("act", xcb.ap[:, t, :], xc.ap[:, t, :], [xc], [xcb])
        r_ = ar.alloc([128, 512], F32)
        i_ = ar.alloc([128, 512], F32)
        th = ar.alloc([128, 512], F32)
        e2 = ar.alloc([128, 512], F32)
        a_sb = ar.alloc([128, TG], F32)
        b_sb = ar.alloc([128, TG], F32)
        hs = [ar.alloc([128, TG], F32) for _ in range(2)]
        for t in range(2):
            for d in range(2):
                rev = (d == 1)
                for tb in range(2):
                    sl = slice(tb * 512, (tb + 1) * 512)
                    pr = self.bank()
                    pi = self.bank()
                    fw.mm([lambda h, pr=pr, d=d, t=t, sl=sl: h.matmul(pr.ap, lhsT=self.lruw.ap[:, 0 * 4 + d * 2 + t, :], rhs=xcb.ap[:, t, sl], start=True, stop=True)],
                          reads=[self.lruw, xcb], writes=[pr])
                    fw.mm([lambda h, pi=pi, d=d, t=t, sl=sl: h.matmul(pi.ap, lhsT=self.lruw.ap[:, 1 * 4 + d * 2 + t, :], rhs=xcb.ap[:, t, sl], start=True, stop=True)],
                          reads=[self.lruw, xcb], writes=[pi])
                    self.act(r_.ap, pr.ap, AF.Sigmoid, [pr, lc], [r_], bias=lc.ap[:, t, 5 + d:6 + d])
                    self.act(i_.ap, pi.ap, AF.Sigmoid, [pi, lc], [i_], bias=lc.ap[:, t, 7 + d:8 + d])
                    self.act(a_sb.ap[:, sl], r_.ap, AF.Exp, [r_, lc], [a_sb], scale=lc.ap[:, t, 9 + d:10 + d])
                    self.act(th.ap, r_.ap, AF.Tanh, [r_, lc], [th], scale=lc.ap[:, t, 11 + d:12 + d])
                    self.act(e2.ap, r_.ap, AF.Exp, [r_, lc], [e2], scale=lc.ap[:, t, 13 + d:14 + d])
                    self.stt(e2.ap, e2.ap, 1.0, th.ap, ALU.add, ALU.mult, [e2, th], [e2])
                    self.act(e2.ap, e2.ap, AF.Sqrt, [e2], [e2])
                    self.tt("dve", i_.ap, i_.ap, xc.ap[:, t, sl], ALU.mult, [i_, xc], [i_])
                    self.tt("dve", b_sb.ap[:, sl], e2.ap, i_.ap, ALU.mult, [e2, i_], [b_sb])
                hd = hs[d]
                for s in range(nseq):
                    sq = slice(s * L, (s + 1) * L)
                    av, bv, hv = a_sb.ap[:, sq], b_sb.ap[:, sq], hd.ap[:, sq]
                    if rev:
                        av, bv, hv = av[:, ::-1], bv[:, ::-1], hv[:, ::-1]
                    init = 0.0 if g == 0 else self.lrust0.ap[:, d, t:t + 1]
                    rd = [a_sb, b_sb] + ([] if g == 0 else [self.lrust0])
                    fw.op("dve", lambda h, av=av, bv=bv, hv=hv, init=init: h.tensor_tensor_scan(out=hv, data0=av, data1=bv, initial=init, op0=ALU.mult, op1=ALU.add),
                          reads=rd, writes=[hd])
                    if g == 0:
                        self.cp("dve", self.lrust.ap[:, s, d, t:t + 1], hv[:, L - 1:L], [hd], [self.lrust])
            self.tt("dve", hs[0].ap, hs[0].ap, hs[1].ap, ALU.add, [hs[0], hs[1]], [hs[0]])
            self.tt("dve", lruy.ap[:, t, :], hs[0].ap, gg.ap[:, t, :], ALU.mult, [hs[0], gg], [lruy])
        if g == 0:
            sem = Buf("lruout")
            for s in range(4):
                for d in range(2):
                    dst = O["nlru"][s, l, d, :].rearrange("(t p) -> p t", p=128)
                    src = self.lrust.ap[:, s, d, :]
                    dd = fw.dma("sp", lambda h, src=src, dst=dst: h.dma_start(out=dst, in_=src, allow_slow_non_contiguous=True), sem, reads=[self.lrust])
                    fw.final.append(dd)

    def merge(self, l, g, attnT, s5y, lruy):
        fw, ar, I = self.fw, self.ar, self.I
        merged = ar.alloc([128, 8, TG], BF16)
        sig = [ar.alloc([128, 512], F32) for _ in range(2)]
        pr = [ar.alloc([128, 512], F32) for _ in range(3)]
        si = [0]

        def br_rhs(kc, sl):
            if kc < 4:
                return attnT.ap[:, kc, sl]
            if kc < 6:
                return s5y.ap[:, kc - 4, sl]
            return lruy.ap[:, kc - 6, sl]

        for sp in range(4):
            c0 = sp * 256
            wb = self.wslice([(0, 4, 256, I["w_br_attn"][l][:, c0:c0 + 256]), (4 * 256, 2, 256, I["w_br_s5"][l][:, c0:c0 + 256]),
                              (6 * 256, 2, 256, I["w_br_lru"][l][:, c0:c0 + 256])])
            wbv = wb.ap.rearrange("p (k n) -> p k n", k=8)
            wg = [self.wslice([(0, 8, 256, I["w_in"][l][:, 2304 + b * 1024 + c0:2304 + b * 1024 + c0 + 256])]) for b in range(3)]
            for t in range(2):
                ft = sp * 2 + t
                for tb in range(2):
                    sl = slice(tb * 512, (tb + 1) * 512)
                    for b, (k0, k1) in enumerate(((0, 4), (4, 6), (6, 8))):
                        gb = self.bank()
                        wgv = wg[b].ap.rearrange("p (k n) -> p k n", k=8)
                        fw.mm([lambda h, gb=gb, wgv=wgv, kc=kc, t=t, tb=tb: h.matmul(gb.ap, lhsT=wgv[:, kc, t * 128:(t + 1) * 128], rhs=self.h_rhs(kc, tb),
                                                                                     start=(kc == 0), stop=(kc == 7)) for kc in range(8)],
                              reads=[wg[b], self.h], writes=[gb])
                        bb = self.bank()
                        fw.mm([lambda h, bb=bb, kc=kc, t=t, sl=sl, k0=k0, k1=k1: h.matmul(bb.ap, lhsT=wbv[:, kc, t * 128:(t + 1) * 128], rhs=br_rhs(kc, sl),
                                                                                         start=(kc == k0), stop=(kc == k1 - 1)) for kc in range(k0, k1)],
                              reads=[wb, attnT, s5y, lruy], writes=[bb])
                        sg = sig[si[0] % 2]
                        si[0] += 1
                        self.act(sg.ap, gb.ap, AF.Sigmoid, [gb], [sg])
                        self.tt("dve", pr[b].ap, bb.ap, sg.ap, ALU.mult, [bb, sg], [pr[b]])
                    self.tt("dve", pr[0].ap, pr[0].ap, pr[1].ap, ALU.add, [pr[0], pr[1]], [pr[0]])
                    self.tt("dve", merged.ap[:, ft, sl], pr[0].ap, pr[2].ap, ALU.add, [pr[0], pr[2]], [merged])
                    self.tick()

        def out_cons(ft, tb, bk):
            sl = slice(tb * 512, (tb + 1) * 512)
            xv = self.x.ap[:, g, ft, sl]
            self.stt(xv, bk.ap, self.mod.ap[:, l, 2 * 8 + ft, g:g + 1], xv, ALU.mult, ALU.add, [bk, self.mod, self.x], [self.x])
            self.tick()

        self.proj_fm(lambda c0, n: [(0, 8, n, I["w_out"][l][:, c0:c0 + n])], 8,
                     lambda kc, tb: merged.ap[:, kc, tb * 512:(tb + 1) * 512], [merged], 8, out_cons)

    def ffn(self, l, g):
        fw, ar, I = self.fw, self.ar, self.I
        gg = ar.alloc([128, 22, TG], BF16)
        a_sb = [ar.alloc([128, TG], F32) for _ in range(2)]
        c_sb = [ar.alloc([128, TG], F32) for _ in range(2)]
        gl = [ar.alloc([128, TG], BF16) for _ in range(2)]
        nseq, L = (4, 256) if g == 0 else (1, 1024)
        fc = self.ffcols
        it = 0
        if l == 0 and g == 0:
            self.bg = self.mod_gen(1, look=2)
        for sp in range(11):
            wa = self.wslice([(0, 8, 256, I["ffn_w_up"][l][:, sp * 256:sp * 256 + 256])])
            wb = self.wslice([(0, 8, 256, I["ffn_w_up"][l][:, 2816 + sp * 256:2816 + sp * 256 + 256])])
            wav = wa.ap.rearrange("p (k n) -> p k n", k=8)
            wbv = wb.ap.rearrange("p (k n) -> p k n", k=8)
            for t in range(2):
                ft = sp * 2 + t
                a_, c_, g_ = a_sb[it % 2], c_sb[it % 2], gl[it % 2]
                it += 1
                for tb in range(2):
                    bk = self.bank()
                    fw.mm([lambda h, bk=bk, kc=kc, t=t, tb=tb: h.matmul(bk.ap, lhsT=wav[:, kc, t * 128:(t + 1) * 128], rhs=self.h_rhs(kc, tb),
                                                                       start=(kc == 0), stop=(kc == 7)) for kc in range(8)], reads=[wa, self.h], writes=[bk])
                    self.cp("act", a_.ap[:, tb * 512:(tb + 1) * 512], bk.ap, [bk], [a_])
                av = a_.ap.rearrange("p (s q) -> p s q", s=nseq)
                cv = c_.ap.rearrange("p (s q) -> p s q", s=nseq)
                self.act(cv, av, AF.Identity, [a_, fc], [c_], bias=fc.ap[:, ft, 3:4], scale=fc.ap[:, ft, 1:2])
                self.stt(cv[:, :, 1:L], av[:, :, 0:L - 1], fc.ap[:, ft, 0:1], cv[:, :, 1:L], ALU.mult, ALU.add, [a_, fc, c_], [c_])
                self.stt(cv[:, :, 0:L - 1], av[:, :, 1:L], fc.ap[:, ft, 2:3], cv[:, :, 0:L - 1], ALU.mult, ALU.add, [a_, fc, c_], [c_])
                self.act(g_.ap, c_.ap, AF.Gelu_apprx_tanh, [c_], [g_])
                for tb in range(2):
                    sl = slice(tb * 512, (tb + 1) * 512)
                    bk = self.bank()
                    fw.mm([lambda h, bk=bk, kc=kc, t=t, tb=tb: h.matmul(bk.ap, lhsT=wbv[:, kc, t * 128:(t + 1) * 128], rhs=self.h_rhs(kc, tb),
                                                                       start=(kc == 0), stop=(kc == 7)) for kc in range(8)], reads=[wb, self.h], writes=[bk])
                    self.tt("dve", gg.ap[:, ft, sl], bk.ap, g_.ap[:, sl], ALU.mult, [bk, g_], [gg])
                self.tick()
        for ft in range(8):
            w0 = self.wslice([(0, 11, 128, I["ffn_w_down"][l][0:1408, ft * 128:(ft + 1) * 128])])
            w1 = self.wslice([(0, 11, 128, I["ffn_w_down"][l][1408:2816, ft * 128:(ft + 1) * 128])])
            wv = [w0.ap[:, 0:1408].rearrange("p (k n) -> p k n", k=11), w1.ap[:, 0:1408].rearrange("p (k n) -> p k n", k=11)]
            for tb in range(2):
                sl = slice(tb * 512, (tb + 1) * 512)
                bk = self.bank()
                fw.mm([lambda h, bk=bk, kc=kc, sl=sl: h.matmul(bk.ap, lhsT=wv[kc // 11][:, kc % 11, :], rhs=gg.ap[:, kc, sl],
                                                               start=(kc == 0), stop=(kc == 21)) for kc in range(22)], reads=[w0, w1, gg], writes=[bk])
                xv = self.x.ap[:, g, ft, sl]
                self.stt(xv, bk.ap, self.mod.ap[:, l, 5 * 8 + ft, g:g + 1], xv, ALU.mult, ALU.add, [bk, self.mod, self.x], [self.x])
            self.tick()
        self.drain()

    def final_norm(self, g):
        fw, ar, O = self.fw, self.ar, self.O
        m = ar.mark()
        rstd = ar.alloc([128, TG], F32)
        self.rms_rstd(g, rstd)
        y = ar.alloc([128, 8, TG], F32)
        for kc in range(8):
            self.stt(y.ap[:, kc, :], self.x.ap[:, g, kc, :], self.gcols.ap[:, 4, kc:kc + 1], rstd.ap, ALU.mult, ALU.mult, [self.x, self.gcols, rstd], [y])
        dst_t = O["yp" if g == 0 else "ys"]
        for tt in range(8):
            for half in range(2):
                s = self.stg_i % 2
                self.stg_i += 1
                stg = self.stg[s]
                bk = self.bank()
                fw.mm([lambda h, bk=bk, j=j, half=half, tt=tt: h.transpose(out=bk.ap[:, j * 128:(j + 1) * 128], in_=y.ap[:, half * 4 + j, tt * 128:(tt + 1) * 128],
                                                                           identity=self.ident_f.ap) for j in range(4)], reads=[y, self.ident_f], writes=[bk])
                self.cp("act" if half else "dve", stg.ap, bk.ap, [bk], [stg])
                dst = dst_t[tt * 128:(tt + 1) * 128, half * 512:(half + 1) * 512]
                dd = fw.dma("sp", lambda h, stg=stg, dst=dst: h.dma_start(out=dst, in_=stg.ap), self.stg_sem[s], reads=[stg])
                fw.final.append(dd)
        ar.release(m)


_W_KEYS = ["w_ada", "b_ada", "g_norm1", "g_norm2", "w_in", "rpb", "s5_lam_re", "s5_lam_im", "s5_log_step", "s5_b_re", "s5_b_im",
           "s5_c_re", "s5_c_im", "s5_d", "s5_w_glu", "lru_conv_w", "lru_conv_b", "lru_w_a", "lru_b_a", "lru_w_x", "lru_b_x", "lru_lam",
           "w_br_attn", "w_br_s5", "w_br_lru", "w_out", "ffn_w_up", "ffn_conv_w", "ffn_conv_b", "ffn_w_down", "g_final"]


def make_in_maps(inp):
    f = lambda a: np.ascontiguousarray(np.asarray(a, dtype=np.float32))
    shared = {k: f(inp[k]) for k in _W_KEYS}
    maps = []
    for i in range(NCORES):
        m = dict(shared)
        m["xp"] = f(inp["x_prompt"][4 * i:4 * i + 4]).reshape(TG, D)
        m["xs"] = f(inp["x_sample"][i]).reshape(TG, D)
        m["ck"] = f(inp["cache_k"][i]).reshape(DEPTH, 512, 512)
        m["cv"] = f(inp["cache_v"][i]).reshape(DEPTH, 512, 512)
        m["s5r"] = f(inp["state_s5_re"][i]).reshape(DEPTH, 2, 1024)
        m["s5i"] = f(inp["state_s5_im"][i]).reshape(DEPTH, 2, 1024)
        m["slru"] = f(inp["state_lru"][i]).reshape(DEPTH, 2, 256)
        m["cvec"] = f(np.stack([np.asarray(inp["c_ctx"]), np.asarray(inp["c"])[i]], axis=0))
        maps.append(m)
    return maps


def assemble(results):
    cat = lambda k: np.concatenate([np.asarray(r[k]) for r in results], axis=0)
    y_prompt = cat("yp").reshape(32, 256, D)
    y_sample = cat("ys").reshape(8, 1024, D)
    new_k = cat("nk").reshape(32, DEPTH, 256, 8, 64)
    new_v = cat("nv").reshape(32, DEPTH, 256, 8, 64)
    ns5r = cat("ns5r").reshape(32, DEPTH, 2, 16, 64)
    ns5i = cat("ns5i").reshape(32, DEPTH, 2, 16, 64)
    nlru = cat("nlru").reshape(32, DEPTH, 2, 256)
    return tuple(np.ascontiguousarray(a, dtype=np.float32) for a in (y_prompt, y_sample, new_k, new_v, ns5r, ns5i, nlru))


def kernel(**inputs):
    nc = Builder().build()
    in_maps = make_in_maps(inputs)
    res = run_bass_kernel_spmd(nc, in_maps, core_ids=list(range(NCORES)))
    return assemble(res.results)


def debug_run(inputs, stage, ncores=1, trace=False):
    b = Builder(stage=stage)
    nc = b.build()
    in_maps = make_in_maps(inputs)[:ncores]
    res = run_bass_kernel_spmd(nc, in_maps, core_ids=list(range(ncores)), trace=trace)
    if trace:
        print("EXEC_NS", stage, res.exec_time_ns)
    return b, res.results
```

```python
import math
import types as _types
import numpy as np
from contextlib import ExitStack
import concourse.bass as bass
import concourse.mybir as mybir
from concourse.bass_utils import run_bass_kernel_spmd

F32 = mybir.dt.float32
BF16 = mybir.dt.bfloat16
I32 = mybir.dt.int32
AF = mybir.ActivationFunctionType
ALU = mybir.AluOpType

NCORES = 8
D = 1024
TG = 1024
DEPTH = 2
NEG = -30000.0
EPS = 1e-6
IN_W = 5376
PAGE = 512


class Buf:
    __slots__ = ("name", "lw", "rd", "dsem", "dcnt", "excl")

    def __init__(self, name, excl=False):
        self.name = name
        self.excl = excl
        self.lw = None
        self.rd = []
        self.dsem = None
        self.dcnt = 0


class Reg:
    __slots__ = ("ap", "bufs", "tag")

    def __init__(self, ap, bufs, tag=None):
        self.ap = ap
        self.bufs = bufs
        self.tag = tag

    def __getitem__(self, k):
        return self.ap[k]


def _freeze(f):
    if getattr(f, "__closure__", None) is None:
        return f
    cells = []
    for c in f.__closure__:
        try:
            cells.append(_types.CellType(c.cell_contents))
        except ValueError:
            cells.append(c)
    return _types.FunctionType(f.__code__, f.__globals__, f.__name__, f.__defaults__, tuple(cells))


class Eng:
    def __init__(self, name):
        self.key = name
        self.cnt = 0
        self.seen = {}
        self.prog = []


class FW:
    def __init__(self, nc, stack):
        self.nc = nc
        self.stack = stack
        self.sems = {}
        self.E = {}
        for n in ("pe", "act", "dve", "pool", "sp"):
            self.sems[n] = stack.enter_context(nc.semaphore("s_" + n))
            self.E[n] = Eng(n)
        self.ndsem = 0
        self.dsem_free = []
        self.final = []

    def _expand(self, lst):
        out = []
        for r in lst:
            if isinstance(r, Buf):
                out.append(r)
            else:
                out.extend(r.bufs)
        return out

    def _need(self, eng, dep, same_ok):
        key, val, clock = dep
        if same_ok and key == eng.key:
            return
        if eng.seen.get(key, 0) >= val:
            return
        eng.prog.append(("wait", key, val))
        eng.seen[key] = val
        for k, v in clock.items():
            if eng.seen.get(k, 0) < v:
                eng.seen[k] = v

    def _deps(self, eng, reads, writes):
        for b in reads:
            if b.lw is not None:
                self._need(eng, b.lw, False)
            if b.excl:
                for r in b.rd:
                    self._need(eng, r, True)
        for b in writes:
            if b.lw is not None:
                self._need(eng, b.lw, True)
            for r in b.rd:
                self._need(eng, r, True)

    def _commit(self, dep, reads, writes):
        for b in reads:
            b.rd.append(dep)
        for b in writes:
            b.lw = dep
            b.rd = []

    def op(self, en, fn, reads=(), writes=()):
        eng = self.E[en]
        reads = self._expand(reads)
        writes = self._expand(writes)
        self._deps(eng, reads, writes)
        eng.cnt += 1
        eng.prog.append(("ins", _freeze(fn), eng.key, 1))
        clock = dict(eng.seen)
        clock[eng.key] = eng.cnt
        dep = (eng.key, eng.cnt, clock)
        self._commit(dep, reads, writes)
        return dep

    def mm(self, fns, reads=(), writes=()):
        eng = self.E["pe"]
        reads = self._expand(reads)
        writes = self._expand(writes)
        self._deps(eng, reads, writes)
        for f in fns[:-1]:
            eng.prog.append(("ins", _freeze(f), None, 0))
        eng.cnt += 1
        eng.prog.append(("ins", _freeze(fns[-1]), eng.key, 1))
        clock = dict(eng.seen)
        clock[eng.key] = eng.cnt
        dep = (eng.key, eng.cnt, clock)
        self._commit(dep, reads, writes)
        return dep

    def dma(self, qn, fn, dbuf, reads=(), writes=()):
        eng = self.E[qn]
        reads = self._expand(reads)
        writes = self._expand(writes)
        self._deps(eng, reads, writes)
        if dbuf.dsem is None:
            self.ndsem += 1
            dbuf.dsem = self.stack.enter_context(self.nc.semaphore(f"d{self.ndsem}"))
        if dbuf.dcnt:
            self._need(eng, ("D%d" % id(dbuf), dbuf.dcnt, {}), False)
        dbuf.dcnt += 16
        key = "D%d" % id(dbuf)
        self.sems[key] = dbuf.dsem
        eng.prog.append(("ins", _freeze(fn), key, 16))
        dep = (key, dbuf.dcnt, dict(eng.seen))
        self._commit(dep, reads, writes)
        return dep

    def emit(self):
        nc = self.nc
        sems = self.sems
        for d in self.final:
            self._need(self.E["sp"], d, False)

        def replay(eng, h):
            for it in eng.prog:
                if it[0] == "wait":
                    h.wait_ge(sems[it[1]], it[2])
                else:
                    ins = it[1](h)
                    if it[2] is not None:
                        ins.then_inc(sems[it[2]], it[3])

        with nc.Block() as block:
            @block.sync
            def _(h):
                replay(self.E["sp"], h)

            @block.scalar
            def _(h):
                replay(self.E["act"], h)

            @block.vector
            def _(h):
                replay(self.E["dve"], h)

            @block.gpsimd
            def _(h):
                replay(self.E["pool"], h)

            @block.tensor
            def _(h):
                replay(self.E["pe"], h)


class Arena:
    def __init__(self, fw, nbytes):
        self.fw = fw
        self.nbytes = nbytes
        self.t = fw.stack.enter_context(fw.nc.sbuf_tensor("arena", [128, nbytes // 2], BF16))
        self.pages = [Buf(f"pg{i}") for i in range((nbytes + PAGE - 1) // PAGE)]
        self.top = 0
        self.peak = 0

    def alloc(self, shape, dt, align=64):
        esz = 4 if dt in (F32, I32) else 2
        n = 1
        for s in shape[1:]:
            n *= s
        nb = n * esz
        if nb >= PAGE:
            align = max(align, PAGE)
        off = (self.top + align - 1) // align * align
        assert off + nb <= self.nbytes, f"arena overflow: need {off + nb} have {self.nbytes}"
        self.top = off + nb
        self.peak = max(self.peak, self.top)
        ap = self.t[0:shape[0], off // 2:(off + nb) // 2]
        if esz == 4:
            ap = ap.bitcast(dt)
        if len(shape) > 2:
            names = " ".join(f"d{i}" for i in range(1, len(shape)))
            kw = {f"d{i}": shape[i] for i in range(1, len(shape))}
            ap = ap.rearrange(f"p ({names}) -> p {names}", **kw)
        pages = self.pages[off // PAGE:(off + nb - 1) // PAGE + 1]
        return Reg(ap, pages)

    def mark(self):
        return self.top

    def release(self, m):
        self.top = m


def na_valid(kr, qr):
    w0 = min(max(qr - 4, 0), 8)
    return w0 <= kr < w0 + 8


class _Stop(Exception):
    pass


class Builder:
    def __init__(self, debug=False, stage=None):
        self.debug = debug
        self.stage = stage
        self.dbg_outs = []
        self.nc = bass.Bass("TRN2", target_bir_lowering=False)
        self.stack = ExitStack()
        self.dram = {}

    def din(self, name, shape):
        t = self.nc.dram_tensor(name, list(shape), F32, kind="ExternalInput")
        self.dram[name] = t
        return t.ap()

    def dout(self, name, shape):
        t = self.nc.dram_tensor(name, list(shape), F32, kind="ExternalOutput")
        self.dram[name] = t
        return t.ap()

    def build(self):
        nc = self.nc
        with self.stack as st:
            fw = self.fw = FW(nc, st)
            I = self.I = {}
            O = self.O = {}
            I["xp"] = self.din("xp", [TG, D])
            I["xs"] = self.din("xs", [TG, D])
            I["ck"] = self.din("ck", [DEPTH, 512, 512])
            I["cv"] = self.din("cv", [DEPTH, 512, 512])
            I["s5r"] = self.din("s5r", [DEPTH, 2, 1024])
            I["s5i"] = self.din("s5i", [DEPTH, 2, 1024])
            I["slru"] = self.din("slru", [DEPTH, 2, 256])
            I["cvec"] = self.din("cvec", [2, D])
            wshapes = {
                "w_ada": [2, D, 6 * D], "b_ada": [2, 6 * D], "g_norm1": [2, D], "g_norm2": [2, D],
                "w_in": [2, D, IN_W], "rpb": [2, 8, 15, 31],
                "s5_lam_re": [2, 2, 16, 64], "s5_lam_im": [2, 2, 16, 64], "s5_log_step": [2, 2, 16],
                "s5_b_re": [2, 16, 64, 16], "s5_b_im": [2, 16, 64, 16],
                "s5_c_re": [2, 2, 16, 16, 64], "s5_c_im": [2, 2, 16, 16, 64],
                "s5_d": [2, 256], "s5_w_glu": [2, 256, 256],
                "lru_conv_w": [2, 4, 256], "lru_conv_b": [2, 256],
                "lru_w_a": [2, 2, 4, 64, 64], "lru_b_a": [2, 2, 256],
                "lru_w_x": [2, 2, 4, 64, 64], "lru_b_x": [2, 2, 256], "lru_lam": [2, 2, 256],
                "w_br_attn": [2, 512, D], "w_br_s5": [2, 256, D], "w_br_lru": [2, 256, D],
                "w_out": [2, D, D], "ffn_w_up": [2, D, 5632], "ffn_conv_w": [2, 3, 2816],
                "ffn_conv_b": [2, 2816], "ffn_w_down": [2, 2816, D], "g_final": [D],
            }
            self.wshapes = wshapes
            for k, s in wshapes.items():
                I[k] = self.din(k, s)
            O["yp"] = self.dout("yp", [TG, D])
            O["ys"] = self.dout("ys", [TG, D])
            O["nk"] = self.dout("nk", [4, DEPTH, 256, 512])
            O["nv"] = self.dout("nv", [4, DEPTH, 256, 512])
            O["ns5r"] = self.dout("ns5r", [4, DEPTH, 2, 1024])
            O["ns5i"] = self.dout("ns5i", [4, DEPTH, 2, 1024])
            O["nlru"] = self.dout("nlru", [4, DEPTH, 2, 256])

            self.ar = Arena(fw, 207 * 1024)
            ps_t = st.enter_context(nc.psum_tensor("psum", [128, 8, 512], F32))
            self.banks = [Reg(ps_t[:, i, :], [Buf(f"bank{i}", excl=True)], i) for i in range(8)]
            self.bank_i = 0
            self.held = set()
            try:
                self.setup()
                self.mod_init()
                self.chk("setup")
                for l in range(DEPTH):
                    if l == 0:
                        self.prep_bg = False
                        for _ in self.layer_prep_gen(0):
                            pass
                        self.bg = self.mod_gen(0, look=4)
                        self.drain()
                    self.chk(f"prep{l}")
                    for g in range(2):
                        self.group_layer(l, g)
                        self.chk(f"gl{l}{g}")
                for g in range(2):
                    self.final_norm(g)
            except _Stop:
                pass
            fw.emit()
        return nc

    def chk(self, name):
        if self.stage == name:
            raise _Stop()

    def dbg(self, name, reg, ap, shape):
        t = self.nc.dram_tensor("dbg_" + name, list(shape), F32, kind="ExternalOutput").ap()
        d = self.fw.dma("sp", lambda h: h.dma_start(out=t, in_=ap), Buf("dbg_" + name), reads=[reg])
        self.fw.final.append(d)
        self.dbg_outs.append("dbg_" + name)

    def bank(self, hold=False):
        while True:
            i = self.bank_i % 8
            self.bank_i += 1
            if i not in self.held:
                break
        if hold:
            self.held.add(i)
        return self.banks[i]

    def unhold(self, bk):
        self.held.discard(bk.tag)

    def col(self, reg, i):
        return reg.ap[:, i:i + 1]

    def setup(self):
        fw, ar, nc, I = self.fw, self.ar, self.nc, self.I
        self.x = ar.alloc([128, 2, 8, TG], F32)
        self.ident_bf = ar.alloc([128, 128], BF16, align=PAGE)
        self.ident_f = ar.alloc([128, 128], F32)
        self.ones_bf = ar.alloc([128, 128], BF16)
        self.epsc = ar.alloc([128, 1], F32)
        self.gcols = ar.alloc([128, 5, 8], F32)
        self.cTb = ar.alloc([128, 8, 2], BF16)
        self.badaT = ar.alloc([128, 2, 48], F32)
        self.mod = ar.alloc([128, 2, 48, 2], F32, align=PAGE)
        self.A1 = ar.alloc([128, 2, 8, 2], F32)
        self.A2 = ar.alloc([128, 2, 8, 2], F32)
        self.s5cols = ar.alloc([128, 16, 8], F32, align=PAGE)
        self.lrucols = ar.alloc([128, 2, 16], F32)
        self.s5d = ar.alloc([128, 2], F32)
        self.ffcols = ar.alloc([128, 22, 4], F32)
        self.lrust0 = ar.alloc([128, 2, 2], F32)
        self.lruw = ar.alloc([128, 8, 128], BF16)
        self.wglu = ar.alloc([128, 2, 256], BF16)
        self.s5w = ar.alloc([128, 16, 4, 128], BF16, align=PAGE)
        self.outst = ar.alloc([128, 4, 2, 8, 2], F32, align=PAGE)
        self.lrust = ar.alloc([128, 4, 2, 2], F32)
        self.slots = [ar.alloc([128, 2048], BF16, align=PAGE) for _ in range(6)]
        self.slot_sem = [Buf(f"slotsem{i}") for i in range(6)]
        self.slot_i = 0
        self.stg = [ar.alloc([128, 256], F32, align=PAGE) for _ in range(2)]
        self.stg_sem = [Buf("stg0"), Buf("stg1")]
        self.stg_i = 0
        self.small_sem = Buf("small")
        self.h = ar.alloc([128, 8, TG], BF16, align=PAGE)
        self.bg = None
        fw.op("pool", lambda h: h.memset(self.epsc.ap, EPS), writes=[self.epsc])
        self.scr_mark = ar.mark()

        idb, idf, ones = self.ident_bf, self.ident_f, self.ones_bf
        fw.op("pool", lambda h: h.memset(idf.ap, 1.0), writes=[idf])
        fw.op("pool", lambda h: h.affine_select(out=idf.ap, in_=idf.ap, pattern=[[-1, 128]], compare_op=ALU.is_equal,
                                                fill=0.0, base=0, channel_multiplier=1), reads=[idf], writes=[idf])
        fw.op("dve", lambda h: h.tensor_copy(out=idb.ap, in_=idf.ap), reads=[idf], writes=[idb])
        fw.op("dve", lambda h: h.memset(ones.ap, 1.0), writes=[ones])

        for g, name in enumerate(("xp", "xs")):
            for tt in range(8):
                for qd in range(4):
                    s = self.stg_i % 2
                    self.stg_i += 1
                    stg = self.stg[s]
                    src = I[name][tt * 128:(tt + 1) * 128, qd * 256:(qd + 1) * 256]
                    fw.dma("sp", lambda h, stg=stg, src=src: h.dma_start(out=stg.ap, in_=src), self.stg_sem[s], writes=[stg])
                    bk = self.bank()
                    fw.mm([lambda h, bk=bk, stg=stg, j=j: h.transpose(out=bk.ap[:, j * 128:(j + 1) * 128], in_=stg.ap[:, j * 128:(j + 1) * 128],
                                                                      identity=idf.ap) for j in range(2)], reads=[stg, idf], writes=[bk])
                    dstv = self.x.ap[:, g, qd * 2:qd * 2 + 2, tt * 128:(tt + 1) * 128]
                    srcv = bk.ap[:, 0:256].rearrange("p (j t) -> p j t", j=2)
                    if qd % 2:
                        fw.op("act", lambda h, dstv=dstv, srcv=srcv: h.activation(out=dstv, in_=srcv, func=AF.Copy), reads=[bk], writes=[self.x])
                    else:
                        fw.op("dve", lambda h, dstv=dstv, srcv=srcv: h.tensor_copy(out=dstv, in_=srcv), reads=[bk], writes=[self.x])

    def small_load(self, dst_ap, src_ap, dst_reg):
        return self.fw.dma("sp", lambda h: h.dma_start(out=dst_ap, in_=src_ap, allow_slow_non_contiguous=True),
                           self.small_sem, writes=[dst_reg])

    def wslice(self, parts):
        s = self.slot_i % 6
        self.slot_i += 1
        slot = self.slots[s]
        for (off, kc, ncols, src) in parts:
            dst = slot.ap[:, off:off + kc * ncols].rearrange("p (k n) -> p k n", k=kc)
            sv = src.rearrange("(k p) n -> p k n", p=128)
            self.fw.dma("pool", lambda h, dst=dst, sv=sv: h.dma_start(out=dst, in_=sv), self.slot_sem[s], writes=[slot])
        return slot

    def mod_init(self):
        fw, ar, I = self.fw, self.ar, self.I
        m = ar.mark()
        cT = ar.alloc([128, 8, 2], F32)
        for cc in range(2):
            self.small_load(cT.ap[:, :, cc], I["cvec"][cc].rearrange("(k p) -> p k", p=128), cT)
            self.small_load(self.badaT.ap[:, cc, :], I["b_ada"][cc].rearrange("(t p) -> p t", p=128), self.badaT)
            self.small_load(self.gcols.ap[:, cc, :], I["g_norm1"][cc].rearrange("(k p) -> p k", p=128), self.gcols)
            self.small_load(self.gcols.ap[:, 2 + cc, :], I["g_norm2"][cc].rearrange("(k p) -> p k", p=128), self.gcols)
        self.small_load(self.gcols.ap[:, 4, :], I["g_final"].rearrange("(k p) -> p k", p=128), self.gcols)
        fw.op("act", lambda h: h.activation(out=self.cTb.ap, in_=cT.ap, func=AF.Silu), reads=[cT], writes=[self.cTb])
        ar.release(m)

    def mod_gen(self, l, look=2):
        fw, I = self.fw, self.I
        cTb, badaT = self.cTb, self.badaT
        pend = []
        nxt = 0
        for s in range(24):
            while nxt < 24 and nxt <= s + look:
                pend.append(self.wslice([(0, 8, 256, I["w_ada"][l][:, nxt * 256:(nxt + 1) * 256])]))
                nxt += 1
            slot = pend.pop(0)
            sv = slot.ap.rearrange("p (k n) -> p k n", k=8)
            bk = self.bank()
            for t in range(2):
                fw.mm([lambda h, bk=bk, sv=sv, t=t, kc=kc: h.matmul(bk.ap[:, t * 2:t * 2 + 2], lhsT=sv[:, kc, t * 128:(t + 1) * 128],
                                                                    rhs=cTb.ap[:, kc, :], start=(kc == 0), stop=(kc == 7)) for kc in range(8)],
                      reads=[slot, cTb], writes=[bk])
            fw.op("dve", lambda h, bk=bk, l=l, s=s: h.tensor_tensor(out=self.mod.ap[:, l, 2 * s:2 * s + 2, :],
                                                                   in0=bk.ap[:, 0:4].rearrange("p (t c) -> p t c", t=2),
                                                                   in1=badaT.ap[:, l, 2 * s:2 * s + 2].unsqueeze(2).to_broadcast([128, 2, 2]),
                                                                   op=ALU.add), reads=[bk, badaT], writes=[self.mod])
            yield
        for (A, comp, gi) in ((self.A1, 1, 0), (self.A2, 4, 2)):
            fw.op("dve", lambda h, A=A, comp=comp, gi=gi, l=l: h.scalar_tensor_tensor(
                out=A.ap[:, l], in0=self.mod.ap[:, l, comp * 8:comp * 8 + 8, :], scalar=1.0,
                in1=self.gcols.ap[:, gi + l, :].unsqueeze(2).to_broadcast([128, 8, 2]), op0=ALU.add, op1=ALU.mult),
                reads=[self.mod, self.gcols], writes=[A])

    def tick(self):
        if self.bg is not None:
            try:
                next(self.bg)
            except StopIteration:
                self.bg = None

    def drain(self):
        while self.bg is not None:
            self.tick()

    def rms_rstd(self, g, rstd):
        fw, ar = self.fw, self.ar
        m = ar.mark()
        sq = [ar.alloc([128, 512], BF16) for _ in range(2)]
        tmp = ar.alloc([128, 512], F32)
        for tb in range(2):
            bk = self.bank()
            for kc in range(8):
                q = sq[kc % 2]
                fw.op("act", lambda h, q=q, kc=kc, tb=tb: h.activation(out=q.ap, in_=self.x.ap[:, g, kc, tb * 512:(tb + 1) * 512], func=AF.Square),
                      reads=[self.x], writes=[q])
                fw.mm([lambda h, bk=bk, q=q, kc=kc: h.matmul(bk.ap, lhsT=self.ones_bf.ap, rhs=q.ap, start=(kc == 0), stop=(kc == 7))],
                      reads=[q, self.ones_bf], writes=[bk])
            fw.op("act", lambda h, bk=bk: h.activation(out=tmp.ap, in_=bk.ap, func=AF.Sqrt, scale=1.0 / D, bias=self.epsc.ap[:, 0:1]),
                  reads=[bk, self.epsc], writes=[tmp])
            fw.op("dve", lambda h, tb=tb: h.reciprocal(out=rstd.ap[:, tb * 512:(tb + 1) * 512], in_=tmp.ap), reads=[tmp], writes=[rstd])
        ar.release(m)

    def norm_mod(self, l, g, A, shcomp):
        fw, ar = self.fw, self.ar
        m = ar.mark()
        rstd = ar.alloc([128, TG], F32)
        self.rms_rstd(g, rstd)
        tmps = [ar.alloc([128, TG], F32) for _ in range(2)]
        for kc in range(8):
            t = tmps[kc % 2]
            fw.op("dve", lambda h, t=t, kc=kc: h.scalar_tensor_tensor(out=t.ap, in0=self.x.ap[:, g, kc, :], scalar=A.ap[:, l, kc, g:g + 1],
                                                                      in1=rstd.ap, op0=ALU.mult, op1=ALU.mult),
                  reads=[self.x, A, rstd], writes=[t])
            fw.op("act", lambda h, t=t, kc=kc: h.activation(out=self.h.ap[:, kc, :], in_=t.ap, func=AF.Identity,
                                                            bias=self.mod.ap[:, l, shcomp * 8 + kc, g:g + 1], scale=1.0),
                  reads=[t, self.mod], writes=[self.h])
        ar.release(m)

    def proj_fm(self, src_cols_fn, ntiles, rhs, rhs_regs, kcs, consume):
        fw = self.fw
        for sp in range(0, ntiles, 2):
            nt = min(2, ntiles - sp)
            slot = self.wslice(src_cols_fn(sp * 128, nt * 128))
            sv = slot.ap[:, 0:kcs * nt * 128].rearrange("p (k n) -> p k n", k=kcs)
            for t in range(nt):
                for tb in range(2):
                    bk = self.bank()
                    fw.mm([lambda h, bk=bk, sv=sv, t=t, tb=tb, kc=kc: h.matmul(bk.ap, lhsT=sv[:, kc, t * 128:(t + 1) * 128], rhs=rhs(kc, tb),
                                                                                start=(kc == 0), stop=(kc == kcs - 1)) for kc in range(kcs)],
                          reads=[slot] + rhs_regs, writes=[bk])
                    consume(sp + t, tb, bk)

    def win_cols(self, l, base):
        return lambda c0, n: [(0, 8, n, self.I["w_in"][l][:, base + c0:base + c0 + n])]

    def h_rhs(self, kc, tb):
        return self.h.ap[:, kc, tb * 512:(tb + 1) * 512]

    def tt(self, en, out, in0, in1, op, R, W):
        self.fw.op(en, lambda h: h.tensor_tensor(out=out, in0=in0, in1=in1, op=op), reads=R, writes=W)

    def ts(self, en, out, in0, s1, s2, op0, op1, R, W):
        if s2 is None:
            self.fw.op(en, lambda h: h.tensor_scalar(out=out, in0=in0, scalar1=s1, scalar2=None, op0=op0), reads=R, writes=W)
        else:
            self.fw.op(en, lambda h: h.tensor_scalar(out=out, in0=in0, scalar1=s1, scalar2=s2, op0=op0, op1=op1), reads=R, writes=W)

    def stt(self, out, in0, sc, in1, op0, op1, R, W, en="dve"):
        self.fw.op(en, lambda h: h.scalar_tensor_tensor(out=out, in0=in0, scalar=sc, in1=in1, op0=op0, op1=op1), reads=R, writes=W)

    def act(self, out, in_, func, R, W, bias=None, scale=None):
        kw = {}
        if bias is not None:
            kw["bias"] = bias
        if scale is not None:
            kw["scale"] = scale
        self.fw.op("act", lambda h: h.activation(out=out, in_=in_, func=func, **kw), reads=R, writes=W)

    def cp(self, en, out, in_, R, W):
        if en == "act":
            self.fw.op("act", lambda h: h.activation(out=out, in_=in_, func=AF.Copy), reads=R, writes=W)
        else:
            self.fw.op(en, lambda h: h.tensor_copy(out=out, in_=in_), reads=R, writes=W)

    def sincos(self, y, n, cos_out, sin_out, Wc, Ws):
        ar = self.ar
        MAGIC = 12582912.0
        m = ar.mark()
        kf = ar.alloc([128, n], F32)
        fc = ar.alloc([128, n], F32)
        self.ts("dve", kf.ap, y.ap, 0.25, MAGIC, ALU.add, ALU.add, [y], [kf])
        self.ts("dve", kf.ap, kf.ap, MAGIC, None, ALU.subtract, None, [kf], [kf])
        self.stt(fc.ap, y.ap, 0.25, kf.ap, ALU.add, ALU.subtract, [y, kf], [fc])
        self.act(cos_out, fc.ap, AF.Sin, [fc], [Wc], scale=2.0 * math.pi)
        self.ts("dve", kf.ap, y.ap, MAGIC, None, ALU.add, None, [y], [kf])
        self.ts("dve", kf.ap, kf.ap, MAGIC, None, ALU.subtract, None, [kf], [kf])
        self.tt("dve", y.ap, y.ap, kf.ap, ALU.subtract, [y, kf], [y])
        self.act(sin_out, y.ap, AF.Sin, [y], [Ws], scale=2.0 * math.pi)
        ar.release(m)

    _ssem_i = 0

    def sload(self, dst_ap, src_ap, dst_reg, q="sp"):
        if not hasattr(self, "ssems"):
            self.ssems = [Buf(f"ss{i}") for i in range(4)]
        s = self.ssems[Builder._ssem_i % 4]
        Builder._ssem_i += 1
        if q == "sp":
            return self.fw.dma("sp", lambda h: h.dma_start(out=dst_ap, in_=src_ap, allow_slow_non_contiguous=True), s, writes=[dst_reg])
        return self.fw.dma("pool", lambda h: h.dma_start(out=dst_ap, in_=src_ap, allow_slow_non_contiguous=True), s, writes=[dst_reg])

    def layer_prep_gen(self, l):
        fw, ar, I = self.fw, self.ar, self.I
        m = ar.mark()
        c = self.s5cols
        lam_r = ar.alloc([128, 16], F32)
        lam_i = ar.alloc([128, 16], F32)
        stp = ar.alloc([128, 16], F32)
        t1 = ar.alloc([128, 16], F32)
        t2 = ar.alloc([128, 16], F32)
        t3 = ar.alloc([128, 16], F32)
        yv = ar.alloc([128, 16], F32)
        cs = ar.alloc([128, 16], F32)
        sn = ar.alloc([128, 16], F32)
        nat = ar.alloc([128, 4, 8, 16], F32)
        Cn = [ar.alloc([128, 4, 2, 64], F32) for _ in range(2)]
        ct2 = ar.alloc([128, 128], F32)
        lam = ar.alloc([128, 2, 2], F32)
        xx = ar.alloc([128, 2, 2], F32)
        pp = ar.alloc([128, 2, 2], F32)
        msk = ar.alloc([128, 8, 8], F32)
        Br = ar.alloc([128, 8, 16], F32)
        Bi = ar.alloc([128, 8, 16], F32)
        bb = [ar.alloc([128, 2, 8, 16], F32) for _ in range(2)]
        tA = ar.alloc([128, 2, 8, 16], F32)
        self.sload(lam_r.ap, I["s5_lam_re"][l].rearrange("d (j gl) n -> (gl n) (d j)", gl=2), lam_r)
        self.sload(lam_i.ap, I["s5_lam_im"][l].rearrange("d (j gl) n -> (gl n) (d j)", gl=2), lam_i)
        for gl in range(2):
            src = bass.AP(I["s5_log_step"].tensor, l * 32 + gl, [[0, 64], [2, 16]])
            self.sload(stp.ap[64 * gl:64 * gl + 64, :], src, stp)
        self.sload(c.ap[:, :, 6], I["s5r"][l].rearrange("d (j q) -> q (d j)", q=128), c)
        self.sload(c.ap[:, :, 7], I["s5i"][l].rearrange("d (j q) -> q (d j)", q=128), c)
        yield
        self.act(stp.ap, stp.ap, AF.Exp, [stp], [stp])
        self.tt("dve", t1.ap, lam_r.ap, stp.ap, ALU.mult, [lam_r, stp], [t1])
        self.tt("dve", t2.ap, lam_i.ap, stp.ap, ALU.mult, [lam_i, stp], [t2])
        self.act(c.ap[:, :, 1], t1.ap, AF.Exp, [t1], [c])
        self.ts("dve", yv.ap, t2.ap, 1.0 / (2.0 * math.pi), None, ALU.mult, None, [t2], [yv])
        self.sincos(yv, 16, cs.ap, sn.ap, cs, sn)
        self.cp("dve", c.ap[:, :, 0], yv.ap, [yv], [c])
        self.tt("dve", c.ap[:, :, 2], c.ap[:, :, 1], cs.ap, ALU.mult, [c, cs], [c])
        self.tt("dve", c.ap[:, :, 3], c.ap[:, :, 1], sn.ap, ALU.mult, [c, sn], [c])
        yield
        self.ts("dve", t1.ap, c.ap[:, :, 2], -1.0, None, ALU.add, None, [c], [t1])
        self.tt("dve", t2.ap, lam_r.ap, lam_r.ap, ALU.mult, [lam_r], [t2])
        self.tt("dve", t3.ap, lam_i.ap, lam_i.ap, ALU.mult, [lam_i], [t3])
        self.tt("dve", t2.ap, t2.ap, t3.ap, ALU.add, [t2, t3], [t2])
        fw.op("dve", lambda h: h.reciprocal(out=t2.ap, in_=t2.ap), reads=[t2], writes=[t2])
        self.tt("dve", t3.ap, t1.ap, lam_r.ap, ALU.mult, [t1, lam_r], [t3])
        self.tt("dve", yv.ap, c.ap[:, :, 3], lam_i.ap, ALU.mult, [c, lam_i], [yv])
        self.tt("dve", t3.ap, t3.ap, yv.ap, ALU.add, [t3, yv], [t3])
        self.tt("dve", c.ap[:, :, 4], t3.ap, t2.ap, ALU.mult, [t3, t2], [c])
        self.tt("dve", t3.ap, c.ap[:, :, 3], lam_r.ap, ALU.mult, [c, lam_r], [t3])
        self.tt("dve", yv.ap, t1.ap, lam_i.ap, ALU.mult, [t1, lam_i], [yv])
        self.tt("dve", t3.ap, t3.ap, yv.ap, ALU.subtract, [t3, yv], [t3])
        self.tt("dve", c.ap[:, :, 5], t3.ap, t2.ap, ALU.mult, [t3, t2], [c])

        yield
        fw.op("pool", lambda h: h.memset(msk.ap, 1.0), writes=[msk])
        for gl in range(2):
            fw.op("pool", lambda h, gl=gl: h.affine_select(out=msk.ap[64 * gl:64 * gl + 64], in_=msk.ap[64 * gl:64 * gl + 64],
                                                           pattern=[[0, 2], [-2, 4], [1, 8]], compare_op=ALU.is_equal, fill=0.0,
                                                           base=-gl, channel_multiplier=0), reads=[msk], writes=[msk])
        self.sload(Br.ap, I["s5_b_re"][l].rearrange("(j gl) n p -> (gl n) j p", gl=2), Br)
        self.sload(Bi.ap, I["s5_b_im"][l].rearrange("(j gl) n p -> (gl n) j p", gl=2), Bi)
        kr = c.ap[:, :, 4].rearrange("p (d j) -> p d j", d=2).unsqueeze(3).to_broadcast([128, 2, 8, 16])
        ki = c.ap[:, :, 5].rearrange("p (d j) -> p d j", d=2).unsqueeze(3).to_broadcast([128, 2, 8, 16])
        Brb = Br.ap.unsqueeze(1).to_broadcast([128, 2, 8, 16])
        Bib = Bi.ap.unsqueeze(1).to_broadcast([128, 2, 8, 16])
        self.tt("dve", bb[0].ap, Brb, kr, ALU.mult, [Br, c], [bb[0]])
        self.tt("dve", tA.ap, Bib, ki, ALU.mult, [Bi, c], [tA])
        self.tt("dve", bb[0].ap, bb[0].ap, tA.ap, ALU.subtract, [bb[0], tA], [bb[0]])
        self.tt("dve", bb[1].ap, Bib, kr, ALU.mult, [Bi, c], [bb[1]])
        self.tt("dve", tA.ap, Brb, ki, ALU.mult, [Br, c], [tA])
        self.tt("dve", bb[1].ap, bb[1].ap, tA.ap, ALU.add, [bb[1], tA], [bb[1]])
        yield
        for ri in range(2):
            for q4 in range(4):
                d, j0 = q4 // 2, (q4 % 2) * 4
                for jj in range(4):
                    j = j0 + jj
                    self.tt("dve", nat.ap[:, jj], bb[ri].ap[:, d, j].unsqueeze(1).to_broadcast([128, 8, 16]),
                            msk.ap[:, j].unsqueeze(2).to_broadcast([128, 8, 16]), ALU.mult, [bb[ri], msk], [nat])
                yield
                bk = self.bank()
                fw.mm([lambda h, bk=bk, jj=jj: h.transpose(out=bk.ap[:, jj * 128:(jj + 1) * 128],
                                                           in_=nat.ap[:, jj].rearrange("p a b -> p (a b)"), identity=self.ident_f.ap)
                       for jj in range(4)], reads=[nat, self.ident_f], writes=[bk])
                self.cp("act", self.s5w.ap[:, d * 8 + j0:d * 8 + j0 + 4, ri, :], bk.ap.rearrange("p (a b) -> p a b", a=4), [bk], [self.s5w])
        for hf in range(2):
            self.sload(Cn[0].ap[:, :, hf, :], I["s5_c_re"][l].rearrange("d g p n -> (d g p) n").rearrange("(q r) n -> r q n", r=128), Cn[0])
            self.sload(Cn[1].ap[:, :, hf, :], I["s5_c_im"][l].rearrange("d g p n -> (d g p) n").rearrange("(q r) n -> r q n", r=128), Cn[1])
        for ri in range(2):
            for q in range(4):
                d, ut = q // 2, q % 2
                bk = self.bank()
                fw.mm([lambda h, bk=bk, ri=ri, q=q: h.matmul(bk.ap[:, 0:128], lhsT=Cn[ri].ap[:, q].rearrange("p a b -> p (a b)"),
                                                             rhs=self.ident_f.ap, start=True, stop=True)],
                      reads=[Cn[ri], self.ident_f], writes=[bk])
                if ri == 0:
                    self.cp("act", ct2.ap, bk.ap[:, 0:128], [bk], [ct2])
                else:
                    self.act(ct2.ap, bk.ap[:, 0:128], AF.Copy, [bk], [ct2], scale=-1.0)
                j0 = ut * 4
                self.tt("dve", self.s5w.ap[:, d * 8 + j0:d * 8 + j0 + 4, 2 + ri, :].rearrange("p j (g q) -> p j g q", g=8),
                        ct2.ap.rearrange("p (g q) -> p g q", g=8).unsqueeze(1).to_broadcast([128, 4, 8, 16]),
                        msk.ap[:, j0:j0 + 4].unsqueeze(3).to_broadcast([128, 4, 8, 16]), ALU.mult, [ct2, msk], [self.s5w])
                yield
        self.sload(self.s5d.ap, I["s5_d"][l].rearrange("(t p) -> p t", p=128), self.s5d)
        yield
        lc = self.lrucols
        for k in range(4):
            self.sload(lc.ap[:, :, k], I["lru_conv_w"][l, k].rearrange("(t p) -> p t", p=128), lc)
        self.sload(lc.ap[:, :, 4], I["lru_conv_b"][l].rearrange("(t p) -> p t", p=128), lc)
        for d in range(2):
            self.sload(lc.ap[:, :, 5 + d], I["lru_b_a"][l, d].rearrange("(t p) -> p t", p=128), lc)
            self.sload(lc.ap[:, :, 7 + d], I["lru_b_x"][l, d].rearrange("(t p) -> p t", p=128), lc)
        for d in range(2):
            self.sload(lam.ap[:, :, d], I["lru_lam"][l, d].rearrange("(t p) -> p t", p=128), lam)
        self.act(xx.ap, lam.ap, AF.Exp, [lam], [xx], scale=-1.0)
        self.ts("dve", pp.ap, xx.ap, -0.25, 1.0 / 3.0, ALU.mult, ALU.add, [xx], [pp])
        self.tt("dve", pp.ap, pp.ap, xx.ap, ALU.mult, [pp, xx], [pp])
        self.ts("dve", pp.ap, pp.ap, -1.0, 0.5, ALU.mult, ALU.add, [pp], [pp])
        self.tt("dve", pp.ap, pp.ap, xx.ap, ALU.mult, [pp, xx], [pp])
        self.ts("dve", pp.ap, pp.ap, -1.0, 1.0, ALU.mult, ALU.add, [pp], [pp])
        self.tt("dve", pp.ap, pp.ap, xx.ap, ALU.mult, [pp, xx], [pp])
        self.ts("dve", lc.ap[:, :, 9:11], pp.ap, -8.0, None, ALU.mult, None, [pp], [lc])
        self.ts("dve", lc.ap[:, :, 11:13], pp.ap, 8.0, None, ALU.mult, None, [pp], [lc])
        self.ts("dve", lc.ap[:, :, 13:15], pp.ap, -16.0, None, ALU.mult, None, [pp], [lc])
        yield
        fw.op("pool", lambda h: h.memset(self.lruw.ap, 0.0), writes=[self.lruw])
        for gi, nm in enumerate(("lru_w_a", "lru_w_x")):
            for d in range(2):
                for t in range(2):
                    for b2 in range(2):
                        idx = gi * 4 + d * 2 + t
                        self.sload(self.lruw.ap[64 * b2:64 * b2 + 64, idx, 64 * b2:64 * b2 + 64], I[nm][l, d, 2 * t + b2], self.lruw, q="pool")
        for d in range(2):
            self.sload(self.lrust0.ap[:, d, :], I["slru"][l, d].rearrange("(t p) -> p t", p=128), self.lrust0)
        self.sload(self.wglu.ap, I["s5_w_glu"][l].rearrange("(k p) n -> p k n", p=128), self.wglu, q="pool")
        yield
        if not self.prep_bg:
            ar.release(m)

    def ffn_prep(self, l):
        I = self.I
        for k in range(3):
            self.sload(self.ffcols.ap[:, :, k], I["ffn_conv_w"][l, k].rearrange("(t p) -> p t", p=128), self.ffcols)
        self.sload(self.ffcols.ap[:, :, 3], I["ffn_conv_b"][l].rearrange("(t p) -> p t", p=128), self.ffcols)

    def group_layer(self, l, g):
        fw, ar, I, O = self.fw, self.ar, self.I, self.O
        if g == 0:
            self.ffn_prep(l)
        self.norm_mod(l, g, self.A1, 0)
        self.chk(f"norm{l}{g}")
        m0 = ar.mark()
        attnT = ar.alloc([128, 4, TG], BF16)
        m1 = ar.mark()
        self.attention(l, g, attnT)
        self.chk(f"attn{l}{g}")
        ar.release(m1)
        s5y = ar.alloc([128, 2, TG], BF16)
        m1 = ar.mark()
        self.s5(l, g, s5y)
        self.chk(f"s5{l}{g}")
        ar.release(m1)
        if g == 0:
            self.store_s5_states(l)
        lruy = ar.alloc([128, 2, TG], BF16)
        m1 = ar.mark()
        self.lru(l, g, lruy)
        self.chk(f"lru{l}{g}")
        ar.release(m1)
        if g == 1 and l + 1 < DEPTH:
            self.prep_bg = True
            self.bg = self.layer_prep_gen(l + 1)
            self.tick()
        self.merge(l, g, attnT, s5y, lruy)
        self.drain()
        self.chk(f"merge{l}{g}")
        ar.release(m0)
        self.norm_mod(l, g, self.A2, 3)
        self.ffn(l, g)
        ar.release(m0)

    def attention(self, l, g, attnT):
        fw, ar, I, O = self.fw, self.ar, self.I, self.O
        q_sb = ar.alloc([128, 4, TG], BF16)
        k_sb = ar.alloc([128, 4, TG], BF16)
        v_aug = ar.alloc([128, 8, 8, 66], BF16)
        fw.op("pool", lambda h: h.memset(v_aug.ap[:, :, :, 64:65], 1.0), writes=[v_aug])

        def q_cons(ft, tb, bk):
            self.act(q_sb.ap[:, ft, tb * 512:(tb + 1) * 512], bk.ap, AF.Copy, [bk], [q_sb], scale=0.125)

        def k_cons(ft, tb, bk):
            self.cp("dve", k_sb.ap[:, ft, tb * 512:(tb + 1) * 512], bk.ap, [bk], [k_sb])

        self.proj_fm(self.win_cols(l, 0), 4, self.h_rhs, [self.h], 8, q_cons)
        self.proj_fm(self.win_cols(l, 512), 4, self.h_rhs, [self.h], 8, k_cons)
        self.chk(f"attq{l}{g}")
        for which in ((1, 2) if g == 0 else (2,)):
            for half in range(2):
                slot = self.wslice([(0, 8, 256, I["w_in"][l][:, which * 512 + half * 256: which * 512 + half * 256 + 256])])
                sv = slot.ap.rearrange("p (k n) -> p k n", k=8)
                for tt in range(8):
                    bk = self.bank()
                    fw.mm([lambda h, bk=bk, sv=sv, tt=tt, kc=kc: h.matmul(bk.ap[:, 0:256], lhsT=self.h.ap[:, kc, tt * 128:(tt + 1) * 128], rhs=sv[:, kc, :],
                                                                          start=(kc == 0), stop=(kc == 7)) for kc in range(8)],
                          reads=[slot, self.h], writes=[bk])
                    if which == 2:
                        self.cp("act", v_aug.ap[:, tt, half * 4:half * 4 + 4, 0:64], bk.ap[:, 0:256].rearrange("p (a b) -> p a b", a=4), [bk], [v_aug])
                    if g == 0:
                        s = self.stg_i % 2
                        self.stg_i += 1
                        stg = self.stg[s]
                        self.cp("dve", stg.ap[:, 0:256], bk.ap[:, 0:256], [bk], [stg])
                        dst = O["nk" if which == 1 else "nv"][tt // 2, l, (tt % 2) * 128:(tt % 2) * 128 + 128, half * 256:half * 256 + 256]
                        d = fw.dma("sp", lambda h, stg=stg, dst=dst: h.dma_start(out=dst, in_=stg.ap[:, 0:256]), self.stg_sem[s], reads=[stg])
                        fw.final.append(d)
                    self.chk(f"kv1{l}{g}")
            if which == 1:
                self.chk(f"kvK{l}{g}")

        self.chk(f"attkv{l}{g}")
        pT = [ar.alloc([128, 512], BF16) for _ in range(2)]
        pi = [0]
        atok = ar.alloc([128, 8, 128], BF16)
        rec = ar.alloc([128, 8], F32)

        def transposes(hp, tts):
            bk = self.bank()
            bkb = bk.ap.bitcast(BF16)
            fw.mm([lambda h, bkb=bkb, i=i, tt=tt: h.transpose(out=bkb[:, i * 128:(i + 1) * 128], in_=atok.ap[:, tt, :], identity=self.ident_bf.ap)
                   for i, tt in enumerate(tts)], reads=[atok, self.ident_bf], writes=[bk])
            n = len(tts)
            self.cp("dve", attnT.ap[:, hp, tts[0] * 128:(tts[0] + n) * 128], bkb[:, 0:n * 128], [bk], [attnT])

        if g == 0:
            UP = [(sq, hp, e) for sq in range(4) for hp in range(4) for e in range(2)]

            def issue_score_p(k):
                sq, hp, e = UP[k]
                pb = 64 * e
                sb = self.bank()
                fw.mm([lambda h, sb=sb, kt=kt, pb=pb, sq=sq, hp=hp: h.matmul(sb.ap[:, kt * 256:(kt + 1) * 256],
                                                                            lhsT=k_sb.ap[pb:pb + 64, hp, sq * 256 + kt * 128:sq * 256 + kt * 128 + 128],
                                                                            rhs=q_sb.ap[pb:pb + 64, hp, sq * 256:sq * 256 + 256], start=True, stop=True) for kt in range(2)],
                      reads=[k_sb, q_sb], writes=[sb])
                return sb

            nxt = issue_score_p(0)
            ob = ov = None
            for k, (sq, hp, e) in enumerate(UP):
                hh = 2 * hp + e
                if e == 0:
                    ob = self.bank(hold=True)
                    ov = ob.ap[:, 0:260].rearrange("p (q e c) -> p q e c", q=2, e=2)
                sb = nxt
                p = pT[k % 2]
                self.act(p.ap, sb.ap, AF.Exp, [sb], [p])
                if k + 1 < len(UP):
                    nxt = issue_score_p(k + 1)
                fns = []
                for qt in range(2):
                    for kt in range(2):
                        fns.append(lambda h, qt=qt, kt=kt, e=e, hh=hh, p=p, ov=ov, sq=sq, st_=(e == 0 and qt == 0 and kt == 0): h.matmul(
                            ov[:, qt, e, :], lhsT=p.ap[:, kt * 256 + qt * 128:kt * 256 + qt * 128 + 128], rhs=v_aug.ap[:, 2 * sq + kt, hh, 0:65],
                            start=st_, stop=(e == 1 and qt == 1 and kt == 1)))
                fw.mm(fns, reads=[p, v_aug], writes=[ob])
                if e == 1:
                    fw.op("dve", lambda h, ov=ov: h.reciprocal(out=rec.ap[:, 0:4].rearrange("p (q e) -> p q e", q=2), in_=ov[:, :, :, 64]), reads=[ob], writes=[rec])
                    self.tt("dve", atok.ap[:, 2 * sq:2 * sq + 2, :].rearrange("p q (e c) -> p q e c", e=2), ov[:, :, :, 0:64],
                            rec.ap[:, 0:4].rearrange("p (q e) -> p q e", q=2).unsqueeze(3).to_broadcast([128, 2, 2, 64]), ALU.mult, [ob, rec], [atok])
                    self.unhold(ob)
                    transposes(hp, [2 * sq, 2 * sq + 1])
        else:
            self.na_attention(l, attnT, q_sb, k_sb, v_aug, pT, atok, rec, transposes)

    def na_attention(self, l, attnT, q_sb, k_sb, v_aug, pT, atok, rec, transposes):
        fw, ar, I = self.fw, self.ar, self.I
        kctxT = ar.alloc([128, 4, 512], BF16)
        vctx = ar.alloc([128, 4, 8, 66], BF16)
        fw.op("pool", lambda h: h.memset(vctx.ap[:, :, :, 64:65], 1.0), writes=[vctx])
        cvsem = Buf("cvsem")
        for tt in range(4):
            fw.dma("pool", lambda h, tt=tt: h.dma_start(out=vctx.ap[:, tt, :, 0:64], in_=I["cv"][l][tt * 128:(tt + 1) * 128, :].rearrange("p (a b) -> p a b", a=8)),
                   cvsem, writes=[vctx])
        mk = ar.mark()
        cktok = ar.alloc([128, 4, 512], BF16)
        cksem = Buf("cksem")
        fw.dma("pool", lambda h: h.dma_start(out=cktok.ap, in_=I["ck"][l].rearrange("(t p) f -> p t f", p=128)), cksem, writes=[cktok])
        for hp in range(4):
            bk = self.bank()
            bkb = bk.ap.bitcast(BF16)
            fw.mm([lambda h, bkb=bkb, tt=tt, hp=hp: h.transpose(out=bkb[:, tt * 128:(tt + 1) * 128], in_=cktok.ap[:, tt, hp * 128:(hp + 1) * 128],
                                                               identity=self.ident_bf.ap) for tt in range(4)], reads=[cktok, self.ident_bf], writes=[bk])
            self.cp("dve", kctxT.ap[:, hp, :], bkb[:, 0:512], [bk], [kctxT])
        ar.release(mk)
        LT = ar.alloc([128, 8, 18, 64], BF16)
        mk = ar.mark()
        rp = ar.alloc([128, 2, 32], F32)
        fw.op("pool", lambda h: h.memset(rp.ap, 0.0), writes=[rp])
        for e in range(2):
            self.sload(rp.ap[0:120, e, 0:31], I["rpb"][l].rearrange("h a x -> (h a) x"), rp)
        Rs = ar.alloc([64, 8, 15], F32)
        RE = ar.alloc([64, 8, 18], F32)
        BB = ar.alloc([64, 2, 127], F32)
        msk = ar.alloc([128, 64], F32)
        m2 = ar.alloc([128, 64], F32)
        bk = self.bank()
        fw.mm([lambda h, bk=bk: h.matmul(bk.ap[0:64, 0:120], lhsT=rp.ap[0:120].rearrange("p a b -> p (a b)"),
                                         rhs=self.ident_f.ap[0:120, 0:120], start=True, stop=True)], reads=[rp, self.ident_f], writes=[bk])
        fw.op("pool", lambda h: h.memset(Rs.ap, 0.0), writes=[Rs])
        for e in range(2):
            self.cp("dve", Rs.ap[32 * e:32 * e + 31].rearrange("p a b -> p (a b)"), bk.ap[32 * e:32 * e + 31, 0:120], [bk], [Rs])
        fw.op("pool", lambda h: h.memset(RE.ap, 0.0), writes=[RE])
        for e in range(2):
            self.cp("dve", RE.ap[32 * e:32 * e + 31, :, e + 1:e + 16], Rs.ap[32 * e:32 * e + 31, :, ::-1], [Rs], [RE])
        fw.op("pool", lambda h: h.memset(BB.ap, 0.0), writes=[BB])
        for e in range(2):
            fw.op("pool", lambda h, e=e: h.memset(BB.ap[32 * e:32 * e + 32, e, :], 1.0), writes=[BB])
            fw.op("pool", lambda h, e=e: h.affine_select(out=BB.ap[32 * e:32 * e + 32, e, :], in_=BB.ap[32 * e:32 * e + 32, e, :], pattern=[[1, 127]],
                                                          compare_op=ALU.is_equal, fill=0.0, base=-48, channel_multiplier=-1), reads=[BB], writes=[BB])
        fw.op("pool", lambda h: h.memset(msk.ap, 0.0), writes=[msk])
        fw.op("pool", lambda h: h.memset(m2.ap, 0.0), writes=[m2])
        for hf in range(2):
            sl = slice(64 * hf, 64 * hf + 64)
            fw.op("pool", lambda h, sl=sl: h.affine_select(out=msk.ap[sl], in_=msk.ap[sl], pattern=[[-1, 64]], compare_op=ALU.is_ge, fill=NEG,
                                                            base=8, channel_multiplier=1), reads=[msk], writes=[msk])
            fw.op("pool", lambda h, sl=sl: h.affine_select(out=msk.ap[sl], in_=msk.ap[sl], pattern=[[0, 64]], compare_op=ALU.is_ge, fill=0.0,
                                                            base=47, channel_multiplier=-1), reads=[msk], writes=[msk])
            fw.op("pool", lambda h, sl=sl: h.affine_select(out=m2.ap[sl], in_=m2.ap[sl], pattern=[[1, 64]], compare_op=ALU.is_ge, fill=NEG,
                                                            base=7, channel_multiplier=-1), reads=[m2], writes=[m2])
            fw.op("pool", lambda h, sl=sl: h.affine_select(out=m2.ap[sl], in_=m2.ap[sl], pattern=[[0, 64]], compare_op=ALU.is_ge, fill=0.0,
                                                            base=-16, channel_multiplier=1), reads=[m2], writes=[m2])
        self.tt("pool", msk.ap, msk.ap, m2.ap, ALU.add, [msk, m2], [msk])
        REf = RE.ap.rearrange("p a b -> p (a b)")
        for q0 in range(0, 64, 3):
            nq = min(3, 64 - q0)
            bk = self.bank()
            for i in range(nq):
                qc = q0 + i
                fw.mm([lambda h, bk=bk, i=i, qc=qc, e=e: h.matmul(bk.ap[64 * e:64 * e + 64, i * 144:(i + 1) * 144], lhsT=BB.ap[:, e, 63 - qc:63 - qc + 64], rhs=REf,
                                                                  start=True, stop=True) for e in range(2)], reads=[BB, RE], writes=[bk])
            outv = LT.ap[:, :, :, q0:q0 + nq].rearrange("p h d q -> p q (h d)")
            self.tt("dve", outv, bk.ap[:, 0:nq * 144].rearrange("p (q n) -> p q n", q=nq),
                    msk.ap[:, q0:q0 + nq].unsqueeze(2).to_broadcast([128, nq, 144]), ALU.add, [bk, msk], [LT])
        ar.release(mk)

        U = []
        for hp in range(4):
            for e in range(2):
                for c in range(2):
                    grp = []
                    for mt in range(8):
                        js = [j for j in range(4 * c, 4 * c + 4) if any(na_valid(2 * mt + ee, 2 * j + r) for ee in range(2) for r in range(2))]
                        if js:
                            grp.append(("loc", mt, js[0], js[-1]))
                    for kt in range(4):
                        grp.append(("ctx", kt, 4 * c, 4 * c + 3))
                    for ui, (kind, mt, ja, jb) in enumerate(grp):
                        U.append((hp, e, c, kind, mt, ja, jb, ui == 0, ui == len(grp) - 1))

        def issue_score(k):
            hp, e, c, kind, mt, ja, jb, gfirst, glast = U[k]
            pb = 64 * e
            hh = 2 * hp + e
            nq = 128 * (jb - ja + 1)
            sb = self.bank()
            if kind == "loc":
                d0 = 2 * ja - 2 * mt + 8
                d1 = 2 * jb + 1 - 2 * mt + 8
                assert 0 <= d0 and d1 < 18, (mt, ja, jb)
                ltv = LT.ap[:, hh, d0:d1 + 1, :].rearrange("p a b -> p (a b)")
                fw.mm([lambda h, sb=sb, mt=mt, ja=ja, nq=nq, pb=pb, hp=hp: h.matmul(sb.ap[:, 0:nq], lhsT=k_sb.ap[pb:pb + 64, hp, mt * 128:(mt + 1) * 128],
                                                                                   rhs=q_sb.ap[pb:pb + 64, hp, ja * 128:ja * 128 + nq], start=True, stop=False),
                       lambda h, sb=sb, ltv=ltv, nq=nq: h.matmul(sb.ap[:, 0:nq], lhsT=self.ident_bf.ap, rhs=ltv, start=False, stop=True)],
                      reads=[k_sb, q_sb, LT, self.ident_bf], writes=[sb])
            else:
                fw.mm([lambda h, sb=sb, mt=mt, ja=ja, nq=nq, pb=pb, hp=hp: h.matmul(sb.ap[:, 0:nq], lhsT=kctxT.ap[pb:pb + 64, hp, mt * 128:(mt + 1) * 128],
                                                                                   rhs=q_sb.ap[pb:pb + 64, hp, ja * 128:ja * 128 + nq], start=True, stop=True)],
                      reads=[kctxT, q_sb], writes=[sb])
            return sb

        nxt = issue_score(0)
        ob = ov = None
        first = True
        for k, (hp, e, c, kind, mt, ja, jb, gfirst, glast) in enumerate(U):
            hh = 2 * hp + e
            nq = 128 * (jb - ja + 1)
            if gfirst:
                ob = self.bank(hold=True)
                ov = ob.ap[:, 0:260].rearrange("p (q c) -> p q c", q=4)
                first = True
            sb = nxt
            p = pT[k % 2]
            self.act(p.ap[:, 0:nq], sb.ap[:, 0:nq], AF.Exp, [sb], [p])
            if k + 1 < len(U):
                nxt = issue_score(k + 1)
            fns = []
            for j in range(ja, jb + 1):
                if kind == "loc":
                    val = [[na_valid(2 * mt + ee, 2 * j + r) for r in range(2)] for ee in range(2)]
                    if not any(val[0]) and not any(val[1]):
                        continue
                    for ee in range(2):
                        for r in range(2):
                            if not val[ee][r]:
                                c0 = (j - ja) * 128 + r * 64
                                fw.op("dve", lambda h, p=p, ee=ee, c0=c0: h.memset(p.ap[64 * ee:64 * ee + 64, c0:c0 + 64], 0.0), reads=[p], writes=[p])
                    rhs = v_aug.ap[:, mt, hh, 0:65]
                else:
                    rhs = vctx.ap[:, mt, hh, 0:65]
                fns.append(lambda h, j=j, ja=ja, p=p, rhs=rhs, st_=first, ov=ov, c=c: h.matmul(ov[:, j - 4 * c, :], lhsT=p.ap[:, (j - ja) * 128:(j - ja + 1) * 128],
                                                                                            rhs=rhs, start=st_, stop=False))
                first = False
            fw.mm(fns, reads=[p, v_aug, vctx], writes=[ob])
            if glast:
                fw.op("dve", lambda h, ov=ov: h.reciprocal(out=rec.ap[:, 0:4], in_=ov[:, :, 64]), reads=[ob], writes=[rec])
                self.tt("dve", atok.ap[:, 4 * c:4 * c + 4, 64 * e:64 * e + 64], ov[:, :, 0:64],
                        rec.ap[:, 0:4].unsqueeze(2).to_broadcast([128, 4, 64]), ALU.mult, [ob, rec], [atok])
                self.unhold(ob)
                if e == 1 and c == 1:
                    transposes(hp, [0, 1, 2, 3])
                    transposes(hp, [4, 5, 6, 7])

    def s5(self, l, g, s5y):
        fw, ar, I, O = self.fw, self.ar, self.I, self.O
        c = self.s5cols
        u_sb = ar.alloc([128, 2, TG], BF16)

        def u_cons(ft, tb, bk):
            self.cp("act", u_sb.ap[:, ft, tb * 512:(tb + 1) * 512], bk.ap, [bk], [u_sb])

        self.proj_fm(self.win_cols(l, 1536), 2, self.h_rhs, [self.h], 8, u_cons)
        Ec = ar.alloc([128, 16, 256], F32)
        Es = ar.alloc([128, 16, 256], F32)
        mk = ar.mark()
        io_i = ar.alloc([128, 256], I32)
        io_f = ar.alloc([128, 256], F32)
        fw.op("pool", lambda h: h.iota(io_i.ap, pattern=[[1, 256]], base=0, channel_multiplier=0), writes=[io_i])
        self.cp("dve", io_f.ap, io_i.ap, [io_i], [io_f])
        yv = ar.alloc([128, 2, 256], F32)
        for q in range(8):
            self.tt("dve", yv.ap, c.ap[:, 2 * q:2 * q + 2, 0].unsqueeze(2).to_broadcast([128, 2, 256]),
                    io_f.ap.unsqueeze(1).to_broadcast([128, 2, 256]), ALU.mult, [c, io_f], [yv])
            yflat = Reg(yv.ap.rearrange("p a b -> p (a b)"), yv.bufs)
            self.sincos(yflat, 512, Ec.ap[:, 2 * q:2 * q + 2, :].rearrange("p a b -> p (a b)"),
                        Es.ap[:, 2 * q:2 * q + 2, :].rearrange("p a b -> p (a b)"), Ec, Es)
        ar.release(mk)
        Kc = ar.alloc([128, 16, 4], F32)
        if g == 1:
            e255c = Ec.ap[:, :, 255]
            e255s = Es.ap[:, :, 255]
            self.tt("dve", Kc.ap[:, :, 0], c.ap[:, :, 2], e255c, ALU.mult, [c, Ec], [Kc])
            self.tt("dve", Kc.ap[:, :, 3], c.ap[:, :, 3], e255s, ALU.mult, [c, Es], [Kc])
            self.tt("dve", Kc.ap[:, :, 0], Kc.ap[:, :, 0], Kc.ap[:, :, 3], ALU.subtract, [Kc], [Kc])
            self.tt("dve", Kc.ap[:, :, 1], c.ap[:, :, 2], e255s, ALU.mult, [c, Es], [Kc])
            self.tt("dve", Kc.ap[:, :, 3], c.ap[:, :, 3], e255c, ALU.mult, [c, Ec], [Kc])
            self.tt("dve", Kc.ap[:, :, 1], Kc.ap[:, :, 1], Kc.ap[:, :, 3], ALU.add, [Kc], [Kc])
            self.ts("dve", Kc.ap[:, :, 2], Kc.ap[:, :, 1], -1.0, None, ALU.mult, None, [Kc], [Kc])
        ygb = ar.alloc([128, 2, TG], BF16)
        bpr = ar.alloc([128, 512], F32)
        bpi = ar.alloc([128, 512], F32)
        grs = [ar.alloc([128, 512], F32) for _ in range(2)]
        gis = [ar.alloc([128, 512], F32) for _ in range(2)]
        t1 = ar.alloc([128, 512], F32)
        t2 = ar.alloc([128, 512], F32)
        p1 = ar.alloc([128, 512], F32)
        p2 = ar.alloc([128, 512], F32)
        unit = [0]
        wr = ar.alloc([128, 512], BF16)
        wi = ar.alloc([128, 512], BF16)
        sm = ar.alloc([128, 16], F32)

        def v2(ap):
            return ap.rearrange("p (s t) -> p s t", s=2)

        def seg(ap, s2, rev):
            v = ap[:, s2 * 256:(s2 + 1) * 256]
            return v[:, ::-1] if rev else v

        units = []
        for ut in range(2):
            for d in range(2):
                for jj in range(4):
                    for ti, tb in enumerate([1, 0] if d == 1 else [0, 1]):
                        units.append((ut, d, jj, ti, tb))

        def issue_bu(k):
            ut, d, jj, ti, tb = units[k]
            dj = d * 8 + ut * 4 + jj
            tsl = slice(tb * 512, (tb + 1) * 512)
            br = self.bank()
            bi = self.bank()
            fw.mm([lambda h, br=br, dj=dj, tsl=tsl, ut=ut: h.matmul(br.ap, lhsT=self.s5w.ap[:, dj, 0, :], rhs=u_sb.ap[:, ut, tsl], start=True, stop=True)],
                  reads=[self.s5w, u_sb], writes=[br])
            fw.mm([lambda h, bi=bi, dj=dj, tsl=tsl, ut=ut: h.matmul(bi.ap, lhsT=self.s5w.ap[:, dj, 1, :], rhs=u_sb.ap[:, ut, tsl], start=True, stop=True)],
                  reads=[self.s5w, u_sb], writes=[bi])
            return br, bi

        ybanks = None
        yfirst = None
        tbk = self.bank(hold=True)
        deferred = []
        prev_last = None
        nxt = issue_bu(0)
        for k, (ut, d, jj, ti, tb) in enumerate(units):
            rev = (d == 1)
            j = ut * 4 + jj
            dj = d * 8 + j
            if d == 0 and jj == 0 and ti == 0:
                ybanks = [self.bank(hold=True), self.bank(hold=True)]
                yfirst = [True, True]
            Ecv = Ec.ap[:, dj, :]
            Esv = Es.ap[:, dj, :]
            if rev:
                Ecv = Ecv[:, ::-1]
                Esv = Esv[:, ::-1]
            Ec2 = Ecv.unsqueeze(1).to_broadcast([128, 2, 256])
            Es2 = Esv.unsqueeze(1).to_broadcast([128, 2, 256])
            rb = c.ap[:, dj, 1:2].to_broadcast([128, 256])
            gr, gi = grs[k % 2], gis[k % 2]
            br, bi = nxt
            self.tt("dve", v2(t2.ap), v2(bi.ap), Es2, ALU.mult, [bi, Es], [t2])
            self.tt("dve", v2(tbk.ap), v2(br.ap), Ec2, ALU.mult, [br, Ec], [tbk])
            self.tt("dve", bpr.ap, tbk.ap, t2.ap, ALU.add, [tbk, t2], [bpr])
            self.tt("dve", v2(t2.ap), v2(br.ap), Es2, ALU.mult, [br, Es], [t2])
            self.tt("dve", v2(tbk.ap), v2(bi.ap), Ec2, ALU.mult, [bi, Ec], [tbk])
            self.tt("dve", bpi.ap, tbk.ap, t2.ap, ALU.subtract, [tbk, t2], [bpi])
            if k + 1 < len(units):
                nxt = issue_bu(k + 1)
            for fn in deferred:
                fn()
            deferred = []
            segs = [1, 0] if rev else [0, 1]
            for sj, s2 in enumerate(segs):
                s = tb * 2 + s2
                si = ti * 2 + sj
                if g == 1:
                    f_r = seg(bpr.ap, s2, rev)[:, 0:1]
                    f_i = seg(bpi.ap, s2, rev)[:, 0:1]
                    if si == 0:
                        hpr, hpi = c.ap[:, dj, 6:7], c.ap[:, dj, 7:8]
                        self.stt(f_r, hpr, c.ap[:, dj, 2:3], f_r, ALU.mult, ALU.add, [c, bpr], [bpr])
                        self.stt(f_i, hpi, c.ap[:, dj, 2:3], f_i, ALU.mult, ALU.add, [c, bpi], [bpi])
                        self.ts("dve", sm.ap[:, 0:1], hpi, c.ap[:, dj, 3:4], None, ALU.mult, None, [c], [sm])
                        self.stt(f_i, hpr, c.ap[:, dj, 3:4], f_i, ALU.mult, ALU.add, [c, bpi], [bpi])
                        self.tt("dve", f_r, f_r, sm.ap[:, 0:1], ALU.subtract, [bpr, sm], [bpr])
                    else:
                        pgr, pgi = prev_last
                        self.stt(f_r, pgr[0], Kc.ap[:, dj, 0:1], f_r, ALU.mult, ALU.add, [pgr[1], Kc, bpr], [bpr])
                        self.stt(f_i, pgi[0], Kc.ap[:, dj, 0:1], f_i, ALU.mult, ALU.add, [pgi[1], Kc, bpi], [bpi])
                        self.stt(f_r, pgi[0], Kc.ap[:, dj, 2:3], f_r, ALU.mult, ALU.add, [pgi[1], Kc, bpr], [bpr])
                        self.stt(f_i, pgr[0], Kc.ap[:, dj, 1:2], f_i, ALU.mult, ALU.add, [pgr[1], Kc, bpi], [bpi])
                for (src, dst) in ((bpr, gr), (bpi, gi)):
                    fw.op("dve", lambda h, src=src, dst=dst, s2=s2, rev=rev, rb=rb: h.tensor_tensor_scan(
                        out=seg(dst.ap, s2, rev), data0=rb, data1=seg(src.ap, s2, rev), initial=0.0, op0=ALU.mult, op1=ALU.add),
                        reads=[src, c], writes=[dst])
                g_r = seg(gr.ap, s2, rev)[:, 255:256]
                g_i = seg(gi.ap, s2, rev)[:, 255:256]
                prev_last = ((g_r, gr), (g_i, gi))
                if g == 0:
                    def state_ops(dj=dj, g_r=g_r, g_i=g_i, gr=gr, gi=gi, s=s, d=d, j=j):
                        e_c = Ec.ap[:, dj, 255:256]
                        e_s = Es.ap[:, dj, 255:256]
                        o_r, o_i = self.outst.ap[:, s, d, j, 0:1], self.outst.ap[:, s, d, j, 1:2]
                        self.tt("dve", sm.ap[:, 1:2], g_i, e_s, ALU.mult, [gi, Es], [sm])
                        self.tt("dve", sm.ap[:, 2:3], g_i, e_c, ALU.mult, [gi, Ec], [sm])
                        self.stt(o_r, g_r, e_c, sm.ap[:, 1:2], ALU.mult, ALU.subtract, [gr, Ec, sm], [self.outst])
                        self.stt(o_i, g_r, e_s, sm.ap[:, 2:3], ALU.mult, ALU.add, [gr, Es, sm], [self.outst])
                    deferred.append(state_ops)
            self.tt("pool", v2(p1.ap), v2(gr.ap), Ec2, ALU.mult, [gr, Ec], [p1])
            self.tt("pool", v2(p2.ap), v2(gi.ap), Es2, ALU.mult, [gi, Es], [p2])
            self.tt("pool", wr.ap, p1.ap, p2.ap, ALU.subtract, [p1, p2], [wr])
            self.tt("pool", v2(p1.ap), v2(gi.ap), Ec2, ALU.mult, [gi, Ec], [p1])
            self.tt("pool", v2(p2.ap), v2(gr.ap), Es2, ALU.mult, [gr, Es], [p2])
            self.tt("pool", wi.ap, p1.ap, p2.ap, ALU.add, [p1, p2], [wi])
            yb = ybanks[tb]
            last = (d == 1 and jj == 3)
            fw.mm([lambda h, yb=yb, dj=dj, st_=yfirst[tb]: h.matmul(yb.ap, lhsT=self.s5w.ap[:, dj, 2, :], rhs=wr.ap, start=st_, stop=False),
                   lambda h, yb=yb, dj=dj, last=last: h.matmul(yb.ap, lhsT=self.s5w.ap[:, dj, 3, :], rhs=wi.ap, start=False, stop=last)],
                  reads=[self.s5w, wr, wi], writes=[yb])
            yfirst[tb] = False
            if d == 1 and jj == 3 and ti == 1:
                for tb2 in range(2):
                    yb = ybanks[tb2]
                    sl = slice(tb2 * 512, (tb2 + 1) * 512)
                    self.stt(t1.ap, u_sb.ap[:, ut, sl], self.s5d.ap[:, ut:ut + 1], yb.ap, ALU.mult, ALU.add, [u_sb, self.s5d, yb], [t1])
                    self.act(ygb.ap[:, ut, sl], t1.ap, AF.Gelu_apprx_tanh, [t1], [ygb])
                    self.unhold(yb)
        for fn in deferred:
            fn()
        self.unhold(tbk)
        for ot in range(2):
            for tb in range(2):
                sl = slice(tb * 512, (tb + 1) * 512)
                bk = self.bank()
                fw.mm([lambda h, bk=bk, kc=kc, ot=ot, sl=sl: h.matmul(bk.ap, lhsT=self.wglu.ap[:, kc, ot * 128:(ot + 1) * 128], rhs=ygb.ap[:, kc, sl],
                                                                      start=(kc == 0), stop=(kc == 1)) for kc in range(2)], reads=[self.wglu, ygb], writes=[bk])
                self.act(t1.ap, bk.ap, AF.Sigmoid, [bk], [t1])
                self.tt("dve", s5y.ap[:, ot, sl], ygb.ap[:, ot, sl], t1.ap, ALU.mult, [ygb, t1], [s5y])

    def store_s5_states(self, l):
        fw, ar, O = self.fw, self.ar, self.O
        mk = ar.mark()
        tr = ar.alloc([128, 128], F32)
        bk = self.bank()
        fw.mm([lambda h: h.transpose(out=bk.ap[:, 0:128], in_=self.outst.ap.rearrange("p s d j r -> p (s d j r)"), identity=self.ident_f.ap)],
              reads=[self.outst, self.ident_f], writes=[bk])
        self.cp("dve", tr.ap, bk.ap[:, 0:128], [bk], [tr])
        sem = Buf("s5out")
        for ri, nm in enumerate(("ns5r", "ns5i")):
            for s in range(4):
                for d in range(2):
                    r0 = ((s * 2 + d) * 8) * 2 + ri
                    src = tr.ap[r0:r0 + 15:2, :]
                    dst = O[nm][s, l, d, :].rearrange("(j q) -> j q", q=128)
                    dd = fw.dma("sp", lambda h, src=src, dst=dst: h.dma_start(out=dst, in_=src), sem, reads=[tr])
                    fw.final.append(dd)
        ar.release(mk)

    def lru(self, l, g, lruy):
        fw, ar, I, O = self.fw, self.ar, self.I, self.O
        lc = self.lrucols
        xr = ar.alloc([128, 2, TG], F32)
        gg = ar.alloc([128, 2, TG], F32)
        xc = ar.alloc([128, 2, TG], F32)
        xcb = ar.alloc([128, 2, TG], BF16)

        def xr_cons(ft, tb, bk):
            self.cp("act", xr.ap[:, ft, tb * 512:(tb + 1) * 512], bk.ap, [bk], [xr])

        def xg_cons(ft, tb, bk):
            self.act(gg.ap[:, ft, tb * 512:(tb + 1) * 512], bk.ap, AF.Gelu_apprx_tanh, [bk], [gg])

        self.proj_fm(self.win_cols(l, 1792), 2, self.h_rhs, [self.h], 8, xr_cons)
        self.proj_fm(self.win_cols(l, 2048), 2, self.h_rhs, [self.h], 8, xg_cons)
        nseq, L = (4, 256) if g == 0 else (1, 1024)
        for t in range(2):
            xv = xr.ap[:, t, :].rearrange("p (s q) -> p s q", s=nseq)
            cv = xc.ap[:, t, :].rearrange("p (s q) -> p s q", s=nseq)
            self.ts("dve", cv, xv, lc.ap[:, t, 2:3], lc.ap[:, t, 4:5], ALU.mult, ALU.add, [xr, lc], [xc])
            for k in (0, 1, 3):
                sh = k - 2
                lo, hi = max(0, -sh), L - max(0, sh)
                self.stt(cv[:, :, lo:hi], xv[:, :, lo + sh:hi + sh], lc.ap[:, t, k:k + 1], cv[:, :, lo:hi], ALU.mult, ALU.add, [xr, lc, xc], [xc])
            self.cp("act", xcb.ap[:, t, :], xc.ap[:, t, :], [xc], [xcb])
        r_ = ar.alloc([128, 512], F32)
        i_ = ar.alloc([128, 512], F32)
        th = ar.alloc([128, 512], F32)
        e2 = ar.alloc([128, 512], F32)
        a_sb = ar.alloc([128, TG], F32)
        b_sb = ar.alloc([128, TG], F32)
        hs = [ar.alloc([128, TG], F32) for _ in range(2)]
        for t in range(2):
            for d in range(2):
                rev = (d == 1)
                for tb in range(2):
                    sl = slice(tb * 512, (tb + 1) * 512)
                    pr = self.bank()
                    pi = self.bank()
                    fw.mm([lambda h, pr=pr, d=d, t=t, sl=sl: h.matmul(pr.ap, lhsT=self.lruw.ap[:, 0 * 4 + d * 2 + t, :], rhs=xcb.ap[:, t, sl], start=True, stop=True)],
                          reads=[self.lruw, xcb], writes=[pr])
                    fw.mm([lambda h, pi=pi, d=d, t=t, sl=sl: h.matmul(pi.ap, lhsT=self.lruw.ap[:, 1 * 4 + d * 2 + t, :], rhs=xcb.ap[:, t, sl], start=True, stop=True)],
                          reads=[self.lruw, xcb], writes=[pi])
                    self.act(r_.ap, pr.ap, AF.Sigmoid, [pr, lc], [r_], bias=lc.ap[:, t, 5 + d:6 + d])
                    self.act(i_.ap, pi.ap, AF.Sigmoid, [pi, lc], [i_], bias=lc.ap[:, t, 7 + d:8 + d])
                    self.act(a_sb.ap[:, sl], r_.ap, AF.Exp, [r_, lc], [a_sb], scale=lc.ap[:, t, 9 + d:10 + d])
                    self.act(th.ap, r_.ap, AF.Tanh, [r_, lc], [th], scale=lc.ap[:, t, 11 + d:12 + d])
                    self.act(e2.ap, r_.ap, AF.Exp, [r_, lc], [e2], scale=lc.ap[:, t, 13 + d:14 + d])
                    self.stt(e2.ap, e2.ap, 1.0, th.ap, ALU.add, ALU.mult, [e2, th], [e2])
                    self.act(e2.ap, e2.ap, AF.Sqrt, [e2], [e2])
                    self.tt("dve", i_.ap, i_.ap, xc.ap[:, t, sl], ALU.mult, [i_, xc], [i_])
                    self.tt("dve", b_sb.ap[:, sl], e2.ap, i_.ap, ALU.mult, [e2, i_], [b_sb])
                hd = hs[d]
                for s in range(nseq):
                    sq = slice(s * L, (s + 1) * L)
                    av, bv, hv = a_sb.ap[:, sq], b_sb.ap[:, sq], hd.ap[:, sq]
                    if rev:
                        av, bv, hv = av[:, ::-1], bv[:, ::-1], hv[:, ::-1]
                    init = 0.0 if g == 0 else self.lrust0.ap[:, d, t:t + 1]
                    rd = [a_sb, b_sb] + ([] if g == 0 else [self.lrust0])
                    fw.op("dve", lambda h, av=av, bv=bv, hv=hv, init=init: h.tensor_tensor_scan(out=hv, data0=av, data1=bv, initial=init, op0=ALU.mult, op1=ALU.add),
                          reads=rd, writes=[hd])
                    if g == 0:
                        self.cp("dve", self.lrust.ap[:, s, d, t:t + 1], hv[:, L - 1:L], [hd], [self.lrust])
            self.tt("dve", hs[0].ap, hs[0].ap, hs[1].ap, ALU.add, [hs[0], hs[1]], [hs[0]])
            self.tt("dve", lruy.ap[:, t, :], hs[0].ap, gg.ap[:, t, :], ALU.mult, [hs[0], gg], [lruy])
        if g == 0:
            sem = Buf("lruout")
            for s in range(4):
                for d in range(2):
                    dst = O["nlru"][s, l, d, :].rearrange("(t p) -> p t", p=128)
                    src = self.lrust.ap[:, s, d, :]
                    dd = fw.dma("sp", lambda h, src=src, dst=dst: h.dma_start(out=dst, in_=src, allow_slow_non_contiguous=True), sem, reads=[self.lrust])
                    fw.final.append(dd)

    def merge(self, l, g, attnT, s5y, lruy):
        fw, ar, I = self.fw, self.ar, self.I
        merged = ar.alloc([128, 8, TG], BF16)
        sig = [ar.alloc([128, 512], F32) for _ in range(2)]
        pr = [ar.alloc([128, 512], F32) for _ in range(3)]
        si = [0]

        def br_rhs(kc, sl):
            if kc < 4:
                return attnT.ap[:, kc, sl]
            if kc < 6:
                return s5y.ap[:, kc - 4, sl]
            return lruy.ap[:, kc - 6, sl]

        for sp in range(4):
            c0 = sp * 256
            wb = self.wslice([(0, 4, 256, I["w_br_attn"][l][:, c0:c0 + 256]), (4 * 256, 2, 256, I["w_br_s5"][l][:, c0:c0 + 256]),
                              (6 * 256, 2, 256, I["w_br_lru"][l][:, c0:c0 + 256])])
            wbv = wb.ap.rearrange("p (k n) -> p k n", k=8)
            wg = [self.wslice([(0, 8, 256, I["w_in"][l][:, 2304 + b * 1024 + c0:2304 + b * 1024 + c0 + 256])]) for b in range(3)]
            for t in range(2):
                ft = sp * 2 + t
                for tb in range(2):
                    sl = slice(tb * 512, (tb + 1) * 512)
                    for b, (k0, k1) in enumerate(((0, 4), (4, 6), (6, 8))):
                        gb = self.bank()
                        wgv = wg[b].ap.rearrange("p (k n) -> p k n", k=8)
                        fw.mm([lambda h, gb=gb, wgv=wgv, kc=kc, t=t, tb=tb: h.matmul(gb.ap, lhsT=wgv[:, kc, t * 128:(t + 1) * 128], rhs=self.h_rhs(kc, tb),
                                                                                     start=(kc == 0), stop=(kc == 7)) for kc in range(8)],
                              reads=[wg[b], self.h], writes=[gb])
                        bb = self.bank()
                        fw.mm([lambda h, bb=bb, kc=kc, t=t, sl=sl, k0=k0, k1=k1: h.matmul(bb.ap, lhsT=wbv[:, kc, t * 128:(t + 1) * 128], rhs=br_rhs(kc, sl),
                                                                                         start=(kc == k0), stop=(kc == k1 - 1)) for kc in range(k0, k1)],
                              reads=[wb, attnT, s5y, lruy], writes=[bb])
                        sg = sig[si[0] % 2]
                        si[0] += 1
                        self.act(sg.ap, gb.ap, AF.Sigmoid, [gb], [sg])
                        self.tt("dve", pr[b].ap, bb.ap, sg.ap, ALU.mult, [bb, sg], [pr[b]])
                    self.tt("dve", pr[0].ap, pr[0].ap, pr[1].ap, ALU.add, [pr[0], pr[1]], [pr[0]])
                    self.tt("dve", merged.ap[:, ft, sl], pr[0].ap, pr[2].ap, ALU.add, [pr[0], pr[2]], [merged])
                    self.tick()

        def out_cons(ft, tb, bk):
            sl = slice(tb * 512, (tb + 1) * 512)
            xv = self.x.ap[:, g, ft, sl]
            self.stt(xv, bk.ap, self.mod.ap[:, l, 2 * 8 + ft, g:g + 1], xv, ALU.mult, ALU.add, [bk, self.mod, self.x], [self.x])
            self.tick()

        self.proj_fm(lambda c0, n: [(0, 8, n, I["w_out"][l][:, c0:c0 + n])], 8,
                     lambda kc, tb: merged.ap[:, kc, tb * 512:(tb + 1) * 512], [merged], 8, out_cons)

    def ffn(self, l, g):
        fw, ar, I = self.fw, self.ar, self.I
        gg = ar.alloc([128, 22, TG], BF16)
        a_sb = [ar.alloc([128, TG], F32) for _ in range(2)]
        c_sb = [ar.alloc([128, TG], F32) for _ in range(2)]
        gl = [ar.alloc([128, TG], BF16) for _ in range(2)]
        nseq, L = (4, 256) if g == 0 else (1, 1024)
        fc = self.ffcols
        it = 0
        if l == 0 and g == 0:
            self.bg = self.mod_gen(1, look=2)
        for sp in range(11):
            wa = self.wslice([(0, 8, 256, I["ffn_w_up"][l][:, sp * 256:sp * 256 + 256])])
            wb = self.wslice([(0, 8, 256, I["ffn_w_up"][l][:, 2816 + sp * 256:2816 + sp * 256 + 256])])
            wav = wa.ap.rearrange("p (k n) -> p k n", k=8)
            wbv = wb.ap.rearrange("p (k n) -> p k n", k=8)
            for t in range(2):
                ft = sp * 2 + t
                a_, c_, g_ = a_sb[it % 2], c_sb[it % 2], gl[it % 2]
                it += 1
                for tb in range(2):
                    bk = self.bank()
                    fw.mm([lambda h, bk=bk, kc=kc, t=t, tb=tb: h.matmul(bk.ap, lhsT=wav[:, kc, t * 128:(t + 1) * 128], rhs=self.h_rhs(kc, tb),
                                                                       start=(kc == 0), stop=(kc == 7)) for kc in range(8)], reads=[wa, self.h], writes=[bk])
                    self.cp("act", a_.ap[:, tb * 512:(tb + 1) * 512], bk.ap, [bk], [a_])
                av = a_.ap.rearrange("p (s q) -> p s q", s=nseq)
                cv = c_.ap.rearrange("p (s q) -> p s q", s=nseq)
                self.act(cv, av, AF.Identity, [a_, fc], [c_], bias=fc.ap[:, ft, 3:4], scale=fc.ap[:, ft, 1:2])
                self.stt(cv[:, :, 1:L], av[:, :, 0:L - 1], fc.ap[:, ft, 0:1], cv[:, :, 1:L], ALU.mult, ALU.add, [a_, fc, c_], [c_])
                self.stt(cv[:, :, 0:L - 1], av[:, :, 1:L], fc.ap[:, ft, 2:3], cv[:, :, 0:L - 1], ALU.mult, ALU.add, [a_, fc, c_], [c_])
                self.act(g_.ap, c_.ap, AF.Gelu_apprx_tanh, [c_], [g_])
                for tb in range(2):
                    sl = slice(tb * 512, (tb + 1) * 512)
                    bk = self.bank()
                    fw.mm([lambda h, bk=bk, kc=kc, t=t, tb=tb: h.matmul(bk.ap, lhsT=wbv[:, kc, t * 128:(t + 1) * 128], rhs=self.h_rhs(kc, tb),
                                                                       start=(kc == 0), stop=(kc == 7)) for kc in range(8)], reads=[wb, self.h], writes=[bk])
                    self.tt("dve", gg.ap[:, ft, sl], bk.ap, g_.ap[:, sl], ALU.mult, [bk, g_], [gg])
                self.tick()
        for ft in range(8):
            w0 = self.wslice([(0, 11, 128, I["ffn_w_down"][l][0:1408, ft * 128:(ft + 1) * 128])])
            w1 = self.wslice([(0, 11, 128, I["ffn_w_down"][l][1408:2816, ft * 128:(ft + 1) * 128])])
            wv = [w0.ap[:, 0:1408].rearrange("p (k n) -> p k n", k=11), w1.ap[:, 0:1408].rearrange("p (k n) -> p k n", k=11)]
            for tb in range(2):
                sl = slice(tb * 512, (tb + 1) * 512)
                bk = self.bank()
                fw.mm([lambda h, bk=bk, kc=kc, sl=sl: h.matmul(bk.ap, lhsT=wv[kc // 11][:, kc % 11, :], rhs=gg.ap[:, kc, sl],
                                                               start=(kc == 0), stop=(kc == 21)) for kc in range(22)], reads=[w0, w1, gg], writes=[bk])
                xv = self.x.ap[:, g, ft, sl]
                self.stt(xv, bk.ap, self.mod.ap[:, l, 5 * 8 + ft, g:g + 1], xv, ALU.mult, ALU.add, [bk, self.mod, self.x], [self.x])
            self.tick()
        self.drain()

    def final_norm(self, g):
        fw, ar, O = self.fw, self.ar, self.O
        m = ar.mark()
        rstd = ar.alloc([128, TG], F32)
        self.rms_rstd(g, rstd)
        y = ar.alloc([128, 8, TG], F32)
        for kc in range(8):
            self.stt(y.ap[:, kc, :], self.x.ap[:, g, kc, :], self.gcols.ap[:, 4, kc:kc + 1], rstd.ap, ALU.mult, ALU.mult, [self.x, self.gcols, rstd], [y])
        dst_t = O["yp" if g == 0 else "ys"]
        for tt in range(8):
            for qd in range(4):
                s = self.stg_i % 2
                self.stg_i += 1
                stg = self.stg[s]
                bk = self.bank()
                fw.mm([lambda h, bk=bk, j=j, qd=qd, tt=tt: h.transpose(out=bk.ap[:, j * 128:(j + 1) * 128], in_=y.ap[:, qd * 2 + j, tt * 128:(tt + 1) * 128],
                                                                       identity=self.ident_f.ap) for j in range(2)], reads=[y, self.ident_f], writes=[bk])
                self.cp("act" if qd % 2 else "dve", stg.ap, bk.ap[:, 0:256], [bk], [stg])
                dst = dst_t[tt * 128:(tt + 1) * 128, qd * 256:(qd + 1) * 256]
                dd = fw.dma("sp", lambda h, stg=stg, dst=dst: h.dma_start(out=dst, in_=stg.ap), self.stg_sem[s], reads=[stg])
                fw.final.append(dd)
        ar.release(m)


_W_KEYS = ["w_ada", "b_ada", "g_norm1", "g_norm2", "w_in", "rpb", "s5_lam_re", "s5_lam_im", "s5_log_step", "s5_b_re", "s5_b_im",
           "s5_c_re", "s5_c_im", "s5_d", "s5_w_glu", "lru_conv_w", "lru_conv_b", "lru_w_a", "lru_b_a", "lru_w_x", "lru_b_x", "lru_lam",
           "w_br_attn", "w_br_s5", "w_br_lru", "w_out", "ffn_w_up", "ffn_conv_w", "ffn_conv_b", "ffn_w_down", "g_final"]


def make_in_maps(inp):
    f = lambda a: np.ascontiguousarray(np.asarray(a, dtype=np.float32))
    shared = {k: f(inp[k]) for k in _W_KEYS}
    maps = []
    for i in range(NCORES):
        m = dict(shared)
        m["xp"] = f(inp["x_prompt"][4 * i:4 * i + 4]).reshape(TG, D)
        m["xs"] = f(inp["x_sample"][i]).reshape(TG, D)
        m["ck"] = f(inp["cache_k"][i]).reshape(DEPTH, 512, 512)
        m["cv"] = f(inp["cache_v"][i]).reshape(DEPTH, 512, 512)
        m["s5r"] = f(inp["state_s5_re"][i]).reshape(DEPTH, 2, 1024)
        m["s5i"] = f(inp["state_s5_im"][i]).reshape(DEPTH, 2, 1024)
        m["slru"] = f(inp["state_lru"][i]).reshape(DEPTH, 2, 256)
        m["cvec"] = f(np.stack([np.asarray(inp["c_ctx"]), np.asarray(inp["c"])[i]], axis=0))
        maps.append(m)
    return maps


def assemble(results):
    cat = lambda k: np.concatenate([np.asarray(r[k]) for r in results], axis=0)
    y_prompt = cat("yp").reshape(32, 256, D)
    y_sample = cat("ys").reshape(8, 1024, D)
    new_k = cat("nk").reshape(32, DEPTH, 256, 8, 64)
    new_v = cat("nv").reshape(32, DEPTH, 256, 8, 64)
    ns5r = cat("ns5r").reshape(32, DEPTH, 2, 16, 64)
    ns5i = cat("ns5i").reshape(32, DEPTH, 2, 16, 64)
    nlru = cat("nlru").reshape(32, DEPTH, 2, 256)
    return tuple(np.ascontiguousarray(a, dtype=np.float32) for a in (y_prompt, y_sample, new_k, new_v, ns5r, ns5i, nlru))


def kernel(**inputs):
    nc = Builder().build()
    in_maps = make_in_maps(inputs)
    res = run_bass_kernel_spmd(nc, in_maps, core_ids=list(range(NCORES)))
    return assemble(res.results)


def debug_run(inputs, stage, ncores=1, trace=False):
    b = Builder(stage=stage)
    nc = b.build()
    in_maps = make_in_maps(inputs)[:ncores]
    res = run_bass_kernel_spmd(nc, in_maps, core_ids=list(range(ncores)), trace=trace)
    if trace:
        print("EXEC_NS", stage, res.exec_time_ns)
    return b, res.results
```

```python
import math
import types as _types
import numpy as np
from contextlib import ExitStack
import concourse.bass as bass
import concourse.mybir as mybir
from concourse.bass_utils import run_bass_kernel_spmd

F32 = mybir.dt.float32
BF16 = mybir.dt.bfloat16
I32 = mybir.dt.int32
AF = mybir.ActivationFunctionType
ALU = mybir.AluOpType

NCORES = 8
D = 1024
TG = 1024
DEPTH = 2
NEG = -30000.0
EPS = 1e-6
IN_W = 5376
PAGE = 512


class Buf:
    __slots__ = ("name", "lw", "rd", "dsem", "dcnt", "excl")

    def __init__(self, name, excl=False):
        self.name = name
        self.excl = excl
        self.lw = None
        self.rd = []
        self.dsem = None
        self.dcnt = 0


class Reg:
    __slots__ = ("ap", "bufs", "tag")

    def __init__(self, ap, bufs, tag=None):
        self.ap = ap
        self.bufs = bufs
        self.tag = tag

    def __getitem__(self, k):
        return self.ap[k]


def _freeze(f):
    if getattr(f, "__closure__", None) is None:
        return f
    cells = []
    for c in f.__closure__:
        try:
            cells.append(_types.CellType(c.cell_contents))
        except ValueError:
            cells.append(c)
    return _types.FunctionType(f.__code__, f.__globals__, f.__name__, f.__defaults__, tuple(cells))


class Eng:
    def __init__(self, name):
        self.key = name
        self.cnt = 0
        self.seen = {}
        self.prog = []


class FW:
    def __init__(self, nc, stack):
        self.nc = nc
        self.stack = stack
        self.sems = {}
        self.E = {}
        for n in ("pe", "act", "dve", "pool", "sp"):
            self.sems[n] = stack.enter_context(nc.semaphore("s_" + n))
            self.E[n] = Eng(n)
        self.ndsem = 0
        self.dsem_free = []
        self.final = []

    def _expand(self, lst):
        out = []
        for r in lst:
            if isinstance(r, Buf):
                out.append(r)
            else:
                out.extend(r.bufs)
        return out

    def _need(self, eng, dep, same_ok):
        key, val, clock = dep
        if same_ok and key == eng.key:
            return
        if eng.seen.get(key, 0) >= val:
            return
        eng.prog.append(("wait", key, val))
        eng.seen[key] = val
        for k, v in clock.items():
            if eng.seen.get(k, 0) < v:
                eng.seen[k] = v

    def _deps(self, eng, reads, writes):
        for b in reads:
            if b.lw is not None:
                self._need(eng, b.lw, False)
            if b.excl:
                for r in b.rd:
                    self._need(eng, r, True)
        for b in writes:
            if b.lw is not None:
                self._need(eng, b.lw, True)
            for r in b.rd:
                self._need(eng, r, True)

    def _commit(self, dep, reads, writes):
        for b in reads:
            b.rd.append(dep)
        for b in writes:
            b.lw = dep
            b.rd = []

    def op(self, en, fn, reads=(), writes=()):
        eng = self.E[en]
        reads = self._expand(reads)
        writes = self._expand(writes)
        self._deps(eng, reads, writes)
        eng.cnt += 1
        eng.prog.append(("ins", _freeze(fn), eng.key, 1))
        clock = dict(eng.seen)
        clock[eng.key] = eng.cnt
        dep = (eng.key, eng.cnt, clock)
        self._commit(dep, reads, writes)
        return dep

    def mm(self, fns, reads=(), writes=()):
        eng = self.E["pe"]
        reads = self._expand(reads)
        writes = self._expand(writes)
        self._deps(eng, reads, writes)
        for f in fns[:-1]:
            eng.prog.append(("ins", _freeze(f), None, 0))
        eng.cnt += 1
        eng.prog.append(("ins", _freeze(fns[-1]), eng.key, 1))
        clock = dict(eng.seen)
        clock[eng.key] = eng.cnt
        dep = (eng.key, eng.cnt, clock)
        self._commit(dep, reads, writes)
        return dep

    def dma(self, qn, fn, dbuf, reads=(), writes=()):
        eng = self.E[qn]
        reads = self._expand(reads)
        writes = self._expand(writes)
        self._deps(eng, reads, writes)
        if dbuf.dsem is None:
            self.ndsem += 1
            dbuf.dsem = self.stack.enter_context(self.nc.semaphore(f"d{self.ndsem}"))
        if dbuf.dcnt:
            self._need(eng, ("D%d" % id(dbuf), dbuf.dcnt, {}), False)
        dbuf.dcnt += 16
        key = "D%d" % id(dbuf)
        self.sems[key] = dbuf.dsem
        eng.prog.append(("ins", _freeze(fn), key, 16))
        dep = (key, dbuf.dcnt, dict(eng.seen))
        self._commit(dep, reads, writes)
        return dep

    def emit(self):
        nc = self.nc
        sems = self.sems
        for d in self.final:
            self._need(self.E["sp"], d, False)

        def replay(eng, h):
            for it in eng.prog:
                if it[0] == "wait":
                    h.wait_ge(sems[it[1]], it[2])
                else:
                    ins = it[1](h)
                    if it[2] is not None:
                        ins.then_inc(sems[it[2]], it[3])

        with nc.Block() as block:
            @block.sync
            def _(h):
                replay(self.E["sp"], h)

            @block.scalar
            def _(h):
                replay(self.E["act"], h)

            @block.vector
            def _(h):
                replay(self.E["dve"], h)

            @block.gpsimd
            def _(h):
                replay(self.E["pool"], h)

            @block.tensor
            def _(h):
                replay(self.E["pe"], h)


class Arena:
    def __init__(self, fw, nbytes):
        self.fw = fw
        self.nbytes = nbytes
        self.t = fw.stack.enter_context(fw.nc.sbuf_tensor("arena", [128, nbytes // 2], BF16))
        self.pages = [Buf(f"pg{i}") for i in range((nbytes + PAGE - 1) // PAGE)]
        self.top = 0
        self.peak = 0

    def alloc(self, shape, dt, align=64):
        esz = 4 if dt in (F32, I32) else 2
        n = 1
        for s in shape[1:]:
            n *= s
        nb = n * esz
        if nb >= PAGE:
            align = max(align, PAGE)
        off = (self.top + align - 1) // align * align
        assert off + nb <= self.nbytes, f"arena overflow: need {off + nb} have {self.nbytes}"
        self.top = off + nb
        self.peak = max(self.peak, self.top)
        ap = self.t[0:shape[0], off // 2:(off + nb) // 2]
        if esz == 4:
            ap = ap.bitcast(dt)
        if len(shape) > 2:
            names = " ".join(f"d{i}" for i in range(1, len(shape)))
            kw = {f"d{i}": shape[i] for i in range(1, len(shape))}
            ap = ap.rearrange(f"p ({names}) -> p {names}", **kw)
        pages = self.pages[off // PAGE:(off + nb - 1) // PAGE + 1]
        return Reg(ap, pages)

    def mark(self):
        return self.top

    def release(self, m):
        self.top = m


def na_valid(kr, qr):
    w0 = min(max(qr - 4, 0), 8)
    return w0 <= kr < w0 + 8


class _Stop(Exception):
    pass


class Builder:
    def __init__(self, debug=False, stage=None):
        self.debug = debug
        self.stage = stage
        self.dbg_outs = []
        self.nc = bass.Bass("TRN2", target_bir_lowering=False)
        self.stack = ExitStack()
        self.dram = {}

    def din(self, name, shape):
        t = self.nc.dram_tensor(name, list(shape), F32, kind="ExternalInput")
        self.dram[name] = t
        return t.ap()

    def dout(self, name, shape):
        t = self.nc.dram_tensor(name, list(shape), F32, kind="ExternalOutput")
        self.dram[name] = t
        return t.ap()

    def build(self):
        nc = self.nc
        with self.stack as st:
            fw = self.fw = FW(nc, st)
            I = self.I = {}
            O = self.O = {}
            I["xp"] = self.din("xp", [TG, D])
            I["xs"] = self.din("xs", [TG, D])
            I["ck"] = self.din("ck", [DEPTH, 512, 512])
            I["cv"] = self.din("cv", [DEPTH, 512, 512])
            I["s5r"] = self.din("s5r", [DEPTH, 2, 1024])
            I["s5i"] = self.din("s5i", [DEPTH, 2, 1024])
            I["slru"] = self.din("slru", [DEPTH, 2, 256])
            I["cvec"] = self.din("cvec", [2, D])
            wshapes = {
                "w_ada": [2, D, 6 * D], "b_ada": [2, 6 * D], "g_norm1": [2, D], "g_norm2": [2, D],
                "w_in": [2, D, IN_W], "rpb": [2, 8, 15, 31],
                "s5_lam_re": [2, 2, 16, 64], "s5_lam_im": [2, 2, 16, 64], "s5_log_step": [2, 2, 16],
                "s5_b_re": [2, 16, 64, 16], "s5_b_im": [2, 16, 64, 16],
                "s5_c_re": [2, 2, 16, 16, 64], "s5_c_im": [2, 2, 16, 16, 64],
                "s5_d": [2, 256], "s5_w_glu": [2, 256, 256],
                "lru_conv_w": [2, 4, 256], "lru_conv_b": [2, 256],
                "lru_w_a": [2, 2, 4, 64, 64], "lru_b_a": [2, 2, 256],
                "lru_w_x": [2, 2, 4, 64, 64], "lru_b_x": [2, 2, 256], "lru_lam": [2, 2, 256],
                "w_br_attn": [2, 512, D], "w_br_s5": [2, 256, D], "w_br_lru": [2, 256, D],
                "w_out": [2, D, D], "ffn_w_up": [2, D, 5632], "ffn_conv_w": [2, 3, 2816],
                "ffn_conv_b": [2, 2816], "ffn_w_down": [2, 2816, D], "g_final": [D],
            }
            self.wshapes = wshapes
            for k, s in wshapes.items():
                I[k] = self.din(k, s)
            O["yp"] = self.dout("yp", [TG, D])
            O["ys"] = self.dout("ys", [TG, D])
            O["nk"] = self.dout("nk", [4, DEPTH, 256, 512])
            O["nv"] = self.dout("nv", [4, DEPTH, 256, 512])
            O["ns5r"] = self.dout("ns5r", [4, DEPTH, 2, 1024])
            O["ns5i"] = self.dout("ns5i", [4, DEPTH, 2, 1024])
            O["nlru"] = self.dout("nlru", [4, DEPTH, 2, 256])

            self.ar = Arena(fw, 207 * 1024)
            ps_t = st.enter_context(nc.psum_tensor("psum", [128, 8, 512], F32))
            self.banks = [Reg(ps_t[:, i, :], [Buf(f"bank{i}", excl=True)], i) for i in range(8)]
            self.bank_i = 0
            self.held = set()
            try:
                self.setup()
                self.mod_init()
                self.chk("setup")
                for l in range(DEPTH):
                    if l == 0:
                        self.prep_bg = False
                        for _ in self.layer_prep_gen(0):
                            pass
                        self.bg = self.mod_gen(0, look=4, s0=0, s1=8, doA=(True, False))
                        self.drain()
                    self.chk(f"prep{l}")
                    for g in range(2):
                        self.group_layer(l, g)
                        self.chk(f"gl{l}{g}")
                for g in range(2):
                    self.final_norm(g)
            except _Stop:
                pass
            fw.emit()
        return nc

    def chk(self, name):
        if self.stage == name:
            raise _Stop()

    def dbg(self, name, reg, ap, shape):
        t = self.nc.dram_tensor("dbg_" + name, list(shape), F32, kind="ExternalOutput").ap()
        d = self.fw.dma("sp", lambda h: h.dma_start(out=t, in_=ap), Buf("dbg_" + name), reads=[reg])
        self.fw.final.append(d)
        self.dbg_outs.append("dbg_" + name)

    def bank(self, hold=False):
        while True:
            i = self.bank_i % 8
            self.bank_i += 1
            if i not in self.held:
                break
        if hold:
            self.held.add(i)
        return self.banks[i]

    def unhold(self, bk):
        self.held.discard(bk.tag)

    def col(self, reg, i):
        return reg.ap[:, i:i + 1]

    def setup(self):
        fw, ar, nc, I = self.fw, self.ar, self.nc, self.I
        self.x = ar.alloc([128, 2, 8, TG], F32)
        self.ident_bf = ar.alloc([128, 128], BF16, align=PAGE)
        self.ident_f = ar.alloc([128, 128], F32)
        self.ones_bf = ar.alloc([128, 128], BF16)
        self.epsc = ar.alloc([128, 1], F32)
        self.gcols = ar.alloc([128, 5, 8], F32)
        self.cTb = ar.alloc([128, 8, 2], BF16)
        self.badaT = ar.alloc([128, 2, 48], F32)
        self.mod = ar.alloc([128, 2, 48, 2], F32, align=PAGE)
        self.A1 = ar.alloc([128, 2, 8, 2], F32)
        self.A2 = ar.alloc([128, 2, 8, 2], F32)
        self.s5cols = ar.alloc([128, 16, 8], F32, align=PAGE)
        self.lrucols = ar.alloc([128, 2, 16], F32)
        self.s5d = ar.alloc([128, 2], F32)
        self.ffcols = ar.alloc([128, 22, 4], F32)
        self.lrust0 = ar.alloc([128, 2, 2], F32)
        self.lruw = ar.alloc([128, 8, 128], BF16)
        self.wglu = ar.alloc([128, 2, 256], BF16)
        self.s5w = ar.alloc([128, 16, 4, 128], BF16, align=PAGE)
        self.outst = ar.alloc([128, 4, 2, 8, 2], F32, align=PAGE)
        self.lrust = ar.alloc([128, 4, 2, 2], F32)
        self.slots = [ar.alloc([128, 2048], BF16, align=PAGE) for _ in range(6)]
        self.slot_sem = [Buf(f"slotsem{i}") for i in range(6)]
        self.slot_i = 0
        self.stg = [ar.alloc([128, 256], F32, align=PAGE) for _ in range(2)]
        self.stg_sem = [Buf("stg0"), Buf("stg1")]
        self.stg_i = 0
        self.small_sem = Buf("small")
        self.h = ar.alloc([128, 8, TG], BF16, align=PAGE)
        self.bg = None
        fw.op("pool", lambda h: h.memset(self.epsc.ap, EPS), writes=[self.epsc])
        self.scr_mark = ar.mark()

        idb, idf, ones = self.ident_bf, self.ident_f, self.ones_bf
        fw.op("pool", lambda h: h.memset(idf.ap, 1.0), writes=[idf])
        fw.op("pool", lambda h: h.affine_select(out=idf.ap, in_=idf.ap, pattern=[[-1, 128]], compare_op=ALU.is_equal,
                                                fill=0.0, base=0, channel_multiplier=1), reads=[idf], writes=[idf])
        fw.op("dve", lambda h: h.tensor_copy(out=idb.ap, in_=idf.ap), reads=[idf], writes=[idb])
        fw.op("dve", lambda h: h.memset(ones.ap, 1.0), writes=[ones])

        for g, name in enumerate(("xp", "xs")):
            for tt in range(8):
                for qd in range(4):
                    s = self.stg_i % 2
                    self.stg_i += 1
                    stg = self.stg[s]
                    src = I[name][tt * 128:(tt + 1) * 128, qd * 256:(qd + 1) * 256]
                    fw.dma("sp", lambda h, stg=stg, src=src: h.dma_start(out=stg.ap, in_=src), self.stg_sem[s], writes=[stg])
                    bk = self.bank()
                    fw.mm([lambda h, bk=bk, stg=stg, j=j: h.transpose(out=bk.ap[:, j * 128:(j + 1) * 128], in_=stg.ap[:, j * 128:(j + 1) * 128],
                                                                      identity=idf.ap) for j in range(2)], reads=[stg, idf], writes=[bk])
                    dstv = self.x.ap[:, g, qd * 2:qd * 2 + 2, tt * 128:(tt + 1) * 128]
                    srcv = bk.ap[:, 0:256].rearrange("p (j t) -> p j t", j=2)
                    if qd % 2:
                        fw.op("act", lambda h, dstv=dstv, srcv=srcv: h.activation(out=dstv, in_=srcv, func=AF.Copy), reads=[bk], writes=[self.x])
                    else:
                        fw.op("dve", lambda h, dstv=dstv, srcv=srcv: h.tensor_copy(out=dstv, in_=srcv), reads=[bk], writes=[self.x])

    def small_load(self, dst_ap, src_ap, dst_reg):
        return self.fw.dma("sp", lambda h: h.dma_start(out=dst_ap, in_=src_ap, allow_slow_non_contiguous=True),
                           self.small_sem, writes=[dst_reg])

    def wslice(self, parts):
        s = self.slot_i % 6
        self.slot_i += 1
        slot = self.slots[s]
        for (off, kc, ncols, src) in parts:
            dst = slot.ap[:, off:off + kc * ncols].rearrange("p (k n) -> p k n", k=kc)
            sv = src.rearrange("(k p) n -> p k n", p=128)
            self.fw.dma("pool", lambda h, dst=dst, sv=sv: h.dma_start(out=dst, in_=sv), self.slot_sem[s], writes=[slot])
        return slot

    def mod_init(self):
        fw, ar, I = self.fw, self.ar, self.I
        m = ar.mark()
        cT = ar.alloc([128, 8, 2], F32)
        for cc in range(2):
            self.small_load(cT.ap[:, :, cc], I["cvec"][cc].rearrange("(k p) -> p k", p=128), cT)
            self.small_load(self.badaT.ap[:, cc, :], I["b_ada"][cc].rearrange("(t p) -> p t", p=128), self.badaT)
            self.small_load(self.gcols.ap[:, cc, :], I["g_norm1"][cc].rearrange("(k p) -> p k", p=128), self.gcols)
            self.small_load(self.gcols.ap[:, 2 + cc, :], I["g_norm2"][cc].rearrange("(k p) -> p k", p=128), self.gcols)
        self.small_load(self.gcols.ap[:, 4, :], I["g_final"].rearrange("(k p) -> p k", p=128), self.gcols)
        fw.op("act", lambda h: h.activation(out=self.cTb.ap, in_=cT.ap, func=AF.Silu), reads=[cT], writes=[self.cTb])
        ar.release(m)

    def mod_gen(self, l, look=2, s0=0, s1=24, doA=(True, True)):
        fw, I = self.fw, self.I
        cTb, badaT = self.cTb, self.badaT
        pend = []
        nxt = s0
        for s in range(s0, s1):
            while nxt < s1 and nxt <= s + look:
                pend.append(self.wslice([(0, 8, 256, I["w_ada"][l][:, nxt * 256:(nxt + 1) * 256])]))
                nxt += 1
            slot = pend.pop(0)
            sv = slot.ap.rearrange("p (k n) -> p k n", k=8)
            bk = self.bank()
            for t in range(2):
                fw.mm([lambda h, bk=bk, sv=sv, t=t, kc=kc: h.matmul(bk.ap[:, t * 2:t * 2 + 2], lhsT=sv[:, kc, t * 128:(t + 1) * 128],
                                                                    rhs=cTb.ap[:, kc, :], start=(kc == 0), stop=(kc == 7)) for kc in range(8)],
                      reads=[slot, cTb], writes=[bk])
            fw.op("dve", lambda h, bk=bk, l=l, s=s: h.tensor_tensor(out=self.mod.ap[:, l, 2 * s:2 * s + 2, :],
                                                                   in0=bk.ap[:, 0:4].rearrange("p (t c) -> p t c", t=2),
                                                                   in1=badaT.ap[:, l, 2 * s:2 * s + 2].unsqueeze(2).to_broadcast([128, 2, 2]),
                                                                   op=ALU.add), reads=[bk, badaT], writes=[self.mod])
            yield
        for ai, (A, comp, gi) in enumerate(((self.A1, 1, 0), (self.A2, 4, 2))):
            if not doA[ai]:
                continue
            fw.op("dve", lambda h, A=A, comp=comp, gi=gi, l=l: h.scalar_tensor_tensor(
                out=A.ap[:, l], in0=self.mod.ap[:, l, comp * 8:comp * 8 + 8, :], scalar=1.0,
                in1=self.gcols.ap[:, gi + l, :].unsqueeze(2).to_broadcast([128, 8, 2]), op0=ALU.add, op1=ALU.mult),
                reads=[self.mod, self.gcols], writes=[A])

    def tick(self):
        if self.bg is not None:
            try:
                next(self.bg)
            except StopIteration:
                self.bg = None

    def drain(self):
        while self.bg is not None:
            self.tick()

    def rms_rstd(self, g, rstd):
        fw, ar = self.fw, self.ar
        m = ar.mark()
        sq = [ar.alloc([128, 512], BF16) for _ in range(2)]
        tmp = ar.alloc([128, 512], F32)
        for tb in range(2):
            bk = self.bank()
            for kc in range(8):
                q = sq[kc % 2]
                fw.op("act", lambda h, q=q, kc=kc, tb=tb: h.activation(out=q.ap, in_=self.x.ap[:, g, kc, tb * 512:(tb + 1) * 512], func=AF.Square),
                      reads=[self.x], writes=[q])
                fw.mm([lambda h, bk=bk, q=q, kc=kc: h.matmul(bk.ap, lhsT=self.ones_bf.ap, rhs=q.ap, start=(kc == 0), stop=(kc == 7))],
                      reads=[q, self.ones_bf], writes=[bk])
            fw.op("act", lambda h, bk=bk: h.activation(out=tmp.ap, in_=bk.ap, func=AF.Sqrt, scale=1.0 / D, bias=self.epsc.ap[:, 0:1]),
                  reads=[bk, self.epsc], writes=[tmp])
            fw.op("dve", lambda h, tb=tb: h.reciprocal(out=rstd.ap[:, tb * 512:(tb + 1) * 512], in_=tmp.ap), reads=[tmp], writes=[rstd])
        ar.release(m)

    def norm_mod(self, l, g, A, shcomp):
        fw, ar = self.fw, self.ar
        m = ar.mark()
        rstd = ar.alloc([128, TG], F32)
        self.rms_rstd(g, rstd)
        tmps = [ar.alloc([128, TG], F32) for _ in range(2)]
        for kc in range(8):
            t = tmps[kc % 2]
            fw.op("dve", lambda h, t=t, kc=kc: h.scalar_tensor_tensor(out=t.ap, in0=self.x.ap[:, g, kc, :], scalar=A.ap[:, l, kc, g:g + 1],
                                                                      in1=rstd.ap, op0=ALU.mult, op1=ALU.mult),
                  reads=[self.x, A, rstd], writes=[t])
            fw.op("act", lambda h, t=t, kc=kc: h.activation(out=self.h.ap[:, kc, :], in_=t.ap, func=AF.Identity,
                                                            bias=self.mod.ap[:, l, shcomp * 8 + kc, g:g + 1], scale=1.0),
                  reads=[t, self.mod], writes=[self.h])
        ar.release(m)

    def proj_fm(self, src_cols_fn, ntiles, rhs, rhs_regs, kcs, consume):
        fw = self.fw
        for sp in range(0, ntiles, 2):
            nt = min(2, ntiles - sp)
            slot = self.wslice(src_cols_fn(sp * 128, nt * 128))
            sv = slot.ap[:, 0:kcs * nt * 128].rearrange("p (k n) -> p k n", k=kcs)
            for t in range(nt):
                for tb in range(2):
                    bk = self.bank()
                    fw.mm([lambda h, bk=bk, sv=sv, t=t, tb=tb, kc=kc: h.matmul(bk.ap, lhsT=sv[:, kc, t * 128:(t + 1) * 128], rhs=rhs(kc, tb),
                                                                                start=(kc == 0), stop=(kc == kcs - 1)) for kc in range(kcs)],
                          reads=[slot] + rhs_regs, writes=[bk])
                    consume(sp + t, tb, bk)

    def win_cols(self, l, base):
        return lambda c0, n: [(0, 8, n, self.I["w_in"][l][:, base + c0:base + c0 + n])]

    def h_rhs(self, kc, tb):
        return self.h.ap[:, kc, tb * 512:(tb + 1) * 512]

    def tt(self, en, out, in0, in1, op, R, W):
        self.fw.op(en, lambda h: h.tensor_tensor(out=out, in0=in0, in1=in1, op=op), reads=R, writes=W)

    def ts(self, en, out, in0, s1, s2, op0, op1, R, W):
        if s2 is None:
            self.fw.op(en, lambda h: h.tensor_scalar(out=out, in0=in0, scalar1=s1, scalar2=None, op0=op0), reads=R, writes=W)
        else:
            self.fw.op(en, lambda h: h.tensor_scalar(out=out, in0=in0, scalar1=s1, scalar2=s2, op0=op0, op1=op1), reads=R, writes=W)

    def stt(self, out, in0, sc, in1, op0, op1, R, W, en="dve"):
        self.fw.op(en, lambda h: h.scalar_tensor_tensor(out=out, in0=in0, scalar=sc, in1=in1, op0=op0, op1=op1), reads=R, writes=W)

    def act(self, out, in_, func, R, W, bias=None, scale=None):
        kw = {}
        if bias is not None:
            kw["bias"] = bias
        if scale is not None:
            kw["scale"] = scale
        self.fw.op("act", lambda h: h.activation(out=out, in_=in_, func=func, **kw), reads=R, writes=W)

    def cp(self, en, out, in_, R, W):
        if en == "act":
            self.fw.op("act", lambda h: h.activation(out=out, in_=in_, func=AF.Copy), reads=R, writes=W)
        else:
            self.fw.op(en, lambda h: h.tensor_copy(out=out, in_=in_), reads=R, writes=W)

    def sincos(self, y, n, cos_out, sin_out, Wc, Ws):
        ar = self.ar
        MAGIC = 12582912.0
        m = ar.mark()
        kf = ar.alloc([128, n], F32)
        fc = ar.alloc([128, n], F32)
        self.ts("dve", kf.ap, y.ap, 0.25, MAGIC, ALU.add, ALU.add, [y], [kf])
        self.ts("dve", kf.ap, kf.ap, MAGIC, None, ALU.subtract, None, [kf], [kf])
        self.stt(fc.ap, y.ap, 0.25, kf.ap, ALU.add, ALU.subtract, [y, kf], [fc])
        self.act(cos_out, fc.ap, AF.Sin, [fc], [Wc], scale=2.0 * math.pi)
        self.ts("dve", kf.ap, y.ap, MAGIC, None, ALU.add, None, [y], [kf])
        self.ts("dve", kf.ap, kf.ap, MAGIC, None, ALU.subtract, None, [kf], [kf])
        self.tt("dve", y.ap, y.ap, kf.ap, ALU.subtract, [y, kf], [y])
        self.act(sin_out, y.ap, AF.Sin, [y], [Ws], scale=2.0 * math.pi)
        ar.release(m)

    _ssem_i = 0

    def sload(self, dst_ap, src_ap, dst_reg, q="sp"):
        if not hasattr(self, "ssems"):
            self.ssems = [Buf(f"ss{i}") for i in range(4)]
        s = self.ssems[Builder._ssem_i % 4]
        Builder._ssem_i += 1
        if q == "sp":
            return self.fw.dma("sp", lambda h: h.dma_start(out=dst_ap, in_=src_ap, allow_slow_non_contiguous=True), s, writes=[dst_reg])
        return self.fw.dma("pool", lambda h: h.dma_start(out=dst_ap, in_=src_ap, allow_slow_non_contiguous=True), s, writes=[dst_reg])

    def layer_prep_gen(self, l):
        fw, ar, I = self.fw, self.ar, self.I
        m = ar.mark()
        c = self.s5cols
        lam_r = ar.alloc([128, 16], F32)
        lam_i = ar.alloc([128, 16], F32)
        stp = ar.alloc([128, 16], F32)
        t1 = ar.alloc([128, 16], F32)
        t2 = ar.alloc([128, 16], F32)
        t3 = ar.alloc([128, 16], F32)
        yv = ar.alloc([128, 16], F32)
        cs = ar.alloc([128, 16], F32)
        sn = ar.alloc([128, 16], F32)
        nat = ar.alloc([128, 4, 8, 16], F32)
        Cn = [ar.alloc([128, 4, 2, 64], F32) for _ in range(2)]
        ct2 = ar.alloc([128, 128], F32)
        lam = ar.alloc([128, 2, 2], F32)
        xx = ar.alloc([128, 2, 2], F32)
        pp = ar.alloc([128, 2, 2], F32)
        msk = ar.alloc([128, 8, 8], F32)
        Br = ar.alloc([128, 8, 16], F32)
        Bi = ar.alloc([128, 8, 16], F32)
        bb = [ar.alloc([128, 2, 8, 16], F32) for _ in range(2)]
        tA = ar.alloc([128, 2, 8, 16], F32)
        self.sload(lam_r.ap, I["s5_lam_re"][l].rearrange("d (j gl) n -> (gl n) (d j)", gl=2), lam_r)
        self.sload(lam_i.ap, I["s5_lam_im"][l].rearrange("d (j gl) n -> (gl n) (d j)", gl=2), lam_i)
        for gl in range(2):
            src = bass.AP(I["s5_log_step"].tensor, l * 32 + gl, [[0, 64], [2, 16]])
            self.sload(stp.ap[64 * gl:64 * gl + 64, :], src, stp)
        self.sload(c.ap[:, :, 6], I["s5r"][l].rearrange("d (j q) -> q (d j)", q=128), c)
        self.sload(c.ap[:, :, 7], I["s5i"][l].rearrange("d (j q) -> q (d j)", q=128), c)
        yield
        self.act(stp.ap, stp.ap, AF.Exp, [stp], [stp])
        self.tt("dve", t1.ap, lam_r.ap, stp.ap, ALU.mult, [lam_r, stp], [t1])
        self.tt("dve", t2.ap, lam_i.ap, stp.ap, ALU.mult, [lam_i, stp], [t2])
        self.act(c.ap[:, :, 1], t1.ap, AF.Exp, [t1], [c])
        self.ts("dve", yv.ap, t2.ap, 1.0 / (2.0 * math.pi), None, ALU.mult, None, [t2], [yv])
        self.sincos(yv, 16, cs.ap, sn.ap, cs, sn)
        self.cp("dve", c.ap[:, :, 0], yv.ap, [yv], [c])
        self.tt("dve", c.ap[:, :, 2], c.ap[:, :, 1], cs.ap, ALU.mult, [c, cs], [c])
        self.tt("dve", c.ap[:, :, 3], c.ap[:, :, 1], sn.ap, ALU.mult, [c, sn], [c])
        yield
        self.ts("dve", t1.ap, c.ap[:, :, 2], -1.0, None, ALU.add, None, [c], [t1])
        self.tt("dve", t2.ap, lam_r.ap, lam_r.ap, ALU.mult, [lam_r], [t2])
        self.tt("dve", t3.ap, lam_i.ap, lam_i.ap, ALU.mult, [lam_i], [t3])
        self.tt("dve", t2.ap, t2.ap, t3.ap, ALU.add, [t2, t3], [t2])
        fw.op("dve", lambda h: h.reciprocal(out=t2.ap, in_=t2.ap), reads=[t2], writes=[t2])
        self.tt("dve", t3.ap, t1.ap, lam_r.ap, ALU.mult, [t1, lam_r], [t3])
        self.tt("dve", yv.ap, c.ap[:, :, 3], lam_i.ap, ALU.mult, [c, lam_i], [yv])
        self.tt("dve", t3.ap, t3.ap, yv.ap, ALU.add, [t3, yv], [t3])
        self.tt("dve", c.ap[:, :, 4], t3.ap, t2.ap, ALU.mult, [t3, t2], [c])
        self.tt("dve", t3.ap, c.ap[:, :, 3], lam_r.ap, ALU.mult, [c, lam_r], [t3])
        self.tt("dve", yv.ap, t1.ap, lam_i.ap, ALU.mult, [t1, lam_i], [yv])
        self.tt("dve", t3.ap, t3.ap, yv.ap, ALU.subtract, [t3, yv], [t3])
        self.tt("dve", c.ap[:, :, 5], t3.ap, t2.ap, ALU.mult, [t3, t2], [c])

        yield
        fw.op("pool", lambda h: h.memset(msk.ap, 1.0), writes=[msk])
        for gl in range(2):
            fw.op("pool", lambda h, gl=gl: h.affine_select(out=msk.ap[64 * gl:64 * gl + 64], in_=msk.ap[64 * gl:64 * gl + 64],
                                                           pattern=[[0, 2], [-2, 4], [1, 8]], compare_op=ALU.is_equal, fill=0.0,
                                                           base=-gl, channel_multiplier=0), reads=[msk], writes=[msk])
        self.sload(Br.ap, I["s5_b_re"][l].rearrange("(j gl) n p -> (gl n) j p", gl=2), Br)
        self.sload(Bi.ap, I["s5_b_im"][l].rearrange("(j gl) n p -> (gl n) j p", gl=2), Bi)
        kr = c.ap[:, :, 4].rearrange("p (d j) -> p d j", d=2).unsqueeze(3).to_broadcast([128, 2, 8, 16])
        ki = c.ap[:, :, 5].rearrange("p (d j) -> p d j", d=2).unsqueeze(3).to_broadcast([128, 2, 8, 16])
        Brb = Br.ap.unsqueeze(1).to_broadcast([128, 2, 8, 16])
        Bib = Bi.ap.unsqueeze(1).to_broadcast([128, 2, 8, 16])
        self.tt("dve", bb[0].ap, Brb, kr, ALU.mult, [Br, c], [bb[0]])
        self.tt("dve", tA.ap, Bib, ki, ALU.mult, [Bi, c], [tA])
        self.tt("dve", bb[0].ap, bb[0].ap, tA.ap, ALU.subtract, [bb[0], tA], [bb[0]])
        self.tt("dve", bb[1].ap, Bib, kr, ALU.mult, [Bi, c], [bb[1]])
        self.tt("dve", tA.ap, Brb, ki, ALU.mult, [Br, c], [tA])
        self.tt("dve", bb[1].ap, bb[1].ap, tA.ap, ALU.add, [bb[1], tA], [bb[1]])
        yield
        for ri in range(2):
            for q4 in range(4):
                d, j0 = q4 // 2, (q4 % 2) * 4
                for jj in range(4):
                    j = j0 + jj
                    self.tt("dve", nat.ap[:, jj], bb[ri].ap[:, d, j].unsqueeze(1).to_broadcast([128, 8, 16]),
                            msk.ap[:, j].unsqueeze(2).to_broadcast([128, 8, 16]), ALU.mult, [bb[ri], msk], [nat])
                yield
                bk = self.bank()
                fw.mm([lambda h, bk=bk, jj=jj: h.transpose(out=bk.ap[:, jj * 128:(jj + 1) * 128],
                                                           in_=nat.ap[:, jj].rearrange("p a b -> p (a b)"), identity=self.ident_f.ap)
                       for jj in range(4)], reads=[nat, self.ident_f], writes=[bk])
                self.cp("act", self.s5w.ap[:, d * 8 + j0:d * 8 + j0 + 4, ri, :], bk.ap.rearrange("p (a b) -> p a b", a=4), [bk], [self.s5w])
        for hf in range(2):
            self.sload(Cn[0].ap[:, :, hf, :], I["s5_c_re"][l].rearrange("d g p n -> (d g p) n").rearrange("(q r) n -> r q n", r=128), Cn[0])
            self.sload(Cn[1].ap[:, :, hf, :], I["s5_c_im"][l].rearrange("d g p n -> (d g p) n").rearrange("(q r) n -> r q n", r=128), Cn[1])
        for ri in range(2):
            for q in range(4):
                d, ut = q // 2, q % 2
                bk = self.bank()
                fw.mm([lambda h, bk=bk, ri=ri, q=q: h.matmul(bk.ap[:, 0:128], lhsT=Cn[ri].ap[:, q].rearrange("p a b -> p (a b)"),
                                                             rhs=self.ident_f.ap, start=True, stop=True)],
                      reads=[Cn[ri], self.ident_f], writes=[bk])
                if ri == 0:
                    self.cp("act", ct2.ap, bk.ap[:, 0:128], [bk], [ct2])
                else:
                    self.act(ct2.ap, bk.ap[:, 0:128], AF.Copy, [bk], [ct2], scale=-1.0)
                j0 = ut * 4
                self.tt("dve", self.s5w.ap[:, d * 8 + j0:d * 8 + j0 + 4, 2 + ri, :].rearrange("p j (g q) -> p j g q", g=8),
                        ct2.ap.rearrange("p (g q) -> p g q", g=8).unsqueeze(1).to_broadcast([128, 4, 8, 16]),
                        msk.ap[:, j0:j0 + 4].unsqueeze(3).to_broadcast([128, 4, 8, 16]), ALU.mult, [ct2, msk], [self.s5w])
                yield
        self.sload(self.s5d.ap, I["s5_d"][l].rearrange("(t p) -> p t", p=128), self.s5d)
        yield
        lc = self.lrucols
        for k in range(4):
            self.sload(lc.ap[:, :, k], I["lru_conv_w"][l, k].rearrange("(t p) -> p t", p=128), lc)
        self.sload(lc.ap[:, :, 4], I["lru_conv_b"][l].rearrange("(t p) -> p t", p=128), lc)
        for d in range(2):
            self.sload(lc.ap[:, :, 5 + d], I["lru_b_a"][l, d].rearrange("(t p) -> p t", p=128), lc)
            self.sload(lc.ap[:, :, 7 + d], I["lru_b_x"][l, d].rearrange("(t p) -> p t", p=128), lc)
        for d in range(2):
            self.sload(lam.ap[:, :, d], I["lru_lam"][l, d].rearrange("(t p) -> p t", p=128), lam)
        self.act(xx.ap, lam.ap, AF.Exp, [lam], [xx], scale=-1.0)
        self.ts("dve", pp.ap, xx.ap, -0.25, 1.0 / 3.0, ALU.mult, ALU.add, [xx], [pp])
        self.tt("dve", pp.ap, pp.ap, xx.ap, ALU.mult, [pp, xx], [pp])
        self.ts("dve", pp.ap, pp.ap, -1.0, 0.5, ALU.mult, ALU.add, [pp], [pp])
        self.tt("dve", pp.ap, pp.ap, xx.ap, ALU.mult, [pp, xx], [pp])
        self.ts("dve", pp.ap, pp.ap, -1.0, 1.0, ALU.mult, ALU.add, [pp], [pp])
        self.tt("dve", pp.ap, pp.ap, xx.ap, ALU.mult, [pp, xx], [pp])
        self.ts("dve", lc.ap[:, :, 9:11], pp.ap, -8.0, None, ALU.mult, None, [pp], [lc])
        self.ts("dve", lc.ap[:, :, 11:13], pp.ap, 8.0, None, ALU.mult, None, [pp], [lc])
        self.ts("dve", lc.ap[:, :, 13:15], pp.ap, -16.0, None, ALU.mult, None, [pp], [lc])
        yield
        fw.op("pool", lambda h: h.memset(self.lruw.ap, 0.0), writes=[self.lruw])
        for gi, nm in enumerate(("lru_w_a", "lru_w_x")):
            for d in range(2):
                for t in range(2):
                    for b2 in range(2):
                        idx = gi * 4 + d * 2 + t
                        self.sload(self.lruw.ap[64 * b2:64 * b2 + 64, idx, 64 * b2:64 * b2 + 64], I[nm][l, d, 2 * t + b2], self.lruw, q="pool")
        for d in range(2):
            self.sload(self.lrust0.ap[:, d, :], I["slru"][l, d].rearrange("(t p) -> p t", p=128), self.lrust0)
        self.sload(self.wglu.ap, I["s5_w_glu"][l].rearrange("(k p) n -> p k n", p=128), self.wglu, q="pool")
        yield
        if not self.prep_bg:
            ar.release(m)

    def ffn_prep(self, l):
        I = self.I
        for k in range(3):
            self.sload(self.ffcols.ap[:, :, k], I["ffn_conv_w"][l, k].rearrange("(t p) -> p t", p=128), self.ffcols)
        self.sload(self.ffcols.ap[:, :, 3], I["ffn_conv_b"][l].rearrange("(t p) -> p t", p=128), self.ffcols)

    def group_layer(self, l, g):
        fw, ar, I, O = self.fw, self.ar, self.I, self.O
        if g == 0:
            self.ffn_prep(l)
        self.norm_mod(l, g, self.A1, 0)
        self.chk(f"norm{l}{g}")
        m0 = ar.mark()
        attnT = ar.alloc([128, 4, TG], BF16)
        m1 = ar.mark()
        self.attention(l, g, attnT)
        self.chk(f"attn{l}{g}")
        ar.release(m1)
        s5y = ar.alloc([128, 2, TG], BF16)
        m1 = ar.mark()
        self.s5(l, g, s5y)
        self.chk(f"s5{l}{g}")
        ar.release(m1)
        if g == 0:
            self.store_s5_states(l)
        lruy = ar.alloc([128, 2, TG], BF16)
        m1 = ar.mark()
        self.lru(l, g, lruy)
        self.chk(f"lru{l}{g}")
        ar.release(m1)
        if g == 1 and l + 1 < DEPTH:
            self.prep_bg = True
            self.bg = self.layer_prep_gen(l + 1)
            self.tick()
        self.merge(l, g, attnT, s5y, lruy)
        self.drain()
        self.chk(f"merge{l}{g}")
        ar.release(m0)
        self.norm_mod(l, g, self.A2, 3)
        self.ffn(l, g)
        ar.release(m0)

    def attention(self, l, g, attnT):
        fw, ar, I, O = self.fw, self.ar, self.I, self.O
        q_sb = ar.alloc([128, 4, TG], BF16)
        k_sb = ar.alloc([128, 4, TG], BF16)
        v_aug = ar.alloc([128, 8, 8, 66], BF16)
        fw.op("pool", lambda h: h.memset(v_aug.ap[:, :, :, 64:65], 1.0), writes=[v_aug])

        def q_cons(ft, tb, bk):
            self.act(q_sb.ap[:, ft, tb * 512:(tb + 1) * 512], bk.ap, AF.Copy, [bk], [q_sb], scale=0.125)

        def k_cons(ft, tb, bk):
            self.cp("dve", k_sb.ap[:, ft, tb * 512:(tb + 1) * 512], bk.ap, [bk], [k_sb])

        self.proj_fm(self.win_cols(l, 0), 4, self.h_rhs, [self.h], 8, q_cons)
        self.proj_fm(self.win_cols(l, 512), 4, self.h_rhs, [self.h], 8, k_cons)
        self.chk(f"attq{l}{g}")
        for which in ((1, 2) if g == 0 else (2,)):
            for half in range(2):
                slot = self.wslice([(0, 8, 256, I["w_in"][l][:, which * 512 + half * 256: which * 512 + half * 256 + 256])])
                sv = slot.ap.rearrange("p (k n) -> p k n", k=8)
                for tt in range(8):
                    bk = self.bank()
                    fw.mm([lambda h, bk=bk, sv=sv, tt=tt, kc=kc: h.matmul(bk.ap[:, 0:256], lhsT=self.h.ap[:, kc, tt * 128:(tt + 1) * 128], rhs=sv[:, kc, :],
                                                                          start=(kc == 0), stop=(kc == 7)) for kc in range(8)],
                          reads=[slot, self.h], writes=[bk])
                    if which == 2:
                        self.cp("act", v_aug.ap[:, tt, half * 4:half * 4 + 4, 0:64], bk.ap[:, 0:256].rearrange("p (a b) -> p a b", a=4), [bk], [v_aug])
                    if g == 0:
                        s = self.stg_i % 2
                        self.stg_i += 1
                        stg = self.stg[s]
                        self.cp("dve", stg.ap[:, 0:256], bk.ap[:, 0:256], [bk], [stg])
                        dst = O["nk" if which == 1 else "nv"][tt // 2, l, (tt % 2) * 128:(tt % 2) * 128 + 128, half * 256:half * 256 + 256]
                        d = fw.dma("sp", lambda h, stg=stg, dst=dst: h.dma_start(out=dst, in_=stg.ap[:, 0:256]), self.stg_sem[s], reads=[stg])
                        fw.final.append(d)
                    self.chk(f"kv1{l}{g}")
            if which == 1:
                self.chk(f"kvK{l}{g}")

        self.chk(f"attkv{l}{g}")
        pT = [ar.alloc([128, 512], BF16) for _ in range(2)]
        pi = [0]
        atok = ar.alloc([128, 8, 128], BF16)
        rec = ar.alloc([128, 8], F32)

        def transposes(hp, tts):
            bk = self.bank()
            bkb = bk.ap.bitcast(BF16)
            fw.mm([lambda h, bkb=bkb, i=i, tt=tt: h.transpose(out=bkb[:, i * 128:(i + 1) * 128], in_=atok.ap[:, tt, :], identity=self.ident_bf.ap)
                   for i, tt in enumerate(tts)], reads=[atok, self.ident_bf], writes=[bk])
            n = len(tts)
            self.cp("dve", attnT.ap[:, hp, tts[0] * 128:(tts[0] + n) * 128], bkb[:, 0:n * 128], [bk], [attnT])

        if g == 0:
            UP = [(sq, hp, e) for sq in range(4) for hp in range(4) for e in range(2)]

            def issue_score_p(k):
                sq, hp, e = UP[k]
                pb = 64 * e
                sb = self.bank()
                fw.mm([lambda h, sb=sb, kt=kt, pb=pb, sq=sq, hp=hp: h.matmul(sb.ap[:, kt * 256:(kt + 1) * 256],
                                                                            lhsT=k_sb.ap[pb:pb + 64, hp, sq * 256 + kt * 128:sq * 256 + kt * 128 + 128],
                                                                            rhs=q_sb.ap[pb:pb + 64, hp, sq * 256:sq * 256 + 256], start=True, stop=True) for kt in range(2)],
                      reads=[k_sb, q_sb], writes=[sb])
                return sb

            nxt = issue_score_p(0)
            ob = ov = None
            for k, (sq, hp, e) in enumerate(UP):
                hh = 2 * hp + e
                if e == 0:
                    ob = self.bank(hold=True)
                    ov = ob.ap[:, 0:260].rearrange("p (q e c) -> p q e c", q=2, e=2)
                sb = nxt
                p = pT[k % 2]
                self.act(p.ap, sb.ap, AF.Exp, [sb], [p])
                if k + 1 < len(UP):
                    nxt = issue_score_p(k + 1)
                fns = []
                for qt in range(2):
                    for kt in range(2):
                        fns.append(lambda h, qt=qt, kt=kt, e=e, hh=hh, p=p, ov=ov, sq=sq, st_=(e == 0 and qt == 0 and kt == 0): h.matmul(
                            ov[:, qt, e, :], lhsT=p.ap[:, kt * 256 + qt * 128:kt * 256 + qt * 128 + 128], rhs=v_aug.ap[:, 2 * sq + kt, hh, 0:65],
                            start=st_, stop=(e == 1 and qt == 1 and kt == 1)))
                fw.mm(fns, reads=[p, v_aug], writes=[ob])
                if e == 1:
                    fw.op("dve", lambda h, ov=ov: h.reciprocal(out=rec.ap[:, 0:4].rearrange("p (q e) -> p q e", q=2), in_=ov[:, :, :, 64]), reads=[ob], writes=[rec])
                    self.tt("dve", atok.ap[:, 2 * sq:2 * sq + 2, :].rearrange("p q (e c) -> p q e c", e=2), ov[:, :, :, 0:64],
                            rec.ap[:, 0:4].rearrange("p (q e) -> p q e", q=2).unsqueeze(3).to_broadcast([128, 2, 2, 64]), ALU.mult, [ob, rec], [atok])
                    self.unhold(ob)
                    transposes(hp, [2 * sq, 2 * sq + 1])
        else:
            self.na_attention(l, attnT, q_sb, k_sb, v_aug, pT, atok, rec, transposes)

    def na_attention(self, l, attnT, q_sb, k_sb, v_aug, pT, atok, rec, transposes):
        fw, ar, I = self.fw, self.ar, self.I
        kctxT = ar.alloc([128, 4, 512], BF16)
        vctx = ar.alloc([128, 4, 8, 66], BF16)
        fw.op("pool", lambda h: h.memset(vctx.ap[:, :, :, 64:65], 1.0), writes=[vctx])
        cvsem = Buf("cvsem")
        for tt in range(4):
            fw.dma("pool", lambda h, tt=tt: h.dma_start(out=vctx.ap[:, tt, :, 0:64], in_=I["cv"][l][tt * 128:(tt + 1) * 128, :].rearrange("p (a b) -> p a b", a=8)),
                   cvsem, writes=[vctx])
        mk = ar.mark()
        cktok = ar.alloc([128, 4, 512], BF16)
        cksem = Buf("cksem")
        fw.dma("pool", lambda h: h.dma_start(out=cktok.ap, in_=I["ck"][l].rearrange("(t p) f -> p t f", p=128)), cksem, writes=[cktok])
        for hp in range(4):
            bk = self.bank()
            bkb = bk.ap.bitcast(BF16)
            fw.mm([lambda h, bkb=bkb, tt=tt, hp=hp: h.transpose(out=bkb[:, tt * 128:(tt + 1) * 128], in_=cktok.ap[:, tt, hp * 128:(hp + 1) * 128],
                                                               identity=self.ident_bf.ap) for tt in range(4)], reads=[cktok, self.ident_bf], writes=[bk])
            self.cp("dve", kctxT.ap[:, hp, :], bkb[:, 0:512], [bk], [kctxT])
        ar.release(mk)
        LT = ar.alloc([128, 8, 18, 64], BF16)
        mk = ar.mark()
        rp = ar.alloc([128, 2, 32], F32)
        fw.op("pool", lambda h: h.memset(rp.ap, 0.0), writes=[rp])
        for e in range(2):
            self.sload(rp.ap[0:120, e, 0:31], I["rpb"][l].rearrange("h a x -> (h a) x"), rp)
        Rs = ar.alloc([64, 8, 15], F32)
        RE = ar.alloc([64, 8, 18], F32)
        BB = ar.alloc([64, 2, 127], F32)
        msk = ar.alloc([128, 64], F32)
        m2 = ar.alloc([128, 64], F32)
        bk = self.bank()
        fw.mm([lambda h, bk=bk: h.matmul(bk.ap[0:64, 0:120], lhsT=rp.ap[0:120].rearrange("p a b -> p (a b)"),
                                         rhs=self.ident_f.ap[0:120, 0:120], start=True, stop=True)], reads=[rp, self.ident_f], writes=[bk])
        fw.op("pool", lambda h: h.memset(Rs.ap, 0.0), writes=[Rs])
        for e in range(2):
            self.cp("dve", Rs.ap[32 * e:32 * e + 31].rearrange("p a b -> p (a b)"), bk.ap[32 * e:32 * e + 31, 0:120], [bk], [Rs])
        fw.op("pool", lambda h: h.memset(RE.ap, 0.0), writes=[RE])
        for e in range(2):
            self.cp("dve", RE.ap[32 * e:32 * e + 31, :, e + 1:e + 16], Rs.ap[32 * e:32 * e + 31, :, ::-1], [Rs], [RE])
        fw.op("pool", lambda h: h.memset(BB.ap, 0.0), writes=[BB])
        for e in range(2):
            fw.op("pool", lambda h, e=e: h.memset(BB.ap[32 * e:32 * e + 32, e, :], 1.0), writes=[BB])
            fw.op("pool", lambda h, e=e: h.affine_select(out=BB.ap[32 * e:32 * e + 32, e, :], in_=BB.ap[32 * e:32 * e + 32, e, :], pattern=[[1, 127]],
                                                          compare_op=ALU.is_equal, fill=0.0, base=-48, channel_multiplier=-1), reads=[BB], writes=[BB])
        fw.op("pool", lambda h: h.memset(msk.ap, 0.0), writes=[msk])
        fw.op("pool", lambda h: h.memset(m2.ap, 0.0), writes=[m2])
        for hf in range(2):
            sl = slice(64 * hf, 64 * hf + 64)
            fw.op("pool", lambda h, sl=sl: h.affine_select(out=msk.ap[sl], in_=msk.ap[sl], pattern=[[-1, 64]], compare_op=ALU.is_ge, fill=NEG,
                                                            base=8, channel_multiplier=1), reads=[msk], writes=[msk])
            fw.op("pool", lambda h, sl=sl: h.affine_select(out=msk.ap[sl], in_=msk.ap[sl], pattern=[[0, 64]], compare_op=ALU.is_ge, fill=0.0,
                                                            base=47, channel_multiplier=-1), reads=[msk], writes=[msk])
            fw.op("pool", lambda h, sl=sl: h.affine_select(out=m2.ap[sl], in_=m2.ap[sl], pattern=[[1, 64]], compare_op=ALU.is_ge, fill=NEG,
                                                            base=7, channel_multiplier=-1), reads=[m2], writes=[m2])
            fw.op("pool", lambda h, sl=sl: h.affine_select(out=m2.ap[sl], in_=m2.ap[sl], pattern=[[0, 64]], compare_op=ALU.is_ge, fill=0.0,
                                                            base=-16, channel_multiplier=1), reads=[m2], writes=[m2])
        self.tt("pool", msk.ap, msk.ap, m2.ap, ALU.add, [msk, m2], [msk])
        REf = RE.ap.rearrange("p a b -> p (a b)")
        for q0 in range(0, 64, 3):
            nq = min(3, 64 - q0)
            bk = self.bank()
            for i in range(nq):
                qc = q0 + i
                fw.mm([lambda h, bk=bk, i=i, qc=qc, e=e: h.matmul(bk.ap[64 * e:64 * e + 64, i * 144:(i + 1) * 144], lhsT=BB.ap[:, e, 63 - qc:63 - qc + 64], rhs=REf,
                                                                  start=True, stop=True) for e in range(2)], reads=[BB, RE], writes=[bk])
            outv = LT.ap[:, :, :, q0:q0 + nq].rearrange("p h d q -> p q (h d)")
            self.tt("dve", outv, bk.ap[:, 0:nq * 144].rearrange("p (q n) -> p q n", q=nq),
                    msk.ap[:, q0:q0 + nq].unsqueeze(2).to_broadcast([128, nq, 144]), ALU.add, [bk, msk], [LT])
        ar.release(mk)

        U = []
        for hp in range(4):
            for e in range(2):
                for c in range(2):
                    grp = []
                    for mt in range(8):
                        js = [j for j in range(4 * c, 4 * c + 4) if any(na_valid(2 * mt + ee, 2 * j + r) for ee in range(2) for r in range(2))]
                        if js:
                            grp.append(("loc", mt, js[0], js[-1]))
                    for kt in range(4):
                        grp.append(("ctx", kt, 4 * c, 4 * c + 3))
                    for ui, (kind, mt, ja, jb) in enumerate(grp):
                        U.append((hp, e, c, kind, mt, ja, jb, ui == 0, ui == len(grp) - 1))

        def issue_score(k):
            hp, e, c, kind, mt, ja, jb, gfirst, glast = U[k]
            pb = 64 * e
            hh = 2 * hp + e
            nq = 128 * (jb - ja + 1)
            sb = self.bank()
            if kind == "loc":
                d0 = 2 * ja - 2 * mt + 8
                d1 = 2 * jb + 1 - 2 * mt + 8
                assert 0 <= d0 and d1 < 18, (mt, ja, jb)
                ltv = LT.ap[:, hh, d0:d1 + 1, :].rearrange("p a b -> p (a b)")
                fw.mm([lambda h, sb=sb, mt=mt, ja=ja, nq=nq, pb=pb, hp=hp: h.matmul(sb.ap[:, 0:nq], lhsT=k_sb.ap[pb:pb + 64, hp, mt * 128:(mt + 1) * 128],
                                                                                   rhs=q_sb.ap[pb:pb + 64, hp, ja * 128:ja * 128 + nq], start=True, stop=False),
                       lambda h, sb=sb, ltv=ltv, nq=nq: h.matmul(sb.ap[:, 0:nq], lhsT=self.ident_bf.ap, rhs=ltv, start=False, stop=True)],
                      reads=[k_sb, q_sb, LT, self.ident_bf], writes=[sb])
            else:
                fw.mm([lambda h, sb=sb, mt=mt, ja=ja, nq=nq, pb=pb, hp=hp: h.matmul(sb.ap[:, 0:nq], lhsT=kctxT.ap[pb:pb + 64, hp, mt * 128:(mt + 1) * 128],
                                                                                   rhs=q_sb.ap[pb:pb + 64, hp, ja * 128:ja * 128 + nq], start=True, stop=True)],
                      reads=[kctxT, q_sb], writes=[sb])
            return sb

        nxt = issue_score(0)
        ob = ov = None
        first = True
        for k, (hp, e, c, kind, mt, ja, jb, gfirst, glast) in enumerate(U):
            hh = 2 * hp + e
            nq = 128 * (jb - ja + 1)
            if gfirst:
                ob = self.bank(hold=True)
                ov = ob.ap[:, 0:260].rearrange("p (q c) -> p q c", q=4)
                first = True
            sb = nxt
            p = pT[k % 2]
            self.act(p.ap[:, 0:nq], sb.ap[:, 0:nq], AF.Exp, [sb], [p])
            if k + 1 < len(U):
                nxt = issue_score(k + 1)
            fns = []
            for j in range(ja, jb + 1):
                if kind == "loc":
                    val = [[na_valid(2 * mt + ee, 2 * j + r) for r in range(2)] for ee in range(2)]
                    if not any(val[0]) and not any(val[1]):
                        continue
                    for ee in range(2):
                        for r in range(2):
                            if not val[ee][r]:
                                c0 = (j - ja) * 128 + r * 64
                                fw.op("dve", lambda h, p=p, ee=ee, c0=c0: h.memset(p.ap[64 * ee:64 * ee + 64, c0:c0 + 64], 0.0), reads=[p], writes=[p])
                    rhs = v_aug.ap[:, mt, hh, 0:65]
                else:
                    rhs = vctx.ap[:, mt, hh, 0:65]
                fns.append(lambda h, j=j, ja=ja, p=p, rhs=rhs, st_=first, ov=ov, c=c: h.matmul(ov[:, j - 4 * c, :], lhsT=p.ap[:, (j - ja) * 128:(j - ja + 1) * 128],
                                                                                            rhs=rhs, start=st_, stop=False))
                first = False
            fw.mm(fns, reads=[p, v_aug, vctx], writes=[ob])
            if glast:
                fw.op("dve", lambda h, ov=ov: h.reciprocal(out=rec.ap[:, 0:4], in_=ov[:, :, 64]), reads=[ob], writes=[rec])
                self.tt("dve", atok.ap[:, 4 * c:4 * c + 4, 64 * e:64 * e + 64], ov[:, :, 0:64],
                        rec.ap[:, 0:4].unsqueeze(2).to_broadcast([128, 4, 64]), ALU.mult, [ob, rec], [atok])
                self.unhold(ob)
                if e == 1 and c == 1:
                    transposes(hp, [0, 1, 2, 3])
                    transposes(hp, [4, 5, 6, 7])

    def s5(self, l, g, s5y):
        fw, ar, I, O = self.fw, self.ar, self.I, self.O
        c = self.s5cols
        u_sb = ar.alloc([128, 2, TG], BF16)

        def u_cons(ft, tb, bk):
            self.cp("act", u_sb.ap[:, ft, tb * 512:(tb + 1) * 512], bk.ap, [bk], [u_sb])

        self.proj_fm(self.win_cols(l, 1536), 2, self.h_rhs, [self.h], 8, u_cons)
        Ec = ar.alloc([128, 16, 256], F32)
        Es = ar.alloc([128, 16, 256], F32)
        mk = ar.mark()
        io_i = ar.alloc([128, 256], I32)
        io_f = ar.alloc([128, 256], F32)
        fw.op("pool", lambda h: h.iota(io_i.ap, pattern=[[1, 256]], base=0, channel_multiplier=0), writes=[io_i])
        self.cp("dve", io_f.ap, io_i.ap, [io_i], [io_f])
        yv = ar.alloc([128, 2, 256], F32)
        for q in range(8):
            self.tt("dve", yv.ap, c.ap[:, 2 * q:2 * q + 2, 0].unsqueeze(2).to_broadcast([128, 2, 256]),
                    io_f.ap.unsqueeze(1).to_broadcast([128, 2, 256]), ALU.mult, [c, io_f], [yv])
            yflat = Reg(yv.ap.rearrange("p a b -> p (a b)"), yv.bufs)
            self.sincos(yflat, 512, Ec.ap[:, 2 * q:2 * q + 2, :].rearrange("p a b -> p (a b)"),
                        Es.ap[:, 2 * q:2 * q + 2, :].rearrange("p a b -> p (a b)"), Ec, Es)
        ar.release(mk)
        Kc = ar.alloc([128, 16, 4], F32)
        if g == 1:
            e255c = Ec.ap[:, :, 255]
            e255s = Es.ap[:, :, 255]
            self.tt("dve", Kc.ap[:, :, 0], c.ap[:, :, 2], e255c, ALU.mult, [c, Ec], [Kc])
            self.tt("dve", Kc.ap[:, :, 3], c.ap[:, :, 3], e255s, ALU.mult, [c, Es], [Kc])
            self.tt("dve", Kc.ap[:, :, 0], Kc.ap[:, :, 0], Kc.ap[:, :, 3], ALU.subtract, [Kc], [Kc])
            self.tt("dve", Kc.ap[:, :, 1], c.ap[:, :, 2], e255s, ALU.mult, [c, Es], [Kc])
            self.tt("dve", Kc.ap[:, :, 3], c.ap[:, :, 3], e255c, ALU.mult, [c, Ec], [Kc])
            self.tt("dve", Kc.ap[:, :, 1], Kc.ap[:, :, 1], Kc.ap[:, :, 3], ALU.add, [Kc], [Kc])
            self.ts("dve", Kc.ap[:, :, 2], Kc.ap[:, :, 1], -1.0, None, ALU.mult, None, [Kc], [Kc])
        ygb = ar.alloc([128, 2, TG], BF16)
        bpr = ar.alloc([128, 512], F32)
        bpi = ar.alloc([128, 512], F32)
        grs = [ar.alloc([128, 512], F32) for _ in range(2)]
        gis = [ar.alloc([128, 512], F32) for _ in range(2)]
        t1 = ar.alloc([128, 512], F32)
        t2 = ar.alloc([128, 512], F32)
        p1 = ar.alloc([128, 512], F32)
        p2 = ar.alloc([128, 512], F32)
        unit = [0]
        wr = ar.alloc([128, 512], BF16)
        wi = ar.alloc([128, 512], BF16)
        sm = ar.alloc([128, 16], F32)

        def v2(ap):
            return ap.rearrange("p (s t) -> p s t", s=2)

        def seg(ap, s2, rev):
            v = ap[:, s2 * 256:(s2 + 1) * 256]
            return v[:, ::-1] if rev else v

        units = []
        for ut in range(2):
            for d in range(2):
                for jj in range(4):
                    for ti, tb in enumerate([1, 0] if d == 1 else [0, 1]):
                        units.append((ut, d, jj, ti, tb))

        def issue_bu(k):
            ut, d, jj, ti, tb = units[k]
            dj = d * 8 + ut * 4 + jj
            tsl = slice(tb * 512, (tb + 1) * 512)
            br = self.bank()
            bi = self.bank()
            fw.mm([lambda h, br=br, dj=dj, tsl=tsl, ut=ut: h.matmul(br.ap, lhsT=self.s5w.ap[:, dj, 0, :], rhs=u_sb.ap[:, ut, tsl], start=True, stop=True)],
                  reads=[self.s5w, u_sb], writes=[br])
            fw.mm([lambda h, bi=bi, dj=dj, tsl=tsl, ut=ut: h.matmul(bi.ap, lhsT=self.s5w.ap[:, dj, 1, :], rhs=u_sb.ap[:, ut, tsl], start=True, stop=True)],
                  reads=[self.s5w, u_sb], writes=[bi])
            return br, bi

        ybanks = None
        yfirst = None
        if l == 0:
            self.bg = self.mod_gen(0, look=2, s0=8, s1=24, doA=(False, True)) if g == 0 else self.mod_gen(1, look=2)
        tbk = self.bank(hold=True)
        tbk2 = self.bank(hold=True)
        deferred = []
        prev_last = None
        nxt = issue_bu(0)
        for k, (ut, d, jj, ti, tb) in enumerate(units):
            rev = (d == 1)
            j = ut * 4 + jj
            dj = d * 8 + j
            if d == 0 and jj == 0 and ti == 0:
                ybanks = [self.bank(hold=True), self.bank(hold=True)]
                yfirst = [True, True]
            Ecv = Ec.ap[:, dj, :]
            Esv = Es.ap[:, dj, :]
            if rev:
                Ecv = Ecv[:, ::-1]
                Esv = Esv[:, ::-1]
            Ec2 = Ecv.unsqueeze(1).to_broadcast([128, 2, 256])
            Es2 = Esv.unsqueeze(1).to_broadcast([128, 2, 256])
            rb = c.ap[:, dj, 1:2].to_broadcast([128, 256])
            gr, gi = grs[k % 2], gis[k % 2]
            br, bi = nxt
            self.tt("dve", v2(t2.ap), v2(bi.ap), Es2, ALU.mult, [bi, Es], [t2])
            self.tt("dve", v2(tbk.ap), v2(br.ap), Ec2, ALU.mult, [br, Ec], [tbk])
            self.tt("dve", v2(tbk2.ap), v2(bi.ap), Ec2, ALU.mult, [bi, Ec], [tbk2])
            self.tt("dve", bpr.ap, tbk.ap, t2.ap, ALU.add, [tbk, t2], [bpr])
            self.tt("dve", v2(t2.ap), v2(br.ap), Es2, ALU.mult, [br, Es], [t2])
            self.tt("dve", bpi.ap, tbk2.ap, t2.ap, ALU.subtract, [tbk2, t2], [bpi])
            if k + 1 < len(units):
                nxt = issue_bu(k + 1)
            for fn in deferred:
                fn()
            deferred = []
            segs = [1, 0] if rev else [0, 1]
            if g == 0:
                for (src, dst) in ((bpr, gr), (bpi, gi)):
                    for s2 in segs:
                        fw.op("dve", lambda h, src=src, dst=dst, s2=s2, rev=rev, rb=rb: h.tensor_tensor_scan(
                            out=seg(dst.ap, s2, rev), data0=rb, data1=seg(src.ap, s2, rev), initial=0.0, op0=ALU.mult, op1=ALU.add),
                            reads=[src, c], writes=[dst])
            for sj, s2 in enumerate(segs):
                s = tb * 2 + s2
                si = ti * 2 + sj
                if g == 1:
                    f_r = seg(bpr.ap, s2, rev)[:, 0:1]
                    f_i = seg(bpi.ap, s2, rev)[:, 0:1]
                    if si == 0:
                        hpr, hpi = c.ap[:, dj, 6:7], c.ap[:, dj, 7:8]
                        self.stt(f_r, hpr, c.ap[:, dj, 2:3], f_r, ALU.mult, ALU.add, [c, bpr], [bpr])
                        self.stt(f_i, hpi, c.ap[:, dj, 2:3], f_i, ALU.mult, ALU.add, [c, bpi], [bpi])
                        self.ts("dve", sm.ap[:, 0:1], hpi, c.ap[:, dj, 3:4], None, ALU.mult, None, [c], [sm])
                        self.stt(f_i, hpr, c.ap[:, dj, 3:4], f_i, ALU.mult, ALU.add, [c, bpi], [bpi])
                        self.tt("dve", f_r, f_r, sm.ap[:, 0:1], ALU.subtract, [bpr, sm], [bpr])
                    else:
                        pgr, pgi = prev_last
                        self.stt(f_r, pgr[0], Kc.ap[:, dj, 0:1], f_r, ALU.mult, ALU.add, [pgr[1], Kc, bpr], [bpr])
                        self.stt(f_i, pgi[0], Kc.ap[:, dj, 0:1], f_i, ALU.mult, ALU.add, [pgi[1], Kc, bpi], [bpi])
                        self.stt(f_r, pgi[0], Kc.ap[:, dj, 2:3], f_r, ALU.mult, ALU.add, [pgi[1], Kc, bpr], [bpr])
                        self.stt(f_i, pgr[0], Kc.ap[:, dj, 1:2], f_i, ALU.mult, ALU.add, [pgr[1], Kc, bpi], [bpi])
                if g == 1:
                    for (src, dst) in ((bpr, gr), (bpi, gi)):
                        fw.op("dve", lambda h, src=src, dst=dst, s2=s2, rev=rev, rb=rb: h.tensor_tensor_scan(
                            out=seg(dst.ap, s2, rev), data0=rb, data1=seg(src.ap, s2, rev), initial=0.0, op0=ALU.mult, op1=ALU.add),
                            reads=[src, c], writes=[dst])
                g_r = seg(gr.ap, s2, rev)[:, 255:256]
                g_i = seg(gi.ap, s2, rev)[:, 255:256]
                prev_last = ((g_r, gr), (g_i, gi))
                if g == 0:
                    def state_ops(dj=dj, g_r=g_r, g_i=g_i, gr=gr, gi=gi, s=s, d=d, j=j):
                        e_c = Ec.ap[:, dj, 255:256]
                        e_s = Es.ap[:, dj, 255:256]
                        o_r, o_i = self.outst.ap[:, s, d, j, 0:1], self.outst.ap[:, s, d, j, 1:2]
                        self.tt("dve", sm.ap[:, 1:2], g_i, e_s, ALU.mult, [gi, Es], [sm])
                        self.tt("dve", sm.ap[:, 2:3], g_i, e_c, ALU.mult, [gi, Ec], [sm])
                        self.stt(o_r, g_r, e_c, sm.ap[:, 1:2], ALU.mult, ALU.subtract, [gr, Ec, sm], [self.outst])
                        self.stt(o_i, g_r, e_s, sm.ap[:, 2:3], ALU.mult, ALU.add, [gr, Es, sm], [self.outst])
                    deferred.append(state_ops)
            self.tt("pool", v2(p1.ap), v2(gr.ap), Ec2, ALU.mult, [gr, Ec], [p1])
            self.tt("pool", v2(p2.ap), v2(gi.ap), Es2, ALU.mult, [gi, Es], [p2])
            self.tt("pool", wr.ap, p1.ap, p2.ap, ALU.subtract, [p1, p2], [wr])
            self.tt("pool", v2(p1.ap), v2(gi.ap), Ec2, ALU.mult, [gi, Ec], [p1])
            self.tt("pool", v2(p2.ap), v2(gr.ap), Es2, ALU.mult, [gr, Es], [p2])
            self.tt("pool", wi.ap, p1.ap, p2.ap, ALU.add, [p1, p2], [wi])
            self.tick()
            yb = ybanks[tb]
            last = (d == 1 and jj == 3)
            fw.mm([lambda h, yb=yb, dj=dj, st_=yfirst[tb]: h.matmul(yb.ap, lhsT=self.s5w.ap[:, dj, 2, :], rhs=wr.ap, start=st_, stop=False),
                   lambda h, yb=yb, dj=dj, last=last: h.matmul(yb.ap, lhsT=self.s5w.ap[:, dj, 3, :], rhs=wi.ap, start=False, stop=last)],
                  reads=[self.s5w, wr, wi], writes=[yb])
            yfirst[tb] = False
            if d == 1 and jj == 3 and ti == 1:
                for tb2 in range(2):
                    yb = ybanks[tb2]
                    sl = slice(tb2 * 512, (tb2 + 1) * 512)
                    self.stt(t1.ap, u_sb.ap[:, ut, sl], self.s5d.ap[:, ut:ut + 1], yb.ap, ALU.mult, ALU.add, [u_sb, self.s5d, yb], [t1])
                    self.act(ygb.ap[:, ut, sl], t1.ap, AF.Gelu_apprx_tanh, [t1], [ygb])
                    self.unhold(yb)
        for fn in deferred:
            fn()
        self.drain()
        self.unhold(tbk)
        self.unhold(tbk2)
        for ot in range(2):
            for tb in range(2):
                sl = slice(tb * 512, (tb + 1) * 512)
                bk = self.bank()
                fw.mm([lambda h, bk=bk, kc=kc, ot=ot, sl=sl: h.matmul(bk.ap, lhsT=self.wglu.ap[:, kc, ot * 128:(ot + 1) * 128], rhs=ygb.ap[:, kc, sl],
                                                                      start=(kc == 0), stop=(kc == 1)) for kc in range(2)], reads=[self.wglu, ygb], writes=[bk])
                self.act(t1.ap, bk.ap, AF.Sigmoid, [bk], [t1])
                self.tt("dve", s5y.ap[:, ot, sl], ygb.ap[:, ot, sl], t1.ap, ALU.mult, [ygb, t1], [s5y])

    def store_s5_states(self, l):
        fw, ar, O = self.fw, self.ar, self.O
        mk = ar.mark()
        tr = ar.alloc([128, 128], F32)
        bk = self.bank()
        fw.mm([lambda h: h.transpose(out=bk.ap[:, 0:128], in_=self.outst.ap.rearrange("p s d j r -> p (s d j r)"), identity=self.ident_f.ap)],
              reads=[self.outst, self.ident_f], writes=[bk])
        self.cp("dve", tr.ap, bk.ap[:, 0:128], [bk], [tr])
        sem = Buf("s5out")
        for ri, nm in enumerate(("ns5r", "ns5i")):
            for s in range(4):
                for d in range(2):
                    r0 = ((s * 2 + d) * 8) * 2 + ri
                    src = tr.ap[r0:r0 + 15:2, :]
                    dst = O[nm][s, l, d, :].rearrange("(j q) -> j q", q=128)
                    dd = fw.dma("sp", lambda h, src=src, dst=dst: h.dma_start(out=dst, in_=src), sem, reads=[tr])
                    fw.final.append(dd)
        ar.release(mk)

    def lru(self, l, g, lruy):
        fw, ar, I, O = self.fw, self.ar, self.I, self.O
        lc = self.lrucols
        xr = ar.alloc([128, 2, TG], F32)
        gg = ar.alloc([128, 2, TG], F32)
        xc = ar.alloc([128, 2, TG], F32)
        xcb = ar.alloc([128, 2, TG], BF16)

        def xr_cons(ft, tb, bk):
            self.cp("act", xr.ap[:, ft, tb * 512:(tb + 1) * 512], bk.ap, [bk], [xr])

        def xg_cons(ft, tb, bk):
            self.act(gg.ap[:, ft, tb * 512:(tb + 1) * 512], bk.ap, AF.Gelu_apprx_tanh, [bk], [gg])

        self.proj_fm(self.win_cols(l, 1792), 2, self.h_rhs, [self.h], 8, xr_cons)
        self.proj_fm(self.win_cols(l, 2048), 2, self.h_rhs, [self.h], 8, xg_cons)
        nseq, L = (4, 256) if g == 0 else (1, 1024)
        for t in range(2):
            xv = xr.ap[:, t, :].rearrange("p (s q) -> p s q", s=nseq)
            cv = xc.ap[:, t, :].rearrange("p (s q) -> p s q", s=nseq)
            self.ts("dve", cv, xv, lc.ap[:, t, 2:3], lc.ap[:, t, 4:5], ALU.mult, ALU.add, [xr, lc], [xc])
            for k in (0, 1, 3):
                sh = k - 2
                lo, hi = max(0, -sh), L - max(0, sh)
                self.stt(cv[:, :, lo:hi], xv[:, :, lo + sh:hi + sh], lc.ap[:, t, k:k + 1], cv[:, :, lo:hi], ALU.mult, ALU.add, [xr, lc, xc], [xc])
            self.cp("act", xcb.ap[:, t, :], xc.ap[:, t, :], [xc], [xcb])
        r_ = ar.alloc([128, 512], F32)
        i_ = ar.alloc([128, 512], F32)
        th = ar.alloc([128, 512], F32)
        e2 = ar.alloc([128, 512], F32)
        a_sb = ar.alloc([128, TG], F32)
        b_sb = ar.alloc([128, TG], F32)
        hs = [ar.alloc([128, TG], F32) for _ in range(2)]
        for t in range(2):
            for d in range(2):
                rev = (d == 1)
                for tb in range(2):
                    sl = slice(tb * 512, (tb + 1) * 512)
                    pr = self.bank()
                    pi = self.bank()
                    fw.mm([lambda h, pr=pr, d=d, t=t, sl=sl: h.matmul(pr.ap, lhsT=self.lruw.ap[:, 0 * 4 + d * 2 + t, :], rhs=xcb.ap[:, t, sl], start=True, stop=True)],
                          reads=[self.lruw, xcb], writes=[pr])
                    fw.mm([lambda h, pi=pi, d=d, t=t, sl=sl: h.matmul(pi.ap, lhsT=self.lruw.ap[:, 1 * 4 + d * 2 + t, :], rhs=xcb.ap[:, t, sl], start=True, stop=True)],
                          reads=[self.lruw, xcb], writes=[pi])
                    self.act(r_.ap, pr.ap, AF.Sigmoid, [pr, lc], [r_], bias=lc.ap[:, t, 5 + d:6 + d])
                    self.act(i_.ap, pi.ap, AF.Sigmoid, [pi, lc], [i_], bias=lc.ap[:, t, 7 + d:8 + d])
                    self.act(a_sb.ap[:, sl], r_.ap, AF.Exp, [r_, lc], [a_sb], scale=lc.ap[:, t, 9 + d:10 + d])
                    self.act(th.ap, r_.ap, AF.Tanh, [r_, lc], [th], scale=lc.ap[:, t, 11 + d:12 + d])
                    self.act(e2.ap, r_.ap, AF.Exp, [r_, lc], [e2], scale=lc.ap[:, t, 13 + d:14 + d])
                    self.stt(e2.ap, e2.ap, 1.0, th.ap, ALU.add, ALU.mult, [e2, th], [e2])
                    self.act(e2.ap, e2.ap, AF.Sqrt, [e2], [e2])
                    self.tt("dve", i_.ap, i_.ap, xc.ap[:, t, sl], ALU.mult, [i_, xc], [i_])
                    self.tt("dve", b_sb.ap[:, sl], e2.ap, i_.ap, ALU.mult, [e2, i_], [b_sb])
                hd = hs[d]
                for s in range(nseq):
                    sq = slice(s * L, (s + 1) * L)
                    av, bv, hv = a_sb.ap[:, sq], b_sb.ap[:, sq], hd.ap[:, sq]
                    if rev:
                        av, bv, hv = av[:, ::-1], bv[:, ::-1], hv[:, ::-1]
                    init = 0.0 if g == 0 else self.lrust0.ap[:, d, t:t + 1]
                    rd = [a_sb, b_sb] + ([] if g == 0 else [self.lrust0])
                    fw.op("dve", lambda h, av=av, bv=bv, hv=hv, init=init: h.tensor_tensor_scan(out=hv, data0=av, data1=bv, initial=init, op0=ALU.mult, op1=ALU.add),
                          reads=rd, writes=[hd])
                    if g == 0:
                        self.cp("dve", self.lrust.ap[:, s, d, t:t + 1], hv[:, L - 1:L], [hd], [self.lrust])
            self.tt("dve", hs[0].ap, hs[0].ap, hs[1].ap, ALU.add, [hs[0], hs[1]], [hs[0]])
            self.tt("dve", lruy.ap[:, t, :], hs[0].ap, gg.ap[:, t, :], ALU.mult, [hs[0], gg], [lruy])
        if g == 0:
            sem = Buf("lruout")
            for s in range(4):
                for d in range(2):
                    dst = O["nlru"][s, l, d, :].rearrange("(t p) -> p t", p=128)
                    src = self.lrust.ap[:, s, d, :]
                    dd = fw.dma("sp", lambda h, src=src, dst=dst: h.dma_start(out=dst, in_=src, allow_slow_non_contiguous=True), sem, reads=[self.lrust])
                    fw.final.append(dd)

    def merge(self, l, g, attnT, s5y, lruy):
        fw, ar, I = self.fw, self.ar, self.I
        merged = ar.alloc([128, 8, TG], BF16)
        sig = [ar.alloc([128, 512], F32) for _ in range(2)]
        pr = [ar.alloc([128, 512], F32) for _ in range(3)]
        si = [0]

        def br_rhs(kc, sl):
            if kc < 4:
                return attnT.ap[:, kc, sl]
            if kc < 6:
                return s5y.ap[:, kc - 4, sl]
            return lruy.ap[:, kc - 6, sl]

        for sp in range(4):
            c0 = sp * 256
            wb = self.wslice([(0, 4, 256, I["w_br_attn"][l][:, c0:c0 + 256]), (4 * 256, 2, 256, I["w_br_s5"][l][:, c0:c0 + 256]),
                              (6 * 256, 2, 256, I["w_br_lru"][l][:, c0:c0 + 256])])
            wbv = wb.ap.rearrange("p (k n) -> p k n", k=8)
            wg = [self.wslice([(0, 8, 256, I["w_in"][l][:, 2304 + b * 1024 + c0:2304 + b * 1024 + c0 + 256])]) for b in range(3)]
            for t in range(2):
                ft = sp * 2 + t
                for tb in range(2):
                    sl = slice(tb * 512, (tb + 1) * 512)
                    for b, (k0, k1) in enumerate(((0, 4), (4, 6), (6, 8))):
                        gb = self.bank()
                        wgv = wg[b].ap.rearrange("p (k n) -> p k n", k=8)
                        fw.mm([lambda h, gb=gb, wgv=wgv, kc=kc, t=t, tb=tb: h.matmul(gb.ap, lhsT=wgv[:, kc, t * 128:(t + 1) * 128], rhs=self.h_rhs(kc, tb),
                                                                                     start=(kc == 0), stop=(kc == 7)) for kc in range(8)],
                              reads=[wg[b], self.h], writes=[gb])
                        bb = self.bank()
                        fw.mm([lambda h, bb=bb, kc=kc, t=t, sl=sl, k0=k0, k1=k1: h.matmul(bb.ap, lhsT=wbv[:, kc, t * 128:(t + 1) * 128], rhs=br_rhs(kc, sl),
                                                                                         start=(kc == k0), stop=(kc == k1 - 1)) for kc in range(k0, k1)],
                              reads=[wb, attnT, s5y, lruy], writes=[bb])
                        sg = sig[si[0] % 2]
                        si[0] += 1
                        self.act(sg.ap, gb.ap, AF.Sigmoid, [gb], [sg])
                        self.tt("dve", pr[b].ap, bb.ap, sg.ap, ALU.mult, [bb, sg], [pr[b]])
                    self.tt("dve", pr[0].ap, pr[0].ap, pr[1].ap, ALU.add, [pr[0], pr[1]], [pr[0]])
                    self.tt("dve", merged.ap[:, ft, sl], pr[0].ap, pr[2].ap, ALU.add, [pr[0], pr[2]], [merged])
                    self.tick()

        def out_cons(ft, tb, bk):
            sl = slice(tb * 512, (tb + 1) * 512)
            xv = self.x.ap[:, g, ft, sl]
            self.stt(xv, bk.ap, self.mod.ap[:, l, 2 * 8 + ft, g:g + 1], xv, ALU.mult, ALU.add, [bk, self.mod, self.x], [self.x])
            self.tick()

        self.proj_fm(lambda c0, n: [(0, 8, n, I["w_out"][l][:, c0:c0 + n])], 8,
                     lambda kc, tb: merged.ap[:, kc, tb * 512:(tb + 1) * 512], [merged], 8, out_cons)

    def ffn(self, l, g):
        fw, ar, I = self.fw, self.ar, self.I
        gg = ar.alloc([128, 22, TG], BF16)
        a_sb = [ar.alloc([128, TG], F32) for _ in range(2)]
        c_sb = [ar.alloc([128, TG], F32) for _ in range(2)]
        gl = [ar.alloc([128, TG], BF16) for _ in range(2)]
        nseq, L = (4, 256) if g == 0 else (1, 1024)
        fc = self.ffcols
        it = 0
        for sp in range(11):
            wa = self.wslice([(0, 8, 256, I["ffn_w_up"][l][:, sp * 256:sp * 256 + 256])])
            wb = self.wslice([(0, 8, 256, I["ffn_w_up"][l][:, 2816 + sp * 256:2816 + sp * 256 + 256])])
            wav = wa.ap.rearrange("p (k n) -> p k n", k=8)
            wbv = wb.ap.rearrange("p (k n) -> p k n", k=8)
            for t in range(2):
                ft = sp * 2 + t
                a_, c_, g_ = a_sb[it % 2], c_sb[it % 2], gl[it % 2]
                it += 1
                for tb in range(2):
                    bk = self.bank()
                    fw.mm([lambda h, bk=bk, kc=kc, t=t, tb=tb: h.matmul(bk.ap, lhsT=wav[:, kc, t * 128:(t + 1) * 128], rhs=self.h_rhs(kc, tb),
                                                                       start=(kc == 0), stop=(kc == 7)) for kc in range(8)], reads=[wa, self.h], writes=[bk])
                    self.cp("act", a_.ap[:, tb * 512:(tb + 1) * 512], bk.ap, [bk], [a_])
                av = a_.ap.rearrange("p (s q) -> p s q", s=nseq)
                cv = c_.ap.rearrange("p (s q) -> p s q", s=nseq)
                self.act(cv, av, AF.Identity, [a_, fc], [c_], bias=fc.ap[:, ft, 3:4], scale=fc.ap[:, ft, 1:2])
                self.stt(cv[:, :, 1:L], av[:, :, 0:L - 1], fc.ap[:, ft, 0:1], cv[:, :, 1:L], ALU.mult, ALU.add, [a_, fc, c_], [c_])
                self.stt(cv[:, :, 0:L - 1], av[:, :, 1:L], fc.ap[:, ft, 2:3], cv[:, :, 0:L - 1], ALU.mult, ALU.add, [a_, fc, c_], [c_])
                self.act(g_.ap, c_.ap, AF.Gelu_apprx_tanh, [c_], [g_])
                for tb in range(2):
                    sl = slice(tb * 512, (tb + 1) * 512)
                    bk = self.bank()
                    fw.mm([lambda h, bk=bk, kc=kc, t=t, tb=tb: h.matmul(bk.ap, lhsT=wbv[:, kc, t * 128:(t + 1) * 128], rhs=self.h_rhs(kc, tb),
                                                                       start=(kc == 0), stop=(kc == 7)) for kc in range(8)], reads=[wb, self.h], writes=[bk])
                    self.tt("dve", gg.ap[:, ft, sl], bk.ap, g_.ap[:, sl], ALU.mult, [bk, g_], [gg])
                self.tick()
        for ft in range(8):
            w0 = self.wslice([(0, 11, 128, I["ffn_w_down"][l][0:1408, ft * 128:(ft + 1) * 128])])
            w1 = self.wslice([(0, 11, 128, I["ffn_w_down"][l][1408:2816, ft * 128:(ft + 1) * 128])])
            wv = [w0.ap[:, 0:1408].rearrange("p (k n) -> p k n", k=11), w1.ap[:, 0:1408].rearrange("p (k n) -> p k n", k=11)]
            for tb in range(2):
                sl = slice(tb * 512, (tb + 1) * 512)
                bk = self.bank()
                fw.mm([lambda h, bk=bk, kc=kc, sl=sl: h.matmul(bk.ap, lhsT=wv[kc // 11][:, kc % 11, :], rhs=gg.ap[:, kc, sl],
                                                               start=(kc == 0), stop=(kc == 21)) for kc in range(22)], reads=[w0, w1, gg], writes=[bk])
                xv = self.x.ap[:, g, ft, sl]
                self.stt(xv, bk.ap, self.mod.ap[:, l, 5 * 8 + ft, g:g + 1], xv, ALU.mult, ALU.add, [bk, self.mod, self.x], [self.x])
            self.tick()
        self.drain()

    def final_norm(self, g):
        fw, ar, O = self.fw, self.ar, self.O
        m = ar.mark()
        rstd = ar.alloc([128, TG], F32)
        self.rms_rstd(g, rstd)
        y = ar.alloc([128, 8, TG], F32)
        for kc in range(8):
            self.stt(y.ap[:, kc, :], self.x.ap[:, g, kc, :], self.gcols.ap[:, 4, kc:kc + 1], rstd.ap, ALU.mult, ALU.mult, [self.x, self.gcols, rstd], [y])
        dst_t = O["yp" if g == 0 else "ys"]
        for tt in range(8):
            for qd in range(4):
                s = self.stg_i % 2
                self.stg_i += 1
                stg = self.stg[s]
                bk = self.bank()
                fw.mm([lambda h, bk=bk, j=j, qd=qd, tt=tt: h.transpose(out=bk.ap[:, j * 128:(j + 1) * 128], in_=y.ap[:, qd * 2 + j, tt * 128:(tt + 1) * 128],
                                                                       identity=self.ident_f.ap) for j in range(2)], reads=[y, self.ident_f], writes=[bk])
                self.cp("act" if qd % 2 else "dve", stg.ap, bk.ap[:, 0:256], [bk], [stg])
                dst = dst_t[tt * 128:(tt + 1) * 128, qd * 256:(qd + 1) * 256]
                dd = fw.dma("sp", lambda h, stg=stg, dst=dst: h.dma_start(out=dst, in_=stg.ap), self.stg_sem[s], reads=[stg])
                fw.final.append(dd)
        ar.release(m)


_W_KEYS = ["w_ada", "b_ada", "g_norm1", "g_norm2", "w_in", "rpb", "s5_lam_re", "s5_lam_im", "s5_log_step", "s5_b_re", "s5_b_im",
           "s5_c_re", "s5_c_im", "s5_d", "s5_w_glu", "lru_conv_w", "lru_conv_b", "lru_w_a", "lru_b_a", "lru_w_x", "lru_b_x", "lru_lam",
           "w_br_attn", "w_br_s5", "w_br_lru", "w_out", "ffn_w_up", "ffn_conv_w", "ffn_conv_b", "ffn_w_down", "g_final"]


def make_in_maps(inp):
    f = lambda a: np.ascontiguousarray(np.asarray(a, dtype=np.float32))
    shared = {k: f(inp[k]) for k in _W_KEYS}
    maps = []
    for i in range(NCORES):
        m = dict(shared)
        m["xp"] = f(inp["x_prompt"][4 * i:4 * i + 4]).reshape(TG, D)
        m["xs"] = f(inp["x_sample"][i]).reshape(TG, D)
        m["ck"] = f(inp["cache_k"][i]).reshape(DEPTH, 512, 512)
        m["cv"] = f(inp["cache_v"][i]).reshape(DEPTH, 512, 512)
        m["s5r"] = f(inp["state_s5_re"][i]).reshape(DEPTH, 2, 1024)
        m["s5i"] = f(inp["state_s5_im"][i]).reshape(DEPTH, 2, 1024)
        m["slru"] = f(inp["state_lru"][i]).reshape(DEPTH, 2, 256)
        m["cvec"] = f(np.stack([np.asarray(inp["c_ctx"]), np.asarray(inp["c"])[i]], axis=0))
        maps.append(m)
    return maps


def assemble(results):
    cat = lambda k: np.concatenate([np.asarray(r[k]) for r in results], axis=0)
    y_prompt = cat("yp").reshape(32, 256, D)
    y_sample = cat("ys").reshape(8, 1024, D)
    new_k = cat("nk").reshape(32, DEPTH, 256, 8, 64)
    new_v = cat("nv").reshape(32, DEPTH, 256, 8, 64)
    ns5r = cat("ns5r").reshape(32, DEPTH, 2, 16, 64)
    ns5i = cat("ns5i").reshape(32, DEPTH, 2, 16, 64)
    nlru = cat("nlru").reshape(32, DEPTH, 2, 256)
    return tuple(np.ascontiguousarray(a, dtype=np.float32) for a in (y_prompt, y_sample, new_k, new_v, ns5r, ns5i, nlru))


def kernel(**inputs):
    nc = Builder().build()
    in_maps = make_in_maps(inputs)
    res = run_bass_kernel_spmd(nc, in_maps, core_ids=list(range(NCORES)))
    return assemble(res.results)


def debug_run(inputs, stage, ncores=1, trace=False):
    b = Builder(stage=stage)
    nc = b.build()
    in_maps = make_in_maps(inputs)[:ncores]
    res = run_bass_kernel_spmd(nc, in_maps, core_ids=list(range(ncores)), trace=trace)
    if trace:
        print("EXEC_NS", stage, res.exec_time_ns)
    return b, res.results
```

```python
import math
import types as _types
import numpy as np
from contextlib import ExitStack
import concourse.bass as bass
import concourse.mybir as mybir
from concourse.bass_utils import run_bass_kernel_spmd

F32 = mybir.dt.float32
BF16 = mybir.dt.bfloat16
I32 = mybir.dt.int32
AF = mybir.ActivationFunctionType
ALU = mybir.AluOpType

NCORES = 8
D = 1024
TG = 1024
DEPTH = 2
NEG = -30000.0
EPS = 1e-6
IN_W = 5376
PAGE = 512


class Buf:
    __slots__ = ("name", "lw", "rd", "dsem", "dcnt", "excl")

    def __init__(self, name, excl=False):
        self.name = name
        self.excl = excl
        self.lw = None
        self.rd = []
        self.dsem = None
        self.dcnt = 0


class Reg:
    __slots__ = ("ap", "bufs", "tag")

    def __init__(self, ap, bufs, tag=None):
        self.ap = ap
        self.bufs = bufs
        self.tag = tag

    def __getitem__(self, k):
        return self.ap[k]


def _freeze(f):
    if getattr(f, "__closure__", None) is None:
        return f
    cells = []
    for c in f.__closure__:
        try:
            cells.append(_types.CellType(c.cell_contents))
        except ValueError:
            cells.append(c)
    return _types.FunctionType(f.__code__, f.__globals__, f.__name__, f.__defaults__, tuple(cells))


class Eng:
    def __init__(self, name):
        self.key = name
        self.cnt = 0
        self.seen = {}
        self.prog = []


class FW:
    def __init__(self, nc, stack):
        self.nc = nc
        self.stack = stack
        self.sems = {}
        self.E = {}
        for n in ("pe", "act", "dve", "pool", "sp"):
            self.sems[n] = stack.enter_context(nc.semaphore("s_" + n))
            self.E[n] = Eng(n)
        self.ndsem = 0
        self.dsem_free = []
        self.final = []

    def _expand(self, lst):
        out = []
        for r in lst:
            if isinstance(r, Buf):
                out.append(r)
            else:
                out.extend(r.bufs)
        return out

    def _need(self, eng, dep, same_ok):
        key, val, clock = dep
        if same_ok and key == eng.key:
            return
        if eng.seen.get(key, 0) >= val:
            return
        eng.prog.append(("wait", key, val))
        eng.seen[key] = val
        for k, v in clock.items():
            if eng.seen.get(k, 0) < v:
                eng.seen[k] = v

    def _deps(self, eng, reads, writes):
        for b in reads:
            if b.lw is not None:
                self._need(eng, b.lw, False)
            if b.excl:
                for r in b.rd:
                    self._need(eng, r, True)
        for b in writes:
            if b.lw is not None:
                self._need(eng, b.lw, True)
            for r in b.rd:
                self._need(eng, r, True)

    def _commit(self, dep, reads, writes):
        for b in reads:
            b.rd.append(dep)
        for b in writes:
            b.lw = dep
            b.rd = []

    def op(self, en, fn, reads=(), writes=()):
        eng = self.E[en]
        reads = self._expand(reads)
        writes = self._expand(writes)
        self._deps(eng, reads, writes)
        eng.cnt += 1
        eng.prog.append(("ins", _freeze(fn), eng.key, 1))
        clock = dict(eng.seen)
        clock[eng.key] = eng.cnt
        dep = (eng.key, eng.cnt, clock)
        self._commit(dep, reads, writes)
        return dep

    def mm(self, fns, reads=(), writes=()):
        eng = self.E["pe"]
        reads = self._expand(reads)
        writes = self._expand(writes)
        self._deps(eng, reads, writes)
        for f in fns[:-1]:
            eng.prog.append(("ins", _freeze(f), None, 0))
        eng.cnt += 1
        eng.prog.append(("ins", _freeze(fns[-1]), eng.key, 1))
        clock = dict(eng.seen)
        clock[eng.key] = eng.cnt
        dep = (eng.key, eng.cnt, clock)
        self._commit(dep, reads, writes)
        return dep

    def dma(self, qn, fn, dbuf, reads=(), writes=()):
        eng = self.E[qn]
        reads = self._expand(reads)
        writes = self._expand(writes)
        self._deps(eng, reads, writes)
        if dbuf.dsem is None:
            self.ndsem += 1
            dbuf.dsem = self.stack.enter_context(self.nc.semaphore(f"d{self.ndsem}"))
        if dbuf.dcnt:
            self._need(eng, ("D%d" % id(dbuf), dbuf.dcnt, {}), False)
        dbuf.dcnt += 16
        key = "D%d" % id(dbuf)
        self.sems[key] = dbuf.dsem
        eng.prog.append(("ins", _freeze(fn), key, 16))
        dep = (key, dbuf.dcnt, dict(eng.seen))
        self._commit(dep, reads, writes)
        return dep

    def emit(self):
        nc = self.nc
        sems = self.sems
        for d in self.final:
            self._need(self.E["sp"], d, False)

        def replay(eng, h):
            for it in eng.prog:
                if it[0] == "wait":
                    h.wait_ge(sems[it[1]], it[2])
                else:
                    ins = it[1](h)
                    if it[2] is not None:
                        ins.then_inc(sems[it[2]], it[3])

        with nc.Block() as block:
            @block.sync
            def _(h):
                replay(self.E["sp"], h)

            @block.scalar
            def _(h):
                replay(self.E["act"], h)

            @block.vector
            def _(h):
                replay(self.E["dve"], h)

            @block.gpsimd
            def _(h):
                replay(self.E["pool"], h)

            @block.tensor
            def _(h):
                replay(self.E["pe"], h)


class Arena:
    def __init__(self, fw, nbytes):
        self.fw = fw
        self.nbytes = nbytes
        self.t = fw.stack.enter_context(fw.nc.sbuf_tensor("arena", [128, nbytes // 2], BF16))
        self.pages = [Buf(f"pg{i}") for i in range((nbytes + PAGE - 1) // PAGE)]
        self.top = 0
        self.peak = 0

    def alloc(self, shape, dt, align=64):
        esz = 4 if dt in (F32, I32) else 2
        n = 1
        for s in shape[1:]:
            n *= s
        nb = n * esz
        if nb >= PAGE:
            align = max(align, PAGE)
        off = (self.top + align - 1) // align * align
        assert off + nb <= self.nbytes, f"arena overflow: need {off + nb} have {self.nbytes}"
        self.top = off + nb
        self.peak = max(self.peak, self.top)
        ap = self.t[0:shape[0], off // 2:(off + nb) // 2]
        if esz == 4:
            ap = ap.bitcast(dt)
        if len(shape) > 2:
            names = " ".join(f"d{i}" for i in range(1, len(shape)))
            kw = {f"d{i}": shape[i] for i in range(1, len(shape))}
            ap = ap.rearrange(f"p ({names}) -> p {names}", **kw)
        pages = self.pages[off // PAGE:(off + nb - 1) // PAGE + 1]
        return Reg(ap, pages)

    def mark(self):
        return self.top

    def release(self, m):
        self.top = m


def na_valid(kr, qr):
    w0 = min(max(qr - 4, 0), 8)
    return w0 <= kr < w0 + 8


class _Stop(Exception):
    pass


class Builder:
    def __init__(self, debug=False, stage=None):
        self.debug = debug
        self.stage = stage
        self.dbg_outs = []
        self.nc = bass.Bass("TRN2", target_bir_lowering=False)
        self.stack = ExitStack()
        self.dram = {}

    def din(self, name, shape):
        t = self.nc.dram_tensor(name, list(shape), F32, kind="ExternalInput")
        self.dram[name] = t
        return t.ap()

    def dout(self, name, shape):
        t = self.nc.dram_tensor(name, list(shape), F32, kind="ExternalOutput")
        self.dram[name] = t
        return t.ap()

    def build(self):
        nc = self.nc
        with self.stack as st:
            fw = self.fw = FW(nc, st)
            I = self.I = {}
            O = self.O = {}
            I["xp"] = self.din("xp", [TG, D])
            I["xs"] = self.din("xs", [TG, D])
            I["ck"] = self.din("ck", [DEPTH, 512, 512])
            I["cv"] = self.din("cv", [DEPTH, 512, 512])
            I["s5r"] = self.din("s5r", [DEPTH, 2, 1024])
            I["s5i"] = self.din("s5i", [DEPTH, 2, 1024])
            I["slru"] = self.din("slru", [DEPTH, 2, 256])
            I["cvec"] = self.din("cvec", [2, D])
            wshapes = {
                "w_ada": [2, D, 6 * D], "b_ada": [2, 6 * D], "g_norm1": [2, D], "g_norm2": [2, D],
                "w_in": [2, D, IN_W], "rpb": [2, 8, 15, 31],
                "s5_lam_re": [2, 2, 16, 64], "s5_lam_im": [2, 2, 16, 64], "s5_log_step": [2, 2, 16],
                "s5_b_re": [2, 16, 64, 16], "s5_b_im": [2, 16, 64, 16],
                "s5_c_re": [2, 2, 16, 16, 64], "s5_c_im": [2, 2, 16, 16, 64],
                "s5_d": [2, 256], "s5_w_glu": [2, 256, 256],
                "lru_conv_w": [2, 4, 256], "lru_conv_b": [2, 256],
                "lru_w_a": [2, 2, 4, 64, 64], "lru_b_a": [2, 2, 256],
                "lru_w_x": [2, 2, 4, 64, 64], "lru_b_x": [2, 2, 256], "lru_lam": [2, 2, 256],
                "w_br_attn": [2, 512, D], "w_br_s5": [2, 256, D], "w_br_lru": [2, 256, D],
                "w_out": [2, D, D], "ffn_w_up": [2, D, 5632], "ffn_conv_w": [2, 3, 2816],
                "ffn_conv_b": [2, 2816], "ffn_w_down": [2, 2816, D], "g_final": [D],
            }
            self.wshapes = wshapes
            for k, s in wshapes.items():
                I[k] = self.din(k, s)
            O["yp"] = self.dout("yp", [TG, D])
            O["ys"] = self.dout("ys", [TG, D])
            O["nk"] = self.dout("nk", [4, DEPTH, 256, 512])
            O["nv"] = self.dout("nv", [4, DEPTH, 256, 512])
            O["ns5r"] = self.dout("ns5r", [4, DEPTH, 2, 1024])
            O["ns5i"] = self.dout("ns5i", [4, DEPTH, 2, 1024])
            O["nlru"] = self.dout("nlru", [4, DEPTH, 2, 256])

            self.ar = Arena(fw, 207 * 1024)
            ps_t = st.enter_context(nc.psum_tensor("psum", [128, 8, 512], F32))
            self.banks = [Reg(ps_t[:, i, :], [Buf(f"bank{i}", excl=True)], i) for i in range(8)]
            self.bank_i = 0
            self.held = set()
            try:
                self.setup()
                self.chk("setup")
                for l in range(DEPTH):
                    if l == 0:
                        self.bg = self.mod_gen(0, look=4, s0=0, s1=8, doA=(True, False))
                        self.drain()
                    self.chk(f"prep{l}")
                    for g in range(2):
                        self.group_layer(l, g)
                        self.chk(f"gl{l}{g}")
                for g in range(2):
                    self.final_norm(g)
            except _Stop:
                pass
            fw.emit()
        return nc

    def chk(self, name):
        if self.stage == name:
            raise _Stop()

    def dbg(self, name, reg, ap, shape):
        t = self.nc.dram_tensor("dbg_" + name, list(shape), F32, kind="ExternalOutput").ap()
        d = self.fw.dma("sp", lambda h: h.dma_start(out=t, in_=ap), Buf("dbg_" + name), reads=[reg])
        self.fw.final.append(d)
        self.dbg_outs.append("dbg_" + name)

    def bank(self, hold=False):
        while True:
            i = self.bank_i % 8
            self.bank_i += 1
            if i not in self.held:
                break
        if hold:
            self.held.add(i)
        return self.banks[i]

    def unhold(self, bk):
        self.held.discard(bk.tag)

    def col(self, reg, i):
        return reg.ap[:, i:i + 1]

    def setup(self):
        fw, ar, nc, I = self.fw, self.ar, self.nc, self.I
        self.x = ar.alloc([128, 2, 8, TG], F32)
        self.ident_bf = ar.alloc([128, 128], BF16, align=PAGE)
        self.ident_f = ar.alloc([128, 128], F32)
        self.ones_bf = ar.alloc([128, 128], BF16)
        self.epsc = ar.alloc([128, 1], F32)
        self.gcols = ar.alloc([128, 5, 8], F32)
        self.cTb = ar.alloc([128, 8, 2], BF16)
        self.badaT = ar.alloc([128, 2, 48], F32)
        self.mod = ar.alloc([128, 2, 48, 2], F32, align=PAGE)
        self.A1 = ar.alloc([128, 2, 8, 2], F32)
        self.A2 = ar.alloc([128, 2, 8, 2], F32)
        self.s5cols = ar.alloc([128, 16, 8], F32, align=PAGE)
        self.lrucols = ar.alloc([128, 2, 16], F32)
        self.s5d = ar.alloc([128, 2], F32)
        self.ffcols = ar.alloc([128, 22, 4], F32)
        self.lrust0 = ar.alloc([128, 2, 2], F32)
        self.lruw = ar.alloc([128, 8, 128], BF16)
        self.wglu = ar.alloc([128, 2, 256], BF16)
        self.s5w = ar.alloc([128, 16, 4, 128], BF16, align=PAGE)
        self.outst = ar.alloc([128, 4, 2, 8, 2], F32, align=PAGE)
        self.lrust = ar.alloc([128, 4, 2, 2], F32)
        self.slots = [ar.alloc([128, 2048], BF16, align=PAGE) for _ in range(6)]
        self.slot_sem = [Buf(f"slotsem{i}") for i in range(6)]
        self.slot_i = 0
        self.stg = [ar.alloc([128, 256], F32, align=PAGE) for _ in range(2)]
        self.stg_sem = [Buf("stg0"), Buf("stg1")]
        self.stg_i = 0
        self.small_sem = Buf("small")
        self.h = ar.alloc([128, 8, TG], BF16, align=PAGE)
        self.bg = None
        fw.op("pool", lambda h: h.memset(self.epsc.ap, EPS), writes=[self.epsc])
        self.scr_mark = ar.mark()

        idb, idf, ones = self.ident_bf, self.ident_f, self.ones_bf
        fw.op("pool", lambda h: h.memset(idf.ap, 1.0), writes=[idf])
        fw.op("pool", lambda h: h.affine_select(out=idf.ap, in_=idf.ap, pattern=[[-1, 128]], compare_op=ALU.is_equal,
                                                fill=0.0, base=0, channel_multiplier=1), reads=[idf], writes=[idf])
        fw.op("dve", lambda h: h.tensor_copy(out=idb.ap, in_=idf.ap), reads=[idf], writes=[idb])
        fw.op("dve", lambda h: h.memset(ones.ap, 1.0), writes=[ones])

        self.scr_mark = ar.mark()
        self.prep_bg = False
        self.mod_init()
        self.bg = self.layer_prep_gen(0)
        for g, name in enumerate(("xp", "xs")):
            for tt in range(8):
                for qd in range(4):
                    s = self.stg_i % 2
                    self.stg_i += 1
                    stg = self.stg[s]
                    src = I[name][tt * 128:(tt + 1) * 128, qd * 256:(qd + 1) * 256]
                    fw.dma("sp", lambda h, stg=stg, src=src: h.dma_start(out=stg.ap, in_=src), self.stg_sem[s], writes=[stg])
                    bk = self.bank()
                    fw.mm([lambda h, bk=bk, stg=stg, j=j: h.transpose(out=bk.ap[:, j * 128:(j + 1) * 128], in_=stg.ap[:, j * 128:(j + 1) * 128],
                                                                      identity=idf.ap) for j in range(2)], reads=[stg, idf], writes=[bk])
                    dstv = self.x.ap[:, g, qd * 2:qd * 2 + 2, tt * 128:(tt + 1) * 128]
                    srcv = bk.ap[:, 0:256].rearrange("p (j t) -> p j t", j=2)
                    if qd % 2:
                        fw.op("act", lambda h, dstv=dstv, srcv=srcv: h.activation(out=dstv, in_=srcv, func=AF.Copy), reads=[bk], writes=[self.x])
                    else:
                        fw.op("dve", lambda h, dstv=dstv, srcv=srcv: h.tensor_copy(out=dstv, in_=srcv), reads=[bk], writes=[self.x])
                    if g == 0:
                        self.tick()
        self.drain()

    def small_load(self, dst_ap, src_ap, dst_reg):
        return self.sload(dst_ap, src_ap, dst_reg)

    def wslice(self, parts):
        s = self.slot_i % 6
        self.slot_i += 1
        slot = self.slots[s]
        for (off, kc, ncols, src) in parts:
            dst = slot.ap[:, off:off + kc * ncols].rearrange("p (k n) -> p k n", k=kc)
            sv = src.rearrange("(k p) n -> p k n", p=128)
            self.fw.dma("pool", lambda h, dst=dst, sv=sv: h.dma_start(out=dst, in_=sv), self.slot_sem[s], writes=[slot])
        return slot

    def mod_init(self):
        fw, ar, I = self.fw, self.ar, self.I
        m = ar.mark()
        cT = ar.alloc([128, 8, 2], F32)
        for cc in range(2):
            self.small_load(cT.ap[:, :, cc], I["cvec"][cc].rearrange("(k p) -> p k", p=128), cT)
            self.small_load(self.badaT.ap[:, cc, :], I["b_ada"][cc].rearrange("(t p) -> p t", p=128), self.badaT)
            self.small_load(self.gcols.ap[:, cc, :], I["g_norm1"][cc].rearrange("(k p) -> p k", p=128), self.gcols)
            self.small_load(self.gcols.ap[:, 2 + cc, :], I["g_norm2"][cc].rearrange("(k p) -> p k", p=128), self.gcols)
        self.small_load(self.gcols.ap[:, 4, :], I["g_final"].rearrange("(k p) -> p k", p=128), self.gcols)
        fw.op("act", lambda h: h.activation(out=self.cTb.ap, in_=cT.ap, func=AF.Silu), reads=[cT], writes=[self.cTb])
        ar.release(m)

    def mod_gen(self, l, look=2, s0=0, s1=24, doA=(True, True)):
        fw, I = self.fw, self.I
        cTb, badaT = self.cTb, self.badaT
        pend = []
        nxt = s0
        for s in range(s0, s1):
            while nxt < s1 and nxt <= s + look:
                pend.append(self.wslice([(0, 8, 256, I["w_ada"][l][:, nxt * 256:(nxt + 1) * 256])]))
                nxt += 1
            slot = pend.pop(0)
            sv = slot.ap.rearrange("p (k n) -> p k n", k=8)
            bk = self.bank()
            for t in range(2):
                fw.mm([lambda h, bk=bk, sv=sv, t=t, kc=kc: h.matmul(bk.ap[:, t * 2:t * 2 + 2], lhsT=sv[:, kc, t * 128:(t + 1) * 128],
                                                                    rhs=cTb.ap[:, kc, :], start=(kc == 0), stop=(kc == 7)) for kc in range(8)],
                      reads=[slot, cTb], writes=[bk])
            fw.op("dve", lambda h, bk=bk, l=l, s=s: h.tensor_tensor(out=self.mod.ap[:, l, 2 * s:2 * s + 2, :],
                                                                   in0=bk.ap[:, 0:4].rearrange("p (t c) -> p t c", t=2),
                                                                   in1=badaT.ap[:, l, 2 * s:2 * s + 2].unsqueeze(2).to_broadcast([128, 2, 2]),
                                                                   op=ALU.add), reads=[bk, badaT], writes=[self.mod])
            yield
        for ai, (A, comp, gi) in enumerate(((self.A1, 1, 0), (self.A2, 4, 2))):
            if not doA[ai]:
                continue
            fw.op("dve", lambda h, A=A, comp=comp, gi=gi, l=l: h.scalar_tensor_tensor(
                out=A.ap[:, l], in0=self.mod.ap[:, l, comp * 8:comp * 8 + 8, :], scalar=1.0,
                in1=self.gcols.ap[:, gi + l, :].unsqueeze(2).to_broadcast([128, 8, 2]), op0=ALU.add, op1=ALU.mult),
                reads=[self.mod, self.gcols], writes=[A])

    def tick(self):
        if self.bg is not None:
            try:
                next(self.bg)
            except StopIteration:
                self.bg = None

    def drain(self):
        while self.bg is not None:
            self.tick()

    def rms_rstd(self, g, rstd):
        fw, ar = self.fw, self.ar
        m = ar.mark()
        sq = [ar.alloc([128, 512], BF16) for _ in range(2)]
        tmp = ar.alloc([128, 512], F32)
        for tb in range(2):
            bk = self.bank()
            for kc in range(8):
                q = sq[kc % 2]
                fw.op("act", lambda h, q=q, kc=kc, tb=tb: h.activation(out=q.ap, in_=self.x.ap[:, g, kc, tb * 512:(tb + 1) * 512], func=AF.Square),
                      reads=[self.x], writes=[q])
                fw.mm([lambda h, bk=bk, q=q, kc=kc: h.matmul(bk.ap, lhsT=self.ones_bf.ap, rhs=q.ap, start=(kc == 0), stop=(kc == 7))],
                      reads=[q, self.ones_bf], writes=[bk])
            fw.op("act", lambda h, bk=bk: h.activation(out=tmp.ap, in_=bk.ap, func=AF.Sqrt, scale=1.0 / D, bias=self.epsc.ap[:, 0:1]),
                  reads=[bk, self.epsc], writes=[tmp])
            fw.op("dve", lambda h, tb=tb: h.reciprocal(out=rstd.ap[:, tb * 512:(tb + 1) * 512], in_=tmp.ap), reads=[tmp], writes=[rstd])
        ar.release(m)

    def norm_mod(self, l, g, A, shcomp):
        fw, ar = self.fw, self.ar
        m = ar.mark()
        rstd = ar.alloc([128, TG], F32)
        self.rms_rstd(g, rstd)
        tmps = [ar.alloc([128, TG], F32) for _ in range(2)]
        for kc in range(8):
            t = tmps[kc % 2]
            fw.op("dve", lambda h, t=t, kc=kc: h.scalar_tensor_tensor(out=t.ap, in0=self.x.ap[:, g, kc, :], scalar=A.ap[:, l, kc, g:g + 1],
                                                                      in1=rstd.ap, op0=ALU.mult, op1=ALU.mult),
                  reads=[self.x, A, rstd], writes=[t])
            fw.op("act", lambda h, t=t, kc=kc: h.activation(out=self.h.ap[:, kc, :], in_=t.ap, func=AF.Identity,
                                                            bias=self.mod.ap[:, l, shcomp * 8 + kc, g:g + 1], scale=1.0),
                  reads=[t, self.mod], writes=[self.h])
        ar.release(m)

    def proj_fm(self, src_cols_fn, ntiles, rhs, rhs_regs, kcs, consume):
        fw = self.fw
        for sp in range(0, ntiles, 2):
            nt = min(2, ntiles - sp)
            slot = self.wslice(src_cols_fn(sp * 128, nt * 128))
            sv = slot.ap[:, 0:kcs * nt * 128].rearrange("p (k n) -> p k n", k=kcs)
            for t in range(nt):
                for tb in range(2):
                    bk = self.bank()
                    fw.mm([lambda h, bk=bk, sv=sv, t=t, tb=tb, kc=kc: h.matmul(bk.ap, lhsT=sv[:, kc, t * 128:(t + 1) * 128], rhs=rhs(kc, tb),
                                                                                start=(kc == 0), stop=(kc == kcs - 1)) for kc in range(kcs)],
                          reads=[slot] + rhs_regs, writes=[bk])
                    consume(sp + t, tb, bk)

    def win_cols(self, l, base):
        return lambda c0, n: [(0, 8, n, self.I["w_in"][l][:, base + c0:base + c0 + n])]

    def h_rhs(self, kc, tb):
        return self.h.ap[:, kc, tb * 512:(tb + 1) * 512]

    def tt(self, en, out, in0, in1, op, R, W):
        self.fw.op(en, lambda h: h.tensor_tensor(out=out, in0=in0, in1=in1, op=op), reads=R, writes=W)

    def ts(self, en, out, in0, s1, s2, op0, op1, R, W):
        if s2 is None:
            self.fw.op(en, lambda h: h.tensor_scalar(out=out, in0=in0, scalar1=s1, scalar2=None, op0=op0), reads=R, writes=W)
        else:
            self.fw.op(en, lambda h: h.tensor_scalar(out=out, in0=in0, scalar1=s1, scalar2=s2, op0=op0, op1=op1), reads=R, writes=W)

    def stt(self, out, in0, sc, in1, op0, op1, R, W, en="dve"):
        self.fw.op(en, lambda h: h.scalar_tensor_tensor(out=out, in0=in0, scalar=sc, in1=in1, op0=op0, op1=op1), reads=R, writes=W)

    def act(self, out, in_, func, R, W, bias=None, scale=None):
        kw = {}
        if bias is not None:
            kw["bias"] = bias
        if scale is not None:
            kw["scale"] = scale
        self.fw.op("act", lambda h: h.activation(out=out, in_=in_, func=func, **kw), reads=R, writes=W)

    def cp(self, en, out, in_, R, W):
        if en == "act":
            self.fw.op("act", lambda h: h.activation(out=out, in_=in_, func=AF.Copy), reads=R, writes=W)
        else:
            self.fw.op(en, lambda h: h.tensor_copy(out=out, in_=in_), reads=R, writes=W)

    def sincos(self, y, n, cos_out, sin_out, Wc, Ws):
        ar = self.ar
        MAGIC = 12582912.0
        m = ar.mark()
        kf = ar.alloc([128, n], F32)
        fc = ar.alloc([128, n], F32)
        self.ts("dve", kf.ap, y.ap, 0.25, MAGIC, ALU.add, ALU.add, [y], [kf])
        self.ts("dve", kf.ap, kf.ap, MAGIC, None, ALU.subtract, None, [kf], [kf])
        self.stt(fc.ap, y.ap, 0.25, kf.ap, ALU.add, ALU.subtract, [y, kf], [fc])
        self.act(cos_out, fc.ap, AF.Sin, [fc], [Wc], scale=2.0 * math.pi)
        self.ts("dve", kf.ap, y.ap, MAGIC, None, ALU.add, None, [y], [kf])
        self.ts("dve", kf.ap, kf.ap, MAGIC, None, ALU.subtract, None, [kf], [kf])
        self.tt("dve", y.ap, y.ap, kf.ap, ALU.subtract, [y, kf], [y])
        self.act(sin_out, y.ap, AF.Sin, [y], [Ws], scale=2.0 * math.pi)
        ar.release(m)

    _ssem_i = 0

    def sload(self, dst_ap, src_ap, dst_reg, q="pool"):
        if not hasattr(self, "ssems"):
            self.ssems = [Buf(f"ss{i}") for i in range(8)]
        s = self.ssems[Builder._ssem_i % 8]
        Builder._ssem_i += 1
        if q == "sp":
            return self.fw.dma("sp", lambda h: h.dma_start(out=dst_ap, in_=src_ap, allow_slow_non_contiguous=True), s, writes=[dst_reg])
        return self.fw.dma("pool", lambda h: h.dma_start(out=dst_ap, in_=src_ap, allow_slow_non_contiguous=True), s, writes=[dst_reg])

    def layer_prep_gen(self, l):
        fw, ar, I = self.fw, self.ar, self.I
        m = ar.mark()
        c = self.s5cols
        lam_r = ar.alloc([128, 16], F32)
        lam_i = ar.alloc([128, 16], F32)
        stp = ar.alloc([128, 16], F32)
        t1 = ar.alloc([128, 16], F32)
        t2 = ar.alloc([128, 16], F32)
        t3 = ar.alloc([128, 16], F32)
        yv = ar.alloc([128, 16], F32)
        cs = ar.alloc([128, 16], F32)
        sn = ar.alloc([128, 16], F32)
        nat = ar.alloc([128, 4, 8, 16], F32)
        Cn = [ar.alloc([128, 4, 2, 64], F32) for _ in range(2)]
        ct2 = ar.alloc([128, 128], F32)
        lam = ar.alloc([128, 2, 2], F32)
        xx = ar.alloc([128, 2, 2], F32)
        pp = ar.alloc([128, 2, 2], F32)
        msk = ar.alloc([128, 8, 8], F32)
        Br = ar.alloc([128, 8, 16], F32)
        Bi = ar.alloc([128, 8, 16], F32)
        bb = [ar.alloc([128, 2, 8, 16], F32) for _ in range(2)]
        tA = ar.alloc([128, 2, 8, 16], F32)
        self.sload(lam_r.ap, I["s5_lam_re"][l].rearrange("d (j gl) n -> (gl n) (d j)", gl=2), lam_r)
        self.sload(lam_i.ap, I["s5_lam_im"][l].rearrange("d (j gl) n -> (gl n) (d j)", gl=2), lam_i)
        for gl in range(2):
            src = bass.AP(I["s5_log_step"].tensor, l * 32 + gl, [[0, 64], [2, 16]])
            self.sload(stp.ap[64 * gl:64 * gl + 64, :], src, stp)
        self.sload(c.ap[:, :, 6], I["s5r"][l].rearrange("d (j q) -> q (d j)", q=128), c)
        self.sload(c.ap[:, :, 7], I["s5i"][l].rearrange("d (j q) -> q (d j)", q=128), c)
        yield
        self.act(stp.ap, stp.ap, AF.Exp, [stp], [stp])
        self.tt("dve", t1.ap, lam_r.ap, stp.ap, ALU.mult, [lam_r, stp], [t1])
        self.tt("dve", t2.ap, lam_i.ap, stp.ap, ALU.mult, [lam_i, stp], [t2])
        self.act(c.ap[:, :, 1], t1.ap, AF.Exp, [t1], [c])
        self.ts("dve", yv.ap, t2.ap, 1.0 / (2.0 * math.pi), None, ALU.mult, None, [t2], [yv])
        self.sincos(yv, 16, cs.ap, sn.ap, cs, sn)
        self.cp("dve", c.ap[:, :, 0], yv.ap, [yv], [c])
        self.tt("dve", c.ap[:, :, 2], c.ap[:, :, 1], cs.ap, ALU.mult, [c, cs], [c])
        self.tt("dve", c.ap[:, :, 3], c.ap[:, :, 1], sn.ap, ALU.mult, [c, sn], [c])
        yield
        self.ts("dve", t1.ap, c.ap[:, :, 2], -1.0, None, ALU.add, None, [c], [t1])
        self.tt("dve", t2.ap, lam_r.ap, lam_r.ap, ALU.mult, [lam_r], [t2])
        self.tt("dve", t3.ap, lam_i.ap, lam_i.ap, ALU.mult, [lam_i], [t3])
        self.tt("dve", t2.ap, t2.ap, t3.ap, ALU.add, [t2, t3], [t2])
        fw.op("dve", lambda h: h.reciprocal(out=t2.ap, in_=t2.ap), reads=[t2], writes=[t2])
        self.tt("dve", t3.ap, t1.ap, lam_r.ap, ALU.mult, [t1, lam_r], [t3])
        self.tt("dve", yv.ap, c.ap[:, :, 3], lam_i.ap, ALU.mult, [c, lam_i], [yv])
        self.tt("dve", t3.ap, t3.ap, yv.ap, ALU.add, [t3, yv], [t3])
        self.tt("dve", c.ap[:, :, 4], t3.ap, t2.ap, ALU.mult, [t3, t2], [c])
        self.tt("dve", t3.ap, c.ap[:, :, 3], lam_r.ap, ALU.mult, [c, lam_r], [t3])
        self.tt("dve", yv.ap, t1.ap, lam_i.ap, ALU.mult, [t1, lam_i], [yv])
        self.tt("dve", t3.ap, t3.ap, yv.ap, ALU.subtract, [t3, yv], [t3])
        self.tt("dve", c.ap[:, :, 5], t3.ap, t2.ap, ALU.mult, [t3, t2], [c])

        yield
        fw.op("pool", lambda h: h.memset(msk.ap, 1.0), writes=[msk])
        for gl in range(2):
            fw.op("pool", lambda h, gl=gl: h.affine_select(out=msk.ap[64 * gl:64 * gl + 64], in_=msk.ap[64 * gl:64 * gl + 64],
                                                           pattern=[[0, 2], [-2, 4], [1, 8]], compare_op=ALU.is_equal, fill=0.0,
                                                           base=-gl, channel_multiplier=0), reads=[msk], writes=[msk])
        self.sload(Br.ap, I["s5_b_re"][l].rearrange("(j gl) n p -> (gl n) j p", gl=2), Br)
        self.sload(Bi.ap, I["s5_b_im"][l].rearrange("(j gl) n p -> (gl n) j p", gl=2), Bi)
        kr = c.ap[:, :, 4].rearrange("p (d j) -> p d j", d=2).unsqueeze(3).to_broadcast([128, 2, 8, 16])
        ki = c.ap[:, :, 5].rearrange("p (d j) -> p d j", d=2).unsqueeze(3).to_broadcast([128, 2, 8, 16])
        Brb = Br.ap.unsqueeze(1).to_broadcast([128, 2, 8, 16])
        Bib = Bi.ap.unsqueeze(1).to_broadcast([128, 2, 8, 16])
        self.tt("dve", bb[0].ap, Brb, kr, ALU.mult, [Br, c], [bb[0]])
        self.tt("dve", tA.ap, Bib, ki, ALU.mult, [Bi, c], [tA])
        self.tt("dve", bb[0].ap, bb[0].ap, tA.ap, ALU.subtract, [bb[0], tA], [bb[0]])
        self.tt("dve", bb[1].ap, Bib, kr, ALU.mult, [Bi, c], [bb[1]])
        self.tt("dve", tA.ap, Brb, ki, ALU.mult, [Br, c], [tA])
        self.tt("dve", bb[1].ap, bb[1].ap, tA.ap, ALU.add, [bb[1], tA], [bb[1]])
        yield
        for ri in range(2):
            for q4 in range(4):
                d, j0 = q4 // 2, (q4 % 2) * 4
                for jj in range(4):
                    j = j0 + jj
                    self.tt("dve", nat.ap[:, jj], bb[ri].ap[:, d, j].unsqueeze(1).to_broadcast([128, 8, 16]),
                            msk.ap[:, j].unsqueeze(2).to_broadcast([128, 8, 16]), ALU.mult, [bb[ri], msk], [nat])
                yield
                bk = self.bank()
                fw.mm([lambda h, bk=bk, jj=jj: h.transpose(out=bk.ap[:, jj * 128:(jj + 1) * 128],
                                                           in_=nat.ap[:, jj].rearrange("p a b -> p (a b)"), identity=self.ident_f.ap)
                       for jj in range(4)], reads=[nat, self.ident_f], writes=[bk])
                self.cp("act", self.s5w.ap[:, d * 8 + j0:d * 8 + j0 + 4, ri, :], bk.ap.rearrange("p (a b) -> p a b", a=4), [bk], [self.s5w])
        for hf in range(2):
            self.sload(Cn[0].ap[:, :, hf, :], I["s5_c_re"][l].rearrange("d g p n -> (d g p) n").rearrange("(q r) n -> r q n", r=128), Cn[0])
            self.sload(Cn[1].ap[:, :, hf, :], I["s5_c_im"][l].rearrange("d g p n -> (d g p) n").rearrange("(q r) n -> r q n", r=128), Cn[1])
        for ri in range(2):
            for q in range(4):
                d, ut = q // 2, q % 2
                bk = self.bank()
                fw.mm([lambda h, bk=bk, ri=ri, q=q: h.matmul(bk.ap[:, 0:128], lhsT=Cn[ri].ap[:, q].rearrange("p a b -> p (a b)"),
                                                             rhs=self.ident_f.ap, start=True, stop=True)],
                      reads=[Cn[ri], self.ident_f], writes=[bk])
                if ri == 0:
                    self.cp("act", ct2.ap, bk.ap[:, 0:128], [bk], [ct2])
                else:
                    self.act(ct2.ap, bk.ap[:, 0:128], AF.Copy, [bk], [ct2], scale=-1.0)
                j0 = ut * 4
                self.tt("dve", self.s5w.ap[:, d * 8 + j0:d * 8 + j0 + 4, 2 + ri, :].rearrange("p j (g q) -> p j g q", g=8),
                        ct2.ap.rearrange("p (g q) -> p g q", g=8).unsqueeze(1).to_broadcast([128, 4, 8, 16]),
                        msk.ap[:, j0:j0 + 4].unsqueeze(3).to_broadcast([128, 4, 8, 16]), ALU.mult, [ct2, msk], [self.s5w])
                yield
        self.sload(self.s5d.ap, I["s5_d"][l].rearrange("(t p) -> p t", p=128), self.s5d)
        yield
        lc = self.lrucols
        for k in range(4):
            self.sload(lc.ap[:, :, k], I["lru_conv_w"][l, k].rearrange("(t p) -> p t", p=128), lc)
        self.sload(lc.ap[:, :, 4], I["lru_conv_b"][l].rearrange("(t p) -> p t", p=128), lc)
        for d in range(2):
            self.sload(lc.ap[:, :, 5 + d], I["lru_b_a"][l, d].rearrange("(t p) -> p t", p=128), lc)
            self.sload(lc.ap[:, :, 7 + d], I["lru_b_x"][l, d].rearrange("(t p) -> p t", p=128), lc)
        for d in range(2):
            self.sload(lam.ap[:, :, d], I["lru_lam"][l, d].rearrange("(t p) -> p t", p=128), lam)
        self.act(xx.ap, lam.ap, AF.Exp, [lam], [xx], scale=-1.0)
        self.ts("dve", pp.ap, xx.ap, -0.25, 1.0 / 3.0, ALU.mult, ALU.add, [xx], [pp])
        self.tt("dve", pp.ap, pp.ap, xx.ap, ALU.mult, [pp, xx], [pp])
        self.ts("dve", pp.ap, pp.ap, -1.0, 0.5, ALU.mult, ALU.add, [pp], [pp])
        self.tt("dve", pp.ap, pp.ap, xx.ap, ALU.mult, [pp, xx], [pp])
        self.ts("dve", pp.ap, pp.ap, -1.0, 1.0, ALU.mult, ALU.add, [pp], [pp])
        self.tt("dve", pp.ap, pp.ap, xx.ap, ALU.mult, [pp, xx], [pp])
        self.ts("dve", lc.ap[:, :, 9:11], pp.ap, -8.0, None, ALU.mult, None, [pp], [lc])
        self.ts("dve", lc.ap[:, :, 11:13], pp.ap, 8.0, None, ALU.mult, None, [pp], [lc])
        self.ts("dve", lc.ap[:, :, 13:15], pp.ap, -16.0, None, ALU.mult, None, [pp], [lc])
        yield
        fw.op("pool", lambda h: h.memset(self.lruw.ap, 0.0), writes=[self.lruw])
        for gi, nm in enumerate(("lru_w_a", "lru_w_x")):
            for d in range(2):
                for t in range(2):
                    for b2 in range(2):
                        idx = gi * 4 + d * 2 + t
                        self.sload(self.lruw.ap[64 * b2:64 * b2 + 64, idx, 64 * b2:64 * b2 + 64], I[nm][l, d, 2 * t + b2], self.lruw, q="pool")
        for d in range(2):
            self.sload(self.lrust0.ap[:, d, :], I["slru"][l, d].rearrange("(t p) -> p t", p=128), self.lrust0)
        self.sload(self.wglu.ap, I["s5_w_glu"][l].rearrange("(k p) n -> p k n", p=128), self.wglu, q="pool")
        yield
        if not self.prep_bg:
            ar.release(m)

    def ffn_prep(self, l):
        I = self.I
        for k in range(3):
            self.sload(self.ffcols.ap[:, :, k], I["ffn_conv_w"][l, k].rearrange("(t p) -> p t", p=128), self.ffcols)
        self.sload(self.ffcols.ap[:, :, 3], I["ffn_conv_b"][l].rearrange("(t p) -> p t", p=128), self.ffcols)

    def group_layer(self, l, g):
        fw, ar, I, O = self.fw, self.ar, self.I, self.O
        if g == 0:
            self.ffn_prep(l)
        self.norm_mod(l, g, self.A1, 0)
        self.chk(f"norm{l}{g}")
        m0 = ar.mark()
        attnT = ar.alloc([128, 4, TG], BF16)
        m1 = ar.mark()
        self.attention(l, g, attnT)
        self.chk(f"attn{l}{g}")
        ar.release(m1)
        s5y = ar.alloc([128, 2, TG], BF16)
        m1 = ar.mark()
        self.s5(l, g, s5y)
        self.chk(f"s5{l}{g}")
        ar.release(m1)
        if g == 0:
            self.store_s5_states(l)
        lruy = ar.alloc([128, 2, TG], BF16)
        m1 = ar.mark()
        self.lru(l, g, lruy)
        self.chk(f"lru{l}{g}")
        ar.release(m1)
        if g == 1 and l + 1 < DEPTH:
            self.prep_bg = True
            self.bg = self.layer_prep_gen(l + 1)
            self.tick()
        self.merge(l, g, attnT, s5y, lruy)
        self.drain()
        self.chk(f"merge{l}{g}")
        ar.release(m0)
        self.norm_mod(l, g, self.A2, 3)
        self.ffn(l, g)
        ar.release(m0)

    def attention(self, l, g, attnT):
        fw, ar, I, O = self.fw, self.ar, self.I, self.O
        q_sb = ar.alloc([128, 4, TG], BF16)
        k_sb = ar.alloc([128, 4, TG], BF16)
        v_aug = ar.alloc([128, 8, 8, 66], BF16)
        fw.op("pool", lambda h: h.memset(v_aug.ap[:, :, :, 64:65], 1.0), writes=[v_aug])

        def q_cons(ft, tb, bk):
            self.act(q_sb.ap[:, ft, tb * 512:(tb + 1) * 512], bk.ap, AF.Copy, [bk], [q_sb], scale=0.125)

        def k_cons(ft, tb, bk):
            self.cp("dve", k_sb.ap[:, ft, tb * 512:(tb + 1) * 512], bk.ap, [bk], [k_sb])

        self.proj_fm(self.win_cols(l, 0), 4, self.h_rhs, [self.h], 8, q_cons)
        self.proj_fm(self.win_cols(l, 512), 4, self.h_rhs, [self.h], 8, k_cons)
        self.chk(f"attq{l}{g}")
        for which in ((1, 2) if g == 0 else (2,)):
            for half in range(2):
                slot = self.wslice([(0, 8, 256, I["w_in"][l][:, which * 512 + half * 256: which * 512 + half * 256 + 256])])
                sv = slot.ap.rearrange("p (k n) -> p k n", k=8)
                for tt in range(8):
                    bk = self.bank()
                    fw.mm([lambda h, bk=bk, sv=sv, tt=tt, kc=kc: h.matmul(bk.ap[:, 0:256], lhsT=self.h.ap[:, kc, tt * 128:(tt + 1) * 128], rhs=sv[:, kc, :],
                                                                          start=(kc == 0), stop=(kc == 7)) for kc in range(8)],
                          reads=[slot, self.h], writes=[bk])
                    if which == 2:
                        self.cp("act", v_aug.ap[:, tt, half * 4:half * 4 + 4, 0:64], bk.ap[:, 0:256].rearrange("p (a b) -> p a b", a=4), [bk], [v_aug])
                    if g == 0:
                        s = self.stg_i % 2
                        self.stg_i += 1
                        stg = self.stg[s]
                        self.cp("dve", stg.ap[:, 0:256], bk.ap[:, 0:256], [bk], [stg])
                        dst = O["nk" if which == 1 else "nv"][tt // 2, l, (tt % 2) * 128:(tt % 2) * 128 + 128, half * 256:half * 256 + 256]
                        d = fw.dma("sp", lambda h, stg=stg, dst=dst: h.dma_start(out=dst, in_=stg.ap[:, 0:256]), self.stg_sem[s], reads=[stg])
                        fw.final.append(d)
                    self.chk(f"kv1{l}{g}")
            if which == 1:
                self.chk(f"kvK{l}{g}")

        self.chk(f"attkv{l}{g}")
        pT = [ar.alloc([128, 512], BF16) for _ in range(2)]
        pi = [0]
        atok = ar.alloc([128, 8, 128], BF16)
        rec = ar.alloc([128, 8], F32)

        def transposes(hp, tts):
            bk = self.bank()
            bkb = bk.ap.bitcast(BF16)
            fw.mm([lambda h, bkb=bkb, i=i, tt=tt: h.transpose(out=bkb[:, i * 128:(i + 1) * 128], in_=atok.ap[:, tt, :], identity=self.ident_bf.ap)
                   for i, tt in enumerate(tts)], reads=[atok, self.ident_bf], writes=[bk])
            n = len(tts)
            self.cp("dve", attnT.ap[:, hp, tts[0] * 128:(tts[0] + n) * 128], bkb[:, 0:n * 128], [bk], [attnT])

        if g == 0:
            UP = [(sq, hp, e) for sq in range(4) for hp in range(4) for e in range(2)]

            def issue_score_p(k):
                sq, hp, e = UP[k]
                pb = 64 * e
                sb = self.bank()
                fw.mm([lambda h, sb=sb, kt=kt, pb=pb, sq=sq, hp=hp: h.matmul(sb.ap[:, kt * 256:(kt + 1) * 256],
                                                                            lhsT=k_sb.ap[pb:pb + 64, hp, sq * 256 + kt * 128:sq * 256 + kt * 128 + 128],
                                                                            rhs=q_sb.ap[pb:pb + 64, hp, sq * 256:sq * 256 + 256], start=True, stop=True) for kt in range(2)],
                      reads=[k_sb, q_sb], writes=[sb])
                return sb

            nxt = issue_score_p(0)
            ob = ov = None
            for k, (sq, hp, e) in enumerate(UP):
                hh = 2 * hp + e
                if e == 0:
                    ob = self.bank(hold=True)
                    ov = ob.ap[:, 0:260].rearrange("p (q e c) -> p q e c", q=2, e=2)
                sb = nxt
                p = pT[k % 2]
                self.act(p.ap, sb.ap, AF.Exp, [sb], [p])
                if k + 1 < len(UP):
                    nxt = issue_score_p(k + 1)
                fns = []
                for qt in range(2):
                    for kt in range(2):
                        fns.append(lambda h, qt=qt, kt=kt, e=e, hh=hh, p=p, ov=ov, sq=sq, st_=(e == 0 and qt == 0 and kt == 0): h.matmul(
                            ov[:, qt, e, :], lhsT=p.ap[:, kt * 256 + qt * 128:kt * 256 + qt * 128 + 128], rhs=v_aug.ap[:, 2 * sq + kt, hh, 0:65],
                            start=st_, stop=(e == 1 and qt == 1 and kt == 1)))
                fw.mm(fns, reads=[p, v_aug], writes=[ob])
                if e == 1:
                    fw.op("dve", lambda h, ov=ov: h.reciprocal(out=rec.ap[:, 0:4].rearrange("p (q e) -> p q e", q=2), in_=ov[:, :, :, 64]), reads=[ob], writes=[rec])
                    self.tt("dve", atok.ap[:, 2 * sq:2 * sq + 2, :].rearrange("p q (e c) -> p q e c", e=2), ov[:, :, :, 0:64],
                            rec.ap[:, 0:4].rearrange("p (q e) -> p q e", q=2).unsqueeze(3).to_broadcast([128, 2, 2, 64]), ALU.mult, [ob, rec], [atok])
                    self.unhold(ob)
                    transposes(hp, [2 * sq, 2 * sq + 1])
        else:
            self.na_attention(l, attnT, q_sb, k_sb, v_aug, pT, atok, rec, transposes)

    def na_attention(self, l, attnT, q_sb, k_sb, v_aug, pT, atok, rec, transposes):
        fw, ar, I = self.fw, self.ar, self.I
        kctxT = ar.alloc([128, 4, 512], BF16)
        vctx = ar.alloc([128, 4, 8, 66], BF16)
        fw.op("pool", lambda h: h.memset(vctx.ap[:, :, :, 64:65], 1.0), writes=[vctx])
        cvsem = Buf("cvsem")
        for tt in range(4):
            fw.dma("pool", lambda h, tt=tt: h.dma_start(out=vctx.ap[:, tt, :, 0:64], in_=I["cv"][l][tt * 128:(tt + 1) * 128, :].rearrange("p (a b) -> p a b", a=8)),
                   cvsem, writes=[vctx])
        mk = ar.mark()
        cktok = ar.alloc([128, 4, 512], BF16)
        cksem = Buf("cksem")
        fw.dma("pool", lambda h: h.dma_start(out=cktok.ap, in_=I["ck"][l].rearrange("(t p) f -> p t f", p=128)), cksem, writes=[cktok])
        for hp in range(4):
            bk = self.bank()
            bkb = bk.ap.bitcast(BF16)
            fw.mm([lambda h, bkb=bkb, tt=tt, hp=hp: h.transpose(out=bkb[:, tt * 128:(tt + 1) * 128], in_=cktok.ap[:, tt, hp * 128:(hp + 1) * 128],
                                                               identity=self.ident_bf.ap) for tt in range(4)], reads=[cktok, self.ident_bf], writes=[bk])
            self.cp("dve", kctxT.ap[:, hp, :], bkb[:, 0:512], [bk], [kctxT])
        ar.release(mk)
        LT = ar.alloc([128, 8, 18, 64], BF16)
        mk = ar.mark()
        rp = ar.alloc([128, 2, 32], F32)
        fw.op("pool", lambda h: h.memset(rp.ap, 0.0), writes=[rp])
        for e in range(2):
            self.sload(rp.ap[0:120, e, 0:31], I["rpb"][l].rearrange("h a x -> (h a) x"), rp)
        Rs = ar.alloc([64, 8, 15], F32)
        RE = ar.alloc([64, 8, 18], F32)
        BB = ar.alloc([64, 2, 127], F32)
        msk = ar.alloc([128, 64], F32)
        m2 = ar.alloc([128, 64], F32)
        bk = self.bank()
        fw.mm([lambda h, bk=bk: h.matmul(bk.ap[0:64, 0:120], lhsT=rp.ap[0:120].rearrange("p a b -> p (a b)"),
                                         rhs=self.ident_f.ap[0:120, 0:120], start=True, stop=True)], reads=[rp, self.ident_f], writes=[bk])
        fw.op("pool", lambda h: h.memset(Rs.ap, 0.0), writes=[Rs])
        for e in range(2):
            self.cp("dve", Rs.ap[32 * e:32 * e + 31].rearrange("p a b -> p (a b)"), bk.ap[32 * e:32 * e + 31, 0:120], [bk], [Rs])
        fw.op("pool", lambda h: h.memset(RE.ap, 0.0), writes=[RE])
        for e in range(2):
            self.cp("dve", RE.ap[32 * e:32 * e + 31, :, e + 1:e + 16], Rs.ap[32 * e:32 * e + 31, :, ::-1], [Rs], [RE])
        fw.op("pool", lambda h: h.memset(BB.ap, 0.0), writes=[BB])
        for e in range(2):
            fw.op("pool", lambda h, e=e: h.memset(BB.ap[32 * e:32 * e + 32, e, :], 1.0), writes=[BB])
            fw.op("pool", lambda h, e=e: h.affine_select(out=BB.ap[32 * e:32 * e + 32, e, :], in_=BB.ap[32 * e:32 * e + 32, e, :], pattern=[[1, 127]],
                                                          compare_op=ALU.is_equal, fill=0.0, base=-48, channel_multiplier=-1), reads=[BB], writes=[BB])
        fw.op("pool", lambda h: h.memset(msk.ap, 0.0), writes=[msk])
        fw.op("pool", lambda h: h.memset(m2.ap, 0.0), writes=[m2])
        for hf in range(2):
            sl = slice(64 * hf, 64 * hf + 64)
            fw.op("pool", lambda h, sl=sl: h.affine_select(out=msk.ap[sl], in_=msk.ap[sl], pattern=[[-1, 64]], compare_op=ALU.is_ge, fill=NEG,
                                                            base=8, channel_multiplier=1), reads=[msk], writes=[msk])
            fw.op("pool", lambda h, sl=sl: h.affine_select(out=msk.ap[sl], in_=msk.ap[sl], pattern=[[0, 64]], compare_op=ALU.is_ge, fill=0.0,
                                                            base=47, channel_multiplier=-1), reads=[msk], writes=[msk])
            fw.op("pool", lambda h, sl=sl: h.affine_select(out=m2.ap[sl], in_=m2.ap[sl], pattern=[[1, 64]], compare_op=ALU.is_ge, fill=NEG,
                                                            base=7, channel_multiplier=-1), reads=[m2], writes=[m2])
            fw.op("pool", lambda h, sl=sl: h.affine_select(out=m2.ap[sl], in_=m2.ap[sl], pattern=[[0, 64]], compare_op=ALU.is_ge, fill=0.0,
                                                            base=-16, channel_multiplier=1), reads=[m2], writes=[m2])
        self.tt("pool", msk.ap, msk.ap, m2.ap, ALU.add, [msk, m2], [msk])
        REf = RE.ap.rearrange("p a b -> p (a b)")
        for q0 in range(0, 64, 3):
            nq = min(3, 64 - q0)
            bk = self.bank()
            for i in range(nq):
                qc = q0 + i
                fw.mm([lambda h, bk=bk, i=i, qc=qc, e=e: h.matmul(bk.ap[64 * e:64 * e + 64, i * 144:(i + 1) * 144], lhsT=BB.ap[:, e, 63 - qc:63 - qc + 64], rhs=REf,
                                                                  start=True, stop=True) for e in range(2)], reads=[BB, RE], writes=[bk])
            outv = LT.ap[:, :, :, q0:q0 + nq].rearrange("p h d q -> p q (h d)")
            self.tt("dve", outv, bk.ap[:, 0:nq * 144].rearrange("p (q n) -> p q n", q=nq),
                    msk.ap[:, q0:q0 + nq].unsqueeze(2).to_broadcast([128, nq, 144]), ALU.add, [bk, msk], [LT])
        ar.release(mk)

        U = []
        for hp in range(4):
            for e in range(2):
                for c in range(2):
                    grp = []
                    for mt in range(8):
                        js = [j for j in range(4 * c, 4 * c + 4) if any(na_valid(2 * mt + ee, 2 * j + r) for ee in range(2) for r in range(2))]
                        if js:
                            grp.append(("loc", mt, js[0], js[-1]))
                    for kt in range(4):
                        grp.append(("ctx", kt, 4 * c, 4 * c + 3))
                    for ui, (kind, mt, ja, jb) in enumerate(grp):
                        U.append((hp, e, c, kind, mt, ja, jb, ui == 0, ui == len(grp) - 1))

        def issue_score(k):
            hp, e, c, kind, mt, ja, jb, gfirst, glast = U[k]
            pb = 64 * e
            hh = 2 * hp + e
            nq = 128 * (jb - ja + 1)
            sb = self.bank()
            if kind == "loc":
                d0 = 2 * ja - 2 * mt + 8
                d1 = 2 * jb + 1 - 2 * mt + 8
                assert 0 <= d0 and d1 < 18, (mt, ja, jb)
                ltv = LT.ap[:, hh, d0:d1 + 1, :].rearrange("p a b -> p (a b)")
                fw.mm([lambda h, sb=sb, mt=mt, ja=ja, nq=nq, pb=pb, hp=hp: h.matmul(sb.ap[:, 0:nq], lhsT=k_sb.ap[pb:pb + 64, hp, mt * 128:(mt + 1) * 128],
                                                                                   rhs=q_sb.ap[pb:pb + 64, hp, ja * 128:ja * 128 + nq], start=True, stop=False),
                       lambda h, sb=sb, ltv=ltv, nq=nq: h.matmul(sb.ap[:, 0:nq], lhsT=self.ident_bf.ap, rhs=ltv, start=False, stop=True)],
                      reads=[k_sb, q_sb, LT, self.ident_bf], writes=[sb])
            else:
                fw.mm([lambda h, sb=sb, mt=mt, ja=ja, nq=nq, pb=pb, hp=hp: h.matmul(sb.ap[:, 0:nq], lhsT=kctxT.ap[pb:pb + 64, hp, mt * 128:(mt + 1) * 128],
                                                                                   rhs=q_sb.ap[pb:pb + 64, hp, ja * 128:ja * 128 + nq], start=True, stop=True)],
                      reads=[kctxT, q_sb], writes=[sb])
            return sb

        nxt = issue_score(0)
        ob = ov = None
        first = True
        for k, (hp, e, c, kind, mt, ja, jb, gfirst, glast) in enumerate(U):
            hh = 2 * hp + e
            nq = 128 * (jb - ja + 1)
            if gfirst:
                ob = self.bank(hold=True)
                ov = ob.ap[:, 0:260].rearrange("p (q c) -> p q c", q=4)
                first = True
            sb = nxt
            p = pT[k % 2]
            self.act(p.ap[:, 0:nq], sb.ap[:, 0:nq], AF.Exp, [sb], [p])
            if k + 1 < len(U):
                nxt = issue_score(k + 1)
            fns = []
            for j in range(ja, jb + 1):
                if kind == "loc":
                    val = [[na_valid(2 * mt + ee, 2 * j + r) for r in range(2)] for ee in range(2)]
                    if not any(val[0]) and not any(val[1]):
                        continue
                    for ee in range(2):
                        for r in range(2):
                            if not val[ee][r]:
                                c0 = (j - ja) * 128 + r * 64
                                fw.op("dve", lambda h, p=p, ee=ee, c0=c0: h.memset(p.ap[64 * ee:64 * ee + 64, c0:c0 + 64], 0.0), reads=[p], writes=[p])
                    rhs = v_aug.ap[:, mt, hh, 0:65]
                else:
                    rhs = vctx.ap[:, mt, hh, 0:65]
                fns.append(lambda h, j=j, ja=ja, p=p, rhs=rhs, st_=first, ov=ov, c=c: h.matmul(ov[:, j - 4 * c, :], lhsT=p.ap[:, (j - ja) * 128:(j - ja + 1) * 128],
                                                                                            rhs=rhs, start=st_, stop=False))
                first = False
            fw.mm(fns, reads=[p, v_aug, vctx], writes=[ob])
            if glast:
                fw.op("dve", lambda h, ov=ov: h.reciprocal(out=rec.ap[:, 0:4], in_=ov[:, :, 64]), reads=[ob], writes=[rec])
                self.tt("dve", atok.ap[:, 4 * c:4 * c + 4, 64 * e:64 * e + 64], ov[:, :, 0:64],
                        rec.ap[:, 0:4].unsqueeze(2).to_broadcast([128, 4, 64]), ALU.mult, [ob, rec], [atok])
                self.unhold(ob)
                if e == 1 and c == 1:
                    transposes(hp, [0, 1, 2, 3])
                    transposes(hp, [4, 5, 6, 7])

    def s5(self, l, g, s5y):
        fw, ar, I, O = self.fw, self.ar, self.I, self.O
        c = self.s5cols
        u_sb = ar.alloc([128, 2, TG], BF16)

        def u_cons(ft, tb, bk):
            self.cp("act", u_sb.ap[:, ft, tb * 512:(tb + 1) * 512], bk.ap, [bk], [u_sb])

        self.proj_fm(self.win_cols(l, 1536), 2, self.h_rhs, [self.h], 8, u_cons)
        Ec = ar.alloc([128, 16, 256], F32)
        Es = ar.alloc([128, 16, 256], F32)
        mk = ar.mark()
        io_i = ar.alloc([128, 256], I32)
        io_f = ar.alloc([128, 256], F32)
        fw.op("pool", lambda h: h.iota(io_i.ap, pattern=[[1, 256]], base=0, channel_multiplier=0), writes=[io_i])
        self.cp("dve", io_f.ap, io_i.ap, [io_i], [io_f])
        yv = ar.alloc([128, 2, 256], F32)
        for q in range(8):
            self.tt("dve", yv.ap, c.ap[:, 2 * q:2 * q + 2, 0].unsqueeze(2).to_broadcast([128, 2, 256]),
                    io_f.ap.unsqueeze(1).to_broadcast([128, 2, 256]), ALU.mult, [c, io_f], [yv])
            yflat = Reg(yv.ap.rearrange("p a b -> p (a b)"), yv.bufs)
            self.sincos(yflat, 512, Ec.ap[:, 2 * q:2 * q + 2, :].rearrange("p a b -> p (a b)"),
                        Es.ap[:, 2 * q:2 * q + 2, :].rearrange("p a b -> p (a b)"), Ec, Es)
        ar.release(mk)
        Kc = ar.alloc([128, 16, 4], F32)
        if g == 1:
            e255c = Ec.ap[:, :, 255]
            e255s = Es.ap[:, :, 255]
            self.tt("dve", Kc.ap[:, :, 0], c.ap[:, :, 2], e255c, ALU.mult, [c, Ec], [Kc])
            self.tt("dve", Kc.ap[:, :, 3], c.ap[:, :, 3], e255s, ALU.mult, [c, Es], [Kc])
            self.tt("dve", Kc.ap[:, :, 0], Kc.ap[:, :, 0], Kc.ap[:, :, 3], ALU.subtract, [Kc], [Kc])
            self.tt("dve", Kc.ap[:, :, 1], c.ap[:, :, 2], e255s, ALU.mult, [c, Es], [Kc])
            self.tt("dve", Kc.ap[:, :, 3], c.ap[:, :, 3], e255c, ALU.mult, [c, Ec], [Kc])
            self.tt("dve", Kc.ap[:, :, 1], Kc.ap[:, :, 1], Kc.ap[:, :, 3], ALU.add, [Kc], [Kc])
            self.ts("dve", Kc.ap[:, :, 2], Kc.ap[:, :, 1], -1.0, None, ALU.mult, None, [Kc], [Kc])
        ygb = ar.alloc([128, 2, TG], BF16)
        bpr = ar.alloc([128, 512], F32)
        bpi = ar.alloc([128, 512], F32)
        grs = [ar.alloc([128, 512], F32) for _ in range(2)]
        gis = [ar.alloc([128, 512], F32) for _ in range(2)]
        t1 = ar.alloc([128, 512], F32)
        t2 = ar.alloc([128, 512], F32)
        p1 = ar.alloc([128, 512], F32)
        p2 = ar.alloc([128, 512], F32)
        unit = [0]
        wr = ar.alloc([128, 512], BF16)
        wi = ar.alloc([128, 512], BF16)
        sm = ar.alloc([128, 16], F32)

        def v2(ap):
            return ap.rearrange("p (s t) -> p s t", s=2)

        def seg(ap, s2, rev):
            v = ap[:, s2 * 256:(s2 + 1) * 256]
            return v[:, ::-1] if rev else v

        units = []
        for ut in range(2):
            for d in range(2):
                for jj in range(4):
                    for ti, tb in enumerate([1, 0] if d == 1 else [0, 1]):
                        units.append((ut, d, jj, ti, tb))

        def issue_bu(k):
            ut, d, jj, ti, tb = units[k]
            dj = d * 8 + ut * 4 + jj
            tsl = slice(tb * 512, (tb + 1) * 512)
            br = self.bank()
            bi = self.bank()
            fw.mm([lambda h, br=br, dj=dj, tsl=tsl, ut=ut: h.matmul(br.ap, lhsT=self.s5w.ap[:, dj, 0, :], rhs=u_sb.ap[:, ut, tsl], start=True, stop=True)],
                  reads=[self.s5w, u_sb], writes=[br])
            fw.mm([lambda h, bi=bi, dj=dj, tsl=tsl, ut=ut: h.matmul(bi.ap, lhsT=self.s5w.ap[:, dj, 1, :], rhs=u_sb.ap[:, ut, tsl], start=True, stop=True)],
                  reads=[self.s5w, u_sb], writes=[bi])
            return br, bi

        ybanks = None
        yfirst = None
        if l == 0:
            self.bg = self.mod_gen(0, look=2, s0=8, s1=24, doA=(False, True)) if g == 0 else self.mod_gen(1, look=2)
        tbk = self.bank(hold=True)
        tbk2 = self.bank(hold=True)
        deferred = []
        prev_last = None
        nxt = issue_bu(0)
        for k, (ut, d, jj, ti, tb) in enumerate(units):
            rev = (d == 1)
            j = ut * 4 + jj
            dj = d * 8 + j
            if d == 0 and jj == 0 and ti == 0:
                ybanks = [self.bank(hold=True), self.bank(hold=True)]
                yfirst = [True, True]
            Ecv = Ec.ap[:, dj, :]
            Esv = Es.ap[:, dj, :]
            if rev:
                Ecv = Ecv[:, ::-1]
                Esv = Esv[:, ::-1]
            Ec2 = Ecv.unsqueeze(1).to_broadcast([128, 2, 256])
            Es2 = Esv.unsqueeze(1).to_broadcast([128, 2, 256])
            rb = c.ap[:, dj, 1:2].to_broadcast([128, 256])
            gr, gi = grs[k % 2], gis[k % 2]
            br, bi = nxt
            self.tt("dve", v2(t2.ap), v2(bi.ap), Es2, ALU.mult, [bi, Es], [t2])
            self.tt("dve", v2(tbk.ap), v2(br.ap), Ec2, ALU.mult, [br, Ec], [tbk])
            self.tt("dve", v2(tbk2.ap), v2(bi.ap), Ec2, ALU.mult, [bi, Ec], [tbk2])
            self.tt("dve", bpr.ap, tbk.ap, t2.ap, ALU.add, [tbk, t2], [bpr])
            self.tt("dve", v2(t2.ap), v2(br.ap), Es2, ALU.mult, [br, Es], [t2])
            self.tt("dve", bpi.ap, tbk2.ap, t2.ap, ALU.subtract, [tbk2, t2], [bpi])
            if k + 1 < len(units):
                nxt = issue_bu(k + 1)
            for fn in deferred:
                fn()
            deferred = []
            segs = [1, 0] if rev else [0, 1]
            if g == 0:
                for (src, dst) in ((bpr, gr), (bpi, gi)):
                    for s2 in segs:
                        fw.op("dve", lambda h, src=src, dst=dst, s2=s2, rev=rev, rb=rb: h.tensor_tensor_scan(
                            out=seg(dst.ap, s2, rev), data0=rb, data1=seg(src.ap, s2, rev), initial=0.0, op0=ALU.mult, op1=ALU.add),
                            reads=[src, c], writes=[dst])
            for sj, s2 in enumerate(segs):
                s = tb * 2 + s2
                si = ti * 2 + sj
                if g == 1:
                    f_r = seg(bpr.ap, s2, rev)[:, 0:1]
                    f_i = seg(bpi.ap, s2, rev)[:, 0:1]
                    if si == 0:
                        hpr, hpi = c.ap[:, dj, 6:7], c.ap[:, dj, 7:8]
                        self.stt(f_r, hpr, c.ap[:, dj, 2:3], f_r, ALU.mult, ALU.add, [c, bpr], [bpr])
                        self.stt(f_i, hpi, c.ap[:, dj, 2:3], f_i, ALU.mult, ALU.add, [c, bpi], [bpi])
                        self.ts("dve", sm.ap[:, 0:1], hpi, c.ap[:, dj, 3:4], None, ALU.mult, None, [c], [sm])
                        self.stt(f_i, hpr, c.ap[:, dj, 3:4], f_i, ALU.mult, ALU.add, [c, bpi], [bpi])
                        self.tt("dve", f_r, f_r, sm.ap[:, 0:1], ALU.subtract, [bpr, sm], [bpr])
                    else:
                        pgr, pgi = prev_last
                        self.stt(f_r, pgr[0], Kc.ap[:, dj, 0:1], f_r, ALU.mult, ALU.add, [pgr[1], Kc, bpr], [bpr])
                        self.stt(f_i, pgi[0], Kc.ap[:, dj, 0:1], f_i, ALU.mult, ALU.add, [pgi[1], Kc, bpi], [bpi])
                        self.stt(f_r, pgi[0], Kc.ap[:, dj, 2:3], f_r, ALU.mult, ALU.add, [pgi[1], Kc, bpr], [bpr])
                        self.stt(f_i, pgr[0], Kc.ap[:, dj, 1:2], f_i, ALU.mult, ALU.add, [pgr[1], Kc, bpi], [bpi])
                if g == 1:
                    for (src, dst) in ((bpr, gr), (bpi, gi)):
                        fw.op("dve", lambda h, src=src, dst=dst, s2=s2, rev=rev, rb=rb: h.tensor_tensor_scan(
                            out=seg(dst.ap, s2, rev), data0=rb, data1=seg(src.ap, s2, rev), initial=0.0, op0=ALU.mult, op1=ALU.add),
                            reads=[src, c], writes=[dst])
                g_r = seg(gr.ap, s2, rev)[:, 255:256]
                g_i = seg(gi.ap, s2, rev)[:, 255:256]
                prev_last = ((g_r, gr), (g_i, gi))
                if g == 0:
                    def state_ops(dj=dj, g_r=g_r, g_i=g_i, gr=gr, gi=gi, s=s, d=d, j=j):
                        e_c = Ec.ap[:, dj, 255:256]
                        e_s = Es.ap[:, dj, 255:256]
                        o_r, o_i = self.outst.ap[:, s, d, j, 0:1], self.outst.ap[:, s, d, j, 1:2]
                        self.tt("dve", sm.ap[:, 1:2], g_i, e_s, ALU.mult, [gi, Es], [sm])
                        self.tt("dve", sm.ap[:, 2:3], g_i, e_c, ALU.mult, [gi, Ec], [sm])
                        self.stt(o_r, g_r, e_c, sm.ap[:, 1:2], ALU.mult, ALU.subtract, [gr, Ec, sm], [self.outst])
                        self.stt(o_i, g_r, e_s, sm.ap[:, 2:3], ALU.mult, ALU.add, [gr, Es, sm], [self.outst])
                    deferred.append(state_ops)
            self.tt("pool", v2(p1.ap), v2(gr.ap), Ec2, ALU.mult, [gr, Ec], [p1])
            self.tt("pool", v2(p2.ap), v2(gi.ap), Es2, ALU.mult, [gi, Es], [p2])
            self.tt("pool", wr.ap, p1.ap, p2.ap, ALU.subtract, [p1, p2], [wr])
            self.tt("pool", v2(p1.ap), v2(gi.ap), Ec2, ALU.mult, [gi, Ec], [p1])
            self.tt("pool", v2(p2.ap), v2(gr.ap), Es2, ALU.mult, [gr, Es], [p2])
            self.tt("pool", wi.ap, p1.ap, p2.ap, ALU.add, [p1, p2], [wi])
            self.tick()
            yb = ybanks[tb]
            last = (d == 1 and jj == 3)
            fw.mm([lambda h, yb=yb, dj=dj, st_=yfirst[tb]: h.matmul(yb.ap, lhsT=self.s5w.ap[:, dj, 2, :], rhs=wr.ap, start=st_, stop=False),
                   lambda h, yb=yb, dj=dj, last=last: h.matmul(yb.ap, lhsT=self.s5w.ap[:, dj, 3, :], rhs=wi.ap, start=False, stop=last)],
                  reads=[self.s5w, wr, wi], writes=[yb])
            yfirst[tb] = False
            if d == 1 and jj == 3 and ti == 1:
                for tb2 in range(2):
                    yb = ybanks[tb2]
                    sl = slice(tb2 * 512, (tb2 + 1) * 512)
                    self.stt(t1.ap, u_sb.ap[:, ut, sl], self.s5d.ap[:, ut:ut + 1], yb.ap, ALU.mult, ALU.add, [u_sb, self.s5d, yb], [t1])
                    self.act(ygb.ap[:, ut, sl], t1.ap, AF.Gelu_apprx_tanh, [t1], [ygb])
                    self.unhold(yb)
        for fn in deferred:
            fn()
        self.drain()
        self.unhold(tbk)
        self.unhold(tbk2)
        for ot in range(2):
            for tb in range(2):
                sl = slice(tb * 512, (tb + 1) * 512)
                bk = self.bank()
                fw.mm([lambda h, bk=bk, kc=kc, ot=ot, sl=sl: h.matmul(bk.ap, lhsT=self.wglu.ap[:, kc, ot * 128:(ot + 1) * 128], rhs=ygb.ap[:, kc, sl],
                                                                      start=(kc == 0), stop=(kc == 1)) for kc in range(2)], reads=[self.wglu, ygb], writes=[bk])
                self.act(t1.ap, bk.ap, AF.Sigmoid, [bk], [t1])
                self.tt("dve", s5y.ap[:, ot, sl], ygb.ap[:, ot, sl], t1.ap, ALU.mult, [ygb, t1], [s5y])

    def store_s5_states(self, l):
        fw, ar, O = self.fw, self.ar, self.O
        mk = ar.mark()
        tr = ar.alloc([128, 128], F32)
        bk = self.bank()
        fw.mm([lambda h: h.transpose(out=bk.ap[:, 0:128], in_=self.outst.ap.rearrange("p s d j r -> p (s d j r)"), identity=self.ident_f.ap)],
              reads=[self.outst, self.ident_f], writes=[bk])
        self.cp("dve", tr.ap, bk.ap[:, 0:128], [bk], [tr])
        sem = Buf("s5out")
        for ri, nm in enumerate(("ns5r", "ns5i")):
            for s in range(4):
                for d in range(2):
                    r0 = ((s * 2 + d) * 8) * 2 + ri
                    src = tr.ap[r0:r0 + 15:2, :]
                    dst = O[nm][s, l, d, :].rearrange("(j q) -> j q", q=128)
                    dd = fw.dma("sp", lambda h, src=src, dst=dst: h.dma_start(out=dst, in_=src), sem, reads=[tr])
                    fw.final.append(dd)
        ar.release(mk)

    def lru(self, l, g, lruy):
        fw, ar, I, O = self.fw, self.ar, self.I, self.O
        lc = self.lrucols
        xr = ar.alloc([128, 2, TG], F32)
        gg = ar.alloc([128, 2, TG], F32)
        xc = ar.alloc([128, 2, TG], F32)
        xcb = ar.alloc([128, 2, TG], BF16)

        def xr_cons(ft, tb, bk):
            self.cp("act", xr.ap[:, ft, tb * 512:(tb + 1) * 512], bk.ap, [bk], [xr])

        def xg_cons(ft, tb, bk):
            self.act(gg.ap[:, ft, tb * 512:(tb + 1) * 512], bk.ap, AF.Gelu_apprx_tanh, [bk], [gg])

        self.proj_fm(self.win_cols(l, 1792), 2, self.h_rhs, [self.h], 8, xr_cons)
        self.proj_fm(self.win_cols(l, 2048), 2, self.h_rhs, [self.h], 8, xg_cons)
        nseq, L = (4, 256) if g == 0 else (1, 1024)
        for t in range(2):
            xv = xr.ap[:, t, :].rearrange("p (s q) -> p s q", s=nseq)
            cv = xc.ap[:, t, :].rearrange("p (s q) -> p s q", s=nseq)
            self.ts("dve", cv, xv, lc.ap[:, t, 2:3], lc.ap[:, t, 4:5], ALU.mult, ALU.add, [xr, lc], [xc])
            for k in (0, 1, 3):
                sh = k - 2
                lo, hi = max(0, -sh), L - max(0, sh)
                self.stt(cv[:, :, lo:hi], xv[:, :, lo + sh:hi + sh], lc.ap[:, t, k:k + 1], cv[:, :, lo:hi], ALU.mult, ALU.add, [xr, lc, xc], [xc])
            self.cp("act", xcb.ap[:, t, :], xc.ap[:, t, :], [xc], [xcb])
        r_ = ar.alloc([128, 512], F32)
        i_ = ar.alloc([128, 512], F32)
        th = ar.alloc([128, 512], F32)
        e2 = ar.alloc([128, 512], F32)
        a_sb = ar.alloc([128, TG], F32)
        b_sb = ar.alloc([128, TG], F32)
        hs = [ar.alloc([128, TG], F32) for _ in range(2)]
        for t in range(2):
            for d in range(2):
                rev = (d == 1)
                for tb in range(2):
                    sl = slice(tb * 512, (tb + 1) * 512)
                    pr = self.bank()
                    pi = self.bank()
                    fw.mm([lambda h, pr=pr, d=d, t=t, sl=sl: h.matmul(pr.ap, lhsT=self.lruw.ap[:, 0 * 4 + d * 2 + t, :], rhs=xcb.ap[:, t, sl], start=True, stop=True)],
                          reads=[self.lruw, xcb], writes=[pr])
                    fw.mm([lambda h, pi=pi, d=d, t=t, sl=sl: h.matmul(pi.ap, lhsT=self.lruw.ap[:, 1 * 4 + d * 2 + t, :], rhs=xcb.ap[:, t, sl], start=True, stop=True)],
                          reads=[self.lruw, xcb], writes=[pi])
                    self.act(r_.ap, pr.ap, AF.Sigmoid, [pr, lc], [r_], bias=lc.ap[:, t, 5 + d:6 + d])
                    self.act(i_.ap, pi.ap, AF.Sigmoid, [pi, lc], [i_], bias=lc.ap[:, t, 7 + d:8 + d])
                    self.act(a_sb.ap[:, sl], r_.ap, AF.Exp, [r_, lc], [a_sb], scale=lc.ap[:, t, 9 + d:10 + d])
                    self.act(th.ap, r_.ap, AF.Tanh, [r_, lc], [th], scale=lc.ap[:, t, 11 + d:12 + d])
                    self.act(e2.ap, r_.ap, AF.Exp, [r_, lc], [e2], scale=lc.ap[:, t, 13 + d:14 + d])
                    self.stt(e2.ap, e2.ap, 1.0, th.ap, ALU.add, ALU.mult, [e2, th], [e2])
                    self.act(e2.ap, e2.ap, AF.Sqrt, [e2], [e2])
                    self.tt("dve", i_.ap, i_.ap, xc.ap[:, t, sl], ALU.mult, [i_, xc], [i_])
                    self.tt("dve", b_sb.ap[:, sl], e2.ap, i_.ap, ALU.mult, [e2, i_], [b_sb])
                hd = hs[d]
                for s in range(nseq):
                    sq = slice(s * L, (s + 1) * L)
                    av, bv, hv = a_sb.ap[:, sq], b_sb.ap[:, sq], hd.ap[:, sq]
                    if rev:
                        av, bv, hv = av[:, ::-1], bv[:, ::-1], hv[:, ::-1]
                    init = 0.0 if g == 0 else self.lrust0.ap[:, d, t:t + 1]
                    rd = [a_sb, b_sb] + ([] if g == 0 else [self.lrust0])
                    fw.op("dve", lambda h, av=av, bv=bv, hv=hv, init=init: h.tensor_tensor_scan(out=hv, data0=av, data1=bv, initial=init, op0=ALU.mult, op1=ALU.add),
                          reads=rd, writes=[hd])
                    if g == 0:
                        self.cp("dve", self.lrust.ap[:, s, d, t:t + 1], hv[:, L - 1:L], [hd], [self.lrust])
            self.tt("dve", hs[0].ap, hs[0].ap, hs[1].ap, ALU.add, [hs[0], hs[1]], [hs[0]])
            self.tt("dve", lruy.ap[:, t, :], hs[0].ap, gg.ap[:, t, :], ALU.mult, [hs[0], gg], [lruy])
        if g == 0:
            sem = Buf("lruout")
            for s in range(4):
                for d in range(2):
                    dst = O["nlru"][s, l, d, :].rearrange("(t p) -> p t", p=128)
                    src = self.lrust.ap[:, s, d, :]
                    dd = fw.dma("sp", lambda h, src=src, dst=dst: h.dma_start(out=dst, in_=src, allow_slow_non_contiguous=True), sem, reads=[self.lrust])
                    fw.final.append(dd)

    def merge(self, l, g, attnT, s5y, lruy):
        fw, ar, I = self.fw, self.ar, self.I
        merged = ar.alloc([128, 8, TG], BF16)
        sig = [ar.alloc([128, 512], F32) for _ in range(2)]
        pr = [ar.alloc([128, 512], F32) for _ in range(3)]
        si = [0]

        def br_rhs(kc, sl):
            if kc < 4:
                return attnT.ap[:, kc, sl]
            if kc < 6:
                return s5y.ap[:, kc - 4, sl]
            return lruy.ap[:, kc - 6, sl]

        for sp in range(4):
            c0 = sp * 256
            wb = self.wslice([(0, 4, 256, I["w_br_attn"][l][:, c0:c0 + 256]), (4 * 256, 2, 256, I["w_br_s5"][l][:, c0:c0 + 256]),
                              (6 * 256, 2, 256, I["w_br_lru"][l][:, c0:c0 + 256])])
            wbv = wb.ap.rearrange("p (k n) -> p k n", k=8)
            wg = [self.wslice([(0, 8, 256, I["w_in"][l][:, 2304 + b * 1024 + c0:2304 + b * 1024 + c0 + 256])]) for b in range(3)]
            for t in range(2):
                ft = sp * 2 + t
                for tb in range(2):
                    sl = slice(tb * 512, (tb + 1) * 512)
                    for b, (k0, k1) in enumerate(((0, 4), (4, 6), (6, 8))):
                        gb = self.bank()
                        wgv = wg[b].ap.rearrange("p (k n) -> p k n", k=8)
                        fw.mm([lambda h, gb=gb, wgv=wgv, kc=kc, t=t, tb=tb: h.matmul(gb.ap, lhsT=wgv[:, kc, t * 128:(t + 1) * 128], rhs=self.h_rhs(kc, tb),
                                                                                     start=(kc == 0), stop=(kc == 7)) for kc in range(8)],
                              reads=[wg[b], self.h], writes=[gb])
                        bb = self.bank()
                        fw.mm([lambda h, bb=bb, kc=kc, t=t, sl=sl, k0=k0, k1=k1: h.matmul(bb.ap, lhsT=wbv[:, kc, t * 128:(t + 1) * 128], rhs=br_rhs(kc, sl),
                                                                                         start=(kc == k0), stop=(kc == k1 - 1)) for kc in range(k0, k1)],
                              reads=[wb, attnT, s5y, lruy], writes=[bb])
                        sg = sig[si[0] % 2]
                        si[0] += 1
                        self.act(sg.ap, gb.ap, AF.Sigmoid, [gb], [sg])
                        self.tt("dve", pr[b].ap, bb.ap, sg.ap, ALU.mult, [bb, sg], [pr[b]])
                    self.tt("dve", pr[0].ap, pr[0].ap, pr[1].ap, ALU.add, [pr[0], pr[1]], [pr[0]])
                    self.tt("dve", merged.ap[:, ft, sl], pr[0].ap, pr[2].ap, ALU.add, [pr[0], pr[2]], [merged])
                    self.tick()

        def out_cons(ft, tb, bk):
            sl = slice(tb * 512, (tb + 1) * 512)
            xv = self.x.ap[:, g, ft, sl]
            self.stt(xv, bk.ap, self.mod.ap[:, l, 2 * 8 + ft, g:g + 1], xv, ALU.mult, ALU.add, [bk, self.mod, self.x], [self.x])
            self.tick()

        self.proj_fm(lambda c0, n: [(0, 8, n, I["w_out"][l][:, c0:c0 + n])], 8,
                     lambda kc, tb: merged.ap[:, kc, tb * 512:(tb + 1) * 512], [merged], 8, out_cons)

    def ffn(self, l, g):
        fw, ar, I = self.fw, self.ar, self.I
        gg = ar.alloc([128, 22, TG], BF16)
        a_sb = [ar.alloc([128, TG], F32) for _ in range(2)]
        c_sb = [ar.alloc([128, TG], F32) for _ in range(2)]
        gl = [ar.alloc([128, TG], BF16) for _ in range(2)]
        nseq, L = (4, 256) if g == 0 else (1, 1024)
        fc = self.ffcols
        it = 0
        for sp in range(11):
            wa = self.wslice([(0, 8, 256, I["ffn_w_up"][l][:, sp * 256:sp * 256 + 256])])
            wb = self.wslice([(0, 8, 256, I["ffn_w_up"][l][:, 2816 + sp * 256:2816 + sp * 256 + 256])])
            wav = wa.ap.rearrange("p (k n) -> p k n", k=8)
            wbv = wb.ap.rearrange("p (k n) -> p k n", k=8)
            for t in range(2):
                ft = sp * 2 + t
                a_, c_, g_ = a_sb[it % 2], c_sb[it % 2], gl[it % 2]
                it += 1
                for tb in range(2):
                    bk = self.bank()
                    fw.mm([lambda h, bk=bk, kc=kc, t=t, tb=tb: h.matmul(bk.ap, lhsT=wav[:, kc, t * 128:(t + 1) * 128], rhs=self.h_rhs(kc, tb),
                                                                       start=(kc == 0), stop=(kc == 7)) for kc in range(8)], reads=[wa, self.h], writes=[bk])
                    self.cp("act", a_.ap[:, tb * 512:(tb + 1) * 512], bk.ap, [bk], [a_])
                av = a_.ap.rearrange("p (s q) -> p s q", s=nseq)
                cv = c_.ap.rearrange("p (s q) -> p s q", s=nseq)
                self.act(cv, av, AF.Identity, [a_, fc], [c_], bias=fc.ap[:, ft, 3:4], scale=fc.ap[:, ft, 1:2])
                self.stt(cv[:, :, 1:L], av[:, :, 0:L - 1], fc.ap[:, ft, 0:1], cv[:, :, 1:L], ALU.mult, ALU.add, [a_, fc, c_], [c_])
                self.stt(cv[:, :, 0:L - 1], av[:, :, 1:L], fc.ap[:, ft, 2:3], cv[:, :, 0:L - 1], ALU.mult, ALU.add, [a_, fc, c_], [c_])
                self.act(g_.ap, c_.ap, AF.Gelu_apprx_tanh, [c_], [g_])
                for tb in range(2):
                    sl = slice(tb * 512, (tb + 1) * 512)
                    bk = self.bank()
                    fw.mm([lambda h, bk=bk, kc=kc, t=t, tb=tb: h.matmul(bk.ap, lhsT=wbv[:, kc, t * 128:(t + 1) * 128], rhs=self.h_rhs(kc, tb),
                                                                       start=(kc == 0), stop=(kc == 7)) for kc in range(8)], reads=[wb, self.h], writes=[bk])
                    self.tt("dve", gg.ap[:, ft, sl], bk.ap, g_.ap[:, sl], ALU.mult, [bk, g_], [gg])
                self.tick()
        for ft in range(8):
            w0 = self.wslice([(0, 11, 128, I["ffn_w_down"][l][0:1408, ft * 128:(ft + 1) * 128])])
            w1 = self.wslice([(0, 11, 128, I["ffn_w_down"][l][1408:2816, ft * 128:(ft + 1) * 128])])
            wv = [w0.ap[:, 0:1408].rearrange("p (k n) -> p k n", k=11), w1.ap[:, 0:1408].rearrange("p (k n) -> p k n", k=11)]
            for tb in range(2):
                sl = slice(tb * 512, (tb + 1) * 512)
                bk = self.bank()
                fw.mm([lambda h, bk=bk, kc=kc, sl=sl: h.matmul(bk.ap, lhsT=wv[kc // 11][:, kc % 11, :], rhs=gg.ap[:, kc, sl],
                                                               start=(kc == 0), stop=(kc == 21)) for kc in range(22)], reads=[w0, w1, gg], writes=[bk])
                xv = self.x.ap[:, g, ft, sl]
                self.stt(xv, bk.ap, self.mod.ap[:, l, 5 * 8 + ft, g:g + 1], xv, ALU.mult, ALU.add, [bk, self.mod, self.x], [self.x])
            self.tick()
        self.drain()

    def final_norm(self, g):
        fw, ar, O = self.fw, self.ar, self.O
        m = ar.mark()
        rstd = ar.alloc([128, TG], F32)
        self.rms_rstd(g, rstd)
        y = ar.alloc([128, 8, TG], F32)
        for kc in range(8):
            self.stt(y.ap[:, kc, :], self.x.ap[:, g, kc, :], self.gcols.ap[:, 4, kc:kc + 1], rstd.ap, ALU.mult, ALU.mult, [self.x, self.gcols, rstd], [y])
        dst_t = O["yp" if g == 0 else "ys"]
        for tt in range(8):
            for qd in range(4):
                s = self.stg_i % 2
                self.stg_i += 1
                stg = self.stg[s]
                bk = self.bank()
                fw.mm([lambda h, bk=bk, j=j, qd=qd, tt=tt: h.transpose(out=bk.ap[:, j * 128:(j + 1) * 128], in_=y.ap[:, qd * 2 + j, tt * 128:(tt + 1) * 128],
                                                                       identity=self.ident_f.ap) for j in range(2)], reads=[y, self.ident_f], writes=[bk])
                self.cp("act" if qd % 2 else "dve", stg.ap, bk.ap[:, 0:256], [bk], [stg])
                dst = dst_t[tt * 128:(tt + 1) * 128, qd * 256:(qd + 1) * 256]
                dd = fw.dma("sp", lambda h, stg=stg, dst=dst: h.dma_start(out=dst, in_=stg.ap), self.stg_sem[s], reads=[stg])
                fw.final.append(dd)
        ar.release(m)


_W_KEYS = ["w_ada", "b_ada", "g_norm1", "g_norm2", "w_in", "rpb", "s5_lam_re", "s5_lam_im", "s5_log_step", "s5_b_re", "s5_b_im",
           "s5_c_re", "s5_c_im", "s5_d", "s5_w_glu", "lru_conv_w", "lru_conv_b", "lru_w_a", "lru_b_a", "lru_w_x", "lru_b_x", "lru_lam",
           "w_br_attn", "w_br_s5", "w_br_lru", "w_out", "ffn_w_up", "ffn_conv_w", "ffn_conv_b", "ffn_w_down", "g_final"]


def make_in_maps(inp):
    f = lambda a: np.ascontiguousarray(np.asarray(a, dtype=np.float32))
    shared = {k: f(inp[k]) for k in _W_KEYS}
    maps = []
    for i in range(NCORES):
        m = dict(shared)
        m["xp"] = f(inp["x_prompt"][4 * i:4 * i + 4]).reshape(TG, D)
        m["xs"] = f(inp["x_sample"][i]).reshape(TG, D)
        m["ck"] = f(inp["cache_k"][i]).reshape(DEPTH, 512, 512)
        m["cv"] = f(inp["cache_v"][i]).reshape(DEPTH, 512, 512)
        m["s5r"] = f(inp["state_s5_re"][i]).reshape(DEPTH, 2, 1024)
        m["s5i"] = f(inp["state_s5_im"][i]).reshape(DEPTH, 2, 1024)
        m["slru"] = f(inp["state_lru"][i]).reshape(DEPTH, 2, 256)
        m["cvec"] = f(np.stack([np.asarray(inp["c_ctx"]), np.asarray(inp["c"])[i]], axis=0))
        maps.append(m)
    return maps


def assemble(results):
    cat = lambda k: np.concatenate([np.asarray(r[k]) for r in results], axis=0)
    y_prompt = cat("yp").reshape(32, 256, D)
    y_sample = cat("ys").reshape(8, 1024, D)
    new_k = cat("nk").reshape(32, DEPTH, 256, 8, 64)
    new_v = cat("nv").reshape(32, DEPTH, 256, 8, 64)
    ns5r = cat("ns5r").reshape(32, DEPTH, 2, 16, 64)
    ns5i = cat("ns5i").reshape(32, DEPTH, 2, 16, 64)
    nlru = cat("nlru").reshape(32, DEPTH, 2, 256)
    return tuple(np.ascontiguousarray(a, dtype=np.float32) for a in (y_prompt, y_sample, new_k, new_v, ns5r, ns5i, nlru))


def kernel(**inputs):
    nc = Builder().build()
    in_maps = make_in_maps(inputs)
    res = run_bass_kernel_spmd(nc, in_maps, core_ids=list(range(NCORES)))
    return assemble(res.results)


def debug_run(inputs, stage, ncores=1, trace=False):
    b = Builder(stage=stage)
    nc = b.build()
    in_maps = make_in_maps(inputs)[:ncores]
    res = run_bass_kernel_spmd(nc, in_maps, core_ids=list(range(ncores)), trace=trace)
    if trace:
        print("EXEC_NS", stage, res.exec_time_ns)
    return b, res.results
```

```python
import math
import types as _types
import numpy as np
from contextlib import ExitStack
import concourse.bass as bass
import concourse.mybir as mybir
from concourse.bass_utils import run_bass_kernel_spmd

F32 = mybir.dt.float32
BF16 = mybir.dt.bfloat16
I32 = mybir.dt.int32
AF = mybir.ActivationFunctionType
ALU = mybir.AluOpType

NCORES = 8
D = 1024
TG = 1024
DEPTH = 2
NEG = -30000.0
EPS = 1e-6
IN_W = 5376
PAGE = 512


class Buf:
    __slots__ = ("name", "lw", "rd", "dsem", "dcnt", "excl")

    def __init__(self, name, excl=False):
        self.name = name
        self.excl = excl
        self.lw = None
        self.rd = []
        self.dsem = None
        self.dcnt = 0


class Reg:
    __slots__ = ("ap", "bufs", "tag")

    def __init__(self, ap, bufs, tag=None):
        self.ap = ap
        self.bufs = bufs
        self.tag = tag

    def __getitem__(self, k):
        return self.ap[k]


def _freeze(f):
    if getattr(f, "__closure__", None) is None:
        return f
    cells = []
    for c in f.__closure__:
        try:
            cells.append(_types.CellType(c.cell_contents))
        except ValueError:
            cells.append(c)
    return _types.FunctionType(f.__code__, f.__globals__, f.__name__, f.__defaults__, tuple(cells))


class Eng:
    def __init__(self, name):
        self.key = name
        self.cnt = 0
        self.seen = {}
        self.prog = []


class FW:
    def __init__(self, nc, stack):
        self.nc = nc
        self.stack = stack
        self.sems = {}
        self.E = {}
        for n in ("pe", "act", "dve", "pool", "sp"):
            self.sems[n] = stack.enter_context(nc.semaphore("s_" + n))
            self.E[n] = Eng(n)
        self.ndsem = 0
        self.dsem_free = []
        self.final = []

    def _expand(self, lst):
        out = []
        for r in lst:
            if isinstance(r, Buf):
                out.append(r)
            else:
                out.extend(r.bufs)
        return out

    def _need(self, eng, dep, same_ok):
        key, val, clock = dep
        if same_ok and key == eng.key:
            return
        if eng.seen.get(key, 0) >= val:
            return
        eng.prog.append(("wait", key, val))
        eng.seen[key] = val
        for k, v in clock.items():
            if eng.seen.get(k, 0) < v:
                eng.seen[k] = v

    def _deps(self, eng, reads, writes):
        for b in reads:
            if b.lw is not None:
                self._need(eng, b.lw, False)
            if b.excl:
                for r in b.rd:
                    self._need(eng, r, True)
        for b in writes:
            if b.lw is not None:
                self._need(eng, b.lw, True)
            for r in b.rd:
                self._need(eng, r, True)

    def _commit(self, dep, reads, writes):
        for b in reads:
            b.rd.append(dep)
        for b in writes:
            b.lw = dep
            b.rd = []

    def op(self, en, fn, reads=(), writes=()):
        eng = self.E[en]
        reads = self._expand(reads)
        writes = self._expand(writes)
        self._deps(eng, reads, writes)
        eng.cnt += 1
        eng.prog.append(("ins", _freeze(fn), eng.key, 1))
        clock = dict(eng.seen)
        clock[eng.key] = eng.cnt
        dep = (eng.key, eng.cnt, clock)
        self._commit(dep, reads, writes)
        return dep

    def mm(self, fns, reads=(), writes=()):
        eng = self.E["pe"]
        reads = self._expand(reads)
        writes = self._expand(writes)
        self._deps(eng, reads, writes)
        for f in fns[:-1]:
            eng.prog.append(("ins", _freeze(f), None, 0))
        eng.cnt += 1
        eng.prog.append(("ins", _freeze(fns[-1]), eng.key, 1))
        clock = dict(eng.seen)
        clock[eng.key] = eng.cnt
        dep = (eng.key, eng.cnt, clock)
        self._commit(dep, reads, writes)
        return dep

    def dma(self, qn, fn, dbuf, reads=(), writes=()):
        eng = self.E[qn]
        reads = self._expand(reads)
        writes = self._expand(writes)
        self._deps(eng, reads, writes)
        if dbuf.dsem is None:
            self.ndsem += 1
            dbuf.dsem = self.stack.enter_context(self.nc.semaphore(f"d{self.ndsem}"))
        if dbuf.dcnt:
            self._need(eng, ("D%d" % id(dbuf), dbuf.dcnt, {}), False)
        dbuf.dcnt += 16
        key = "D%d" % id(dbuf)
        self.sems[key] = dbuf.dsem
        eng.prog.append(("ins", _freeze(fn), key, 16))
        dep = (key, dbuf.dcnt, dict(eng.seen))
        self._commit(dep, reads, writes)
        return dep

    def emit(self):
        nc = self.nc
        sems = self.sems
        for d in self.final:
            self._need(self.E["sp"], d, False)

        def replay(eng, h):
            for it in eng.prog:
                if it[0] == "wait":
                    h.wait_ge(sems[it[1]], it[2])
                else:
                    ins = it[1](h)
                    if it[2] is not None:
                        ins.then_inc(sems[it[2]], it[3])

        with nc.Block() as block:
            @block.sync
            def _(h):
                replay(self.E["sp"], h)

            @block.scalar
            def _(h):
                replay(self.E["act"], h)

            @block.vector
            def _(h):
                replay(self.E["dve"], h)

            @block.gpsimd
            def _(h):
                replay(self.E["pool"], h)

            @block.tensor
            def _(h):
                replay(self.E["pe"], h)


class Arena:
    def __init__(self, fw, nbytes):
        self.fw = fw
        self.nbytes = nbytes
        self.t = fw.stack.enter_context(fw.nc.sbuf_tensor("arena", [128, nbytes // 2], BF16))
        self.pages = [Buf(f"pg{i}") for i in range((nbytes + PAGE - 1) // PAGE)]
        self.top = 0
        self.peak = 0

    def alloc(self, shape, dt, align=64):
        esz = 4 if dt in (F32, I32) else 2
        n = 1
        for s in shape[1:]:
            n *= s
        nb = n * esz
        if nb >= PAGE:
            align = max(align, PAGE)
        off = (self.top + align - 1) // align * align
        assert off + nb <= self.nbytes, f"arena overflow: need {off + nb} have {self.nbytes}"
        self.top = off + nb
        self.peak = max(self.peak, self.top)
        ap = self.t[0:shape[0], off // 2:(off + nb) // 2]
        if esz == 4:
            ap = ap.bitcast(dt)
        if len(shape) > 2:
            names = " ".join(f"d{i}" for i in range(1, len(shape)))
            kw = {f"d{i}": shape[i] for i in range(1, len(shape))}
            ap = ap.rearrange(f"p ({names}) -> p {names}", **kw)
        pages = self.pages[off // PAGE:(off + nb - 1) // PAGE + 1]
        return Reg(ap, pages)

    def mark(self):
        return self.top

    def release(self, m):
        self.top = m


def na_valid(kr, qr):
    w0 = min(max(qr - 4, 0), 8)
    return w0 <= kr < w0 + 8


class _Stop(Exception):
    pass


class Builder:
    def __init__(self, debug=False, stage=None):
        self.debug = debug
        self.stage = stage
        self.dbg_outs = []
        self.nc = bass.Bass("TRN2", target_bir_lowering=False)
        self.stack = ExitStack()
        self.dram = {}

    def din(self, name, shape):
        t = self.nc.dram_tensor(name, list(shape), F32, kind="ExternalInput")
        self.dram[name] = t
        return t.ap()

    def dout(self, name, shape):
        t = self.nc.dram_tensor(name, list(shape), F32, kind="ExternalOutput")
        self.dram[name] = t
        return t.ap()

    def build(self):
        nc = self.nc
        with self.stack as st:
            fw = self.fw = FW(nc, st)
            I = self.I = {}
            O = self.O = {}
            I["xp"] = self.din("xp", [TG, D])
            I["xs"] = self.din("xs", [TG, D])
            I["ck"] = self.din("ck", [DEPTH, 512, 512])
            I["cv"] = self.din("cv", [DEPTH, 512, 512])
            I["s5r"] = self.din("s5r", [DEPTH, 2, 1024])
            I["s5i"] = self.din("s5i", [DEPTH, 2, 1024])
            I["slru"] = self.din("slru", [DEPTH, 2, 256])
            I["cvec"] = self.din("cvec", [2, D])
            wshapes = {
                "w_ada": [2, D, 6 * D], "b_ada": [2, 6 * D], "g_norm1": [2, D], "g_norm2": [2, D],
                "w_in": [2, D, IN_W], "rpb": [2, 8, 15, 31],
                "s5_lam_re": [2, 2, 16, 64], "s5_lam_im": [2, 2, 16, 64], "s5_log_step": [2, 2, 16],
                "s5_b_re": [2, 16, 64, 16], "s5_b_im": [2, 16, 64, 16],
                "s5_c_re": [2, 2, 16, 16, 64], "s5_c_im": [2, 2, 16, 16, 64],
                "s5_d": [2, 256], "s5_w_glu": [2, 256, 256],
                "lru_conv_w": [2, 4, 256], "lru_conv_b": [2, 256],
                "lru_w_a": [2, 2, 4, 64, 64], "lru_b_a": [2, 2, 256],
                "lru_w_x": [2, 2, 4, 64, 64], "lru_b_x": [2, 2, 256], "lru_lam": [2, 2, 256],
                "w_br_attn": [2, 512, D], "w_br_s5": [2, 256, D], "w_br_lru": [2, 256, D],
                "w_out": [2, D, D], "ffn_w_up": [2, D, 5632], "ffn_conv_w": [2, 3, 2816],
                "ffn_conv_b": [2, 2816], "ffn_w_down": [2, 2816, D], "g_final": [D],
            }
            self.wshapes = wshapes
            for k, s in wshapes.items():
                I[k] = self.din(k, s)
            O["yp"] = self.dout("yp", [TG, D])
            O["ys"] = self.dout("ys", [TG, D])
            O["nk"] = self.dout("nk", [4, DEPTH, 256, 512])
            O["nv"] = self.dout("nv", [4, DEPTH, 256, 512])
            O["ns5r"] = self.dout("ns5r", [4, DEPTH, 2, 1024])
            O["ns5i"] = self.dout("ns5i", [4, DEPTH, 2, 1024])
            O["nlru"] = self.dout("nlru", [4, DEPTH, 2, 256])

            self.ar = Arena(fw, 207 * 1024)
            ps_t = st.enter_context(nc.psum_tensor("psum", [128, 8, 512], F32))
            self.banks = [Reg(ps_t[:, i, :], [Buf(f"bank{i}", excl=True)], i) for i in range(8)]
            self.bank_i = 0
            self.held = set()
            try:
                self.setup()
                self.chk("setup")
                for l in range(DEPTH):
                    if l == 0:
                        self.bg = self.mod_gen(0, look=4, s0=0, s1=8, doA=(True, False))
                        self.drain()
                    self.chk(f"prep{l}")
                    for g in range(2):
                        self.group_layer(l, g)
                        self.chk(f"gl{l}{g}")
                self.final_norm(1)
            except _Stop:
                pass
            fw.emit()
        return nc

    def chk(self, name):
        if self.stage == name:
            raise _Stop()

    def dbg(self, name, reg, ap, shape):
        t = self.nc.dram_tensor("dbg_" + name, list(shape), F32, kind="ExternalOutput").ap()
        d = self.fw.dma("sp", lambda h: h.dma_start(out=t, in_=ap), Buf("dbg_" + name), reads=[reg])
        self.fw.final.append(d)
        self.dbg_outs.append("dbg_" + name)

    def bank(self, hold=False):
        while True:
            i = self.bank_i % 8
            self.bank_i += 1
            if i not in self.held:
                break
        if hold:
            self.held.add(i)
        return self.banks[i]

    def unhold(self, bk):
        self.held.discard(bk.tag)

    def col(self, reg, i):
        return reg.ap[:, i:i + 1]

    def setup(self):
        fw, ar, nc, I = self.fw, self.ar, self.nc, self.I
        self.x = ar.alloc([128, 2, 8, TG], F32)
        self.ident_bf = ar.alloc([128, 128], BF16, align=PAGE)
        self.ident_f = ar.alloc([128, 128], F32)
        self.ones_bf = ar.alloc([128, 128], BF16)
        self.epsc = ar.alloc([128, 1], F32)
        self.gcols = ar.alloc([128, 5, 8], F32)
        self.cTb = ar.alloc([128, 8, 2], BF16)
        self.badaT = ar.alloc([128, 2, 48], F32)
        self.mod = ar.alloc([128, 2, 48, 2], F32, align=PAGE)
        self.A1 = ar.alloc([128, 2, 8, 2], F32)
        self.A2 = ar.alloc([128, 2, 8, 2], F32)
        self.s5cols = ar.alloc([128, 16, 8], F32, align=PAGE)
        self.lrucols = ar.alloc([128, 2, 16], F32)
        self.s5d = ar.alloc([128, 2], F32)
        self.ffcols = ar.alloc([128, 22, 4], F32)
        self.lrust0 = ar.alloc([128, 2, 2], F32)
        self.lruw = ar.alloc([128, 8, 128], BF16)
        self.wglu = ar.alloc([128, 2, 256], BF16)
        self.s5w = ar.alloc([128, 16, 4, 128], BF16, align=PAGE)
        self.outst = ar.alloc([128, 4, 2, 8, 2], F32, align=PAGE)
        self.lrust = ar.alloc([128, 4, 2, 2], F32)
        self.slots = [ar.alloc([128, 2048], BF16, align=PAGE) for _ in range(6)]
        self.slot_sem = [Buf(f"slotsem{i}") for i in range(6)]
        self.slot_i = 0
        self.stg = [ar.alloc([128, 256], F32, align=PAGE) for _ in range(2)]
        self.stg_sem = [Buf("stg0"), Buf("stg1")]
        self.stg_i = 0
        self.small_sem = Buf("small")
        self.h = ar.alloc([128, 8, TG], BF16, align=PAGE)
        self.bg = None
        fw.op("pool", lambda h: h.memset(self.epsc.ap, EPS), writes=[self.epsc])
        self.scr_mark = ar.mark()

        idb, idf, ones = self.ident_bf, self.ident_f, self.ones_bf
        fw.op("pool", lambda h: h.memset(idf.ap, 1.0), writes=[idf])
        fw.op("pool", lambda h: h.affine_select(out=idf.ap, in_=idf.ap, pattern=[[-1, 128]], compare_op=ALU.is_equal,
                                                fill=0.0, base=0, channel_multiplier=1), reads=[idf], writes=[idf])
        fw.op("dve", lambda h: h.tensor_copy(out=idb.ap, in_=idf.ap), reads=[idf], writes=[idb])
        fw.op("dve", lambda h: h.memset(ones.ap, 1.0), writes=[ones])

        self.scr_mark = ar.mark()
        self.prep_bg = False
        self.mod_init()
        self.bg = self.layer_prep_gen(0)
        for g, name in enumerate(("xp", "xs")):
            for tt in range(8):
                for qd in range(4):
                    s = self.stg_i % 2
                    self.stg_i += 1
                    stg = self.stg[s]
                    src = I[name][tt * 128:(tt + 1) * 128, qd * 256:(qd + 1) * 256]
                    fw.dma("sp", lambda h, stg=stg, src=src: h.dma_start(out=stg.ap, in_=src), self.stg_sem[s], writes=[stg])
                    bk = self.bank()
                    fw.mm([lambda h, bk=bk, stg=stg, j=j: h.transpose(out=bk.ap[:, j * 128:(j + 1) * 128], in_=stg.ap[:, j * 128:(j + 1) * 128],
                                                                      identity=idf.ap) for j in range(2)], reads=[stg, idf], writes=[bk])
                    dstv = self.x.ap[:, g, qd * 2:qd * 2 + 2, tt * 128:(tt + 1) * 128]
                    srcv = bk.ap[:, 0:256].rearrange("p (j t) -> p j t", j=2)
                    if qd % 2:
                        fw.op("act", lambda h, dstv=dstv, srcv=srcv: h.activation(out=dstv, in_=srcv, func=AF.Copy), reads=[bk], writes=[self.x])
                    else:
                        fw.op("dve", lambda h, dstv=dstv, srcv=srcv: h.tensor_copy(out=dstv, in_=srcv), reads=[bk], writes=[self.x])
                    if g == 0:
                        self.tick()
        self.drain()

    def small_load(self, dst_ap, src_ap, dst_reg):
        return self.sload(dst_ap, src_ap, dst_reg)

    def wslice(self, parts):
        s = self.slot_i % 6
        self.slot_i += 1
        slot = self.slots[s]
        for (off, kc, ncols, src) in parts:
            dst = slot.ap[:, off:off + kc * ncols].rearrange("p (k n) -> p k n", k=kc)
            sv = src.rearrange("(k p) n -> p k n", p=128)
            self.fw.dma("pool", lambda h, dst=dst, sv=sv: h.dma_start(out=dst, in_=sv), self.slot_sem[s], writes=[slot])
        return slot

    def mod_init(self):
        fw, ar, I = self.fw, self.ar, self.I
        m = ar.mark()
        cT = ar.alloc([128, 8, 2], F32)
        for cc in range(2):
            self.small_load(cT.ap[:, :, cc], I["cvec"][cc].rearrange("(k p) -> p k", p=128), cT)
            self.small_load(self.badaT.ap[:, cc, :], I["b_ada"][cc].rearrange("(t p) -> p t", p=128), self.badaT)
            self.small_load(self.gcols.ap[:, cc, :], I["g_norm1"][cc].rearrange("(k p) -> p k", p=128), self.gcols)
            self.small_load(self.gcols.ap[:, 2 + cc, :], I["g_norm2"][cc].rearrange("(k p) -> p k", p=128), self.gcols)
        self.small_load(self.gcols.ap[:, 4, :], I["g_final"].rearrange("(k p) -> p k", p=128), self.gcols)
        fw.op("act", lambda h: h.activation(out=self.cTb.ap, in_=cT.ap, func=AF.Silu), reads=[cT], writes=[self.cTb])
        ar.release(m)

    def mod_gen(self, l, look=2, s0=0, s1=24, doA=(True, True)):
        fw, I = self.fw, self.I
        cTb, badaT = self.cTb, self.badaT
        pend = []
        nxt = s0
        for s in range(s0, s1):
            while nxt < s1 and nxt <= s + look:
                pend.append(self.wslice([(0, 8, 256, I["w_ada"][l][:, nxt * 256:(nxt + 1) * 256])]))
                nxt += 1
            slot = pend.pop(0)
            sv = slot.ap.rearrange("p (k n) -> p k n", k=8)
            bk = self.bank()
            for t in range(2):
                fw.mm([lambda h, bk=bk, sv=sv, t=t, kc=kc: h.matmul(bk.ap[:, t * 2:t * 2 + 2], lhsT=sv[:, kc, t * 128:(t + 1) * 128],
                                                                    rhs=cTb.ap[:, kc, :], start=(kc == 0), stop=(kc == 7)) for kc in range(8)],
                      reads=[slot, cTb], writes=[bk])
            fw.op("dve", lambda h, bk=bk, l=l, s=s: h.tensor_tensor(out=self.mod.ap[:, l, 2 * s:2 * s + 2, :],
                                                                   in0=bk.ap[:, 0:4].rearrange("p (t c) -> p t c", t=2),
                                                                   in1=badaT.ap[:, l, 2 * s:2 * s + 2].unsqueeze(2).to_broadcast([128, 2, 2]),
                                                                   op=ALU.add), reads=[bk, badaT], writes=[self.mod])
            yield
        for ai, (A, comp, gi) in enumerate(((self.A1, 1, 0), (self.A2, 4, 2))):
            if not doA[ai]:
                continue
            fw.op("dve", lambda h, A=A, comp=comp, gi=gi, l=l: h.scalar_tensor_tensor(
                out=A.ap[:, l], in0=self.mod.ap[:, l, comp * 8:comp * 8 + 8, :], scalar=1.0,
                in1=self.gcols.ap[:, gi + l, :].unsqueeze(2).to_broadcast([128, 8, 2]), op0=ALU.add, op1=ALU.mult),
                reads=[self.mod, self.gcols], writes=[A])

    def tick(self):
        if self.bg is not None:
            try:
                next(self.bg)
            except StopIteration:
                self.bg = None

    def drain(self):
        while self.bg is not None:
            self.tick()

    def rms_rstd(self, g, rstd):
        fw, ar = self.fw, self.ar
        m = ar.mark()
        sq = [ar.alloc([128, 512], BF16) for _ in range(2)]
        tmp = ar.alloc([128, 512], F32)
        for tb in range(2):
            bk = self.bank()
            for kc in range(8):
                q = sq[kc % 2]
                fw.op("act", lambda h, q=q, kc=kc, tb=tb: h.activation(out=q.ap, in_=self.x.ap[:, g, kc, tb * 512:(tb + 1) * 512], func=AF.Square),
                      reads=[self.x], writes=[q])
                fw.mm([lambda h, bk=bk, q=q, kc=kc: h.matmul(bk.ap, lhsT=self.ones_bf.ap, rhs=q.ap, start=(kc == 0), stop=(kc == 7))],
                      reads=[q, self.ones_bf], writes=[bk])
            fw.op("act", lambda h, bk=bk: h.activation(out=tmp.ap, in_=bk.ap, func=AF.Sqrt, scale=1.0 / D, bias=self.epsc.ap[:, 0:1]),
                  reads=[bk, self.epsc], writes=[tmp])
            fw.op("dve", lambda h, tb=tb: h.reciprocal(out=rstd.ap[:, tb * 512:(tb + 1) * 512], in_=tmp.ap), reads=[tmp], writes=[rstd])
        ar.release(m)

    def norm_mod(self, l, g, A, shcomp):
        fw, ar = self.fw, self.ar
        m = ar.mark()
        rstd = ar.alloc([128, TG], F32)
        self.rms_rstd(g, rstd)
        tmps = [ar.alloc([128, TG], F32) for _ in range(2)]
        for kc in range(8):
            t = tmps[kc % 2]
            fw.op("dve", lambda h, t=t, kc=kc: h.scalar_tensor_tensor(out=t.ap, in0=self.x.ap[:, g, kc, :], scalar=A.ap[:, l, kc, g:g + 1],
                                                                      in1=rstd.ap, op0=ALU.mult, op1=ALU.mult),
                  reads=[self.x, A, rstd], writes=[t])
            fw.op("act", lambda h, t=t, kc=kc: h.activation(out=self.h.ap[:, kc, :], in_=t.ap, func=AF.Identity,
                                                            bias=self.mod.ap[:, l, shcomp * 8 + kc, g:g + 1], scale=1.0),
                  reads=[t, self.mod], writes=[self.h])
        ar.release(m)

    def proj_fm(self, src_cols_fn, ntiles, rhs, rhs_regs, kcs, consume):
        fw = self.fw
        for sp in range(0, ntiles, 2):
            nt = min(2, ntiles - sp)
            slot = self.wslice(src_cols_fn(sp * 128, nt * 128))
            sv = slot.ap[:, 0:kcs * nt * 128].rearrange("p (k n) -> p k n", k=kcs)
            for t in range(nt):
                for tb in range(2):
                    bk = self.bank()
                    fw.mm([lambda h, bk=bk, sv=sv, t=t, tb=tb, kc=kc: h.matmul(bk.ap, lhsT=sv[:, kc, t * 128:(t + 1) * 128], rhs=rhs(kc, tb),
                                                                                start=(kc == 0), stop=(kc == kcs - 1)) for kc in range(kcs)],
                          reads=[slot] + rhs_regs, writes=[bk])
                    consume(sp + t, tb, bk)

    def win_cols(self, l, base):
        return lambda c0, n: [(0, 8, n, self.I["w_in"][l][:, base + c0:base + c0 + n])]

    def h_rhs(self, kc, tb):
        return self.h.ap[:, kc, tb * 512:(tb + 1) * 512]

    def tt(self, en, out, in0, in1, op, R, W):
        self.fw.op(en, lambda h: h.tensor_tensor(out=out, in0=in0, in1=in1, op=op), reads=R, writes=W)

    def ts(self, en, out, in0, s1, s2, op0, op1, R, W):
        if s2 is None:
            self.fw.op(en, lambda h: h.tensor_scalar(out=out, in0=in0, scalar1=s1, scalar2=None, op0=op0), reads=R, writes=W)
        else:
            self.fw.op(en, lambda h: h.tensor_scalar(out=out, in0=in0, scalar1=s1, scalar2=s2, op0=op0, op1=op1), reads=R, writes=W)

    def stt(self, out, in0, sc, in1, op0, op1, R, W, en="dve"):
        self.fw.op(en, lambda h: h.scalar_tensor_tensor(out=out, in0=in0, scalar=sc, in1=in1, op0=op0, op1=op1), reads=R, writes=W)

    def act(self, out, in_, func, R, W, bias=None, scale=None):
        kw = {}
        if bias is not None:
            kw["bias"] = bias
        if scale is not None:
            kw["scale"] = scale
        self.fw.op("act", lambda h: h.activation(out=out, in_=in_, func=func, **kw), reads=R, writes=W)

    def cp(self, en, out, in_, R, W):
        if en == "act":
            self.fw.op("act", lambda h: h.activation(out=out, in_=in_, func=AF.Copy), reads=R, writes=W)
        else:
            self.fw.op(en, lambda h: h.tensor_copy(out=out, in_=in_), reads=R, writes=W)

    def sincos(self, y, n, cos_out, sin_out, Wc, Ws):
        ar = self.ar
        MAGIC = 12582912.0
        m = ar.mark()
        kf = ar.alloc([128, n], F32)
        fc = ar.alloc([128, n], F32)
        self.ts("dve", kf.ap, y.ap, 0.25, MAGIC, ALU.add, ALU.add, [y], [kf])
        self.ts("dve", kf.ap, kf.ap, MAGIC, None, ALU.subtract, None, [kf], [kf])
        self.stt(fc.ap, y.ap, 0.25, kf.ap, ALU.add, ALU.subtract, [y, kf], [fc])
        self.act(cos_out, fc.ap, AF.Sin, [fc], [Wc], scale=2.0 * math.pi)
        self.ts("dve", kf.ap, y.ap, MAGIC, None, ALU.add, None, [y], [kf])
        self.ts("dve", kf.ap, kf.ap, MAGIC, None, ALU.subtract, None, [kf], [kf])
        self.tt("dve", y.ap, y.ap, kf.ap, ALU.subtract, [y, kf], [y])
        self.act(sin_out, y.ap, AF.Sin, [y], [Ws], scale=2.0 * math.pi)
        ar.release(m)

    _ssem_i = 0

    def sload(self, dst_ap, src_ap, dst_reg, q="pool"):
        if not hasattr(self, "ssems"):
            self.ssems = [Buf(f"ss{i}") for i in range(8)]
        s = self.ssems[Builder._ssem_i % 8]
        Builder._ssem_i += 1
        if q == "sp":
            return self.fw.dma("sp", lambda h: h.dma_start(out=dst_ap, in_=src_ap, allow_slow_non_contiguous=True), s, writes=[dst_reg])
        return self.fw.dma("pool", lambda h: h.dma_start(out=dst_ap, in_=src_ap, allow_slow_non_contiguous=True), s, writes=[dst_reg])

    def layer_prep_gen(self, l):
        fw, ar, I = self.fw, self.ar, self.I
        m = ar.mark()
        c = self.s5cols
        lam_r = ar.alloc([128, 16], F32)
        lam_i = ar.alloc([128, 16], F32)
        stp = ar.alloc([128, 16], F32)
        t1 = ar.alloc([128, 16], F32)
        t2 = ar.alloc([128, 16], F32)
        t3 = ar.alloc([128, 16], F32)
        yv = ar.alloc([128, 16], F32)
        cs = ar.alloc([128, 16], F32)
        sn = ar.alloc([128, 16], F32)
        nat = ar.alloc([128, 4, 8, 16], F32)
        Cn = [ar.alloc([128, 4, 2, 64], F32) for _ in range(2)]
        ct2 = ar.alloc([128, 128], F32)
        lam = ar.alloc([128, 2, 2], F32)
        xx = ar.alloc([128, 2, 2], F32)
        pp = ar.alloc([128, 2, 2], F32)
        msk = ar.alloc([128, 8, 8], F32)
        Br = ar.alloc([128, 8, 16], F32)
        Bi = ar.alloc([128, 8, 16], F32)
        bb = [ar.alloc([128, 2, 8, 16], F32) for _ in range(2)]
        tA = ar.alloc([128, 2, 8, 16], F32)
        self.sload(lam_r.ap, I["s5_lam_re"][l].rearrange("d (j gl) n -> (gl n) (d j)", gl=2), lam_r)
        self.sload(lam_i.ap, I["s5_lam_im"][l].rearrange("d (j gl) n -> (gl n) (d j)", gl=2), lam_i)
        for gl in range(2):
            src = bass.AP(I["s5_log_step"].tensor, l * 32 + gl, [[0, 64], [2, 16]])
            self.sload(stp.ap[64 * gl:64 * gl + 64, :], src, stp)
        self.sload(c.ap[:, :, 6], I["s5r"][l].rearrange("d (j q) -> q (d j)", q=128), c)
        self.sload(c.ap[:, :, 7], I["s5i"][l].rearrange("d (j q) -> q (d j)", q=128), c)
        yield
        self.act(stp.ap, stp.ap, AF.Exp, [stp], [stp])
        self.tt("dve", t1.ap, lam_r.ap, stp.ap, ALU.mult, [lam_r, stp], [t1])
        self.tt("dve", t2.ap, lam_i.ap, stp.ap, ALU.mult, [lam_i, stp], [t2])
        self.act(c.ap[:, :, 1], t1.ap, AF.Exp, [t1], [c])
        self.ts("dve", yv.ap, t2.ap, 1.0 / (2.0 * math.pi), None, ALU.mult, None, [t2], [yv])
        self.sincos(yv, 16, cs.ap, sn.ap, cs, sn)
        self.cp("dve", c.ap[:, :, 0], yv.ap, [yv], [c])
        self.tt("dve", c.ap[:, :, 2], c.ap[:, :, 1], cs.ap, ALU.mult, [c, cs], [c])
        self.tt("dve", c.ap[:, :, 3], c.ap[:, :, 1], sn.ap, ALU.mult, [c, sn], [c])
        yield
        self.ts("dve", t1.ap, c.ap[:, :, 2], -1.0, None, ALU.add, None, [c], [t1])
        self.tt("dve", t2.ap, lam_r.ap, lam_r.ap, ALU.mult, [lam_r], [t2])
        self.tt("dve", t3.ap, lam_i.ap, lam_i.ap, ALU.mult, [lam_i], [t3])
        self.tt("dve", t2.ap, t2.ap, t3.ap, ALU.add, [t2, t3], [t2])
        fw.op("dve", lambda h: h.reciprocal(out=t2.ap, in_=t2.ap), reads=[t2], writes=[t2])
        self.tt("dve", t3.ap, t1.ap, lam_r.ap, ALU.mult, [t1, lam_r], [t3])
        self.tt("dve", yv.ap, c.ap[:, :, 3], lam_i.ap, ALU.mult, [c, lam_i], [yv])
        self.tt("dve", t3.ap, t3.ap, yv.ap, ALU.add, [t3, yv], [t3])
        self.tt("dve", c.ap[:, :, 4], t3.ap, t2.ap, ALU.mult, [t3, t2], [c])
        self.tt("dve", t3.ap, c.ap[:, :, 3], lam_r.ap, ALU.mult, [c, lam_r], [t3])
        self.tt("dve", yv.ap, t1.ap, lam_i.ap, ALU.mult, [t1, lam_i], [yv])
        self.tt("dve", t3.ap, t3.ap, yv.ap, ALU.subtract, [t3, yv], [t3])
        self.tt("dve", c.ap[:, :, 5], t3.ap, t2.ap, ALU.mult, [t3, t2], [c])

        yield
        fw.op("pool", lambda h: h.memset(msk.ap, 1.0), writes=[msk])
        for gl in range(2):
            fw.op("pool", lambda h, gl=gl: h.affine_select(out=msk.ap[64 * gl:64 * gl + 64], in_=msk.ap[64 * gl:64 * gl + 64],
                                                           pattern=[[0, 2], [-2, 4], [1, 8]], compare_op=ALU.is_equal, fill=0.0,
                                                           base=-gl, channel_multiplier=0), reads=[msk], writes=[msk])
        self.sload(Br.ap, I["s5_b_re"][l].rearrange("(j gl) n p -> (gl n) j p", gl=2), Br)
        self.sload(Bi.ap, I["s5_b_im"][l].rearrange("(j gl) n p -> (gl n) j p", gl=2), Bi)
        kr = c.ap[:, :, 4].rearrange("p (d j) -> p d j", d=2).unsqueeze(3).to_broadcast([128, 2, 8, 16])
        ki = c.ap[:, :, 5].rearrange("p (d j) -> p d j", d=2).unsqueeze(3).to_broadcast([128, 2, 8, 16])
        Brb = Br.ap.unsqueeze(1).to_broadcast([128, 2, 8, 16])
        Bib = Bi.ap.unsqueeze(1).to_broadcast([128, 2, 8, 16])
        self.tt("dve", bb[0].ap, Brb, kr, ALU.mult, [Br, c], [bb[0]])
        self.tt("dve", tA.ap, Bib, ki, ALU.mult, [Bi, c], [tA])
        self.tt("dve", bb[0].ap, bb[0].ap, tA.ap, ALU.subtract, [bb[0], tA], [bb[0]])
        self.tt("dve", bb[1].ap, Bib, kr, ALU.mult, [Bi, c], [bb[1]])
        self.tt("dve", tA.ap, Brb, ki, ALU.mult, [Br, c], [tA])
        self.tt("dve", bb[1].ap, bb[1].ap, tA.ap, ALU.add, [bb[1], tA], [bb[1]])
        yield
        for ri in range(2):
            for q4 in range(4):
                d, j0 = q4 // 2, (q4 % 2) * 4
                for jj in range(4):
                    j = j0 + jj
                    self.tt("dve", nat.ap[:, jj], bb[ri].ap[:, d, j].unsqueeze(1).to_broadcast([128, 8, 16]),
                            msk.ap[:, j].unsqueeze(2).to_broadcast([128, 8, 16]), ALU.mult, [bb[ri], msk], [nat])
                yield
                bk = self.bank()
                fw.mm([lambda h, bk=bk, jj=jj: h.transpose(out=bk.ap[:, jj * 128:(jj + 1) * 128],
                                                           in_=nat.ap[:, jj].rearrange("p a b -> p (a b)"), identity=self.ident_f.ap)
                       for jj in range(4)], reads=[nat, self.ident_f], writes=[bk])
                self.cp("act", self.s5w.ap[:, d * 8 + j0:d * 8 + j0 + 4, ri, :], bk.ap.rearrange("p (a b) -> p a b", a=4), [bk], [self.s5w])
        for hf in range(2):
            self.sload(Cn[0].ap[:, :, hf, :], I["s5_c_re"][l].rearrange("d g p n -> (d g p) n").rearrange("(q r) n -> r q n", r=128), Cn[0])
            self.sload(Cn[1].ap[:, :, hf, :], I["s5_c_im"][l].rearrange("d g p n -> (d g p) n").rearrange("(q r) n -> r q n", r=128), Cn[1])
        for ri in range(2):
            for q in range(4):
                d, ut = q // 2, q % 2
                bk = self.bank()
                fw.mm([lambda h, bk=bk, ri=ri, q=q: h.matmul(bk.ap[:, 0:128], lhsT=Cn[ri].ap[:, q].rearrange("p a b -> p (a b)"),
                                                             rhs=self.ident_f.ap, start=True, stop=True)],
                      reads=[Cn[ri], self.ident_f], writes=[bk])
                if ri == 0:
                    self.cp("act", ct2.ap, bk.ap[:, 0:128], [bk], [ct2])
                else:
                    self.act(ct2.ap, bk.ap[:, 0:128], AF.Copy, [bk], [ct2], scale=-1.0)
                j0 = ut * 4
                self.tt("dve", self.s5w.ap[:, d * 8 + j0:d * 8 + j0 + 4, 2 + ri, :].rearrange("p j (g q) -> p j g q", g=8),
                        ct2.ap.rearrange("p (g q) -> p g q", g=8).unsqueeze(1).to_broadcast([128, 4, 8, 16]),
                        msk.ap[:, j0:j0 + 4].unsqueeze(3).to_broadcast([128, 4, 8, 16]), ALU.mult, [ct2, msk], [self.s5w])
                yield
        self.sload(self.s5d.ap, I["s5_d"][l].rearrange("(t p) -> p t", p=128), self.s5d)
        yield
        lc = self.lrucols
        for k in range(4):
            self.sload(lc.ap[:, :, k], I["lru_conv_w"][l, k].rearrange("(t p) -> p t", p=128), lc)
        self.sload(lc.ap[:, :, 4], I["lru_conv_b"][l].rearrange("(t p) -> p t", p=128), lc)
        for d in range(2):
            self.sload(lc.ap[:, :, 5 + d], I["lru_b_a"][l, d].rearrange("(t p) -> p t", p=128), lc)
            self.sload(lc.ap[:, :, 7 + d], I["lru_b_x"][l, d].rearrange("(t p) -> p t", p=128), lc)
        for d in range(2):
            self.sload(lam.ap[:, :, d], I["lru_lam"][l, d].rearrange("(t p) -> p t", p=128), lam)
        self.act(xx.ap, lam.ap, AF.Exp, [lam], [xx], scale=-1.0)
        self.ts("dve", pp.ap, xx.ap, -0.25, 1.0 / 3.0, ALU.mult, ALU.add, [xx], [pp])
        self.tt("dve", pp.ap, pp.ap, xx.ap, ALU.mult, [pp, xx], [pp])
        self.ts("dve", pp.ap, pp.ap, -1.0, 0.5, ALU.mult, ALU.add, [pp], [pp])
        self.tt("dve", pp.ap, pp.ap, xx.ap, ALU.mult, [pp, xx], [pp])
        self.ts("dve", pp.ap, pp.ap, -1.0, 1.0, ALU.mult, ALU.add, [pp], [pp])
        self.tt("dve", pp.ap, pp.ap, xx.ap, ALU.mult, [pp, xx], [pp])
        self.ts("dve", lc.ap[:, :, 9:11], pp.ap, -8.0, None, ALU.mult, None, [pp], [lc])
        self.ts("dve", lc.ap[:, :, 11:13], pp.ap, 8.0, None, ALU.mult, None, [pp], [lc])
        self.ts("dve", lc.ap[:, :, 13:15], pp.ap, -16.0, None, ALU.mult, None, [pp], [lc])
        yield
        fw.op("pool", lambda h: h.memset(self.lruw.ap, 0.0), writes=[self.lruw])
        for gi, nm in enumerate(("lru_w_a", "lru_w_x")):
            for d in range(2):
                for t in range(2):
                    for b2 in range(2):
                        idx = gi * 4 + d * 2 + t
                        self.sload(self.lruw.ap[64 * b2:64 * b2 + 64, idx, 64 * b2:64 * b2 + 64], I[nm][l, d, 2 * t + b2], self.lruw, q="pool")
        for d in range(2):
            self.sload(self.lrust0.ap[:, d, :], I["slru"][l, d].rearrange("(t p) -> p t", p=128), self.lrust0)
        self.sload(self.wglu.ap, I["s5_w_glu"][l].rearrange("(k p) n -> p k n", p=128), self.wglu, q="pool")
        yield
        if not self.prep_bg:
            ar.release(m)

    def ffn_prep(self, l):
        I = self.I
        for k in range(3):
            self.sload(self.ffcols.ap[:, :, k], I["ffn_conv_w"][l, k].rearrange("(t p) -> p t", p=128), self.ffcols)
        self.sload(self.ffcols.ap[:, :, 3], I["ffn_conv_b"][l].rearrange("(t p) -> p t", p=128), self.ffcols)

    def group_layer(self, l, g):
        fw, ar, I, O = self.fw, self.ar, self.I, self.O
        if g == 0:
            self.ffn_prep(l)
        self.norm_mod(l, g, self.A1, 0)
        self.chk(f"norm{l}{g}")
        m0 = ar.mark()
        attnT = ar.alloc([128, 4, TG], BF16)
        m1 = ar.mark()
        self.attention(l, g, attnT)
        self.chk(f"attn{l}{g}")
        ar.release(m1)
        s5y = ar.alloc([128, 2, TG], BF16)
        m1 = ar.mark()
        self.s5(l, g, s5y)
        self.chk(f"s5{l}{g}")
        ar.release(m1)
        if g == 0:
            self.store_s5_states(l)
        lruy = ar.alloc([128, 2, TG], BF16)
        m1 = ar.mark()
        self.lru(l, g, lruy)
        self.chk(f"lru{l}{g}")
        ar.release(m1)
        if g == 1 and l + 1 < DEPTH:
            self.prep_bg = True
            self.bg = self.layer_prep_gen(l + 1)
            self.tick()
        if g == 1 and l + 1 == DEPTH:
            self.bg = self.final_norm_gen(0)
            self.tick()
        self.merge(l, g, attnT, s5y, lruy)
        self.drain()
        self.chk(f"merge{l}{g}")
        ar.release(m0)
        self.norm_mod(l, g, self.A2, 3)
        self.ffn(l, g)
        ar.release(m0)

    def attention(self, l, g, attnT):
        fw, ar, I, O = self.fw, self.ar, self.I, self.O
        q_sb = ar.alloc([128, 4, TG], BF16)
        k_sb = ar.alloc([128, 4, TG], BF16)
        v_aug = ar.alloc([128, 8, 8, 66], BF16)
        fw.op("pool", lambda h: h.memset(v_aug.ap[:, :, :, 64:65], 1.0), writes=[v_aug])

        def q_cons(ft, tb, bk):
            self.act(q_sb.ap[:, ft, tb * 512:(tb + 1) * 512], bk.ap, AF.Copy, [bk], [q_sb], scale=0.125)

        def k_cons(ft, tb, bk):
            self.cp("dve", k_sb.ap[:, ft, tb * 512:(tb + 1) * 512], bk.ap, [bk], [k_sb])

        self.proj_fm(self.win_cols(l, 0), 4, self.h_rhs, [self.h], 8, q_cons)
        self.proj_fm(self.win_cols(l, 512), 4, self.h_rhs, [self.h], 8, k_cons)
        self.chk(f"attq{l}{g}")
        for which in ((1, 2) if g == 0 else (2,)):
            for half in range(2):
                slot = self.wslice([(0, 8, 256, I["w_in"][l][:, which * 512 + half * 256: which * 512 + half * 256 + 256])])
                sv = slot.ap.rearrange("p (k n) -> p k n", k=8)
                for tt in range(8):
                    bk = self.bank()
                    fw.mm([lambda h, bk=bk, sv=sv, tt=tt, kc=kc: h.matmul(bk.ap[:, 0:256], lhsT=self.h.ap[:, kc, tt * 128:(tt + 1) * 128], rhs=sv[:, kc, :],
                                                                          start=(kc == 0), stop=(kc == 7)) for kc in range(8)],
                          reads=[slot, self.h], writes=[bk])
                    if which == 2:
                        self.cp("act", v_aug.ap[:, tt, half * 4:half * 4 + 4, 0:64], bk.ap[:, 0:256].rearrange("p (a b) -> p a b", a=4), [bk], [v_aug])
                    if g == 0:
                        s = self.stg_i % 2
                        self.stg_i += 1
                        stg = self.stg[s]
                        self.cp("dve", stg.ap[:, 0:256], bk.ap[:, 0:256], [bk], [stg])
                        dst = O["nk" if which == 1 else "nv"][tt // 2, l, (tt % 2) * 128:(tt % 2) * 128 + 128, half * 256:half * 256 + 256]
                        d = fw.dma("sp", lambda h, stg=stg, dst=dst: h.dma_start(out=dst, in_=stg.ap[:, 0:256]), self.stg_sem[s], reads=[stg])
                        fw.final.append(d)
                    self.chk(f"kv1{l}{g}")
            if which == 1:
                self.chk(f"kvK{l}{g}")

        self.chk(f"attkv{l}{g}")
        pT = [ar.alloc([128, 512], BF16) for _ in range(2)]
        pi = [0]
        atok = ar.alloc([128, 8, 128], BF16)
        rec = ar.alloc([128, 8], F32)

        def transposes(hp, tts):
            bk = self.bank()
            bkb = bk.ap.bitcast(BF16)
            fw.mm([lambda h, bkb=bkb, i=i, tt=tt: h.transpose(out=bkb[:, i * 128:(i + 1) * 128], in_=atok.ap[:, tt, :], identity=self.ident_bf.ap)
                   for i, tt in enumerate(tts)], reads=[atok, self.ident_bf], writes=[bk])
            n = len(tts)
            self.cp("dve", attnT.ap[:, hp, tts[0] * 128:(tts[0] + n) * 128], bkb[:, 0:n * 128], [bk], [attnT])

        if g == 0:
            UP = [(sq, hp, e) for sq in range(4) for hp in range(4) for e in range(2)]

            def issue_score_p(k):
                sq, hp, e = UP[k]
                pb = 64 * e
                sb = self.bank()
                fw.mm([lambda h, sb=sb, kt=kt, pb=pb, sq=sq, hp=hp: h.matmul(sb.ap[:, kt * 256:(kt + 1) * 256],
                                                                            lhsT=k_sb.ap[pb:pb + 64, hp, sq * 256 + kt * 128:sq * 256 + kt * 128 + 128],
                                                                            rhs=q_sb.ap[pb:pb + 64, hp, sq * 256:sq * 256 + 256], start=True, stop=True) for kt in range(2)],
                      reads=[k_sb, q_sb], writes=[sb])
                return sb

            nxt = issue_score_p(0)
            ob = ov = None
            for k, (sq, hp, e) in enumerate(UP):
                hh = 2 * hp + e
                if e == 0:
                    ob = self.bank(hold=True)
                    ov = ob.ap[:, 0:260].rearrange("p (q e c) -> p q e c", q=2, e=2)
                sb = nxt
                p = pT[k % 2]
                self.act(p.ap, sb.ap, AF.Exp, [sb], [p])
                if k + 1 < len(UP):
                    nxt = issue_score_p(k + 1)
                fns = []
                for qt in range(2):
                    for kt in range(2):
                        fns.append(lambda h, qt=qt, kt=kt, e=e, hh=hh, p=p, ov=ov, sq=sq, st_=(e == 0 and qt == 0 and kt == 0): h.matmul(
                            ov[:, qt, e, :], lhsT=p.ap[:, kt * 256 + qt * 128:kt * 256 + qt * 128 + 128], rhs=v_aug.ap[:, 2 * sq + kt, hh, 0:65],
                            start=st_, stop=(e == 1 and qt == 1 and kt == 1)))
                fw.mm(fns, reads=[p, v_aug], writes=[ob])
                if e == 1:
                    fw.op("dve", lambda h, ov=ov: h.reciprocal(out=rec.ap[:, 0:4].rearrange("p (q e) -> p q e", q=2), in_=ov[:, :, :, 64]), reads=[ob], writes=[rec])
                    self.tt("dve", atok.ap[:, 2 * sq:2 * sq + 2, :].rearrange("p q (e c) -> p q e c", e=2), ov[:, :, :, 0:64],
                            rec.ap[:, 0:4].rearrange("p (q e) -> p q e", q=2).unsqueeze(3).to_broadcast([128, 2, 2, 64]), ALU.mult, [ob, rec], [atok])
                    self.unhold(ob)
                    transposes(hp, [2 * sq, 2 * sq + 1])
        else:
            self.na_attention(l, attnT, q_sb, k_sb, v_aug, pT, atok, rec, transposes)

    def na_attention(self, l, attnT, q_sb, k_sb, v_aug, pT, atok, rec, transposes):
        fw, ar, I = self.fw, self.ar, self.I
        kctxT = ar.alloc([128, 4, 512], BF16)
        vctx = ar.alloc([128, 4, 8, 66], BF16)
        fw.op("pool", lambda h: h.memset(vctx.ap[:, :, :, 64:65], 1.0), writes=[vctx])
        cvsem = Buf("cvsem")
        for tt in range(4):
            fw.dma("pool", lambda h, tt=tt: h.dma_start(out=vctx.ap[:, tt, :, 0:64], in_=I["cv"][l][tt * 128:(tt + 1) * 128, :].rearrange("p (a b) -> p a b", a=8)),
                   cvsem, writes=[vctx])
        mk = ar.mark()
        cktok = ar.alloc([128, 4, 512], BF16)
        cksem = Buf("cksem")
        fw.dma("pool", lambda h: h.dma_start(out=cktok.ap, in_=I["ck"][l].rearrange("(t p) f -> p t f", p=128)), cksem, writes=[cktok])
        for hp in range(4):
            bk = self.bank()
            bkb = bk.ap.bitcast(BF16)
            fw.mm([lambda h, bkb=bkb, tt=tt, hp=hp: h.transpose(out=bkb[:, tt * 128:(tt + 1) * 128], in_=cktok.ap[:, tt, hp * 128:(hp + 1) * 128],
                                                               identity=self.ident_bf.ap) for tt in range(4)], reads=[cktok, self.ident_bf], writes=[bk])
            self.cp("dve", kctxT.ap[:, hp, :], bkb[:, 0:512], [bk], [kctxT])
        ar.release(mk)
        LT = ar.alloc([128, 8, 18, 64], BF16)
        mk = ar.mark()
        rp = ar.alloc([128, 2, 32], F32)
        fw.op("pool", lambda h: h.memset(rp.ap, 0.0), writes=[rp])
        for e in range(2):
            self.sload(rp.ap[0:120, e, 0:31], I["rpb"][l].rearrange("h a x -> (h a) x"), rp)
        Rs = ar.alloc([64, 8, 15], F32)
        RE = ar.alloc([64, 8, 18], F32)
        BB = ar.alloc([64, 2, 127], F32)
        msk = ar.alloc([128, 64], F32)
        m2 = ar.alloc([128, 64], F32)
        bk = self.bank()
        fw.mm([lambda h, bk=bk: h.matmul(bk.ap[0:64, 0:120], lhsT=rp.ap[0:120].rearrange("p a b -> p (a b)"),
                                         rhs=self.ident_f.ap[0:120, 0:120], start=True, stop=True)], reads=[rp, self.ident_f], writes=[bk])
        fw.op("pool", lambda h: h.memset(Rs.ap, 0.0), writes=[Rs])
        for e in range(2):
            self.cp("dve", Rs.ap[32 * e:32 * e + 31].rearrange("p a b -> p (a b)"), bk.ap[32 * e:32 * e + 31, 0:120], [bk], [Rs])
        fw.op("pool", lambda h: h.memset(RE.ap, 0.0), writes=[RE])
        for e in range(2):
            self.cp("dve", RE.ap[32 * e:32 * e + 31, :, e + 1:e + 16], Rs.ap[32 * e:32 * e + 31, :, ::-1], [Rs], [RE])
        fw.op("pool", lambda h: h.memset(BB.ap, 0.0), writes=[BB])
        for e in range(2):
            fw.op("pool", lambda h, e=e: h.memset(BB.ap[32 * e:32 * e + 32, e, :], 1.0), writes=[BB])
            fw.op("pool", lambda h, e=e: h.affine_select(out=BB.ap[32 * e:32 * e + 32, e, :], in_=BB.ap[32 * e:32 * e + 32, e, :], pattern=[[1, 127]],
                                                          compare_op=ALU.is_equal, fill=0.0, base=-48, channel_multiplier=-1), reads=[BB], writes=[BB])
        fw.op("pool", lambda h: h.memset(msk.ap, 0.0), writes=[msk])
        fw.op("pool", lambda h: h.memset(m2.ap, 0.0), writes=[m2])
        for hf in range(2):
            sl = slice(64 * hf, 64 * hf + 64)
            fw.op("pool", lambda h, sl=sl: h.affine_select(out=msk.ap[sl], in_=msk.ap[sl], pattern=[[-1, 64]], compare_op=ALU.is_ge, fill=NEG,
                                                            base=8, channel_multiplier=1), reads=[msk], writes=[msk])
            fw.op("pool", lambda h, sl=sl: h.affine_select(out=msk.ap[sl], in_=msk.ap[sl], pattern=[[0, 64]], compare_op=ALU.is_ge, fill=0.0,
                                                            base=47, channel_multiplier=-1), reads=[msk], writes=[msk])
            fw.op("pool", lambda h, sl=sl: h.affine_select(out=m2.ap[sl], in_=m2.ap[sl], pattern=[[1, 64]], compare_op=ALU.is_ge, fill=NEG,
                                                            base=7, channel_multiplier=-1), reads=[m2], writes=[m2])
            fw.op("pool", lambda h, sl=sl: h.affine_select(out=m2.ap[sl], in_=m2.ap[sl], pattern=[[0, 64]], compare_op=ALU.is_ge, fill=0.0,
                                                            base=-16, channel_multiplier=1), reads=[m2], writes=[m2])
        self.tt("pool", msk.ap, msk.ap, m2.ap, ALU.add, [msk, m2], [msk])
        REf = RE.ap.rearrange("p a b -> p (a b)")
        for q0 in range(0, 64, 3):
            nq = min(3, 64 - q0)
            bk = self.bank()
            for i in range(nq):
                qc = q0 + i
                fw.mm([lambda h, bk=bk, i=i, qc=qc, e=e: h.matmul(bk.ap[64 * e:64 * e + 64, i * 144:(i + 1) * 144], lhsT=BB.ap[:, e, 63 - qc:63 - qc + 64], rhs=REf,
                                                                  start=True, stop=True) for e in range(2)], reads=[BB, RE], writes=[bk])
            outv = LT.ap[:, :, :, q0:q0 + nq].rearrange("p h d q -> p q (h d)")
            self.tt("dve", outv, bk.ap[:, 0:nq * 144].rearrange("p (q n) -> p q n", q=nq),
                    msk.ap[:, q0:q0 + nq].unsqueeze(2).to_broadcast([128, nq, 144]), ALU.add, [bk, msk], [LT])
        LTf = LT.ap.rearrange("p h d q -> p (h d q)")
        for hq in range(4):
            self.act(LTf[:, hq * 2304:(hq + 1) * 2304], LTf[:, hq * 2304:(hq + 1) * 2304], AF.Exp, [LT], [LT])
        ar.release(mk)

        U = []
        for hp in range(4):
            for e in range(2):
                for c in range(2):
                    grp = []
                    for mt in range(8):
                        js = [j for j in range(4 * c, 4 * c + 4) if any(na_valid(2 * mt + ee, 2 * j + r) for ee in range(2) for r in range(2))]
                        if js:
                            grp.append(("loc", mt, js[0], js[-1]))
                    for kt in range(4):
                        grp.append(("ctx", kt, 4 * c, 4 * c + 3))
                    for ui, (kind, mt, ja, jb) in enumerate(grp):
                        U.append((hp, e, c, kind, mt, ja, jb, ui == 0, ui == len(grp) - 1))

        def issue_score(k):
            hp, e, c, kind, mt, ja, jb, gfirst, glast = U[k]
            pb = 64 * e
            hh = 2 * hp + e
            nq = 128 * (jb - ja + 1)
            sb = self.bank()
            if kind == "loc":
                d0 = 2 * ja - 2 * mt + 8
                d1 = 2 * jb + 1 - 2 * mt + 8
                assert 0 <= d0 and d1 < 18, (mt, ja, jb)
                ltv = LT.ap[:, hh, d0:d1 + 1, :].rearrange("p a b -> p (a b)")
                fw.mm([lambda h, sb=sb, mt=mt, ja=ja, nq=nq, pb=pb, hp=hp: h.matmul(sb.ap[:, 0:nq], lhsT=k_sb.ap[pb:pb + 64, hp, mt * 128:(mt + 1) * 128],
                                                                                   rhs=q_sb.ap[pb:pb + 64, hp, ja * 128:ja * 128 + nq], start=True, stop=True)],
                      reads=[k_sb, q_sb], writes=[sb])
            else:
                fw.mm([lambda h, sb=sb, mt=mt, ja=ja, nq=nq, pb=pb, hp=hp: h.matmul(sb.ap[:, 0:nq], lhsT=kctxT.ap[pb:pb + 64, hp, mt * 128:(mt + 1) * 128],
                                                                                   rhs=q_sb.ap[pb:pb + 64, hp, ja * 128:ja * 128 + nq], start=True, stop=True)],
                      reads=[kctxT, q_sb], writes=[sb])
            return sb

        nxt = issue_score(0)
        ob = ov = None
        first = True
        for k, (hp, e, c, kind, mt, ja, jb, gfirst, glast) in enumerate(U):
            hh = 2 * hp + e
            nq = 128 * (jb - ja + 1)
            if gfirst:
                ob = self.bank(hold=True)
                ov = ob.ap[:, 0:260].rearrange("p (q c) -> p q c", q=4)
                first = True
            sb = nxt
            p = pT[k % 2]
            self.act(p.ap[:, 0:nq], sb.ap[:, 0:nq], AF.Exp, [sb], [p])
            if k + 1 < len(U):
                nxt = issue_score(k + 1)
            if kind == "loc":
                d0 = 2 * ja - 2 * mt + 8
                d1 = 2 * jb + 1 - 2 * mt + 8
                ltv = LT.ap[:, hh, d0:d1 + 1, :].rearrange("p a b -> p (a b)")
                self.tt("dve", p.ap[:, 0:nq], p.ap[:, 0:nq], ltv, ALU.mult, [p, LT], [p])
            fns = []
            for j in range(ja, jb + 1):
                if kind == "loc":
                    val = [[na_valid(2 * mt + ee, 2 * j + r) for r in range(2)] for ee in range(2)]
                    if not any(val[0]) and not any(val[1]):
                        continue
                    for ee in range(2):
                        for r in range(2):
                            if not val[ee][r]:
                                c0 = (j - ja) * 128 + r * 64
                                fw.op("dve", lambda h, p=p, ee=ee, c0=c0: h.memset(p.ap[64 * ee:64 * ee + 64, c0:c0 + 64], 0.0), reads=[p], writes=[p])
                    rhs = v_aug.ap[:, mt, hh, 0:65]
                else:
                    rhs = vctx.ap[:, mt, hh, 0:65]
                fns.append(lambda h, j=j, ja=ja, p=p, rhs=rhs, st_=first, ov=ov, c=c: h.matmul(ov[:, j - 4 * c, :], lhsT=p.ap[:, (j - ja) * 128:(j - ja + 1) * 128],
                                                                                            rhs=rhs, start=st_, stop=False))
                first = False
            fw.mm(fns, reads=[p, v_aug, vctx], writes=[ob])
            if glast:
                fw.op("dve", lambda h, ov=ov: h.reciprocal(out=rec.ap[:, 0:4], in_=ov[:, :, 64]), reads=[ob], writes=[rec])
                self.tt("dve", atok.ap[:, 4 * c:4 * c + 4, 64 * e:64 * e + 64], ov[:, :, 0:64],
                        rec.ap[:, 0:4].unsqueeze(2).to_broadcast([128, 4, 64]), ALU.mult, [ob, rec], [atok])
                self.unhold(ob)
                if e == 1 and c == 1:
                    transposes(hp, [0, 1, 2, 3])
                    transposes(hp, [4, 5, 6, 7])

    def s5(self, l, g, s5y):
        fw, ar, I, O = self.fw, self.ar, self.I, self.O
        c = self.s5cols
        u_sb = ar.alloc([128, 2, TG], BF16)

        def u_cons(ft, tb, bk):
            self.cp("act", u_sb.ap[:, ft, tb * 512:(tb + 1) * 512], bk.ap, [bk], [u_sb])

        self.proj_fm(self.win_cols(l, 1536), 2, self.h_rhs, [self.h], 8, u_cons)
        Ec = ar.alloc([128, 16, 256], F32)
        Es = ar.alloc([128, 16, 256], F32)
        mk = ar.mark()
        io_i = ar.alloc([128, 256], I32)
        io_f = ar.alloc([128, 256], F32)
        fw.op("pool", lambda h: h.iota(io_i.ap, pattern=[[1, 256]], base=0, channel_multiplier=0), writes=[io_i])
        self.cp("dve", io_f.ap, io_i.ap, [io_i], [io_f])
        yv = ar.alloc([128, 2, 256], F32)
        for q in range(8):
            self.tt("dve", yv.ap, c.ap[:, 2 * q:2 * q + 2, 0].unsqueeze(2).to_broadcast([128, 2, 256]),
                    io_f.ap.unsqueeze(1).to_broadcast([128, 2, 256]), ALU.mult, [c, io_f], [yv])
            yflat = Reg(yv.ap.rearrange("p a b -> p (a b)"), yv.bufs)
            self.sincos(yflat, 512, Ec.ap[:, 2 * q:2 * q + 2, :].rearrange("p a b -> p (a b)"),
                        Es.ap[:, 2 * q:2 * q + 2, :].rearrange("p a b -> p (a b)"), Ec, Es)
        ar.release(mk)
        Kc = ar.alloc([128, 16, 4], F32)
        if g == 1:
            e255c = Ec.ap[:, :, 255]
            e255s = Es.ap[:, :, 255]
            self.tt("dve", Kc.ap[:, :, 0], c.ap[:, :, 2], e255c, ALU.mult, [c, Ec], [Kc])
            self.tt("dve", Kc.ap[:, :, 3], c.ap[:, :, 3], e255s, ALU.mult, [c, Es], [Kc])
            self.tt("dve", Kc.ap[:, :, 0], Kc.ap[:, :, 0], Kc.ap[:, :, 3], ALU.subtract, [Kc], [Kc])
            self.tt("dve", Kc.ap[:, :, 1], c.ap[:, :, 2], e255s, ALU.mult, [c, Es], [Kc])
            self.tt("dve", Kc.ap[:, :, 3], c.ap[:, :, 3], e255c, ALU.mult, [c, Ec], [Kc])
            self.tt("dve", Kc.ap[:, :, 1], Kc.ap[:, :, 1], Kc.ap[:, :, 3], ALU.add, [Kc], [Kc])
            self.ts("dve", Kc.ap[:, :, 2], Kc.ap[:, :, 1], -1.0, None, ALU.mult, None, [Kc], [Kc])
        ygb = ar.alloc([128, 2, TG], BF16)
        bpr = ar.alloc([128, 512], F32)
        bpi = ar.alloc([128, 512], F32)
        grs = [ar.alloc([128, 512], F32) for _ in range(2)]
        gis = [ar.alloc([128, 512], F32) for _ in range(2)]
        t1 = ar.alloc([128, 512], F32)
        t2 = ar.alloc([128, 512], F32)
        p1 = ar.alloc([128, 512], F32)
        p2 = ar.alloc([128, 512], F32)
        unit = [0]
        wr = ar.alloc([128, 512], BF16)
        wi = ar.alloc([128, 512], BF16)
        sm = ar.alloc([128, 16], F32)

        def v2(ap):
            return ap.rearrange("p (s t) -> p s t", s=2)

        def seg(ap, s2, rev):
            v = ap[:, s2 * 256:(s2 + 1) * 256]
            return v[:, ::-1] if rev else v

        units = []
        for ut in range(2):
            for d in range(2):
                for jj in range(4):
                    for ti, tb in enumerate([1, 0] if d == 1 else [0, 1]):
                        units.append((ut, d, jj, ti, tb))

        def issue_bu(k):
            ut, d, jj, ti, tb = units[k]
            dj = d * 8 + ut * 4 + jj
            tsl = slice(tb * 512, (tb + 1) * 512)
            br = self.bank()
            bi = self.bank()
            fw.mm([lambda h, br=br, dj=dj, tsl=tsl, ut=ut: h.matmul(br.ap, lhsT=self.s5w.ap[:, dj, 0, :], rhs=u_sb.ap[:, ut, tsl], start=True, stop=True)],
                  reads=[self.s5w, u_sb], writes=[br])
            fw.mm([lambda h, bi=bi, dj=dj, tsl=tsl, ut=ut: h.matmul(bi.ap, lhsT=self.s5w.ap[:, dj, 1, :], rhs=u_sb.ap[:, ut, tsl], start=True, stop=True)],
                  reads=[self.s5w, u_sb], writes=[bi])
            return br, bi

        ybanks = None
        yfirst = None
        if l == 0:
            self.bg = self.mod_gen(0, look=2, s0=8, s1=24, doA=(False, True)) if g == 0 else self.mod_gen(1, look=2)
        tbk = self.bank(hold=True)
        tbk2 = self.bank(hold=True)
        deferred = []
        prev_last = None
        nxt = issue_bu(0)
        for k, (ut, d, jj, ti, tb) in enumerate(units):
            rev = (d == 1)
            j = ut * 4 + jj
            dj = d * 8 + j
            if d == 0 and jj == 0 and ti == 0:
                ybanks = [self.bank(hold=True), self.bank(hold=True)]
                yfirst = [True, True]
            Ecv = Ec.ap[:, dj, :]
            Esv = Es.ap[:, dj, :]
            if rev:
                Ecv = Ecv[:, ::-1]
                Esv = Esv[:, ::-1]
            Ec2 = Ecv.unsqueeze(1).to_broadcast([128, 2, 256])
            Es2 = Esv.unsqueeze(1).to_broadcast([128, 2, 256])
            rb = c.ap[:, dj, 1:2].to_broadcast([128, 256])
            gr, gi = grs[k % 2], gis[k % 2]
            br, bi = nxt
            self.tt("dve", v2(t2.ap), v2(bi.ap), Es2, ALU.mult, [bi, Es], [t2])
            self.tt("dve", v2(tbk.ap), v2(br.ap), Ec2, ALU.mult, [br, Ec], [tbk])
            self.tt("dve", v2(tbk2.ap), v2(bi.ap), Ec2, ALU.mult, [bi, Ec], [tbk2])
            self.tt("dve", bpr.ap, tbk.ap, t2.ap, ALU.add, [tbk, t2], [bpr])
            self.tt("dve", v2(t2.ap), v2(br.ap), Es2, ALU.mult, [br, Es], [t2])
            self.tt("dve", bpi.ap, tbk2.ap, t2.ap, ALU.subtract, [tbk2, t2], [bpi])
            if k + 1 < len(units):
                nxt = issue_bu(k + 1)
            for fn in deferred:
                fn()
            deferred = []
            segs = [1, 0] if rev else [0, 1]
            if g == 0:
                for (src, dst) in ((bpr, gr), (bpi, gi)):
                    for s2 in segs:
                        fw.op("dve", lambda h, src=src, dst=dst, s2=s2, rev=rev, rb=rb: h.tensor_tensor_scan(
                            out=seg(dst.ap, s2, rev), data0=rb, data1=seg(src.ap, s2, rev), initial=0.0, op0=ALU.mult, op1=ALU.add),
                            reads=[src, c], writes=[dst])
            for sj, s2 in enumerate(segs):
                s = tb * 2 + s2
                si = ti * 2 + sj
                if g == 1:
                    f_r = seg(bpr.ap, s2, rev)[:, 0:1]
                    f_i = seg(bpi.ap, s2, rev)[:, 0:1]
                    if si == 0:
                        hpr, hpi = c.ap[:, dj, 6:7], c.ap[:, dj, 7:8]
                        self.stt(f_r, hpr, c.ap[:, dj, 2:3], f_r, ALU.mult, ALU.add, [c, bpr], [bpr])
                        self.stt(f_i, hpi, c.ap[:, dj, 2:3], f_i, ALU.mult, ALU.add, [c, bpi], [bpi])
                        self.ts("dve", sm.ap[:, 0:1], hpi, c.ap[:, dj, 3:4], None, ALU.mult, None, [c], [sm])
                        self.stt(f_i, hpr, c.ap[:, dj, 3:4], f_i, ALU.mult, ALU.add, [c, bpi], [bpi])
                        self.tt("dve", f_r, f_r, sm.ap[:, 0:1], ALU.subtract, [bpr, sm], [bpr])
                    else:
                        pgr, pgi = prev_last
                        self.stt(f_r, pgr[0], Kc.ap[:, dj, 0:1], f_r, ALU.mult, ALU.add, [pgr[1], Kc, bpr], [bpr])
                        self.stt(f_i, pgi[0], Kc.ap[:, dj, 0:1], f_i, ALU.mult, ALU.add, [pgi[1], Kc, bpi], [bpi])
                        self.stt(f_r, pgi[0], Kc.ap[:, dj, 2:3], f_r, ALU.mult, ALU.add, [pgi[1], Kc, bpr], [bpr])
                        self.stt(f_i, pgr[0], Kc.ap[:, dj, 1:2], f_i, ALU.mult, ALU.add, [pgr[1], Kc, bpi], [bpi])
                if g == 1:
                    for (src, dst) in ((bpr, gr), (bpi, gi)):
                        fw.op("dve", lambda h, src=src, dst=dst, s2=s2, rev=rev, rb=rb: h.tensor_tensor_scan(
                            out=seg(dst.ap, s2, rev), data0=rb, data1=seg(src.ap, s2, rev), initial=0.0, op0=ALU.mult, op1=ALU.add),
                            reads=[src, c], writes=[dst])
                g_r = seg(gr.ap, s2, rev)[:, 255:256]
                g_i = seg(gi.ap, s2, rev)[:, 255:256]
                prev_last = ((g_r, gr), (g_i, gi))
                if g == 0:
                    def state_ops(dj=dj, g_r=g_r, g_i=g_i, gr=gr, gi=gi, s=s, d=d, j=j):
                        e_c = Ec.ap[:, dj, 255:256]
                        e_s = Es.ap[:, dj, 255:256]
                        o_r, o_i = self.outst.ap[:, s, d, j, 0:1], self.outst.ap[:, s, d, j, 1:2]
                        self.tt("dve", sm.ap[:, 1:2], g_i, e_s, ALU.mult, [gi, Es], [sm])
                        self.tt("dve", sm.ap[:, 2:3], g_i, e_c, ALU.mult, [gi, Ec], [sm])
                        self.stt(o_r, g_r, e_c, sm.ap[:, 1:2], ALU.mult, ALU.subtract, [gr, Ec, sm], [self.outst])
                        self.stt(o_i, g_r, e_s, sm.ap[:, 2:3], ALU.mult, ALU.add, [gr, Es, sm], [self.outst])
                    deferred.append(state_ops)
            self.tt("pool", v2(p1.ap), v2(gr.ap), Ec2, ALU.mult, [gr, Ec], [p1])
            self.tt("pool", v2(p2.ap), v2(gi.ap), Es2, ALU.mult, [gi, Es], [p2])
            self.tt("pool", wr.ap, p1.ap, p2.ap, ALU.subtract, [p1, p2], [wr])
            self.tt("pool", v2(p1.ap), v2(gi.ap), Ec2, ALU.mult, [gi, Ec], [p1])
            self.tt("pool", v2(p2.ap), v2(gr.ap), Es2, ALU.mult, [gr, Es], [p2])
            self.tt("pool", wi.ap, p1.ap, p2.ap, ALU.add, [p1, p2], [wi])
            self.tick()
            yb = ybanks[tb]
            last = (d == 1 and jj == 3)
            fw.mm([lambda h, yb=yb, dj=dj, st_=yfirst[tb]: h.matmul(yb.ap, lhsT=self.s5w.ap[:, dj, 2, :], rhs=wr.ap, start=st_, stop=False),
                   lambda h, yb=yb, dj=dj, last=last: h.matmul(yb.ap, lhsT=self.s5w.ap[:, dj, 3, :], rhs=wi.ap, start=False, stop=last)],
                  reads=[self.s5w, wr, wi], writes=[yb])
            yfirst[tb] = False
            if d == 1 and jj == 3 and ti == 1:
                for tb2 in range(2):
                    yb = ybanks[tb2]
                    sl = slice(tb2 * 512, (tb2 + 1) * 512)
                    self.stt(t1.ap, u_sb.ap[:, ut, sl], self.s5d.ap[:, ut:ut + 1], yb.ap, ALU.mult, ALU.add, [u_sb, self.s5d, yb], [t1])
                    self.act(ygb.ap[:, ut, sl], t1.ap, AF.Gelu_apprx_tanh, [t1], [ygb])
                    self.unhold(yb)
        for fn in deferred:
            fn()
        self.drain()
        self.unhold(tbk)
        self.unhold(tbk2)
        for ot in range(2):
            for tb in range(2):
                sl = slice(tb * 512, (tb + 1) * 512)
                bk = self.bank()
                fw.mm([lambda h, bk=bk, kc=kc, ot=ot, sl=sl: h.matmul(bk.ap, lhsT=self.wglu.ap[:, kc, ot * 128:(ot + 1) * 128], rhs=ygb.ap[:, kc, sl],
                                                                      start=(kc == 0), stop=(kc == 1)) for kc in range(2)], reads=[self.wglu, ygb], writes=[bk])
                self.act(t1.ap, bk.ap, AF.Sigmoid, [bk], [t1])
                self.tt("dve", s5y.ap[:, ot, sl], ygb.ap[:, ot, sl], t1.ap, ALU.mult, [ygb, t1], [s5y])

    def store_s5_states(self, l):
        fw, ar, O = self.fw, self.ar, self.O
        mk = ar.mark()
        tr = ar.alloc([128, 128], F32)
        bk = self.bank()
        fw.mm([lambda h: h.transpose(out=bk.ap[:, 0:128], in_=self.outst.ap.rearrange("p s d j r -> p (s d j r)"), identity=self.ident_f.ap)],
              reads=[self.outst, self.ident_f], writes=[bk])
        self.cp("dve", tr.ap, bk.ap[:, 0:128], [bk], [tr])
        sem = Buf("s5out")
        for ri, nm in enumerate(("ns5r", "ns5i")):
            for s in range(4):
                for d in range(2):
                    r0 = ((s * 2 + d) * 8) * 2 + ri
                    src = tr.ap[r0:r0 + 15:2, :]
                    dst = O[nm][s, l, d, :].rearrange("(j q) -> j q", q=128)
                    dd = fw.dma("sp", lambda h, src=src, dst=dst: h.dma_start(out=dst, in_=src), sem, reads=[tr])
                    fw.final.append(dd)
        ar.release(mk)

    def lru(self, l, g, lruy):
        fw, ar, I, O = self.fw, self.ar, self.I, self.O
        lc = self.lrucols
        xr = ar.alloc([128, 2, TG], F32)
        gg = ar.alloc([128, 2, TG], F32)
        xc = ar.alloc([128, 2, TG], F32)
        xcb = ar.alloc([128, 2, TG], BF16)

        def xr_cons(ft, tb, bk):
            self.cp("act", xr.ap[:, ft, tb * 512:(tb + 1) * 512], bk.ap, [bk], [xr])

        def xg_cons(ft, tb, bk):
            self.act(gg.ap[:, ft, tb * 512:(tb + 1) * 512], bk.ap, AF.Gelu_apprx_tanh, [bk], [gg])

        self.proj_fm(self.win_cols(l, 1792), 2, self.h_rhs, [self.h], 8, xr_cons)
        self.proj_fm(self.win_cols(l, 2048), 2, self.h_rhs, [self.h], 8, xg_cons)
        nseq, L = (4, 256) if g == 0 else (1, 1024)
        for t in range(2):
            xv = xr.ap[:, t, :].rearrange("p (s q) -> p s q", s=nseq)
            cv = xc.ap[:, t, :].rearrange("p (s q) -> p s q", s=nseq)
            self.ts("dve", cv, xv, lc.ap[:, t, 2:3], lc.ap[:, t, 4:5], ALU.mult, ALU.add, [xr, lc], [xc])
            for k in (0, 1, 3):
                sh = k - 2
                lo, hi = max(0, -sh), L - max(0, sh)
                self.stt(cv[:, :, lo:hi], xv[:, :, lo + sh:hi + sh], lc.ap[:, t, k:k + 1], cv[:, :, lo:hi], ALU.mult, ALU.add, [xr, lc, xc], [xc])
            self.cp("act", xcb.ap[:, t, :], xc.ap[:, t, :], [xc], [xcb])
        r_ = ar.alloc([128, 512], F32)
        i_ = ar.alloc([128, 512], F32)
        th = ar.alloc([128, 512], F32)
        e2 = ar.alloc([128, 512], F32)
        a_sb = ar.alloc([128, TG], F32)
        b_sb = ar.alloc([128, TG], F32)
        hs = [ar.alloc([128, TG], F32) for _ in range(2)]
        for t in range(2):
            for d in range(2):
                rev = (d == 1)
                for tb in range(2):
                    sl = slice(tb * 512, (tb + 1) * 512)
                    pr = self.bank()
                    pi = self.bank()
                    fw.mm([lambda h, pr=pr, d=d, t=t, sl=sl: h.matmul(pr.ap, lhsT=self.lruw.ap[:, 0 * 4 + d * 2 + t, :], rhs=xcb.ap[:, t, sl], start=True, stop=True)],
                          reads=[self.lruw, xcb], writes=[pr])
                    fw.mm([lambda h, pi=pi, d=d, t=t, sl=sl: h.matmul(pi.ap, lhsT=self.lruw.ap[:, 1 * 4 + d * 2 + t, :], rhs=xcb.ap[:, t, sl], start=True, stop=True)],
                          reads=[self.lruw, xcb], writes=[pi])
                    self.act(r_.ap, pr.ap, AF.Sigmoid, [pr, lc], [r_], bias=lc.ap[:, t, 5 + d:6 + d])
                    self.act(i_.ap, pi.ap, AF.Sigmoid, [pi, lc], [i_], bias=lc.ap[:, t, 7 + d:8 + d])
                    self.act(a_sb.ap[:, sl], r_.ap, AF.Exp, [r_, lc], [a_sb], scale=lc.ap[:, t, 9 + d:10 + d])
                    self.act(th.ap, r_.ap, AF.Tanh, [r_, lc], [th], scale=lc.ap[:, t, 11 + d:12 + d])
                    self.act(e2.ap, r_.ap, AF.Exp, [r_, lc], [e2], scale=lc.ap[:, t, 13 + d:14 + d])
                    self.stt(e2.ap, e2.ap, 1.0, th.ap, ALU.add, ALU.mult, [e2, th], [e2])
                    self.act(e2.ap, e2.ap, AF.Sqrt, [e2], [e2])
                    self.tt("dve", i_.ap, i_.ap, xc.ap[:, t, sl], ALU.mult, [i_, xc], [i_])
                    self.tt("dve", b_sb.ap[:, sl], e2.ap, i_.ap, ALU.mult, [e2, i_], [b_sb])
                hd = hs[d]
                for s in range(nseq):
                    sq = slice(s * L, (s + 1) * L)
                    av, bv, hv = a_sb.ap[:, sq], b_sb.ap[:, sq], hd.ap[:, sq]
                    if rev:
                        av, bv, hv = av[:, ::-1], bv[:, ::-1], hv[:, ::-1]
                    init = 0.0 if g == 0 else self.lrust0.ap[:, d, t:t + 1]
                    rd = [a_sb, b_sb] + ([] if g == 0 else [self.lrust0])
                    fw.op("dve", lambda h, av=av, bv=bv, hv=hv, init=init: h.tensor_tensor_scan(out=hv, data0=av, data1=bv, initial=init, op0=ALU.mult, op1=ALU.add),
                          reads=rd, writes=[hd])
                    if g == 0:
                        self.cp("dve", self.lrust.ap[:, s, d, t:t + 1], hv[:, L - 1:L], [hd], [self.lrust])
            self.tt("dve", hs[0].ap, hs[0].ap, hs[1].ap, ALU.add, [hs[0], hs[1]], [hs[0]])
            self.tt("dve", lruy.ap[:, t, :], hs[0].ap, gg.ap[:, t, :], ALU.mult, [hs[0], gg], [lruy])
        if g == 0:
            sem = Buf("lruout")
            for s in range(4):
                for d in range(2):
                    dst = O["nlru"][s, l, d, :].rearrange("(t p) -> p t", p=128)
                    src = self.lrust.ap[:, s, d, :]
                    dd = fw.dma("sp", lambda h, src=src, dst=dst: h.dma_start(out=dst, in_=src, allow_slow_non_contiguous=True), sem, reads=[self.lrust])
                    fw.final.append(dd)

    def merge(self, l, g, attnT, s5y, lruy):
        fw, ar, I = self.fw, self.ar, self.I
        merged = ar.alloc([128, 8, TG], BF16)
        sig = [ar.alloc([128, 512], F32) for _ in range(2)]
        pr = [ar.alloc([128, 512], F32) for _ in range(3)]
        si = [0]

        def br_rhs(kc, sl):
            if kc < 4:
                return attnT.ap[:, kc, sl]
            if kc < 6:
                return s5y.ap[:, kc - 4, sl]
            return lruy.ap[:, kc - 6, sl]

        for sp in range(4):
            c0 = sp * 256
            wb = self.wslice([(0, 4, 256, I["w_br_attn"][l][:, c0:c0 + 256]), (4 * 256, 2, 256, I["w_br_s5"][l][:, c0:c0 + 256]),
                              (6 * 256, 2, 256, I["w_br_lru"][l][:, c0:c0 + 256])])
            wbv = wb.ap.rearrange("p (k n) -> p k n", k=8)
            wg = [self.wslice([(0, 8, 256, I["w_in"][l][:, 2304 + b * 1024 + c0:2304 + b * 1024 + c0 + 256])]) for b in range(3)]
            for t in range(2):
                ft = sp * 2 + t
                for tb in range(2):
                    sl = slice(tb * 512, (tb + 1) * 512)
                    for b, (k0, k1) in enumerate(((0, 4), (4, 6), (6, 8))):
                        gb = self.bank()
                        wgv = wg[b].ap.rearrange("p (k n) -> p k n", k=8)
                        fw.mm([lambda h, gb=gb, wgv=wgv, kc=kc, t=t, tb=tb: h.matmul(gb.ap, lhsT=wgv[:, kc, t * 128:(t + 1) * 128], rhs=self.h_rhs(kc, tb),
                                                                                     start=(kc == 0), stop=(kc == 7)) for kc in range(8)],
                              reads=[wg[b], self.h], writes=[gb])
                        bb = self.bank()
                        fw.mm([lambda h, bb=bb, kc=kc, t=t, sl=sl, k0=k0, k1=k1: h.matmul(bb.ap, lhsT=wbv[:, kc, t * 128:(t + 1) * 128], rhs=br_rhs(kc, sl),
                                                                                         start=(kc == k0), stop=(kc == k1 - 1)) for kc in range(k0, k1)],
                              reads=[wb, attnT, s5y, lruy], writes=[bb])
                        sg = sig[si[0] % 2]
                        si[0] += 1
                        self.act(sg.ap, gb.ap, AF.Sigmoid, [gb], [sg])
                        self.tt("dve", pr[b].ap, bb.ap, sg.ap, ALU.mult, [bb, sg], [pr[b]])
                    self.tt("dve", pr[0].ap, pr[0].ap, pr[1].ap, ALU.add, [pr[0], pr[1]], [pr[0]])
                    self.tt("dve", merged.ap[:, ft, sl], pr[0].ap, pr[2].ap, ALU.add, [pr[0], pr[2]], [merged])
                    self.tick()

        def out_cons(ft, tb, bk):
            sl = slice(tb * 512, (tb + 1) * 512)
            xv = self.x.ap[:, g, ft, sl]
            self.stt(xv, bk.ap, self.mod.ap[:, l, 2 * 8 + ft, g:g + 1], xv, ALU.mult, ALU.add, [bk, self.mod, self.x], [self.x])
            self.tick()

        self.proj_fm(lambda c0, n: [(0, 8, n, I["w_out"][l][:, c0:c0 + n])], 8,
                     lambda kc, tb: merged.ap[:, kc, tb * 512:(tb + 1) * 512], [merged], 8, out_cons)

    def ffn(self, l, g):
        fw, ar, I = self.fw, self.ar, self.I
        gg = ar.alloc([128, 22, TG], BF16)
        a_sb = [ar.alloc([128, TG], F32) for _ in range(2)]
        c_sb = [ar.alloc([128, TG], F32) for _ in range(2)]
        gl = [ar.alloc([128, TG], BF16) for _ in range(2)]
        nseq, L = (4, 256) if g == 0 else (1, 1024)
        fc = self.ffcols
        it = 0
        for sp in range(11):
            wa = self.wslice([(0, 8, 256, I["ffn_w_up"][l][:, sp * 256:sp * 256 + 256])])
            wb = self.wslice([(0, 8, 256, I["ffn_w_up"][l][:, 2816 + sp * 256:2816 + sp * 256 + 256])])
            wav = wa.ap.rearrange("p (k n) -> p k n", k=8)
            wbv = wb.ap.rearrange("p (k n) -> p k n", k=8)
            for t in range(2):
                ft = sp * 2 + t
                a_, c_, g_ = a_sb[it % 2], c_sb[it % 2], gl[it % 2]
                it += 1
                for tb in range(2):
                    bk = self.bank()
                    fw.mm([lambda h, bk=bk, kc=kc, t=t, tb=tb: h.matmul(bk.ap, lhsT=wav[:, kc, t * 128:(t + 1) * 128], rhs=self.h_rhs(kc, tb),
                                                                       start=(kc == 0), stop=(kc == 7)) for kc in range(8)], reads=[wa, self.h], writes=[bk])
                    self.cp("act", a_.ap[:, tb * 512:(tb + 1) * 512], bk.ap, [bk], [a_])
                av = a_.ap.rearrange("p (s q) -> p s q", s=nseq)
                cv = c_.ap.rearrange("p (s q) -> p s q", s=nseq)
                self.act(cv, av, AF.Identity, [a_, fc], [c_], bias=fc.ap[:, ft, 3:4], scale=fc.ap[:, ft, 1:2])
                self.stt(cv[:, :, 1:L], av[:, :, 0:L - 1], fc.ap[:, ft, 0:1], cv[:, :, 1:L], ALU.mult, ALU.add, [a_, fc, c_], [c_])
                self.stt(cv[:, :, 0:L - 1], av[:, :, 1:L], fc.ap[:, ft, 2:3], cv[:, :, 0:L - 1], ALU.mult, ALU.add, [a_, fc, c_], [c_])
                self.act(g_.ap, c_.ap, AF.Gelu_apprx_tanh, [c_], [g_])
                for tb in range(2):
                    sl = slice(tb * 512, (tb + 1) * 512)
                    bk = self.bank()
                    fw.mm([lambda h, bk=bk, kc=kc, t=t, tb=tb: h.matmul(bk.ap, lhsT=wbv[:, kc, t * 128:(t + 1) * 128], rhs=self.h_rhs(kc, tb),
                                                                       start=(kc == 0), stop=(kc == 7)) for kc in range(8)], reads=[wb, self.h], writes=[bk])
                    self.tt("dve", gg.ap[:, ft, sl], bk.ap, g_.ap[:, sl], ALU.mult, [bk, g_], [gg])
                self.tick()
        for ft in range(8):
            w0 = self.wslice([(0, 11, 128, I["ffn_w_down"][l][0:1408, ft * 128:(ft + 1) * 128])])
            w1 = self.wslice([(0, 11, 128, I["ffn_w_down"][l][1408:2816, ft * 128:(ft + 1) * 128])])
            wv = [w0.ap[:, 0:1408].rearrange("p (k n) -> p k n", k=11), w1.ap[:, 0:1408].rearrange("p (k n) -> p k n", k=11)]
            for tb in range(2):
                sl = slice(tb * 512, (tb + 1) * 512)
                bk = self.bank()
                fw.mm([lambda h, bk=bk, kc=kc, sl=sl: h.matmul(bk.ap, lhsT=wv[kc // 11][:, kc % 11, :], rhs=gg.ap[:, kc, sl],
                                                               start=(kc == 0), stop=(kc == 21)) for kc in range(22)], reads=[w0, w1, gg], writes=[bk])
                xv = self.x.ap[:, g, ft, sl]
                self.stt(xv, bk.ap, self.mod.ap[:, l, 5 * 8 + ft, g:g + 1], xv, ALU.mult, ALU.add, [bk, self.mod, self.x], [self.x])
            self.tick()
        self.drain()

    def final_norm_gen(self, g):
        fw, ar, O = self.fw, self.ar, self.O
        rstd = ar.alloc([128, TG], F32)
        yts = [ar.alloc([128, 8, 128], F32) for _ in range(2)]
        self.rms_rstd(g, rstd)
        yield
        dst_t = O["yp" if g == 0 else "ys"]
        for tt in range(8):
            yt = yts[tt % 2]
            tsl = slice(tt * 128, (tt + 1) * 128)
            for kc in range(8):
                self.stt(yt.ap[:, kc, :], self.x.ap[:, g, kc, tsl], self.gcols.ap[:, 4, kc:kc + 1], rstd.ap[:, tsl], ALU.mult, ALU.mult,
                         [self.x, self.gcols, rstd], [yt])
            yield
            for qd in range(4):
                s = self.stg_i % 2
                self.stg_i += 1
                stg = self.stg[s]
                bk = self.bank()
                fw.mm([lambda h, bk=bk, j=j, qd=qd, yt=yt: h.transpose(out=bk.ap[:, j * 128:(j + 1) * 128], in_=yt.ap[:, qd * 2 + j, :],
                                                                       identity=self.ident_f.ap) for j in range(2)], reads=[yt, self.ident_f], writes=[bk])
                self.cp("act" if qd % 2 else "dve", stg.ap, bk.ap[:, 0:256], [bk], [stg])
                dst = dst_t[tt * 128:(tt + 1) * 128, qd * 256:(qd + 1) * 256]
                dd = fw.dma("sp", lambda h, stg=stg, dst=dst: h.dma_start(out=dst, in_=stg.ap), self.stg_sem[s], reads=[stg])
                fw.final.append(dd)
                if qd == 1:
                    yield
            yield

    def final_norm(self, g):
        fw, ar, O = self.fw, self.ar, self.O
        m = ar.mark()
        rstd = ar.alloc([128, TG], F32)
        self.rms_rstd(g, rstd)
        y = ar.alloc([128, 8, TG], F32)
        for kc in range(8):
            self.stt(y.ap[:, kc, :], self.x.ap[:, g, kc, :], self.gcols.ap[:, 4, kc:kc + 1], rstd.ap, ALU.mult, ALU.mult, [self.x, self.gcols, rstd], [y])
        dst_t = O["yp" if g == 0 else "ys"]
        for tt in range(8):
            for qd in range(4):
                s = self.stg_i % 2
                self.stg_i += 1
                stg = self.stg[s]
                bk = self.bank()
                fw.mm([lambda h, bk=bk, j=j, qd=qd, tt=tt: h.transpose(out=bk.ap[:, j * 128:(j + 1) * 128], in_=y.ap[:, qd * 2 + j, tt * 128:(tt + 1) * 128],
                                                                       identity=self.ident_f.ap) for j in range(2)], reads=[y, self.ident_f], writes=[bk])
                self.cp("act" if qd % 2 else "dve", stg.ap, bk.ap[:, 0:256], [bk], [stg])
                dst = dst_t[tt * 128:(tt + 1) * 128, qd * 256:(qd + 1) * 256]
                dd = fw.dma("sp", lambda h, stg=stg, dst=dst: h.dma_start(out=dst, in_=stg.ap), self.stg_sem[s], reads=[stg])
                fw.final.append(dd)
        ar.release(m)


_W_KEYS = ["w_ada", "b_ada", "g_norm1", "g_norm2", "w_in", "rpb", "s5_lam_re", "s5_lam_im", "s5_log_step", "s5_b_re", "s5_b_im",
           "s5_c_re", "s5_c_im", "s5_d", "s5_w_glu", "lru_conv_w", "lru_conv_b", "lru_w_a", "lru_b_a", "lru_w_x", "lru_b_x", "lru_lam",
           "w_br_attn", "w_br_s5", "w_br_lru", "w_out", "ffn_w_up", "ffn_conv_w", "ffn_conv_b", "ffn_w_down", "g_final"]


def make_in_maps(inp):
    f = lambda a: np.ascontiguousarray(np.asarray(a, dtype=np.float32))
    shared = {k: f(inp[k]) for k in _W_KEYS}
    maps = []
    for i in range(NCORES):
        m = dict(shared)
        m["xp"] = f(inp["x_prompt"][4 * i:4 * i + 4]).reshape(TG, D)
        m["xs"] = f(inp["x_sample"][i]).reshape(TG, D)
        m["ck"] = f(inp["cache_k"][i]).reshape(DEPTH, 512, 512)
        m["cv"] = f(inp["cache_v"][i]).reshape(DEPTH, 512, 512)
        m["s5r"] = f(inp["state_s5_re"][i]).reshape(DEPTH, 2, 1024)
        m["s5i"] = f(inp["state_s5_im"][i]).reshape(DEPTH, 2, 1024)
        m["slru"] = f(inp["state_lru"][i]).reshape(DEPTH, 2, 256)
        m["cvec"] = f(np.stack([np.asarray(inp["c_ctx"]), np.asarray(inp["c"])[i]], axis=0))
        maps.append(m)
    return maps


def assemble(results):
    cat = lambda k: np.concatenate([np.asarray(r[k]) for r in results], axis=0)
    y_prompt = cat("yp").reshape(32, 256, D)
    y_sample = cat("ys").reshape(8, 1024, D)
    new_k = cat("nk").reshape(32, DEPTH, 256, 8, 64)
    new_v = cat("nv").reshape(32, DEPTH, 256, 8, 64)
    ns5r = cat("ns5r").reshape(32, DEPTH, 2, 16, 64)
    ns5i = cat("ns5i").reshape(32, DEPTH, 2, 16, 64)
    nlru = cat("nlru").reshape(32, DEPTH, 2, 256)
    return tuple(np.ascontiguousarray(a, dtype=np.float32) for a in (y_prompt, y_sample, new_k, new_v, ns5r, ns5i, nlru))


def kernel(**inputs):
    nc = Builder().build()
    in_maps = make_in_maps(inputs)
    res = run_bass_kernel_spmd(nc, in_maps, core_ids=list(range(NCORES)))
    return assemble(res.results)


def debug_run(inputs, stage, ncores=1, trace=False):
    b = Builder(stage=stage)
    nc = b.build()
    in_maps = make_in_maps(inputs)[:ncores]
    res = run_bass_kernel_spmd(nc, in_maps, core_ids=list(range(ncores)), trace=trace)
    if trace:
        print("EXEC_NS", stage, res.exec_time_ns)
    return b, res.results
```

```python
import math
import types as _types
import numpy as np
from contextlib import ExitStack
import concourse.bass as bass
import concourse.mybir as mybir
from concourse.bass_utils import run_bass_kernel_spmd

F32 = mybir.dt.float32
BF16 = mybir.dt.bfloat16
I32 = mybir.dt.int32
AF = mybir.ActivationFunctionType
ALU = mybir.AluOpType

NCORES = 8
D = 1024
TG = 1024
DEPTH = 2
NEG = -30000.0
EPS = 1e-6
IN_W = 5376
PAGE = 512


class Buf:
    __slots__ = ("name", "lw", "rd", "dsem", "dcnt", "excl")

    def __init__(self, name, excl=False):
        self.name = name
        self.excl = excl
        self.lw = None
        self.rd = []
        self.dsem = None
        self.dcnt = 0


class Reg:
    __slots__ = ("ap", "bufs", "tag")

    def __init__(self, ap, bufs, tag=None):
        self.ap = ap
        self.bufs = bufs
        self.tag = tag

    def __getitem__(self, k):
        return self.ap[k]


def _freeze(f):
    if getattr(f, "__closure__", None) is None:
        return f
    cells = []
    for c in f.__closure__:
        try:
            cells.append(_types.CellType(c.cell_contents))
        except ValueError:
            cells.append(c)
    return _types.FunctionType(f.__code__, f.__globals__, f.__name__, f.__defaults__, tuple(cells))


class Eng:
    def __init__(self, name):
        self.key = name
        self.cnt = 0
        self.seen = {}
        self.prog = []


class FW:
    def __init__(self, nc, stack):
        self.nc = nc
        self.stack = stack
        self.sems = {}
        self.E = {}
        for n in ("pe", "act", "dve", "pool", "sp"):
            self.sems[n] = stack.enter_context(nc.semaphore("s_" + n))
            self.E[n] = Eng(n)
        self.ndsem = 0
        self.dsem_free = []
        self.final = []

    def _expand(self, lst):
        out = []
        for r in lst:
            if isinstance(r, Buf):
                out.append(r)
            else:
                out.extend(r.bufs)
        return out

    def _need(self, eng, dep, same_ok):
        key, val, clock = dep
        if same_ok and key == eng.key:
            return
        if eng.seen.get(key, 0) >= val:
            return
        eng.prog.append(("wait", key, val))
        eng.seen[key] = val
        for k, v in clock.items():
            if eng.seen.get(k, 0) < v:
                eng.seen[k] = v

    def _deps(self, eng, reads, writes):
        for b in reads:
            if b.lw is not None:
                self._need(eng, b.lw, False)
            if b.excl:
                for r in b.rd:
                    self._need(eng, r, True)
        for b in writes:
            if b.lw is not None:
                self._need(eng, b.lw, True)
            for r in b.rd:
                self._need(eng, r, True)

    def _commit(self, dep, reads, writes):
        for b in reads:
            b.rd.append(dep)
        for b in writes:
            b.lw = dep
            b.rd = []

    def op(self, en, fn, reads=(), writes=()):
        eng = self.E[en]
        reads = self._expand(reads)
        writes = self._expand(writes)
        self._deps(eng, reads, writes)
        eng.cnt += 1
        eng.prog.append(("ins", _freeze(fn), eng.key, 1))
        clock = dict(eng.seen)
        clock[eng.key] = eng.cnt
        dep = (eng.key, eng.cnt, clock)
        self._commit(dep, reads, writes)
        return dep

    def mm(self, fns, reads=(), writes=()):
        eng = self.E["pe"]
        reads = self._expand(reads)
        writes = self._expand(writes)
        self._deps(eng, reads, writes)
        for f in fns[:-1]:
            eng.prog.append(("ins", _freeze(f), None, 0))
        eng.cnt += 1
        eng.prog.append(("ins", _freeze(fns[-1]), eng.key, 1))
        clock = dict(eng.seen)
        clock[eng.key] = eng.cnt
        dep = (eng.key, eng.cnt, clock)
        self._commit(dep, reads, writes)
        return dep

    def dma(self, qn, fn, dbuf, reads=(), writes=()):
        eng = self.E[qn]
        reads = self._expand(reads)
        writes = self._expand(writes)
        self._deps(eng, reads, writes)
        if dbuf.dsem is None:
            self.ndsem += 1
            dbuf.dsem = self.stack.enter_context(self.nc.semaphore(f"d{self.ndsem}"))
        if dbuf.dcnt:
            self._need(eng, ("D%d" % id(dbuf), dbuf.dcnt, {}), False)
        dbuf.dcnt += 16
        key = "D%d" % id(dbuf)
        self.sems[key] = dbuf.dsem
        eng.prog.append(("ins", _freeze(fn), key, 16))
        dep = (key, dbuf.dcnt, dict(eng.seen))
        self._commit(dep, reads, writes)
        return dep

    def emit(self):
        nc = self.nc
        sems = self.sems
        for d in self.final:
            self._need(self.E["sp"], d, False)

        def replay(eng, h):
            for it in eng.prog:
                if it[0] == "wait":
                    h.wait_ge(sems[it[1]], it[2])
                else:
                    ins = it[1](h)
                    if it[2] is not None:
                        ins.then_inc(sems[it[2]], it[3])

        with nc.Block() as block:
            @block.sync
            def _(h):
                replay(self.E["sp"], h)

            @block.scalar
            def _(h):
                replay(self.E["act"], h)

            @block.vector
            def _(h):
                replay(self.E["dve"], h)

            @block.gpsimd
            def _(h):
                replay(self.E["pool"], h)

            @block.tensor
            def _(h):
                replay(self.E["pe"], h)


class Arena:
    def __init__(self, fw, nbytes):
        self.fw = fw
        self.nbytes = nbytes
        self.t = fw.stack.enter_context(fw.nc.sbuf_tensor("arena", [128, nbytes // 2], BF16))
        self.pages = [Buf(f"pg{i}") for i in range((nbytes + PAGE - 1) // PAGE)]
        self.top = 0
        self.peak = 0

    def alloc(self, shape, dt, align=64):
        esz = 4 if dt in (F32, I32) else 2
        n = 1
        for s in shape[1:]:
            n *= s
        nb = n * esz
        if nb >= PAGE:
            align = max(align, PAGE)
        off = (self.top + align - 1) // align * align
        assert off + nb <= self.nbytes, f"arena overflow: need {off + nb} have {self.nbytes}"
        self.top = off + nb
        self.peak = max(self.peak, self.top)
        ap = self.t[0:shape[0], off // 2:(off + nb) // 2]
        if esz == 4:
            ap = ap.bitcast(dt)
        if len(shape) > 2:
            names = " ".join(f"d{i}" for i in range(1, len(shape)))
            kw = {f"d{i}": shape[i] for i in range(1, len(shape))}
            ap = ap.rearrange(f"p ({names}) -> p {names}", **kw)
        pages = self.pages[off // PAGE:(off + nb - 1) // PAGE + 1]
        return Reg(ap, pages)

    def mark(self):
        return self.top

    def release(self, m):
        self.top = m


def na_valid(kr, qr):
    w0 = min(max(qr - 4, 0), 8)
    return w0 <= kr < w0 + 8


class _Stop(Exception):
    pass


class Builder:
    def __init__(self, debug=False, stage=None):
        self.debug = debug
        self.stage = stage
        self.dbg_outs = []
        self.nc = bass.Bass("TRN2", target_bir_lowering=False)
        self.stack = ExitStack()
        self.dram = {}

    def din(self, name, shape):
        t = self.nc.dram_tensor(name, list(shape), F32, kind="ExternalInput")
        self.dram[name] = t
        return t.ap()

    def dout(self, name, shape):
        t = self.nc.dram_tensor(name, list(shape), F32, kind="ExternalOutput")
        self.dram[name] = t
        return t.ap()

    def build(self):
        nc = self.nc
        with self.stack as st:
            fw = self.fw = FW(nc, st)
            I = self.I = {}
            O = self.O = {}
            I["xp"] = self.din("xp", [TG, D])
            I["xs"] = self.din("xs", [TG, D])
            I["ck"] = self.din("ck", [DEPTH, 512, 512])
            I["cv"] = self.din("cv", [DEPTH, 512, 512])
            I["s5r"] = self.din("s5r", [DEPTH, 2, 1024])
            I["s5i"] = self.din("s5i", [DEPTH, 2, 1024])
            I["slru"] = self.din("slru", [DEPTH, 2, 256])
            I["cvec"] = self.din("cvec", [2, D])
            wshapes = {
                "w_ada": [2, D, 6 * D], "b_ada": [2, 6 * D], "g_norm1": [2, D], "g_norm2": [2, D],
                "w_in": [2, D, IN_W], "rpb": [2, 8, 15, 31],
                "s5_lam_re": [2, 2, 16, 64], "s5_lam_im": [2, 2, 16, 64], "s5_log_step": [2, 2, 16],
                "s5_b_re": [2, 16, 64, 16], "s5_b_im": [2, 16, 64, 16],
                "s5_c_re": [2, 2, 16, 16, 64], "s5_c_im": [2, 2, 16, 16, 64],
                "s5_d": [2, 256], "s5_w_glu": [2, 256, 256],
                "lru_conv_w": [2, 4, 256], "lru_conv_b": [2, 256],
                "lru_w_a": [2, 2, 4, 64, 64], "lru_b_a": [2, 2, 256],
                "lru_w_x": [2, 2, 4, 64, 64], "lru_b_x": [2, 2, 256], "lru_lam": [2, 2, 256],
                "w_br_attn": [2, 512, D], "w_br_s5": [2, 256, D], "w_br_lru": [2, 256, D],
                "w_out": [2, D, D], "ffn_w_up": [2, D, 5632], "ffn_conv_w": [2, 3, 2816],
                "ffn_conv_b": [2, 2816], "ffn_w_down": [2, 2816, D], "g_final": [D],
            }
            self.wshapes = wshapes
            for k, s in wshapes.items():
                I[k] = self.din(k, s)
            O["yp"] = self.dout("yp", [TG, D])
            O["ys"] = self.dout("ys", [TG, D])
            O["nk"] = self.dout("nk", [4, DEPTH, 256, 512])
            O["nv"] = self.dout("nv", [4, DEPTH, 256, 512])
            O["ns5r"] = self.dout("ns5r", [4, DEPTH, 2, 1024])
            O["ns5i"] = self.dout("ns5i", [4, DEPTH, 2, 1024])
            O["nlru"] = self.dout("nlru", [4, DEPTH, 2, 256])

            self.ar = Arena(fw, 207 * 1024)
            ps_t = st.enter_context(nc.psum_tensor("psum", [128, 8, 512], F32))
            self.banks = [Reg(ps_t[:, i, :], [Buf(f"bank{i}", excl=True)], i) for i in range(8)]
            self.bank_i = 0
            self.held = set()
            try:
                self.setup()
                self.chk("setup")
                for l in range(DEPTH):
                    if l == 0:
                        self.bg = self.mod_gen(0, look=4, s0=0, s1=8, doA=(True, False))
                        self.drain()
                    self.chk(f"prep{l}")
                    for g in range(2):
                        self.group_layer(l, g)
                        self.chk(f"gl{l}{g}")
                self.final_norm(1)
            except _Stop:
                pass
            fw.emit()
        return nc

    def chk(self, name):
        if self.stage == name:
            raise _Stop()

    def dbg(self, name, reg, ap, shape):
        t = self.nc.dram_tensor("dbg_" + name, list(shape), F32, kind="ExternalOutput").ap()
        d = self.fw.dma("sp", lambda h: h.dma_start(out=t, in_=ap), Buf("dbg_" + name), reads=[reg])
        self.fw.final.append(d)
        self.dbg_outs.append("dbg_" + name)

    def bank(self, hold=False):
        while True:
            i = self.bank_i % 8
            self.bank_i += 1
            if i not in self.held:
                break
        if hold:
            self.held.add(i)
        return self.banks[i]

    def unhold(self, bk):
        self.held.discard(bk.tag)

    def col(self, reg, i):
        return reg.ap[:, i:i + 1]

    def setup(self):
        fw, ar, nc, I = self.fw, self.ar, self.nc, self.I
        self.x = ar.alloc([128, 2, 8, TG], F32)
        self.ident_bf = ar.alloc([128, 128], BF16, align=PAGE)
        self.ident_f = ar.alloc([128, 128], F32)
        self.ones_bf = ar.alloc([128, 128], BF16)
        self.epsc = ar.alloc([128, 1], F32)
        self.gcols = ar.alloc([128, 5, 8], F32)
        self.cTb = ar.alloc([128, 8, 2], BF16)
        self.badaT = ar.alloc([128, 2, 48], F32)
        self.mod = ar.alloc([128, 2, 48, 2], F32, align=PAGE)
        self.A1 = ar.alloc([128, 2, 8, 2], F32)
        self.A2 = ar.alloc([128, 2, 8, 2], F32)
        self.s5cols = ar.alloc([128, 16, 8], F32, align=PAGE)
        self.lrucols = ar.alloc([128, 2, 16], F32)
        self.s5d = ar.alloc([128, 2], F32)
        self.ffcols = ar.alloc([128, 22, 4], F32)
        self.lrust0 = ar.alloc([128, 2, 2], F32)
        self.lruw = ar.alloc([128, 8, 128], BF16)
        self.wglu = ar.alloc([128, 2, 256], BF16)
        self.s5w = ar.alloc([128, 16, 4, 128], BF16, align=PAGE)
        self.outst = ar.alloc([128, 4, 2, 8, 2], F32, align=PAGE)
        self.lrust = ar.alloc([128, 4, 2, 2], F32)
        self.slots = [ar.alloc([128, 2048], BF16, align=PAGE) for _ in range(6)]
        self.slot_sem = [Buf(f"slotsem{i}") for i in range(6)]
        self.slot_i = 0
        self.stg = [ar.alloc([128, 256], F32, align=PAGE) for _ in range(2)]
        self.stg_sem = [Buf("stg0"), Buf("stg1")]
        self.stg_i = 0
        self.small_sem = Buf("small")
        self.h = ar.alloc([128, 8, TG], BF16, align=PAGE)
        self.bg = None
        fw.op("pool", lambda h: h.memset(self.epsc.ap, EPS), writes=[self.epsc])
        self.scr_mark = ar.mark()

        idb, idf, ones = self.ident_bf, self.ident_f, self.ones_bf
        fw.op("pool", lambda h: h.memset(idf.ap, 1.0), writes=[idf])
        fw.op("pool", lambda h: h.affine_select(out=idf.ap, in_=idf.ap, pattern=[[-1, 128]], compare_op=ALU.is_equal,
                                                fill=0.0, base=0, channel_multiplier=1), reads=[idf], writes=[idf])
        fw.op("dve", lambda h: h.tensor_copy(out=idb.ap, in_=idf.ap), reads=[idf], writes=[idb])
        fw.op("dve", lambda h: h.memset(ones.ap, 1.0), writes=[ones])

        self.scr_mark = ar.mark()
        self.prep_bg = False
        self.mod_init()
        self.bg = self.layer_prep_gen(0)
        for g, name in enumerate(("xp", "xs")):
            for tt in range(8):
                for qd in range(4):
                    s = self.stg_i % 2
                    self.stg_i += 1
                    stg = self.stg[s]
                    src = I[name][tt * 128:(tt + 1) * 128, qd * 256:(qd + 1) * 256]
                    fw.dma("sp", lambda h, stg=stg, src=src: h.dma_start(out=stg.ap, in_=src), self.stg_sem[s], writes=[stg])
                    bk = self.bank()
                    fw.mm([lambda h, bk=bk, stg=stg, j=j: h.transpose(out=bk.ap[:, j * 128:(j + 1) * 128], in_=stg.ap[:, j * 128:(j + 1) * 128],
                                                                      identity=idf.ap) for j in range(2)], reads=[stg, idf], writes=[bk])
                    dstv = self.x.ap[:, g, qd * 2:qd * 2 + 2, tt * 128:(tt + 1) * 128]
                    srcv = bk.ap[:, 0:256].rearrange("p (j t) -> p j t", j=2)
                    if qd % 2:
                        fw.op("act", lambda h, dstv=dstv, srcv=srcv: h.activation(out=dstv, in_=srcv, func=AF.Copy), reads=[bk], writes=[self.x])
                    else:
                        fw.op("dve", lambda h, dstv=dstv, srcv=srcv: h.tensor_copy(out=dstv, in_=srcv), reads=[bk], writes=[self.x])
                    if g == 0:
                        self.tick()
        self.drain()

    def small_load(self, dst_ap, src_ap, dst_reg):
        return self.sload(dst_ap, src_ap, dst_reg)

    def wslice(self, parts):
        s = self.slot_i % 6
        self.slot_i += 1
        slot = self.slots[s]
        for (off, kc, ncols, src) in parts:
            dst = slot.ap[:, off:off + kc * ncols].rearrange("p (k n) -> p k n", k=kc)
            sv = src.rearrange("(k p) n -> p k n", p=128)
            self.fw.dma("pool", lambda h, dst=dst, sv=sv: h.dma_start(out=dst, in_=sv), self.slot_sem[s], writes=[slot])
        return slot

    def mod_init(self):
        fw, ar, I = self.fw, self.ar, self.I
        m = ar.mark()
        cT = ar.alloc([128, 8, 2], F32)
        for cc in range(2):
            self.small_load(cT.ap[:, :, cc], I["cvec"][cc].rearrange("(k p) -> p k", p=128), cT)
            self.small_load(self.badaT.ap[:, cc, :], I["b_ada"][cc].rearrange("(t p) -> p t", p=128), self.badaT)
            self.small_load(self.gcols.ap[:, cc, :], I["g_norm1"][cc].rearrange("(k p) -> p k", p=128), self.gcols)
            self.small_load(self.gcols.ap[:, 2 + cc, :], I["g_norm2"][cc].rearrange("(k p) -> p k", p=128), self.gcols)
        self.small_load(self.gcols.ap[:, 4, :], I["g_final"].rearrange("(k p) -> p k", p=128), self.gcols)
        fw.op("act", lambda h: h.activation(out=self.cTb.ap, in_=cT.ap, func=AF.Silu), reads=[cT], writes=[self.cTb])
        ar.release(m)

    def mod_gen(self, l, look=2, s0=0, s1=24, doA=(True, True)):
        fw, I = self.fw, self.I
        cTb, badaT = self.cTb, self.badaT
        pend = []
        nxt = s0
        for s in range(s0, s1):
            while nxt < s1 and nxt <= s + look:
                pend.append(self.wslice([(0, 8, 256, I["w_ada"][l][:, nxt * 256:(nxt + 1) * 256])]))
                nxt += 1
            slot = pend.pop(0)
            sv = slot.ap.rearrange("p (k n) -> p k n", k=8)
            bk = self.bank()
            for t in range(2):
                fw.mm([lambda h, bk=bk, sv=sv, t=t, kc=kc: h.matmul(bk.ap[:, t * 2:t * 2 + 2], lhsT=sv[:, kc, t * 128:(t + 1) * 128],
                                                                    rhs=cTb.ap[:, kc, :], start=(kc == 0), stop=(kc == 7)) for kc in range(8)],
                      reads=[slot, cTb], writes=[bk])
            fw.op("dve", lambda h, bk=bk, l=l, s=s: h.tensor_tensor(out=self.mod.ap[:, l, 2 * s:2 * s + 2, :],
                                                                   in0=bk.ap[:, 0:4].rearrange("p (t c) -> p t c", t=2),
                                                                   in1=badaT.ap[:, l, 2 * s:2 * s + 2].unsqueeze(2).to_broadcast([128, 2, 2]),
                                                                   op=ALU.add), reads=[bk, badaT], writes=[self.mod])
            yield
        for ai, (A, comp, gi) in enumerate(((self.A1, 1, 0), (self.A2, 4, 2))):
            if not doA[ai]:
                continue
            fw.op("dve", lambda h, A=A, comp=comp, gi=gi, l=l: h.scalar_tensor_tensor(
                out=A.ap[:, l], in0=self.mod.ap[:, l, comp * 8:comp * 8 + 8, :], scalar=1.0,
                in1=self.gcols.ap[:, gi + l, :].unsqueeze(2).to_broadcast([128, 8, 2]), op0=ALU.add, op1=ALU.mult),
                reads=[self.mod, self.gcols], writes=[A])

    def tick(self):
        if self.bg is not None:
            try:
                next(self.bg)
            except StopIteration:
                self.bg = None

    def drain(self):
        while self.bg is not None:
            self.tick()

    def rms_rstd(self, g, rstd):
        fw, ar = self.fw, self.ar
        m = ar.mark()
        sq = [ar.alloc([128, 512], BF16) for _ in range(2)]
        tmp = ar.alloc([128, 512], F32)
        for tb in range(2):
            bk = self.bank()
            for kc in range(8):
                q = sq[kc % 2]
                fw.op("act", lambda h, q=q, kc=kc, tb=tb: h.activation(out=q.ap, in_=self.x.ap[:, g, kc, tb * 512:(tb + 1) * 512], func=AF.Square),
                      reads=[self.x], writes=[q])
                fw.mm([lambda h, bk=bk, q=q, kc=kc: h.matmul(bk.ap, lhsT=self.ones_bf.ap, rhs=q.ap, start=(kc == 0), stop=(kc == 7))],
                      reads=[q, self.ones_bf], writes=[bk])
            fw.op("act", lambda h, bk=bk: h.activation(out=tmp.ap, in_=bk.ap, func=AF.Sqrt, scale=1.0 / D, bias=self.epsc.ap[:, 0:1]),
                  reads=[bk, self.epsc], writes=[tmp])
            fw.op("dve", lambda h, tb=tb: h.reciprocal(out=rstd.ap[:, tb * 512:(tb + 1) * 512], in_=tmp.ap), reads=[tmp], writes=[rstd])
        ar.release(m)

    def norm_mod(self, l, g, A, shcomp):
        fw, ar = self.fw, self.ar
        m = ar.mark()
        rstd = ar.alloc([128, TG], F32)
        self.rms_rstd(g, rstd)
        tmps = [ar.alloc([128, TG], F32) for _ in range(2)]
        for kc in range(8):
            t = tmps[kc % 2]
            fw.op("dve", lambda h, t=t, kc=kc: h.scalar_tensor_tensor(out=t.ap, in0=self.x.ap[:, g, kc, :], scalar=A.ap[:, l, kc, g:g + 1],
                                                                      in1=rstd.ap, op0=ALU.mult, op1=ALU.mult),
                  reads=[self.x, A, rstd], writes=[t])
            fw.op("act", lambda h, t=t, kc=kc: h.activation(out=self.h.ap[:, kc, :], in_=t.ap, func=AF.Identity,
                                                            bias=self.mod.ap[:, l, shcomp * 8 + kc, g:g + 1], scale=1.0),
                  reads=[t, self.mod], writes=[self.h])
        ar.release(m)

    def proj_fm(self, src_cols_fn, ntiles, rhs, rhs_regs, kcs, consume):
        fw = self.fw
        for sp in range(0, ntiles, 2):
            nt = min(2, ntiles - sp)
            slot = self.wslice(src_cols_fn(sp * 128, nt * 128))
            sv = slot.ap[:, 0:kcs * nt * 128].rearrange("p (k n) -> p k n", k=kcs)
            for t in range(nt):
                for tb in range(2):
                    bk = self.bank()
                    fw.mm([lambda h, bk=bk, sv=sv, t=t, tb=tb, kc=kc: h.matmul(bk.ap, lhsT=sv[:, kc, t * 128:(t + 1) * 128], rhs=rhs(kc, tb),
                                                                                start=(kc == 0), stop=(kc == kcs - 1)) for kc in range(kcs)],
                          reads=[slot] + rhs_regs, writes=[bk])
                    consume(sp + t, tb, bk)

    def win_cols(self, l, base):
        return lambda c0, n: [(0, 8, n, self.I["w_in"][l][:, base + c0:base + c0 + n])]

    def h_rhs(self, kc, tb):
        return self.h.ap[:, kc, tb * 512:(tb + 1) * 512]

    def tt(self, en, out, in0, in1, op, R, W):
        self.fw.op(en, lambda h: h.tensor_tensor(out=out, in0=in0, in1=in1, op=op), reads=R, writes=W)

    def ts(self, en, out, in0, s1, s2, op0, op1, R, W):
        if s2 is None:
            self.fw.op(en, lambda h: h.tensor_scalar(out=out, in0=in0, scalar1=s1, scalar2=None, op0=op0), reads=R, writes=W)
        else:
            self.fw.op(en, lambda h: h.tensor_scalar(out=out, in0=in0, scalar1=s1, scalar2=s2, op0=op0, op1=op1), reads=R, writes=W)

    def stt(self, out, in0, sc, in1, op0, op1, R, W, en="dve"):
        self.fw.op(en, lambda h: h.scalar_tensor_tensor(out=out, in0=in0, scalar=sc, in1=in1, op0=op0, op1=op1), reads=R, writes=W)

    def act(self, out, in_, func, R, W, bias=None, scale=None):
        kw = {}
        if bias is not None:
            kw["bias"] = bias
        if scale is not None:
            kw["scale"] = scale
        self.fw.op("act", lambda h: h.activation(out=out, in_=in_, func=func, **kw), reads=R, writes=W)

    def cp(self, en, out, in_, R, W):
        if en == "act":
            self.fw.op("act", lambda h: h.activation(out=out, in_=in_, func=AF.Copy), reads=R, writes=W)
        else:
            self.fw.op(en, lambda h: h.tensor_copy(out=out, in_=in_), reads=R, writes=W)

    def sincos(self, y, n, cos_out, sin_out, Wc, Ws):
        ar = self.ar
        MAGIC = 12582912.0
        m = ar.mark()
        kf = ar.alloc([128, n], F32)
        fc = ar.alloc([128, n], F32)
        self.ts("dve", kf.ap, y.ap, 0.25, MAGIC, ALU.add, ALU.add, [y], [kf])
        self.ts("dve", kf.ap, kf.ap, MAGIC, None, ALU.subtract, None, [kf], [kf])
        self.stt(fc.ap, y.ap, 0.25, kf.ap, ALU.add, ALU.subtract, [y, kf], [fc])
        self.act(cos_out, fc.ap, AF.Sin, [fc], [Wc], scale=2.0 * math.pi)
        self.ts("dve", kf.ap, y.ap, MAGIC, None, ALU.add, None, [y], [kf])
        self.ts("dve", kf.ap, kf.ap, MAGIC, None, ALU.subtract, None, [kf], [kf])
        self.tt("dve", y.ap, y.ap, kf.ap, ALU.subtract, [y, kf], [y])
        self.act(sin_out, y.ap, AF.Sin, [y], [Ws], scale=2.0 * math.pi)
        ar.release(m)

    _ssem_i = 0

    def sload(self, dst_ap, src_ap, dst_reg, q="pool"):
        if not hasattr(self, "ssems"):
            self.ssems = [Buf(f"ss{i}") for i in range(8)]
        s = self.ssems[Builder._ssem_i % 8]
        Builder._ssem_i += 1
        if q == "sp":
            return self.fw.dma("sp", lambda h: h.dma_start(out=dst_ap, in_=src_ap, allow_slow_non_contiguous=True), s, writes=[dst_reg])
        return self.fw.dma("pool", lambda h: h.dma_start(out=dst_ap, in_=src_ap, allow_slow_non_contiguous=True), s, writes=[dst_reg])

    def layer_prep_gen(self, l):
        fw, ar, I = self.fw, self.ar, self.I
        m = ar.mark()
        c = self.s5cols
        lam_r = ar.alloc([128, 16], F32)
        lam_i = ar.alloc([128, 16], F32)
        stp = ar.alloc([128, 16], F32)
        t1 = ar.alloc([128, 16], F32)
        t2 = ar.alloc([128, 16], F32)
        t3 = ar.alloc([128, 16], F32)
        yv = ar.alloc([128, 16], F32)
        cs = ar.alloc([128, 16], F32)
        sn = ar.alloc([128, 16], F32)
        nat = ar.alloc([128, 4, 8, 16], F32)
        Cn = [ar.alloc([128, 4, 2, 64], F32) for _ in range(2)]
        ct2 = ar.alloc([128, 128], F32)
        lam = ar.alloc([128, 2, 2], F32)
        xx = ar.alloc([128, 2, 2], F32)
        pp = ar.alloc([128, 2, 2], F32)
        msk = ar.alloc([128, 8, 8], F32)
        Br = ar.alloc([128, 8, 16], F32)
        Bi = ar.alloc([128, 8, 16], F32)
        bb = [ar.alloc([128, 2, 8, 16], F32) for _ in range(2)]
        tA = ar.alloc([128, 2, 8, 16], F32)
        self.sload(lam_r.ap, I["s5_lam_re"][l].rearrange("d (j gl) n -> (gl n) (d j)", gl=2), lam_r)
        self.sload(lam_i.ap, I["s5_lam_im"][l].rearrange("d (j gl) n -> (gl n) (d j)", gl=2), lam_i)
        for gl in range(2):
            src = bass.AP(I["s5_log_step"].tensor, l * 32 + gl, [[0, 64], [2, 16]])
            self.sload(stp.ap[64 * gl:64 * gl + 64, :], src, stp)
        self.sload(c.ap[:, :, 6], I["s5r"][l].rearrange("d (j q) -> q (d j)", q=128), c)
        self.sload(c.ap[:, :, 7], I["s5i"][l].rearrange("d (j q) -> q (d j)", q=128), c)
        yield
        self.act(stp.ap, stp.ap, AF.Exp, [stp], [stp])
        self.tt("dve", t1.ap, lam_r.ap, stp.ap, ALU.mult, [lam_r, stp], [t1])
        self.tt("dve", t2.ap, lam_i.ap, stp.ap, ALU.mult, [lam_i, stp], [t2])
        self.act(c.ap[:, :, 1], t1.ap, AF.Exp, [t1], [c])
        self.ts("dve", yv.ap, t2.ap, 1.0 / (2.0 * math.pi), None, ALU.mult, None, [t2], [yv])
        self.sincos(yv, 16, cs.ap, sn.ap, cs, sn)
        self.cp("dve", c.ap[:, :, 0], yv.ap, [yv], [c])
        self.tt("dve", c.ap[:, :, 2], c.ap[:, :, 1], cs.ap, ALU.mult, [c, cs], [c])
        self.tt("dve", c.ap[:, :, 3], c.ap[:, :, 1], sn.ap, ALU.mult, [c, sn], [c])
        yield
        self.ts("dve", t1.ap, c.ap[:, :, 2], -1.0, None, ALU.add, None, [c], [t1])
        self.tt("dve", t2.ap, lam_r.ap, lam_r.ap, ALU.mult, [lam_r], [t2])
        self.tt("dve", t3.ap, lam_i.ap, lam_i.ap, ALU.mult, [lam_i], [t3])
        self.tt("dve", t2.ap, t2.ap, t3.ap, ALU.add, [t2, t3], [t2])
        fw.op("dve", lambda h: h.reciprocal(out=t2.ap, in_=t2.ap), reads=[t2], writes=[t2])
        self.tt("dve", t3.ap, t1.ap, lam_r.ap, ALU.mult, [t1, lam_r], [t3])
        self.tt("dve", yv.ap, c.ap[:, :, 3], lam_i.ap, ALU.mult, [c, lam_i], [yv])
        self.tt("dve", t3.ap, t3.ap, yv.ap, ALU.add, [t3, yv], [t3])
        self.tt("dve", c.ap[:, :, 4], t3.ap, t2.ap, ALU.mult, [t3, t2], [c])
        self.tt("dve", t3.ap, c.ap[:, :, 3], lam_r.ap, ALU.mult, [c, lam_r], [t3])
        self.tt("dve", yv.ap, t1.ap, lam_i.ap, ALU.mult, [t1, lam_i], [yv])
        self.tt("dve", t3.ap, t3.ap, yv.ap, ALU.subtract, [t3, yv], [t3])
        self.tt("dve", c.ap[:, :, 5], t3.ap, t2.ap, ALU.mult, [t3, t2], [c])

        yield
        fw.op("pool", lambda h: h.memset(msk.ap, 1.0), writes=[msk])
        for gl in range(2):
            fw.op("pool", lambda h, gl=gl: h.affine_select(out=msk.ap[64 * gl:64 * gl + 64], in_=msk.ap[64 * gl:64 * gl + 64],
                                                           pattern=[[0, 2], [-2, 4], [1, 8]], compare_op=ALU.is_equal, fill=0.0,
                                                           base=-gl, channel_multiplier=0), reads=[msk], writes=[msk])
        self.sload(Br.ap, I["s5_b_re"][l].rearrange("(j gl) n p -> (gl n) j p", gl=2), Br)
        self.sload(Bi.ap, I["s5_b_im"][l].rearrange("(j gl) n p -> (gl n) j p", gl=2), Bi)
        kr = c.ap[:, :, 4].rearrange("p (d j) -> p d j", d=2).unsqueeze(3).to_broadcast([128, 2, 8, 16])
        ki = c.ap[:, :, 5].rearrange("p (d j) -> p d j", d=2).unsqueeze(3).to_broadcast([128, 2, 8, 16])
        Brb = Br.ap.unsqueeze(1).to_broadcast([128, 2, 8, 16])
        Bib = Bi.ap.unsqueeze(1).to_broadcast([128, 2, 8, 16])
        self.tt("dve", bb[0].ap, Brb, kr, ALU.mult, [Br, c], [bb[0]])
        self.tt("dve", tA.ap, Bib, ki, ALU.mult, [Bi, c], [tA])
        self.tt("dve", bb[0].ap, bb[0].ap, tA.ap, ALU.subtract, [bb[0], tA], [bb[0]])
        self.tt("dve", bb[1].ap, Bib, kr, ALU.mult, [Bi, c], [bb[1]])
        self.tt("dve", tA.ap, Brb, ki, ALU.mult, [Br, c], [tA])
        self.tt("dve", bb[1].ap, bb[1].ap, tA.ap, ALU.add, [bb[1], tA], [bb[1]])
        yield
        for ri in range(2):
            for q4 in range(4):
                d, j0 = q4 // 2, (q4 % 2) * 4
                for jj in range(4):
                    j = j0 + jj
                    self.tt("dve", nat.ap[:, jj], bb[ri].ap[:, d, j].unsqueeze(1).to_broadcast([128, 8, 16]),
                            msk.ap[:, j].unsqueeze(2).to_broadcast([128, 8, 16]), ALU.mult, [bb[ri], msk], [nat])
                yield
                bk = self.bank()
                fw.mm([lambda h, bk=bk, jj=jj: h.transpose(out=bk.ap[:, jj * 128:(jj + 1) * 128],
                                                           in_=nat.ap[:, jj].rearrange("p a b -> p (a b)"), identity=self.ident_f.ap)
                       for jj in range(4)], reads=[nat, self.ident_f], writes=[bk])
                self.cp("act", self.s5w.ap[:, d * 8 + j0:d * 8 + j0 + 4, ri, :], bk.ap.rearrange("p (a b) -> p a b", a=4), [bk], [self.s5w])
        for hf in range(2):
            self.sload(Cn[0].ap[:, :, hf, :], I["s5_c_re"][l].rearrange("d g p n -> (d g p) n").rearrange("(q r) n -> r q n", r=128), Cn[0])
            self.sload(Cn[1].ap[:, :, hf, :], I["s5_c_im"][l].rearrange("d g p n -> (d g p) n").rearrange("(q r) n -> r q n", r=128), Cn[1])
        for ri in range(2):
            for q in range(4):
                d, ut = q // 2, q % 2
                bk = self.bank()
                fw.mm([lambda h, bk=bk, ri=ri, q=q: h.matmul(bk.ap[:, 0:128], lhsT=Cn[ri].ap[:, q].rearrange("p a b -> p (a b)"),
                                                             rhs=self.ident_f.ap, start=True, stop=True)],
                      reads=[Cn[ri], self.ident_f], writes=[bk])
                if ri == 0:
                    self.cp("act", ct2.ap, bk.ap[:, 0:128], [bk], [ct2])
                else:
                    self.act(ct2.ap, bk.ap[:, 0:128], AF.Copy, [bk], [ct2], scale=-1.0)
                j0 = ut * 4
                self.tt("dve", self.s5w.ap[:, d * 8 + j0:d * 8 + j0 + 4, 2 + ri, :].rearrange("p j (g q) -> p j g q", g=8),
                        ct2.ap.rearrange("p (g q) -> p g q", g=8).unsqueeze(1).to_broadcast([128, 4, 8, 16]),
                        msk.ap[:, j0:j0 + 4].unsqueeze(3).to_broadcast([128, 4, 8, 16]), ALU.mult, [ct2, msk], [self.s5w])
                yield
        self.sload(self.s5d.ap, I["s5_d"][l].rearrange("(t p) -> p t", p=128), self.s5d)
        yield
        lc = self.lrucols
        for k in range(4):
            self.sload(lc.ap[:, :, k], I["lru_conv_w"][l, k].rearrange("(t p) -> p t", p=128), lc)
        self.sload(lc.ap[:, :, 4], I["lru_conv_b"][l].rearrange("(t p) -> p t", p=128), lc)
        for d in range(2):
            self.sload(lc.ap[:, :, 5 + d], I["lru_b_a"][l, d].rearrange("(t p) -> p t", p=128), lc)
            self.sload(lc.ap[:, :, 7 + d], I["lru_b_x"][l, d].rearrange("(t p) -> p t", p=128), lc)
        for d in range(2):
            self.sload(lam.ap[:, :, d], I["lru_lam"][l, d].rearrange("(t p) -> p t", p=128), lam)
        self.act(xx.ap, lam.ap, AF.Exp, [lam], [xx], scale=-1.0)
        self.ts("dve", pp.ap, xx.ap, -0.25, 1.0 / 3.0, ALU.mult, ALU.add, [xx], [pp])
        self.tt("dve", pp.ap, pp.ap, xx.ap, ALU.mult, [pp, xx], [pp])
        self.ts("dve", pp.ap, pp.ap, -1.0, 0.5, ALU.mult, ALU.add, [pp], [pp])
        self.tt("dve", pp.ap, pp.ap, xx.ap, ALU.mult, [pp, xx], [pp])
        self.ts("dve", pp.ap, pp.ap, -1.0, 1.0, ALU.mult, ALU.add, [pp], [pp])
        self.tt("dve", pp.ap, pp.ap, xx.ap, ALU.mult, [pp, xx], [pp])
        self.ts("dve", lc.ap[:, :, 9:11], pp.ap, -8.0, None, ALU.mult, None, [pp], [lc])
        self.ts("dve", lc.ap[:, :, 11:13], pp.ap, 8.0, None, ALU.mult, None, [pp], [lc])
        self.ts("dve", lc.ap[:, :, 13:15], pp.ap, -16.0, None, ALU.mult, None, [pp], [lc])
        yield
        fw.op("pool", lambda h: h.memset(self.lruw.ap, 0.0), writes=[self.lruw])
        for gi, nm in enumerate(("lru_w_a", "lru_w_x")):
            for d in range(2):
                for t in range(2):
                    for b2 in range(2):
                        idx = gi * 4 + d * 2 + t
                        self.sload(self.lruw.ap[64 * b2:64 * b2 + 64, idx, 64 * b2:64 * b2 + 64], I[nm][l, d, 2 * t + b2], self.lruw, q="pool")
        for d in range(2):
            self.sload(self.lrust0.ap[:, d, :], I["slru"][l, d].rearrange("(t p) -> p t", p=128), self.lrust0)
        self.sload(self.wglu.ap, I["s5_w_glu"][l].rearrange("(k p) n -> p k n", p=128), self.wglu, q="pool")
        yield
        if not self.prep_bg:
            ar.release(m)

    def ffn_prep(self, l):
        I = self.I
        for k in range(3):
            self.sload(self.ffcols.ap[:, :, k], I["ffn_conv_w"][l, k].rearrange("(t p) -> p t", p=128), self.ffcols)
        self.sload(self.ffcols.ap[:, :, 3], I["ffn_conv_b"][l].rearrange("(t p) -> p t", p=128), self.ffcols)

    def group_layer(self, l, g):
        fw, ar, I, O = self.fw, self.ar, self.I, self.O
        if g == 0:
            self.ffn_prep(l)
        self.norm_mod(l, g, self.A1, 0)
        self.chk(f"norm{l}{g}")
        m0 = ar.mark()
        attnT = ar.alloc([128, 4, TG], BF16)
        m1 = ar.mark()
        self.attention(l, g, attnT)
        self.chk(f"attn{l}{g}")
        ar.release(m1)
        s5y = ar.alloc([128, 2, TG], BF16)
        m1 = ar.mark()
        self.s5(l, g, s5y)
        self.chk(f"s5{l}{g}")
        ar.release(m1)
        if g == 0:
            self.store_s5_states(l)
        lruy = ar.alloc([128, 2, TG], BF16)
        m1 = ar.mark()
        self.lru(l, g, lruy)
        self.chk(f"lru{l}{g}")
        ar.release(m1)
        if g == 1 and l + 1 < DEPTH:
            self.prep_bg = True
            self.bg = self.layer_prep_gen(l + 1)
            self.tick()
        if g == 1 and l + 1 == DEPTH:
            self.bg = self.final_norm_gen(0)
            self.tick()
        self.merge(l, g, attnT, s5y, lruy)
        self.drain()
        self.chk(f"merge{l}{g}")
        ar.release(m0)
        self.norm_mod(l, g, self.A2, 3)
        self.ffn(l, g)
        ar.release(m0)

    def attention(self, l, g, attnT):
        fw, ar, I, O = self.fw, self.ar, self.I, self.O
        q_sb = ar.alloc([128, 4, TG], BF16)
        k_sb = ar.alloc([128, 4, TG], BF16)
        v_aug = ar.alloc([128, 8, 8, 66], BF16)
        fw.op("pool", lambda h: h.memset(v_aug.ap[:, :, :, 64:65], 1.0), writes=[v_aug])

        def q_cons(ft, tb, bk):
            self.act(q_sb.ap[:, ft, tb * 512:(tb + 1) * 512], bk.ap, AF.Copy, [bk], [q_sb], scale=0.125)

        def k_cons(ft, tb, bk):
            self.cp("dve", k_sb.ap[:, ft, tb * 512:(tb + 1) * 512], bk.ap, [bk], [k_sb])

        self.proj_fm(self.win_cols(l, 0), 4, self.h_rhs, [self.h], 8, q_cons)
        self.proj_fm(self.win_cols(l, 512), 4, self.h_rhs, [self.h], 8, k_cons)
        self.chk(f"attq{l}{g}")
        for which in ((1, 2) if g == 0 else (2,)):
            for half in range(2):
                slot = self.wslice([(0, 8, 256, I["w_in"][l][:, which * 512 + half * 256: which * 512 + half * 256 + 256])])
                sv = slot.ap.rearrange("p (k n) -> p k n", k=8)
                for tt in range(8):
                    bk = self.bank()
                    fw.mm([lambda h, bk=bk, sv=sv, tt=tt, kc=kc: h.matmul(bk.ap[:, 0:256], lhsT=self.h.ap[:, kc, tt * 128:(tt + 1) * 128], rhs=sv[:, kc, :],
                                                                          start=(kc == 0), stop=(kc == 7)) for kc in range(8)],
                          reads=[slot, self.h], writes=[bk])
                    if which == 2:
                        self.cp("act", v_aug.ap[:, tt, half * 4:half * 4 + 4, 0:64], bk.ap[:, 0:256].rearrange("p (a b) -> p a b", a=4), [bk], [v_aug])
                    if g == 0:
                        s = self.stg_i % 2
                        self.stg_i += 1
                        stg = self.stg[s]
                        self.cp("dve", stg.ap[:, 0:256], bk.ap[:, 0:256], [bk], [stg])
                        dst = O["nk" if which == 1 else "nv"][tt // 2, l, (tt % 2) * 128:(tt % 2) * 128 + 128, half * 256:half * 256 + 256]
                        d = fw.dma("sp", lambda h, stg=stg, dst=dst: h.dma_start(out=dst, in_=stg.ap[:, 0:256]), self.stg_sem[s], reads=[stg])
                        fw.final.append(d)
                    self.chk(f"kv1{l}{g}")
            if which == 1:
                self.chk(f"kvK{l}{g}")

        self.chk(f"attkv{l}{g}")
        pT = [ar.alloc([128, 512], BF16) for _ in range(2)]
        pi = [0]
        atok = ar.alloc([128, 8, 128], BF16)
        rec = ar.alloc([128, 8], F32)

        def transposes(hp, tts):
            bk = self.bank()
            bkb = bk.ap.bitcast(BF16)
            fw.mm([lambda h, bkb=bkb, i=i, tt=tt: h.transpose(out=bkb[:, i * 128:(i + 1) * 128], in_=atok.ap[:, tt, :], identity=self.ident_bf.ap)
                   for i, tt in enumerate(tts)], reads=[atok, self.ident_bf], writes=[bk])
            n = len(tts)
            self.cp("dve", attnT.ap[:, hp, tts[0] * 128:(tts[0] + n) * 128], bkb[:, 0:n * 128], [bk], [attnT])

        if g == 0:
            UP = [(sq, hp, e) for sq in range(4) for hp in range(4) for e in range(2)]

            def issue_score_p(k):
                sq, hp, e = UP[k]
                pb = 64 * e
                sb = self.bank()
                fw.mm([lambda h, sb=sb, kt=kt, pb=pb, sq=sq, hp=hp: h.matmul(sb.ap[:, kt * 256:(kt + 1) * 256],
                                                                            lhsT=k_sb.ap[pb:pb + 64, hp, sq * 256 + kt * 128:sq * 256 + kt * 128 + 128],
                                                                            rhs=q_sb.ap[pb:pb + 64, hp, sq * 256:sq * 256 + 256], start=True, stop=True) for kt in range(2)],
                      reads=[k_sb, q_sb], writes=[sb])
                return sb

            nxt = issue_score_p(0)
            ob = ov = None
            for k, (sq, hp, e) in enumerate(UP):
                hh = 2 * hp + e
                if e == 0:
                    ob = self.bank(hold=True)
                    ov = ob.ap[:, 0:260].rearrange("p (q e c) -> p q e c", q=2, e=2)
                sb = nxt
                p = pT[k % 2]
                self.act(p.ap, sb.ap, AF.Exp, [sb], [p])
                if k + 1 < len(UP):
                    nxt = issue_score_p(k + 1)
                fns = []
                for qt in range(2):
                    for kt in range(2):
                        fns.append(lambda h, qt=qt, kt=kt, e=e, hh=hh, p=p, ov=ov, sq=sq, st_=(e == 0 and qt == 0 and kt == 0): h.matmul(
                            ov[:, qt, e, :], lhsT=p.ap[:, kt * 256 + qt * 128:kt * 256 + qt * 128 + 128], rhs=v_aug.ap[:, 2 * sq + kt, hh, 0:65],
                            start=st_, stop=(e == 1 and qt == 1 and kt == 1)))
                fw.mm(fns, reads=[p, v_aug], writes=[ob])
                if e == 1:
                    fw.op("dve", lambda h, ov=ov: h.reciprocal(out=rec.ap[:, 0:4].rearrange("p (q e) -> p q e", q=2), in_=ov[:, :, :, 64]), reads=[ob], writes=[rec])
                    self.tt("dve", atok.ap[:, 2 * sq:2 * sq + 2, :].rearrange("p q (e c) -> p q e c", e=2), ov[:, :, :, 0:64],
                            rec.ap[:, 0:4].rearrange("p (q e) -> p q e", q=2).unsqueeze(3).to_broadcast([128, 2, 2, 64]), ALU.mult, [ob, rec], [atok])
                    self.unhold(ob)
                    transposes(hp, [2 * sq, 2 * sq + 1])
        else:
            self.na_attention(l, attnT, q_sb, k_sb, v_aug, pT, atok, rec, transposes)

    def na_attention(self, l, attnT, q_sb, k_sb, v_aug, pT, atok, rec, transposes):
        fw, ar, I = self.fw, self.ar, self.I
        kctxT = ar.alloc([128, 4, 512], BF16)
        vctx = ar.alloc([128, 4, 8, 66], BF16)
        fw.op("pool", lambda h: h.memset(vctx.ap[:, :, :, 64:65], 1.0), writes=[vctx])
        cvsem = Buf("cvsem")
        for tt in range(4):
            fw.dma("pool", lambda h, tt=tt: h.dma_start(out=vctx.ap[:, tt, :, 0:64], in_=I["cv"][l][tt * 128:(tt + 1) * 128, :].rearrange("p (a b) -> p a b", a=8)),
                   cvsem, writes=[vctx])
        mk = ar.mark()
        cktok = ar.alloc([128, 4, 512], BF16)
        cksem = Buf("cksem")
        fw.dma("pool", lambda h: h.dma_start(out=cktok.ap, in_=I["ck"][l].rearrange("(t p) f -> p t f", p=128)), cksem, writes=[cktok])
        for hp in range(4):
            bk = self.bank()
            bkb = bk.ap.bitcast(BF16)
            fw.mm([lambda h, bkb=bkb, tt=tt, hp=hp: h.transpose(out=bkb[:, tt * 128:(tt + 1) * 128], in_=cktok.ap[:, tt, hp * 128:(hp + 1) * 128],
                                                               identity=self.ident_bf.ap) for tt in range(4)], reads=[cktok, self.ident_bf], writes=[bk])
            self.cp("dve", kctxT.ap[:, hp, :], bkb[:, 0:512], [bk], [kctxT])
        ar.release(mk)
        LT = ar.alloc([128, 8, 18, 64], BF16)
        mk = ar.mark()
        rp = ar.alloc([128, 2, 32], F32)
        fw.op("pool", lambda h: h.memset(rp.ap, 0.0), writes=[rp])
        for e in range(2):
            self.sload(rp.ap[0:120, e, 0:31], I["rpb"][l].rearrange("h a x -> (h a) x"), rp)
        Rs = ar.alloc([64, 8, 15], F32)
        RE = ar.alloc([64, 8, 18], F32)
        BB = ar.alloc([64, 2, 127], F32)
        msk = ar.alloc([128, 64], F32)
        m2 = ar.alloc([128, 64], F32)
        bk = self.bank()
        fw.mm([lambda h, bk=bk: h.matmul(bk.ap[0:64, 0:120], lhsT=rp.ap[0:120].rearrange("p a b -> p (a b)"),
                                         rhs=self.ident_f.ap[0:120, 0:120], start=True, stop=True)], reads=[rp, self.ident_f], writes=[bk])
        fw.op("pool", lambda h: h.memset(Rs.ap, 0.0), writes=[Rs])
        for e in range(2):
            self.cp("dve", Rs.ap[32 * e:32 * e + 31].rearrange("p a b -> p (a b)"), bk.ap[32 * e:32 * e + 31, 0:120], [bk], [Rs])
        fw.op("pool", lambda h: h.memset(RE.ap, 0.0), writes=[RE])
        for e in range(2):
            self.cp("dve", RE.ap[32 * e:32 * e + 31, :, e + 1:e + 16], Rs.ap[32 * e:32 * e + 31, :, ::-1], [Rs], [RE])
        fw.op("pool", lambda h: h.memset(BB.ap, 0.0), writes=[BB])
        for e in range(2):
            fw.op("pool", lambda h, e=e: h.memset(BB.ap[32 * e:32 * e + 32, e, :], 1.0), writes=[BB])
            fw.op("pool", lambda h, e=e: h.affine_select(out=BB.ap[32 * e:32 * e + 32, e, :], in_=BB.ap[32 * e:32 * e + 32, e, :], pattern=[[1, 127]],
                                                          compare_op=ALU.is_equal, fill=0.0, base=-48, channel_multiplier=-1), reads=[BB], writes=[BB])
        fw.op("pool", lambda h: h.memset(msk.ap, 0.0), writes=[msk])
        fw.op("pool", lambda h: h.memset(m2.ap, 0.0), writes=[m2])
        for hf in range(2):
            sl = slice(64 * hf, 64 * hf + 64)
            fw.op("pool", lambda h, sl=sl: h.affine_select(out=msk.ap[sl], in_=msk.ap[sl], pattern=[[-1, 64]], compare_op=ALU.is_ge, fill=NEG,
                                                            base=8, channel_multiplier=1), reads=[msk], writes=[msk])
            fw.op("pool", lambda h, sl=sl: h.affine_select(out=msk.ap[sl], in_=msk.ap[sl], pattern=[[0, 64]], compare_op=ALU.is_ge, fill=0.0,
                                                            base=47, channel_multiplier=-1), reads=[msk], writes=[msk])
            fw.op("pool", lambda h, sl=sl: h.affine_select(out=m2.ap[sl], in_=m2.ap[sl], pattern=[[1, 64]], compare_op=ALU.is_ge, fill=NEG,
                                                            base=7, channel_multiplier=-1), reads=[m2], writes=[m2])
            fw.op("pool", lambda h, sl=sl: h.affine_select(out=m2.ap[sl], in_=m2.ap[sl], pattern=[[0, 64]], compare_op=ALU.is_ge, fill=0.0,
                                                            base=-16, channel_multiplier=1), reads=[m2], writes=[m2])
        self.tt("pool", msk.ap, msk.ap, m2.ap, ALU.add, [msk, m2], [msk])
        REf = RE.ap.rearrange("p a b -> p (a b)")
        for q0 in range(0, 64, 3):
            nq = min(3, 64 - q0)
            bk = self.bank()
            for i in range(nq):
                qc = q0 + i
                fw.mm([lambda h, bk=bk, i=i, qc=qc, e=e: h.matmul(bk.ap[64 * e:64 * e + 64, i * 144:(i + 1) * 144], lhsT=BB.ap[:, e, 63 - qc:63 - qc + 64], rhs=REf,
                                                                  start=True, stop=True) for e in range(2)], reads=[BB, RE], writes=[bk])
            outv = LT.ap[:, :, :, q0:q0 + nq].rearrange("p h d q -> p q (h d)")
            self.tt("dve", outv, bk.ap[:, 0:nq * 144].rearrange("p (q n) -> p q n", q=nq),
                    msk.ap[:, q0:q0 + nq].unsqueeze(2).to_broadcast([128, nq, 144]), ALU.add, [bk, msk], [LT])
        LTf = LT.ap.rearrange("p h d q -> p (h d q)")
        for hq in range(4):
            self.act(LTf[:, hq * 2304:(hq + 1) * 2304], LTf[:, hq * 2304:(hq + 1) * 2304], AF.Exp, [LT], [LT])
        ar.release(mk)

        U = []
        for hp in range(4):
            for e in range(2):
                for c in range(2):
                    grp = []
                    for mt in range(8):
                        js = [j for j in range(4 * c, 4 * c + 4) if any(na_valid(2 * mt + ee, 2 * j + r) for ee in range(2) for r in range(2))]
                        if js:
                            grp.append(("loc", mt, js[0], js[-1]))
                    for kt in range(4):
                        grp.append(("ctx", kt, 4 * c, 4 * c + 3))
                    for ui, (kind, mt, ja, jb) in enumerate(grp):
                        U.append((hp, e, c, kind, mt, ja, jb, ui == 0, ui == len(grp) - 1))

        def issue_score(k):
            hp, e, c, kind, mt, ja, jb, gfirst, glast = U[k]
            pb = 64 * e
            hh = 2 * hp + e
            nq = 128 * (jb - ja + 1)
            sb = self.bank()
            if kind == "loc":
                d0 = 2 * ja - 2 * mt + 8
                d1 = 2 * jb + 1 - 2 * mt + 8
                assert 0 <= d0 and d1 < 18, (mt, ja, jb)
                ltv = LT.ap[:, hh, d0:d1 + 1, :].rearrange("p a b -> p (a b)")
                fw.mm([lambda h, sb=sb, mt=mt, ja=ja, nq=nq, pb=pb, hp=hp: h.matmul(sb.ap[:, 0:nq], lhsT=k_sb.ap[pb:pb + 64, hp, mt * 128:(mt + 1) * 128],
                                                                                   rhs=q_sb.ap[pb:pb + 64, hp, ja * 128:ja * 128 + nq], start=True, stop=True)],
                      reads=[k_sb, q_sb], writes=[sb])
            else:
                fw.mm([lambda h, sb=sb, mt=mt, ja=ja, nq=nq, pb=pb, hp=hp: h.matmul(sb.ap[:, 0:nq], lhsT=kctxT.ap[pb:pb + 64, hp, mt * 128:(mt + 1) * 128],
                                                                                   rhs=q_sb.ap[pb:pb + 64, hp, ja * 128:ja * 128 + nq], start=True, stop=True)],
                      reads=[kctxT, q_sb], writes=[sb])
            return sb

        nxt = issue_score(0)
        ob = ov = None
        first = True
        for k, (hp, e, c, kind, mt, ja, jb, gfirst, glast) in enumerate(U):
            hh = 2 * hp + e
            nq = 128 * (jb - ja + 1)
            if gfirst:
                ob = self.bank(hold=True)
                ov = ob.ap[:, 0:260].rearrange("p (q c) -> p q c", q=4)
                first = True
            sb = nxt
            p = pT[k % 2]
            self.act(p.ap[:, 0:nq], sb.ap[:, 0:nq], AF.Exp, [sb], [p])
            if k + 1 < len(U):
                nxt = issue_score(k + 1)
            if kind == "loc":
                d0 = 2 * ja - 2 * mt + 8
                d1 = 2 * jb + 1 - 2 * mt + 8
                ltv = LT.ap[:, hh, d0:d1 + 1, :].rearrange("p a b -> p (a b)")
                self.tt("dve", p.ap[:, 0:nq], p.ap[:, 0:nq], ltv, ALU.mult, [p, LT], [p])
            fns = []
            for j in range(ja, jb + 1):
                if kind == "loc":
                    val = [[na_valid(2 * mt + ee, 2 * j + r) for r in range(2)] for ee in range(2)]
                    if not any(val[0]) and not any(val[1]):
                        continue
                    for ee in range(2):
                        for r in range(2):
                            if not val[ee][r]:
                                c0 = (j - ja) * 128 + r * 64
                                fw.op("dve", lambda h, p=p, ee=ee, c0=c0: h.memset(p.ap[64 * ee:64 * ee + 64, c0:c0 + 64], 0.0), reads=[p], writes=[p])
                    rhs = v_aug.ap[:, mt, hh, 0:65]
                else:
                    rhs = vctx.ap[:, mt, hh, 0:65]
                fns.append(lambda h, j=j, ja=ja, p=p, rhs=rhs, st_=first, ov=ov, c=c: h.matmul(ov[:, j - 4 * c, :], lhsT=p.ap[:, (j - ja) * 128:(j - ja + 1) * 128],
                                                                                            rhs=rhs, start=st_, stop=False))
                first = False
            fw.mm(fns, reads=[p, v_aug, vctx], writes=[ob])
            if glast:
                fw.op("dve", lambda h, ov=ov: h.reciprocal(out=rec.ap[:, 0:4], in_=ov[:, :, 64]), reads=[ob], writes=[rec])
                self.tt("dve", atok.ap[:, 4 * c:4 * c + 4, 64 * e:64 * e + 64], ov[:, :, 0:64],
                        rec.ap[:, 0:4].unsqueeze(2).to_broadcast([128, 4, 64]), ALU.mult, [ob, rec], [atok])
                self.unhold(ob)
                if e == 1 and c == 1:
                    transposes(hp, [0, 1, 2, 3])
                    transposes(hp, [4, 5, 6, 7])

    def s5(self, l, g, s5y):
        fw, ar, I, O = self.fw, self.ar, self.I, self.O
        c = self.s5cols
        u_sb = ar.alloc([128, 2, TG], BF16)

        def u_cons(ft, tb, bk):
            self.cp("act", u_sb.ap[:, ft, tb * 512:(tb + 1) * 512], bk.ap, [bk], [u_sb])

        self.proj_fm(self.win_cols(l, 1536), 2, self.h_rhs, [self.h], 8, u_cons)
        Ec = ar.alloc([128, 16, 256], F32)
        Es = ar.alloc([128, 16, 256], F32)
        mk = ar.mark()
        io_i = ar.alloc([128, 256], I32)
        io_f = ar.alloc([128, 256], F32)
        fw.op("pool", lambda h: h.iota(io_i.ap, pattern=[[1, 256]], base=0, channel_multiplier=0), writes=[io_i])
        self.cp("dve", io_f.ap, io_i.ap, [io_i], [io_f])
        yv = ar.alloc([128, 2, 256], F32)
        for q in range(8):
            self.tt("dve", yv.ap, c.ap[:, 2 * q:2 * q + 2, 0].unsqueeze(2).to_broadcast([128, 2, 256]),
                    io_f.ap.unsqueeze(1).to_broadcast([128, 2, 256]), ALU.mult, [c, io_f], [yv])
            yflat = Reg(yv.ap.rearrange("p a b -> p (a b)"), yv.bufs)
            self.sincos(yflat, 512, Ec.ap[:, 2 * q:2 * q + 2, :].rearrange("p a b -> p (a b)"),
                        Es.ap[:, 2 * q:2 * q + 2, :].rearrange("p a b -> p (a b)"), Ec, Es)
        ar.release(mk)
        Kc = ar.alloc([128, 16, 4], F32)
        if g == 0:
            self.ts("dve", Kc.ap[:, :, 3], Es.ap[:, :, 255], -1.0, None, ALU.mult, None, [Es], [Kc])
        if g == 1:
            e255c = Ec.ap[:, :, 255]
            e255s = Es.ap[:, :, 255]
            self.tt("dve", Kc.ap[:, :, 0], c.ap[:, :, 2], e255c, ALU.mult, [c, Ec], [Kc])
            self.tt("dve", Kc.ap[:, :, 3], c.ap[:, :, 3], e255s, ALU.mult, [c, Es], [Kc])
            self.tt("dve", Kc.ap[:, :, 0], Kc.ap[:, :, 0], Kc.ap[:, :, 3], ALU.subtract, [Kc], [Kc])
            self.tt("dve", Kc.ap[:, :, 1], c.ap[:, :, 2], e255s, ALU.mult, [c, Es], [Kc])
            self.tt("dve", Kc.ap[:, :, 3], c.ap[:, :, 3], e255c, ALU.mult, [c, Ec], [Kc])
            self.tt("dve", Kc.ap[:, :, 1], Kc.ap[:, :, 1], Kc.ap[:, :, 3], ALU.add, [Kc], [Kc])
            self.ts("dve", Kc.ap[:, :, 2], Kc.ap[:, :, 1], -1.0, None, ALU.mult, None, [Kc], [Kc])
        ygb = ar.alloc([128, 2, TG], BF16)
        bpr = ar.alloc([128, 512], F32)
        bpi = ar.alloc([128, 512], F32)
        grs = [ar.alloc([128, 512], F32) for _ in range(2)]
        gis = [ar.alloc([128, 512], F32) for _ in range(2)]
        t1 = ar.alloc([128, 512], F32)
        t2 = ar.alloc([128, 512], F32)
        p1 = ar.alloc([128, 512], F32)
        p2 = ar.alloc([128, 512], F32)
        unit = [0]
        wr = ar.alloc([128, 512], BF16)
        wi = ar.alloc([128, 512], BF16)
        sm = ar.alloc([128, 16], F32)

        def v2(ap):
            return ap.rearrange("p (s t) -> p s t", s=2)

        def seg(ap, s2, rev):
            v = ap[:, s2 * 256:(s2 + 1) * 256]
            return v[:, ::-1] if rev else v

        units = []
        for ut in range(2):
            for d in range(2):
                for jj in range(4):
                    for ti, tb in enumerate([1, 0] if d == 1 else [0, 1]):
                        units.append((ut, d, jj, ti, tb))

        def issue_bu(k):
            ut, d, jj, ti, tb = units[k]
            dj = d * 8 + ut * 4 + jj
            tsl = slice(tb * 512, (tb + 1) * 512)
            br = self.bank()
            bi = self.bank()
            fw.mm([lambda h, br=br, dj=dj, tsl=tsl, ut=ut: h.matmul(br.ap, lhsT=self.s5w.ap[:, dj, 0, :], rhs=u_sb.ap[:, ut, tsl], start=True, stop=True)],
                  reads=[self.s5w, u_sb], writes=[br])
            fw.mm([lambda h, bi=bi, dj=dj, tsl=tsl, ut=ut: h.matmul(bi.ap, lhsT=self.s5w.ap[:, dj, 1, :], rhs=u_sb.ap[:, ut, tsl], start=True, stop=True)],
                  reads=[self.s5w, u_sb], writes=[bi])
            return br, bi

        ybanks = None
        yfirst = None
        if l == 0:
            self.bg = self.mod_gen(0, look=2, s0=8, s1=24, doA=(False, True)) if g == 0 else self.mod_gen(1, look=2)
        tbk = self.bank(hold=True)
        tbk2 = self.bank(hold=True)
        deferred = []
        deferred_mul = []
        prev_last = None
        nxt = issue_bu(0)
        for k, (ut, d, jj, ti, tb) in enumerate(units):
            rev = (d == 1)
            j = ut * 4 + jj
            dj = d * 8 + j
            if d == 0 and jj == 0 and ti == 0:
                ybanks = [self.bank(hold=True), self.bank(hold=True)]
                yfirst = [True, True]
            Ecv = Ec.ap[:, dj, :]
            Esv = Es.ap[:, dj, :]
            if rev:
                Ecv = Ecv[:, ::-1]
                Esv = Esv[:, ::-1]
            Ec2 = Ecv.unsqueeze(1).to_broadcast([128, 2, 256])
            Es2 = Esv.unsqueeze(1).to_broadcast([128, 2, 256])
            rb = c.ap[:, dj, 1:2].to_broadcast([128, 256])
            gr, gi = grs[k % 2], gis[k % 2]
            br, bi = nxt
            self.tt("dve", v2(t2.ap), v2(bi.ap), Es2, ALU.mult, [bi, Es], [t2])
            self.tt("dve", v2(tbk.ap), v2(br.ap), Ec2, ALU.mult, [br, Ec], [tbk])
            self.tt("dve", v2(tbk2.ap), v2(bi.ap), Ec2, ALU.mult, [bi, Ec], [tbk2])
            self.tt("dve", bpr.ap, tbk.ap, t2.ap, ALU.add, [tbk, t2], [bpr])
            self.tt("dve", v2(t2.ap), v2(br.ap), Es2, ALU.mult, [br, Es], [t2])
            self.tt("dve", bpi.ap, tbk2.ap, t2.ap, ALU.subtract, [tbk2, t2], [bpi])
            if k + 1 < len(units):
                nxt = issue_bu(k + 1)
            for fn in deferred:
                fn()
            deferred = []
            segs = [1, 0] if rev else [0, 1]
            if g == 0:
                for (src, dst) in ((bpr, gr), (bpi, gi)):
                    for s2 in segs:
                        fw.op("dve", lambda h, src=src, dst=dst, s2=s2, rev=rev, rb=rb: h.tensor_tensor_scan(
                            out=seg(dst.ap, s2, rev), data0=rb, data1=seg(src.ap, s2, rev), initial=0.0, op0=ALU.mult, op1=ALU.add),
                            reads=[src, c], writes=[dst])
            for sj, s2 in enumerate(segs):
                s = tb * 2 + s2
                si = ti * 2 + sj
                if g == 1:
                    f_r = seg(bpr.ap, s2, rev)[:, 0:1]
                    f_i = seg(bpi.ap, s2, rev)[:, 0:1]
                    if si == 0:
                        hpr, hpi = c.ap[:, dj, 6:7], c.ap[:, dj, 7:8]
                        self.stt(f_r, hpr, c.ap[:, dj, 2:3], f_r, ALU.mult, ALU.add, [c, bpr], [bpr])
                        self.stt(f_i, hpi, c.ap[:, dj, 2:3], f_i, ALU.mult, ALU.add, [c, bpi], [bpi])
                        self.ts("dve", sm.ap[:, 0:1], hpi, c.ap[:, dj, 3:4], None, ALU.mult, None, [c], [sm])
                        self.stt(f_i, hpr, c.ap[:, dj, 3:4], f_i, ALU.mult, ALU.add, [c, bpi], [bpi])
                        self.tt("dve", f_r, f_r, sm.ap[:, 0:1], ALU.subtract, [bpr, sm], [bpr])
                    else:
                        pgr, pgi = prev_last
                        self.stt(f_r, pgr[0], Kc.ap[:, dj, 0:1], f_r, ALU.mult, ALU.add, [pgr[1], Kc, bpr], [bpr])
                        self.stt(f_i, pgi[0], Kc.ap[:, dj, 0:1], f_i, ALU.mult, ALU.add, [pgi[1], Kc, bpi], [bpi])
                        self.stt(f_r, pgi[0], Kc.ap[:, dj, 2:3], f_r, ALU.mult, ALU.add, [pgi[1], Kc, bpr], [bpr])
                        self.stt(f_i, pgr[0], Kc.ap[:, dj, 1:2], f_i, ALU.mult, ALU.add, [pgr[1], Kc, bpi], [bpi])
                if g == 1:
                    for (src, dst) in ((bpr, gr), (bpi, gi)):
                        fw.op("dve", lambda h, src=src, dst=dst, s2=s2, rev=rev, rb=rb: h.tensor_tensor_scan(
                            out=seg(dst.ap, s2, rev), data0=rb, data1=seg(src.ap, s2, rev), initial=0.0, op0=ALU.mult, op1=ALU.add),
                            reads=[src, c], writes=[dst])
                g_r = seg(gr.ap, s2, rev)[:, 255:256]
                g_i = seg(gi.ap, s2, rev)[:, 255:256]
                prev_last = ((g_r, gr), (g_i, gi))
                if g == 0:
                    def state_ops(dj=dj, g_r=g_r, g_i=g_i, gr=gr, gi=gi, s=s, d=d, j=j, sj=sj):
                        e_c = Ec.ap[:, dj, 255:256]
                        e_s = Es.ap[:, dj, 255:256]
                        n_s = Kc.ap[:, dj, 3:4]
                        o_r, o_i = self.outst.ap[:, s, d, j, 0:1], self.outst.ap[:, s, d, j, 1:2]
                        ta, tb_ = sm.ap[:, 8 + 2 * sj:9 + 2 * sj], sm.ap[:, 9 + 2 * sj:10 + 2 * sj]
                        self.act(ta, g_i, AF.Identity, [gi, Kc], [sm], scale=n_s)
                        self.act(tb_, g_i, AF.Identity, [gi, Ec], [sm], scale=e_c)
                        self.act(o_r, g_r, AF.Identity, [gr, Ec, sm], [self.outst], scale=e_c, bias=ta)
                        self.act(o_i, g_r, AF.Identity, [gr, Es, sm], [self.outst], scale=e_s, bias=tb_)
                    deferred.append(state_ops)
            self.tt("pool", v2(p1.ap), v2(gr.ap), Ec2, ALU.mult, [gr, Ec], [p1])
            self.tt("pool", v2(p2.ap), v2(gi.ap), Es2, ALU.mult, [gi, Es], [p2])
            self.tt("pool", wr.ap, p1.ap, p2.ap, ALU.subtract, [p1, p2], [wr])
            self.tt("pool", v2(p1.ap), v2(gi.ap), Ec2, ALU.mult, [gi, Ec], [p1])
            self.tt("pool", v2(p2.ap), v2(gr.ap), Es2, ALU.mult, [gr, Es], [p2])
            self.tt("pool", wi.ap, p1.ap, p2.ap, ALU.add, [p1, p2], [wi])
            self.tick()
            yb = ybanks[tb]
            last = (d == 1 and jj == 3)
            fw.mm([lambda h, yb=yb, dj=dj, st_=yfirst[tb]: h.matmul(yb.ap, lhsT=self.s5w.ap[:, dj, 2, :], rhs=wr.ap, start=st_, stop=False),
                   lambda h, yb=yb, dj=dj, last=last: h.matmul(yb.ap, lhsT=self.s5w.ap[:, dj, 3, :], rhs=wi.ap, start=False, stop=last)],
                  reads=[self.s5w, wr, wi], writes=[yb])
            yfirst[tb] = False
            if d == 1 and jj == 3 and ti == 1:
                for tb2 in range(2):
                    yb = ybanks[tb2]
                    sl = slice(tb2 * 512, (tb2 + 1) * 512)
                    self.stt(t1.ap, u_sb.ap[:, ut, sl], self.s5d.ap[:, ut:ut + 1], yb.ap, ALU.mult, ALU.add, [u_sb, self.s5d, yb], [t1])
                    self.act(ygb.ap[:, ut, sl], t1.ap, AF.Gelu_apprx_tanh, [t1], [ygb])
                    self.unhold(yb)
        for fn in deferred:
            fn()
        self.drain()
        self.unhold(tbk)
        self.unhold(tbk2)
        for ot in range(2):
            for tb in range(2):
                sl = slice(tb * 512, (tb + 1) * 512)
                bk = self.bank()
                fw.mm([lambda h, bk=bk, kc=kc, ot=ot, sl=sl: h.matmul(bk.ap, lhsT=self.wglu.ap[:, kc, ot * 128:(ot + 1) * 128], rhs=ygb.ap[:, kc, sl],
                                                                      start=(kc == 0), stop=(kc == 1)) for kc in range(2)], reads=[self.wglu, ygb], writes=[bk])
                self.act(t1.ap, bk.ap, AF.Sigmoid, [bk], [t1])
                self.tt("dve", s5y.ap[:, ot, sl], ygb.ap[:, ot, sl], t1.ap, ALU.mult, [ygb, t1], [s5y])

    def store_s5_states(self, l):
        fw, ar, O = self.fw, self.ar, self.O
        mk = ar.mark()
        tr = ar.alloc([128, 128], F32)
        bk = self.bank()
        fw.mm([lambda h: h.transpose(out=bk.ap[:, 0:128], in_=self.outst.ap.rearrange("p s d j r -> p (s d j r)"), identity=self.ident_f.ap)],
              reads=[self.outst, self.ident_f], writes=[bk])
        self.cp("dve", tr.ap, bk.ap[:, 0:128], [bk], [tr])
        sem = Buf("s5out")
        for ri, nm in enumerate(("ns5r", "ns5i")):
            for s in range(4):
                for d in range(2):
                    r0 = ((s * 2 + d) * 8) * 2 + ri
                    src = tr.ap[r0:r0 + 15:2, :]
                    dst = O[nm][s, l, d, :].rearrange("(j q) -> j q", q=128)
                    dd = fw.dma("sp", lambda h, src=src, dst=dst: h.dma_start(out=dst, in_=src), sem, reads=[tr])
                    fw.final.append(dd)
        ar.release(mk)

    def lru(self, l, g, lruy):
        fw, ar, I, O = self.fw, self.ar, self.I, self.O
        lc = self.lrucols
        xr = ar.alloc([128, 2, TG], F32)
        gg = ar.alloc([128, 2, TG], F32)
        xc = ar.alloc([128, 2, TG], F32)
        xcb = ar.alloc([128, 2, TG], BF16)

        def xr_cons(ft, tb, bk):
            self.cp("act", xr.ap[:, ft, tb * 512:(tb + 1) * 512], bk.ap, [bk], [xr])

        def xg_cons(ft, tb, bk):
            self.act(gg.ap[:, ft, tb * 512:(tb + 1) * 512], bk.ap, AF.Gelu_apprx_tanh, [bk], [gg])

        self.proj_fm(self.win_cols(l, 1792), 2, self.h_rhs, [self.h], 8, xr_cons)
        self.proj_fm(self.win_cols(l, 2048), 2, self.h_rhs, [self.h], 8, xg_cons)
        nseq, L = (4, 256) if g == 0 else (1, 1024)
        for t in range(2):
            xv = xr.ap[:, t, :].rearrange("p (s q) -> p s q", s=nseq)
            cv = xc.ap[:, t, :].rearrange("p (s q) -> p s q", s=nseq)
            self.ts("dve", cv, xv, lc.ap[:, t, 2:3], lc.ap[:, t, 4:5], ALU.mult, ALU.add, [xr, lc], [xc])
            for k in (0, 1, 3):
                sh = k - 2
                lo, hi = max(0, -sh), L - max(0, sh)
                self.stt(cv[:, :, lo:hi], xv[:, :, lo + sh:hi + sh], lc.ap[:, t, k:k + 1], cv[:, :, lo:hi], ALU.mult, ALU.add, [xr, lc, xc], [xc])
            self.cp("act", xcb.ap[:, t, :], xc.ap[:, t, :], [xc], [xcb])
        r_ = ar.alloc([128, 512], F32)
        i_ = ar.alloc([128, 512], F32)
        th = ar.alloc([128, 512], F32)
        e2 = ar.alloc([128, 512], F32)
        a_sb = ar.alloc([128, TG], F32)
        b_sb = ar.alloc([128, TG], F32)
        hs = [ar.alloc([128, TG], F32) for _ in range(2)]
        for t in range(2):
            for d in range(2):
                rev = (d == 1)
                for tb in range(2):
                    sl = slice(tb * 512, (tb + 1) * 512)
                    pr = self.bank()
                    pi = self.bank()
                    fw.mm([lambda h, pr=pr, d=d, t=t, sl=sl: h.matmul(pr.ap, lhsT=self.lruw.ap[:, 0 * 4 + d * 2 + t, :], rhs=xcb.ap[:, t, sl], start=True, stop=True)],
                          reads=[self.lruw, xcb], writes=[pr])
                    fw.mm([lambda h, pi=pi, d=d, t=t, sl=sl: h.matmul(pi.ap, lhsT=self.lruw.ap[:, 1 * 4 + d * 2 + t, :], rhs=xcb.ap[:, t, sl], start=True, stop=True)],
                          reads=[self.lruw, xcb], writes=[pi])
                    self.act(r_.ap, pr.ap, AF.Sigmoid, [pr, lc], [r_], bias=lc.ap[:, t, 5 + d:6 + d])
                    self.act(i_.ap, pi.ap, AF.Sigmoid, [pi, lc], [i_], bias=lc.ap[:, t, 7 + d:8 + d])
                    self.act(a_sb.ap[:, sl], r_.ap, AF.Exp, [r_, lc], [a_sb], scale=lc.ap[:, t, 9 + d:10 + d])
                    self.act(th.ap, r_.ap, AF.Tanh, [r_, lc], [th], scale=lc.ap[:, t, 11 + d:12 + d])
                    self.act(e2.ap, r_.ap, AF.Exp, [r_, lc], [e2], scale=lc.ap[:, t, 13 + d:14 + d])
                    self.stt(e2.ap, e2.ap, 1.0, th.ap, ALU.add, ALU.mult, [e2, th], [e2])
                    self.act(e2.ap, e2.ap, AF.Sqrt, [e2], [e2])
                    self.tt("dve", i_.ap, i_.ap, xc.ap[:, t, sl], ALU.mult, [i_, xc], [i_])
                    self.tt("dve", b_sb.ap[:, sl], e2.ap, i_.ap, ALU.mult, [e2, i_], [b_sb])
                hd = hs[d]
                for s in range(nseq):
                    sq = slice(s * L, (s + 1) * L)
                    av, bv, hv = a_sb.ap[:, sq], b_sb.ap[:, sq], hd.ap[:, sq]
                    if rev:
                        av, bv, hv = av[:, ::-1], bv[:, ::-1], hv[:, ::-1]
                    init = 0.0 if g == 0 else self.lrust0.ap[:, d, t:t + 1]
                    rd = [a_sb, b_sb] + ([] if g == 0 else [self.lrust0])
                    fw.op("dve", lambda h, av=av, bv=bv, hv=hv, init=init: h.tensor_tensor_scan(out=hv, data0=av, data1=bv, initial=init, op0=ALU.mult, op1=ALU.add),
                          reads=rd, writes=[hd])
                    if g == 0:
                        self.cp("dve", self.lrust.ap[:, s, d, t:t + 1], hv[:, L - 1:L], [hd], [self.lrust])
            self.tt("dve", hs[0].ap, hs[0].ap, hs[1].ap, ALU.add, [hs[0], hs[1]], [hs[0]])
            self.tt("dve", lruy.ap[:, t, :], hs[0].ap, gg.ap[:, t, :], ALU.mult, [hs[0], gg], [lruy])
        if g == 0:
            sem = Buf("lruout")
            for s in range(4):
                for d in range(2):
                    dst = O["nlru"][s, l, d, :].rearrange("(t p) -> p t", p=128)
                    src = self.lrust.ap[:, s, d, :]
                    dd = fw.dma("sp", lambda h, src=src, dst=dst: h.dma_start(out=dst, in_=src, allow_slow_non_contiguous=True), sem, reads=[self.lrust])
                    fw.final.append(dd)

    def merge(self, l, g, attnT, s5y, lruy):
        fw, ar, I = self.fw, self.ar, self.I
        merged = ar.alloc([128, 8, TG], BF16)
        sig = [ar.alloc([128, 512], F32) for _ in range(2)]
        pr = [ar.alloc([128, 512], F32) for _ in range(3)]
        si = [0]

        def br_rhs(kc, sl):
            if kc < 4:
                return attnT.ap[:, kc, sl]
            if kc < 6:
                return s5y.ap[:, kc - 4, sl]
            return lruy.ap[:, kc - 6, sl]

        for sp in range(4):
            c0 = sp * 256
            wb = self.wslice([(0, 4, 256, I["w_br_attn"][l][:, c0:c0 + 256]), (4 * 256, 2, 256, I["w_br_s5"][l][:, c0:c0 + 256]),
                              (6 * 256, 2, 256, I["w_br_lru"][l][:, c0:c0 + 256])])
            wbv = wb.ap.rearrange("p (k n) -> p k n", k=8)
            wg = [self.wslice([(0, 8, 256, I["w_in"][l][:, 2304 + b * 1024 + c0:2304 + b * 1024 + c0 + 256])]) for b in range(3)]
            for t in range(2):
                ft = sp * 2 + t
                for tb in range(2):
                    sl = slice(tb * 512, (tb + 1) * 512)
                    for b, (k0, k1) in enumerate(((0, 4), (4, 6), (6, 8))):
                        gb = self.bank()
                        wgv = wg[b].ap.rearrange("p (k n) -> p k n", k=8)
                        fw.mm([lambda h, gb=gb, wgv=wgv, kc=kc, t=t, tb=tb: h.matmul(gb.ap, lhsT=wgv[:, kc, t * 128:(t + 1) * 128], rhs=self.h_rhs(kc, tb),
                                                                                     start=(kc == 0), stop=(kc == 7)) for kc in range(8)],
                              reads=[wg[b], self.h], writes=[gb])
                        bb = self.bank()
                        fw.mm([lambda h, bb=bb, kc=kc, t=t, sl=sl, k0=k0, k1=k1: h.matmul(bb.ap, lhsT=wbv[:, kc, t * 128:(t + 1) * 128], rhs=br_rhs(kc, sl),
                                                                                         start=(kc == k0), stop=(kc == k1 - 1)) for kc in range(k0, k1)],
                              reads=[wb, attnT, s5y, lruy], writes=[bb])
                        sg = sig[si[0] % 2]
                        si[0] += 1
                        self.act(sg.ap, gb.ap, AF.Sigmoid, [gb], [sg])
                        self.tt("dve", pr[b].ap, bb.ap, sg.ap, ALU.mult, [bb, sg], [pr[b]])
                    self.tt("dve", pr[0].ap, pr[0].ap, pr[1].ap, ALU.add, [pr[0], pr[1]], [pr[0]])
                    self.tt("dve", merged.ap[:, ft, sl], pr[0].ap, pr[2].ap, ALU.add, [pr[0], pr[2]], [merged])
                    self.tick()

        def out_cons(ft, tb, bk):
            sl = slice(tb * 512, (tb + 1) * 512)
            xv = self.x.ap[:, g, ft, sl]
            self.stt(xv, bk.ap, self.mod.ap[:, l, 2 * 8 + ft, g:g + 1], xv, ALU.mult, ALU.add, [bk, self.mod, self.x], [self.x])
            self.tick()

        self.proj_fm(lambda c0, n: [(0, 8, n, I["w_out"][l][:, c0:c0 + n])], 8,
                     lambda kc, tb: merged.ap[:, kc, tb * 512:(tb + 1) * 512], [merged], 8, out_cons)

    def ffn(self, l, g):
        fw, ar, I = self.fw, self.ar, self.I
        gg = ar.alloc([128, 22, TG], BF16)
        a_sb = [ar.alloc([128, TG], F32) for _ in range(2)]
        c_sb = [ar.alloc([128, TG], F32) for _ in range(2)]
        gl = [ar.alloc([128, TG], BF16) for _ in range(2)]
        nseq, L = (4, 256) if g == 0 else (1, 1024)
        fc = self.ffcols
        it = 0
        for sp in range(11):
            wa = self.wslice([(0, 8, 256, I["ffn_w_up"][l][:, sp * 256:sp * 256 + 256])])
            wb = self.wslice([(0, 8, 256, I["ffn_w_up"][l][:, 2816 + sp * 256:2816 + sp * 256 + 256])])
            wav = wa.ap.rearrange("p (k n) -> p k n", k=8)
            wbv = wb.ap.rearrange("p (k n) -> p k n", k=8)
            for t in range(2):
                ft = sp * 2 + t
                a_, c_, g_ = a_sb[it % 2], c_sb[it % 2], gl[it % 2]
                it += 1
                for tb in range(2):
                    bk = self.bank()
                    fw.mm([lambda h, bk=bk, kc=kc, t=t, tb=tb: h.matmul(bk.ap, lhsT=wav[:, kc, t * 128:(t + 1) * 128], rhs=self.h_rhs(kc, tb),
                                                                       start=(kc == 0), stop=(kc == 7)) for kc in range(8)], reads=[wa, self.h], writes=[bk])
                    self.cp("act", a_.ap[:, tb * 512:(tb + 1) * 512], bk.ap, [bk], [a_])
                av = a_.ap.rearrange("p (s q) -> p s q", s=nseq)
                cv = c_.ap.rearrange("p (s q) -> p s q", s=nseq)
                self.act(cv, av, AF.Identity, [a_, fc], [c_], bias=fc.ap[:, ft, 3:4], scale=fc.ap[:, ft, 1:2])
                self.stt(cv[:, :, 1:L], av[:, :, 0:L - 1], fc.ap[:, ft, 0:1], cv[:, :, 1:L], ALU.mult, ALU.add, [a_, fc, c_], [c_])
                self.stt(cv[:, :, 0:L - 1], av[:, :, 1:L], fc.ap[:, ft, 2:3], cv[:, :, 0:L - 1], ALU.mult, ALU.add, [a_, fc, c_], [c_])
                self.act(g_.ap, c_.ap, AF.Gelu_apprx_tanh, [c_], [g_])
                for tb in range(2):
                    sl = slice(tb * 512, (tb + 1) * 512)
                    bk = self.bank()
                    fw.mm([lambda h, bk=bk, kc=kc, t=t, tb=tb: h.matmul(bk.ap, lhsT=wbv[:, kc, t * 128:(t + 1) * 128], rhs=self.h_rhs(kc, tb),
                                                                       start=(kc == 0), stop=(kc == 7)) for kc in range(8)], reads=[wb, self.h], writes=[bk])
                    self.tt("dve", gg.ap[:, ft, sl], bk.ap, g_.ap[:, sl], ALU.mult, [bk, g_], [gg])
                self.tick()
        for ft in range(8):
            w0 = self.wslice([(0, 11, 128, I["ffn_w_down"][l][0:1408, ft * 128:(ft + 1) * 128])])
            w1 = self.wslice([(0, 11, 128, I["ffn_w_down"][l][1408:2816, ft * 128:(ft + 1) * 128])])
            wv = [w0.ap[:, 0:1408].rearrange("p (k n) -> p k n", k=11), w1.ap[:, 0:1408].rearrange("p (k n) -> p k n", k=11)]
            for tb in range(2):
                sl = slice(tb * 512, (tb + 1) * 512)
                bk = self.bank()
                fw.mm([lambda h, bk=bk, kc=kc, sl=sl: h.matmul(bk.ap, lhsT=wv[kc // 11][:, kc % 11, :], rhs=gg.ap[:, kc, sl],
                                                               start=(kc == 0), stop=(kc == 21)) for kc in range(22)], reads=[w0, w1, gg], writes=[bk])
                xv = self.x.ap[:, g, ft, sl]
                self.stt(xv, bk.ap, self.mod.ap[:, l, 5 * 8 + ft, g:g + 1], xv, ALU.mult, ALU.add, [bk, self.mod, self.x], [self.x])
            self.tick()
        self.drain()

    def final_norm_gen(self, g):
        fw, ar, O = self.fw, self.ar, self.O
        rstd = ar.alloc([128, TG], F32)
        yts = [ar.alloc([128, 8, 128], F32) for _ in range(2)]
        self.rms_rstd(g, rstd)
        yield
        dst_t = O["yp" if g == 0 else "ys"]
        for tt in range(8):
            yt = yts[tt % 2]
            tsl = slice(tt * 128, (tt + 1) * 128)
            for kc in range(8):
                self.stt(yt.ap[:, kc, :], self.x.ap[:, g, kc, tsl], self.gcols.ap[:, 4, kc:kc + 1], rstd.ap[:, tsl], ALU.mult, ALU.mult,
                         [self.x, self.gcols, rstd], [yt])
            yield
            for qd in range(4):
                s = self.stg_i % 2
                self.stg_i += 1
                stg = self.stg[s]
                bk = self.bank()
                fw.mm([lambda h, bk=bk, j=j, qd=qd, yt=yt: h.transpose(out=bk.ap[:, j * 128:(j + 1) * 128], in_=yt.ap[:, qd * 2 + j, :],
                                                                       identity=self.ident_f.ap) for j in range(2)], reads=[yt, self.ident_f], writes=[bk])
                self.cp("act" if qd % 2 else "dve", stg.ap, bk.ap[:, 0:256], [bk], [stg])
                dst = dst_t[tt * 128:(tt + 1) * 128, qd * 256:(qd + 1) * 256]
                dd = fw.dma("sp", lambda h, stg=stg, dst=dst: h.dma_start(out=dst, in_=stg.ap), self.stg_sem[s], reads=[stg])
                fw.final.append(dd)
                if qd == 1:
                    yield
            yield

    def final_norm(self, g):
        fw, ar, O = self.fw, self.ar, self.O
        m = ar.mark()
        rstd = ar.alloc([128, TG], F32)
        self.rms_rstd(g, rstd)
        y = ar.alloc([128, 8, TG], F32)
        for kc in range(8):
            self.stt(y.ap[:, kc, :], self.x.ap[:, g, kc, :], self.gcols.ap[:, 4, kc:kc + 1], rstd.ap, ALU.mult, ALU.mult, [self.x, self.gcols, rstd], [y])
        dst_t = O["yp" if g == 0 else "ys"]
        for tt in range(8):
            for qd in range(4):
                s = self.stg_i % 2
                self.stg_i += 1
                stg = self.stg[s]
                bk = self.bank()
                fw.mm([lambda h, bk=bk, j=j, qd=qd, tt=tt: h.transpose(out=bk.ap[:, j * 128:(j + 1) * 128], in_=y.ap[:, qd * 2 + j, tt * 128:(tt + 1) * 128],
                                                                       identity=self.ident_f.ap) for j in range(2)], reads=[y, self.ident_f], writes=[bk])
                self.cp("act" if qd % 2 else "dve", stg.ap, bk.ap[:, 0:256], [bk], [stg])
                dst = dst_t[tt * 128:(tt + 1) * 128, qd * 256:(qd + 1) * 256]
                dd = fw.dma("sp", lambda h, stg=stg, dst=dst: h.dma_start(out=dst, in_=stg.ap), self.stg_sem[s], reads=[stg])
                fw.final.append(dd)
        ar.release(m)


_W_KEYS = ["w_ada", "b_ada", "g_norm1", "g_norm2", "w_in", "rpb", "s5_lam_re", "s5_lam_im", "s5_log_step", "s5_b_re", "s5_b_im",
           "s5_c_re", "s5_c_im", "s5_d", "s5_w_glu", "lru_conv_w", "lru_conv_b", "lru_w_a", "lru_b_a", "lru_w_x", "lru_b_x", "lru_lam",
           "w_br_attn", "w_br_s5", "w_br_lru", "w_out", "ffn_w_up", "ffn_conv_w", "ffn_conv_b", "ffn_w_down", "g_final"]


def make_in_maps(inp):
    f = lambda a: np.ascontiguousarray(np.asarray(a, dtype=np.float32))
    shared = {k: f(inp[k]) for k in _W_KEYS}
    maps = []
    for i in range(NCORES):
        m = dict(shared)
        m["xp"] = f(inp["x_prompt"][4 * i:4 * i + 4]).reshape(TG, D)
        m["xs"] = f(inp["x_sample"][i]).reshape(TG, D)
        m["ck"] = f(inp["cache_k"][i]).reshape(DEPTH, 512, 512)
        m["cv"] = f(inp["cache_v"][i]).reshape(DEPTH, 512, 512)
        m["s5r"] = f(inp["state_s5_re"][i]).reshape(DEPTH, 2, 1024)
        m["s5i"] = f(inp["state_s5_im"][i]).reshape(DEPTH, 2, 1024)
        m["slru"] = f(inp["state_lru"][i]).reshape(DEPTH, 2, 256)
        m["cvec"] = f(np.stack([np.asarray(inp["c_ctx"]), np.asarray(inp["c"])[i]], axis=0))
        maps.append(m)
    return maps


def assemble(results):
    cat = lambda k: np.concatenate([np.asarray(r[k]) for r in results], axis=0)
    y_prompt = cat("yp").reshape(32, 256, D)
    y_sample = cat("ys").reshape(8, 1024, D)
    new_k = cat("nk").reshape(32, DEPTH, 256, 8, 64)
    new_v = cat("nv").reshape(32, DEPTH, 256, 8, 64)
    ns5r = cat("ns5r").reshape(32, DEPTH, 2, 16, 64)
    ns5i = cat("ns5i").reshape(32, DEPTH, 2, 16, 64)
    nlru = cat("nlru").reshape(32, DEPTH, 2, 256)
    return tuple(np.ascontiguousarray(a, dtype=np.float32) for a in (y_prompt, y_sample, new_k, new_v, ns5r, ns5i, nlru))


def kernel(**inputs):
    nc = Builder().build()
    in_maps = make_in_maps(inputs)
    res = run_bass_kernel_spmd(nc, in_maps, core_ids=list(range(NCORES)))
    return assemble(res.results)


def debug_run(inputs, stage, ncores=1, trace=False):
    b = Builder(stage=stage)
    nc = b.build()
    in_maps = make_in_maps(inputs)[:ncores]
    res = run_bass_kernel_spmd(nc, in_maps, core_ids=list(range(ncores)), trace=trace)
    if trace:
        print("EXEC_NS", stage, res.exec_time_ns)
    return b, res.results
```

```python
import math
import types as _types
import numpy as np
from contextlib import ExitStack
import concourse.bass as bass
import concourse.mybir as mybir
from concourse.bass_utils import run_bass_kernel_spmd

F32 = mybir.dt.float32
BF16 = mybir.dt.bfloat16
I32 = mybir.dt.int32
AF = mybir.ActivationFunctionType
ALU = mybir.AluOpType

NCORES = 8
D = 1024
TG = 1024
DEPTH = 2
NEG = -30000.0
EPS = 1e-6
IN_W = 5376
PAGE = 512


class Buf:
    __slots__ = ("name", "lw", "rd", "dsem", "dcnt", "excl")

    def __init__(self, name, excl=False):
        self.name = name
        self.excl = excl
        self.lw = None
        self.rd = []
        self.dsem = None
        self.dcnt = 0


class Reg:
    __slots__ = ("ap", "bufs", "tag")

    def __init__(self, ap, bufs, tag=None):
        self.ap = ap
        self.bufs = bufs
        self.tag = tag

    def __getitem__(self, k):
        return self.ap[k]


def _freeze(f):
    if getattr(f, "__closure__", None) is None:
        return f
    cells = []
    for c in f.__closure__:
        try:
            cells.append(_types.CellType(c.cell_contents))
        except ValueError:
            cells.append(c)
    return _types.FunctionType(f.__code__, f.__globals__, f.__name__, f.__defaults__, tuple(cells))


class Eng:
    def __init__(self, name):
        self.key = name
        self.cnt = 0
        self.seen = {}
        self.prog = []


class FW:
    def __init__(self, nc, stack):
        self.nc = nc
        self.stack = stack
        self.sems = {}
        self.E = {}
        for n in ("pe", "act", "dve", "pool", "sp"):
            self.sems[n] = stack.enter_context(nc.semaphore("s_" + n))
            self.E[n] = Eng(n)
        self.ndsem = 0
        self.dsem_free = []
        self.final = []

    def _expand(self, lst):
        out = []
        for r in lst:
            if isinstance(r, Buf):
                out.append(r)
            else:
                out.extend(r.bufs)
        return out

    def _need(self, eng, dep, same_ok):
        key, val, clock = dep
        if same_ok and key == eng.key:
            return
        if eng.seen.get(key, 0) >= val:
            return
        eng.prog.append(("wait", key, val))
        eng.seen[key] = val
        for k, v in clock.items():
            if eng.seen.get(k, 0) < v:
                eng.seen[k] = v

    def _deps(self, eng, reads, writes):
        for b in reads:
            if b.lw is not None:
                self._need(eng, b.lw, False)
            if b.excl:
                for r in b.rd:
                    self._need(eng, r, True)
        for b in writes:
            if b.lw is not None:
                self._need(eng, b.lw, True)
            for r in b.rd:
                self._need(eng, r, True)

    def _commit(self, dep, reads, writes):
        for b in reads:
            b.rd.append(dep)
        for b in writes:
            b.lw = dep
            b.rd = []

    def op(self, en, fn, reads=(), writes=()):
        eng = self.E[en]
        reads = self._expand(reads)
        writes = self._expand(writes)
        self._deps(eng, reads, writes)
        eng.cnt += 1
        eng.prog.append(("ins", _freeze(fn), eng.key, 1))
        clock = dict(eng.seen)
        clock[eng.key] = eng.cnt
        dep = (eng.key, eng.cnt, clock)
        self._commit(dep, reads, writes)
        return dep

    def mm(self, fns, reads=(), writes=()):
        eng = self.E["pe"]
        reads = self._expand(reads)
        writes = self._expand(writes)
        self._deps(eng, reads, writes)
        for f in fns[:-1]:
            eng.prog.append(("ins", _freeze(f), None, 0))
        eng.cnt += 1
        eng.prog.append(("ins", _freeze(fns[-1]), eng.key, 1))
        clock = dict(eng.seen)
        clock[eng.key] = eng.cnt
        dep = (eng.key, eng.cnt, clock)
        self._commit(dep, reads, writes)
        return dep

    def dma(self, qn, fn, dbuf, reads=(), writes=()):
        eng = self.E[qn]
        reads = self._expand(reads)
        writes = self._expand(writes)
        self._deps(eng, reads, writes)
        if dbuf.dsem is None:
            self.ndsem += 1
            dbuf.dsem = self.stack.enter_context(self.nc.semaphore(f"d{self.ndsem}"))
        if dbuf.dcnt:
            self._need(eng, ("D%d" % id(dbuf), dbuf.dcnt, {}), False)
        dbuf.dcnt += 16
        key = "D%d" % id(dbuf)
        self.sems[key] = dbuf.dsem
        eng.prog.append(("ins", _freeze(fn), key, 16))
        dep = (key, dbuf.dcnt, dict(eng.seen))
        self._commit(dep, reads, writes)
        return dep

    def emit(self):
        nc = self.nc
        sems = self.sems
        for d in self.final:
            self._need(self.E["sp"], d, False)

        def replay(eng, h):
            for it in eng.prog:
                if it[0] == "wait":
                    h.wait_ge(sems[it[1]], it[2])
                else:
                    ins = it[1](h)
                    if it[2] is not None:
                        ins.then_inc(sems[it[2]], it[3])

        with nc.Block() as block:
            @block.sync
            def _(h):
                replay(self.E["sp"], h)

            @block.scalar
            def _(h):
                replay(self.E["act"], h)

            @block.vector
            def _(h):
                replay(self.E["dve"], h)

            @block.gpsimd
            def _(h):
                replay(self.E["pool"], h)

            @block.tensor
            def _(h):
                replay(self.E["pe"], h)


class Arena:
    def __init__(self, fw, nbytes):
        self.fw = fw
        self.nbytes = nbytes
        self.t = fw.stack.enter_context(fw.nc.sbuf_tensor("arena", [128, nbytes // 2], BF16))
        self.pages = [Buf(f"pg{i}") for i in range((nbytes + PAGE - 1) // PAGE)]
        self.top = 0
        self.peak = 0

    def alloc(self, shape, dt, align=64):
        esz = 4 if dt in (F32, I32) else 2
        n = 1
        for s in shape[1:]:
            n *= s
        nb = n * esz
        if nb >= PAGE:
            align = max(align, PAGE)
        off = (self.top + align - 1) // align * align
        assert off + nb <= self.nbytes, f"arena overflow: need {off + nb} have {self.nbytes}"
        self.top = off + nb
        self.peak = max(self.peak, self.top)
        ap = self.t[0:shape[0], off // 2:(off + nb) // 2]
        if esz == 4:
            ap = ap.bitcast(dt)
        if len(shape) > 2:
            names = " ".join(f"d{i}" for i in range(1, len(shape)))
            kw = {f"d{i}": shape[i] for i in range(1, len(shape))}
            ap = ap.rearrange(f"p ({names}) -> p {names}", **kw)
        pages = self.pages[off // PAGE:(off + nb - 1) // PAGE + 1]
        return Reg(ap, pages)

    def mark(self):
        return self.top

    def release(self, m):
        self.top = m


def na_valid(kr, qr):
    w0 = min(max(qr - 4, 0), 8)
    return w0 <= kr < w0 + 8


class _Stop(Exception):
    pass


class Builder:
    def __init__(self, debug=False, stage=None):
        self.debug = debug
        self.stage = stage
        self.dbg_outs = []
        self.nc = bass.Bass("TRN2", target_bir_lowering=False)
        self.stack = ExitStack()
        self.dram = {}

    def din(self, name, shape):
        t = self.nc.dram_tensor(name, list(shape), F32, kind="ExternalInput")
        self.dram[name] = t
        return t.ap()

    def dout(self, name, shape):
        t = self.nc.dram_tensor(name, list(shape), F32, kind="ExternalOutput")
        self.dram[name] = t
        return t.ap()

    def build(self):
        nc = self.nc
        with self.stack as st:
            fw = self.fw = FW(nc, st)
            I = self.I = {}
            O = self.O = {}
            I["xp"] = self.din("xp", [TG, D])
            I["xs"] = self.din("xs", [TG, D])
            I["ck"] = self.din("ck", [DEPTH, 512, 512])
            I["cv"] = self.din("cv", [DEPTH, 512, 512])
            I["s5r"] = self.din("s5r", [DEPTH, 2, 1024])
            I["s5i"] = self.din("s5i", [DEPTH, 2, 1024])
            I["slru"] = self.din("slru", [DEPTH, 2, 256])
            I["cvec"] = self.din("cvec", [2, D])
            wshapes = {
                "w_ada": [2, D, 6 * D], "b_ada": [2, 6 * D], "g_norm1": [2, D], "g_norm2": [2, D],
                "w_in": [2, D, IN_W], "rpb": [2, 8, 15, 31],
                "s5_lam_re": [2, 2, 16, 64], "s5_lam_im": [2, 2, 16, 64], "s5_log_step": [2, 2, 16],
                "s5_b_re": [2, 16, 64, 16], "s5_b_im": [2, 16, 64, 16],
                "s5_c_re": [2, 2, 16, 16, 64], "s5_c_im": [2, 2, 16, 16, 64],
                "s5_d": [2, 256], "s5_w_glu": [2, 256, 256],
                "lru_conv_w": [2, 4, 256], "lru_conv_b": [2, 256],
                "lru_w_a": [2, 2, 4, 64, 64], "lru_b_a": [2, 2, 256],
                "lru_w_x": [2, 2, 4, 64, 64], "lru_b_x": [2, 2, 256], "lru_lam": [2, 2, 256],
                "w_br_attn": [2, 512, D], "w_br_s5": [2, 256, D], "w_br_lru": [2, 256, D],
                "w_out": [2, D, D], "ffn_w_up": [2, D, 5632], "ffn_conv_w": [2, 3, 2816],
                "ffn_conv_b": [2, 2816], "ffn_w_down": [2, 2816, D], "g_final": [D],
            }
            self.wshapes = wshapes
            for k, s in wshapes.items():
                I[k] = self.din(k, s)
            O["yp"] = self.dout("yp", [TG, D])
            O["ys"] = self.dout("ys", [TG, D])
            O["nk"] = self.dout("nk", [4, DEPTH, 256, 512])
            O["nv"] = self.dout("nv", [4, DEPTH, 256, 512])
            O["ns5r"] = self.dout("ns5r", [4, DEPTH, 2, 1024])
            O["ns5i"] = self.dout("ns5i", [4, DEPTH, 2, 1024])
            O["nlru"] = self.dout("nlru", [4, DEPTH, 2, 256])

            self.ar = Arena(fw, 207 * 1024)
            ps_t = st.enter_context(nc.psum_tensor("psum", [128, 8, 512], F32))
            self.banks = [Reg(ps_t[:, i, :], [Buf(f"bank{i}", excl=True)], i) for i in range(8)]
            self.bank_i = 0
            self.held = set()
            try:
                self.setup()
                self.chk("setup")
                for l in range(DEPTH):
                    if l == 0:
                        self.bg = self.mod_gen(0, look=4, s0=0, s1=8, doA=(True, False))
                        self.drain()
                    self.chk(f"prep{l}")
                    for g in range(2):
                        self.group_layer(l, g)
                        self.chk(f"gl{l}{g}")
                self.final_norm(1)
            except _Stop:
                pass
            fw.emit()
        return nc

    def chk(self, name):
        if self.stage == name:
            raise _Stop()

    def dbg(self, name, reg, ap, shape):
        t = self.nc.dram_tensor("dbg_" + name, list(shape), F32, kind="ExternalOutput").ap()
        d = self.fw.dma("sp", lambda h: h.dma_start(out=t, in_=ap), Buf("dbg_" + name), reads=[reg])
        self.fw.final.append(d)
        self.dbg_outs.append("dbg_" + name)

    def bank(self, hold=False):
        while True:
            i = self.bank_i % 8
            self.bank_i += 1
            if i not in self.held:
                break
        if hold:
            self.held.add(i)
        return self.banks[i]

    def unhold(self, bk):
        self.held.discard(bk.tag)

    def col(self, reg, i):
        return reg.ap[:, i:i + 1]

    def setup(self):
        fw, ar, nc, I = self.fw, self.ar, self.nc, self.I
        self.x = ar.alloc([128, 2, 8, TG], F32)
        self.ident_bf = ar.alloc([128, 128], BF16, align=PAGE)
        self.ident_f = ar.alloc([128, 128], F32)
        self.ones_bf = ar.alloc([128, 128], BF16)
        self.epsc = ar.alloc([128, 1], F32)
        self.gcols = ar.alloc([128, 5, 8], F32)
        self.cTb = ar.alloc([128, 8, 2], BF16)
        self.badaT = ar.alloc([128, 2, 48], F32)
        self.mod = ar.alloc([128, 2, 48, 2], F32, align=PAGE)
        self.A1 = ar.alloc([128, 2, 8, 2], F32)
        self.A2 = ar.alloc([128, 2, 8, 2], F32)
        self.s5cols = ar.alloc([128, 16, 8], F32, align=PAGE)
        self.lrucols = ar.alloc([128, 2, 16], F32)
        self.s5d = ar.alloc([128, 2], F32)
        self.ffcols = ar.alloc([128, 22, 4], F32)
        self.lrust0 = ar.alloc([128, 2, 2], F32)
        self.lruw = ar.alloc([128, 8, 128], BF16)
        self.wglu = ar.alloc([128, 2, 256], BF16)
        self.s5w = ar.alloc([128, 16, 5, 128], BF16, align=PAGE)
        self.outst = ar.alloc([128, 4, 2, 8, 2], F32, align=PAGE)
        self.lrust = ar.alloc([128, 4, 2, 2], F32)
        self.slots = [ar.alloc([128, 2048], BF16, align=PAGE) for _ in range(6)]
        self.slot_sem = [Buf(f"slotsem{i}") for i in range(6)]
        self.slot_i = 0
        self.stg = [ar.alloc([128, 256], F32, align=PAGE) for _ in range(2)]
        self.stg_sem = [Buf("stg0"), Buf("stg1")]
        self.stg_i = 0
        self.small_sem = Buf("small")
        self.h = ar.alloc([128, 8, TG], BF16, align=PAGE)
        self.bg = None
        fw.op("pool", lambda h: h.memset(self.epsc.ap, EPS), writes=[self.epsc])
        self.scr_mark = ar.mark()

        idb, idf, ones = self.ident_bf, self.ident_f, self.ones_bf
        fw.op("pool", lambda h: h.memset(idf.ap, 1.0), writes=[idf])
        fw.op("pool", lambda h: h.affine_select(out=idf.ap, in_=idf.ap, pattern=[[-1, 128]], compare_op=ALU.is_equal,
                                                fill=0.0, base=0, channel_multiplier=1), reads=[idf], writes=[idf])
        fw.op("dve", lambda h: h.tensor_copy(out=idb.ap, in_=idf.ap), reads=[idf], writes=[idb])
        fw.op("dve", lambda h: h.memset(ones.ap, 1.0), writes=[ones])

        self.scr_mark = ar.mark()
        self.prep_bg = False
        self.mod_init()
        self.bg = self.layer_prep_gen(0)
        for g, name in enumerate(("xp", "xs")):
            for tt in range(8):
                for qd in range(4):
                    s = self.stg_i % 2
                    self.stg_i += 1
                    stg = self.stg[s]
                    src = I[name][tt * 128:(tt + 1) * 128, qd * 256:(qd + 1) * 256]
                    fw.dma("sp", lambda h, stg=stg, src=src: h.dma_start(out=stg.ap, in_=src), self.stg_sem[s], writes=[stg])
                    bk = self.bank()
                    fw.mm([lambda h, bk=bk, stg=stg, j=j: h.transpose(out=bk.ap[:, j * 128:(j + 1) * 128], in_=stg.ap[:, j * 128:(j + 1) * 128],
                                                                      identity=idf.ap) for j in range(2)], reads=[stg, idf], writes=[bk])
                    dstv = self.x.ap[:, g, qd * 2:qd * 2 + 2, tt * 128:(tt + 1) * 128]
                    srcv = bk.ap[:, 0:256].rearrange("p (j t) -> p j t", j=2)
                    if qd % 2:
                        fw.op("act", lambda h, dstv=dstv, srcv=srcv: h.activation(out=dstv, in_=srcv, func=AF.Copy), reads=[bk], writes=[self.x])
                    else:
                        fw.op("dve", lambda h, dstv=dstv, srcv=srcv: h.tensor_copy(out=dstv, in_=srcv), reads=[bk], writes=[self.x])
                    if g == 0:
                        self.tick()
        self.drain()

    def small_load(self, dst_ap, src_ap, dst_reg):
        return self.sload(dst_ap, src_ap, dst_reg)

    def wslice(self, parts):
        s = self.slot_i % 6
        self.slot_i += 1
        slot = self.slots[s]
        for (off, kc, ncols, src) in parts:
            dst = slot.ap[:, off:off + kc * ncols].rearrange("p (k n) -> p k n", k=kc)
            sv = src.rearrange("(k p) n -> p k n", p=128)
            self.fw.dma("pool", lambda h, dst=dst, sv=sv: h.dma_start(out=dst, in_=sv), self.slot_sem[s], writes=[slot])
        return slot

    def mod_init(self):
        fw, ar, I = self.fw, self.ar, self.I
        m = ar.mark()
        cT = ar.alloc([128, 8, 2], F32)
        for cc in range(2):
            self.small_load(cT.ap[:, :, cc], I["cvec"][cc].rearrange("(k p) -> p k", p=128), cT)
            self.small_load(self.badaT.ap[:, cc, :], I["b_ada"][cc].rearrange("(t p) -> p t", p=128), self.badaT)
            self.small_load(self.gcols.ap[:, cc, :], I["g_norm1"][cc].rearrange("(k p) -> p k", p=128), self.gcols)
            self.small_load(self.gcols.ap[:, 2 + cc, :], I["g_norm2"][cc].rearrange("(k p) -> p k", p=128), self.gcols)
        self.small_load(self.gcols.ap[:, 4, :], I["g_final"].rearrange("(k p) -> p k", p=128), self.gcols)
        fw.op("act", lambda h: h.activation(out=self.cTb.ap, in_=cT.ap, func=AF.Silu), reads=[cT], writes=[self.cTb])
        ar.release(m)

    def mod_gen(self, l, look=2, s0=0, s1=24, doA=(True, True)):
        fw, I = self.fw, self.I
        cTb, badaT = self.cTb, self.badaT
        pend = []
        nxt = s0
        for s in range(s0, s1):
            while nxt < s1 and nxt <= s + look:
                pend.append(self.wslice([(0, 8, 256, I["w_ada"][l][:, nxt * 256:(nxt + 1) * 256])]))
                nxt += 1
            slot = pend.pop(0)
            sv = slot.ap.rearrange("p (k n) -> p k n", k=8)
            bk = self.bank()
            for t in range(2):
                fw.mm([lambda h, bk=bk, sv=sv, t=t, kc=kc: h.matmul(bk.ap[:, t * 2:t * 2 + 2], lhsT=sv[:, kc, t * 128:(t + 1) * 128],
                                                                    rhs=cTb.ap[:, kc, :], start=(kc == 0), stop=(kc == 7)) for kc in range(8)],
                      reads=[slot, cTb], writes=[bk])
            fw.op("dve", lambda h, bk=bk, l=l, s=s: h.tensor_tensor(out=self.mod.ap[:, l, 2 * s:2 * s + 2, :],
                                                                   in0=bk.ap[:, 0:4].rearrange("p (t c) -> p t c", t=2),
                                                                   in1=badaT.ap[:, l, 2 * s:2 * s + 2].unsqueeze(2).to_broadcast([128, 2, 2]),
                                                                   op=ALU.add), reads=[bk, badaT], writes=[self.mod])
            yield
        for ai, (A, comp, gi) in enumerate(((self.A1, 1, 0), (self.A2, 4, 2))):
            if not doA[ai]:
                continue
            fw.op("dve", lambda h, A=A, comp=comp, gi=gi, l=l: h.scalar_tensor_tensor(
                out=A.ap[:, l], in0=self.mod.ap[:, l, comp * 8:comp * 8 + 8, :], scalar=1.0,
                in1=self.gcols.ap[:, gi + l, :].unsqueeze(2).to_broadcast([128, 8, 2]), op0=ALU.add, op1=ALU.mult),
                reads=[self.mod, self.gcols], writes=[A])

    def tick(self):
        if self.bg is not None:
            try:
                next(self.bg)
            except StopIteration:
                self.bg = None

    def drain(self):
        while self.bg is not None:
            self.tick()

    def rms_rstd(self, g, rstd):
        fw, ar = self.fw, self.ar
        m = ar.mark()
        sq = [ar.alloc([128, 512], BF16) for _ in range(2)]
        tmp = ar.alloc([128, 512], F32)
        for tb in range(2):
            bk = self.bank()
            for kc in range(8):
                q = sq[kc % 2]
                fw.op("act", lambda h, q=q, kc=kc, tb=tb: h.activation(out=q.ap, in_=self.x.ap[:, g, kc, tb * 512:(tb + 1) * 512], func=AF.Square),
                      reads=[self.x], writes=[q])
                fw.mm([lambda h, bk=bk, q=q, kc=kc: h.matmul(bk.ap, lhsT=self.ones_bf.ap, rhs=q.ap, start=(kc == 0), stop=(kc == 7))],
                      reads=[q, self.ones_bf], writes=[bk])
            fw.op("act", lambda h, bk=bk: h.activation(out=tmp.ap, in_=bk.ap, func=AF.Sqrt, scale=1.0 / D, bias=self.epsc.ap[:, 0:1]),
                  reads=[bk, self.epsc], writes=[tmp])
            fw.op("dve", lambda h, tb=tb: h.reciprocal(out=rstd.ap[:, tb * 512:(tb + 1) * 512], in_=tmp.ap), reads=[tmp], writes=[rstd])
        ar.release(m)

    def norm_mod(self, l, g, A, shcomp):
        fw, ar = self.fw, self.ar
        m = ar.mark()
        rstd = ar.alloc([128, TG], F32)
        self.rms_rstd(g, rstd)
        tmps = [ar.alloc([128, TG], F32) for _ in range(2)]
        for kc in range(8):
            t = tmps[kc % 2]
            fw.op("dve", lambda h, t=t, kc=kc: h.scalar_tensor_tensor(out=t.ap, in0=self.x.ap[:, g, kc, :], scalar=A.ap[:, l, kc, g:g + 1],
                                                                      in1=rstd.ap, op0=ALU.mult, op1=ALU.mult),
                  reads=[self.x, A, rstd], writes=[t])
            fw.op("act", lambda h, t=t, kc=kc: h.activation(out=self.h.ap[:, kc, :], in_=t.ap, func=AF.Identity,
                                                            bias=self.mod.ap[:, l, shcomp * 8 + kc, g:g + 1], scale=1.0),
                  reads=[t, self.mod], writes=[self.h])
        ar.release(m)

    def proj_fm(self, src_cols_fn, ntiles, rhs, rhs_regs, kcs, consume):
        fw = self.fw
        for sp in range(0, ntiles, 2):
            nt = min(2, ntiles - sp)
            slot = self.wslice(src_cols_fn(sp * 128, nt * 128))
            sv = slot.ap[:, 0:kcs * nt * 128].rearrange("p (k n) -> p k n", k=kcs)
            for t in range(nt):
                for tb in range(2):
                    bk = self.bank()
                    fw.mm([lambda h, bk=bk, sv=sv, t=t, tb=tb, kc=kc: h.matmul(bk.ap, lhsT=sv[:, kc, t * 128:(t + 1) * 128], rhs=rhs(kc, tb),
                                                                                start=(kc == 0), stop=(kc == kcs - 1)) for kc in range(kcs)],
                          reads=[slot] + rhs_regs, writes=[bk])
                    consume(sp + t, tb, bk)

    def win_cols(self, l, base):
        return lambda c0, n: [(0, 8, n, self.I["w_in"][l][:, base + c0:base + c0 + n])]

    def h_rhs(self, kc, tb):
        return self.h.ap[:, kc, tb * 512:(tb + 1) * 512]

    def tt(self, en, out, in0, in1, op, R, W):
        self.fw.op(en, lambda h: h.tensor_tensor(out=out, in0=in0, in1=in1, op=op), reads=R, writes=W)

    def ts(self, en, out, in0, s1, s2, op0, op1, R, W):
        if s2 is None:
            self.fw.op(en, lambda h: h.tensor_scalar(out=out, in0=in0, scalar1=s1, scalar2=None, op0=op0), reads=R, writes=W)
        else:
            self.fw.op(en, lambda h: h.tensor_scalar(out=out, in0=in0, scalar1=s1, scalar2=s2, op0=op0, op1=op1), reads=R, writes=W)

    def stt(self, out, in0, sc, in1, op0, op1, R, W, en="dve"):
        self.fw.op(en, lambda h: h.scalar_tensor_tensor(out=out, in0=in0, scalar=sc, in1=in1, op0=op0, op1=op1), reads=R, writes=W)

    def act(self, out, in_, func, R, W, bias=None, scale=None):
        kw = {}
        if bias is not None:
            kw["bias"] = bias
        if scale is not None:
            kw["scale"] = scale
        self.fw.op("act", lambda h: h.activation(out=out, in_=in_, func=func, **kw), reads=R, writes=W)

    def cp(self, en, out, in_, R, W):
        if en == "act":
            self.fw.op("act", lambda h: h.activation(out=out, in_=in_, func=AF.Copy), reads=R, writes=W)
        else:
            self.fw.op(en, lambda h: h.tensor_copy(out=out, in_=in_), reads=R, writes=W)

    def sincos(self, y, n, cos_out, sin_out, Wc, Ws):
        ar = self.ar
        MAGIC = 12582912.0
        m = ar.mark()
        kf = ar.alloc([128, n], F32)
        fc = ar.alloc([128, n], F32)
        self.ts("dve", kf.ap, y.ap, 0.25, MAGIC, ALU.add, ALU.add, [y], [kf])
        self.ts("dve", kf.ap, kf.ap, MAGIC, None, ALU.subtract, None, [kf], [kf])
        self.stt(fc.ap, y.ap, 0.25, kf.ap, ALU.add, ALU.subtract, [y, kf], [fc])
        self.act(cos_out, fc.ap, AF.Sin, [fc], [Wc], scale=2.0 * math.pi)
        self.ts("dve", kf.ap, y.ap, MAGIC, None, ALU.add, None, [y], [kf])
        self.ts("dve", kf.ap, kf.ap, MAGIC, None, ALU.subtract, None, [kf], [kf])
        self.tt("dve", y.ap, y.ap, kf.ap, ALU.subtract, [y, kf], [y])
        self.act(sin_out, y.ap, AF.Sin, [y], [Ws], scale=2.0 * math.pi)
        ar.release(m)

    _ssem_i = 0

    def sload(self, dst_ap, src_ap, dst_reg, q="pool"):
        if not hasattr(self, "ssems"):
            self.ssems = [Buf(f"ss{i}") for i in range(8)]
        s = self.ssems[Builder._ssem_i % 8]
        Builder._ssem_i += 1
        if q == "sp":
            return self.fw.dma("sp", lambda h: h.dma_start(out=dst_ap, in_=src_ap, allow_slow_non_contiguous=True), s, writes=[dst_reg])
        return self.fw.dma("pool", lambda h: h.dma_start(out=dst_ap, in_=src_ap, allow_slow_non_contiguous=True), s, writes=[dst_reg])

    def layer_prep_gen(self, l):
        fw, ar, I = self.fw, self.ar, self.I
        m = ar.mark()
        c = self.s5cols
        lam_r = ar.alloc([128, 16], F32)
        lam_i = ar.alloc([128, 16], F32)
        stp = ar.alloc([128, 16], F32)
        t1 = ar.alloc([128, 16], F32)
        t2 = ar.alloc([128, 16], F32)
        t3 = ar.alloc([128, 16], F32)
        yv = ar.alloc([128, 16], F32)
        cs = ar.alloc([128, 16], F32)
        sn = ar.alloc([128, 16], F32)
        nat = ar.alloc([128, 4, 8, 16], F32)
        Cn = [ar.alloc([128, 4, 2, 64], F32) for _ in range(2)]
        ct2 = ar.alloc([128, 128], F32)
        lam = ar.alloc([128, 2, 2], F32)
        xx = ar.alloc([128, 2, 2], F32)
        pp = ar.alloc([128, 2, 2], F32)
        msk = ar.alloc([128, 8, 8], F32)
        Br = ar.alloc([128, 8, 16], F32)
        Bi = ar.alloc([128, 8, 16], F32)
        bb = [ar.alloc([128, 2, 8, 16], F32) for _ in range(2)]
        tA = ar.alloc([128, 2, 8, 16], F32)
        self.sload(lam_r.ap, I["s5_lam_re"][l].rearrange("d (j gl) n -> (gl n) (d j)", gl=2), lam_r)
        self.sload(lam_i.ap, I["s5_lam_im"][l].rearrange("d (j gl) n -> (gl n) (d j)", gl=2), lam_i)
        for gl in range(2):
            src = bass.AP(I["s5_log_step"].tensor, l * 32 + gl, [[0, 64], [2, 16]])
            self.sload(stp.ap[64 * gl:64 * gl + 64, :], src, stp)
        self.sload(c.ap[:, :, 6], I["s5r"][l].rearrange("d (j q) -> q (d j)", q=128), c)
        self.sload(c.ap[:, :, 7], I["s5i"][l].rearrange("d (j q) -> q (d j)", q=128), c)
        yield
        self.act(stp.ap, stp.ap, AF.Exp, [stp], [stp])
        self.tt("dve", t1.ap, lam_r.ap, stp.ap, ALU.mult, [lam_r, stp], [t1])
        self.tt("dve", t2.ap, lam_i.ap, stp.ap, ALU.mult, [lam_i, stp], [t2])
        self.act(c.ap[:, :, 1], t1.ap, AF.Exp, [t1], [c])
        self.ts("dve", yv.ap, t2.ap, 1.0 / (2.0 * math.pi), None, ALU.mult, None, [t2], [yv])
        self.sincos(yv, 16, cs.ap, sn.ap, cs, sn)
        self.cp("dve", c.ap[:, :, 0], yv.ap, [yv], [c])
        self.tt("dve", c.ap[:, :, 2], c.ap[:, :, 1], cs.ap, ALU.mult, [c, cs], [c])
        self.tt("dve", c.ap[:, :, 3], c.ap[:, :, 1], sn.ap, ALU.mult, [c, sn], [c])
        yield
        self.ts("dve", t1.ap, c.ap[:, :, 2], -1.0, None, ALU.add, None, [c], [t1])
        self.tt("dve", t2.ap, lam_r.ap, lam_r.ap, ALU.mult, [lam_r], [t2])
        self.tt("dve", t3.ap, lam_i.ap, lam_i.ap, ALU.mult, [lam_i], [t3])
        self.tt("dve", t2.ap, t2.ap, t3.ap, ALU.add, [t2, t3], [t2])
        fw.op("dve", lambda h: h.reciprocal(out=t2.ap, in_=t2.ap), reads=[t2], writes=[t2])
        self.tt("dve", t3.ap, t1.ap, lam_r.ap, ALU.mult, [t1, lam_r], [t3])
        self.tt("dve", yv.ap, c.ap[:, :, 3], lam_i.ap, ALU.mult, [c, lam_i], [yv])
        self.tt("dve", t3.ap, t3.ap, yv.ap, ALU.add, [t3, yv], [t3])
        self.tt("dve", c.ap[:, :, 4], t3.ap, t2.ap, ALU.mult, [t3, t2], [c])
        self.tt("dve", t3.ap, c.ap[:, :, 3], lam_r.ap, ALU.mult, [c, lam_r], [t3])
        self.tt("dve", yv.ap, t1.ap, lam_i.ap, ALU.mult, [t1, lam_i], [yv])
        self.tt("dve", t3.ap, t3.ap, yv.ap, ALU.subtract, [t3, yv], [t3])
        self.tt("dve", c.ap[:, :, 5], t3.ap, t2.ap, ALU.mult, [t3, t2], [c])

        yield
        fw.op("pool", lambda h: h.memset(msk.ap, 1.0), writes=[msk])
        for gl in range(2):
            fw.op("pool", lambda h, gl=gl: h.affine_select(out=msk.ap[64 * gl:64 * gl + 64], in_=msk.ap[64 * gl:64 * gl + 64],
                                                           pattern=[[0, 2], [-2, 4], [1, 8]], compare_op=ALU.is_equal, fill=0.0,
                                                           base=-gl, channel_multiplier=0), reads=[msk], writes=[msk])
        self.sload(Br.ap, I["s5_b_re"][l].rearrange("(j gl) n p -> (gl n) j p", gl=2), Br)
        self.sload(Bi.ap, I["s5_b_im"][l].rearrange("(j gl) n p -> (gl n) j p", gl=2), Bi)
        kr = c.ap[:, :, 4].rearrange("p (d j) -> p d j", d=2).unsqueeze(3).to_broadcast([128, 2, 8, 16])
        ki = c.ap[:, :, 5].rearrange("p (d j) -> p d j", d=2).unsqueeze(3).to_broadcast([128, 2, 8, 16])
        Brb = Br.ap.unsqueeze(1).to_broadcast([128, 2, 8, 16])
        Bib = Bi.ap.unsqueeze(1).to_broadcast([128, 2, 8, 16])
        self.tt("dve", bb[0].ap, Brb, kr, ALU.mult, [Br, c], [bb[0]])
        self.tt("dve", tA.ap, Bib, ki, ALU.mult, [Bi, c], [tA])
        self.tt("dve", bb[0].ap, bb[0].ap, tA.ap, ALU.subtract, [bb[0], tA], [bb[0]])
        self.tt("dve", bb[1].ap, Bib, kr, ALU.mult, [Bi, c], [bb[1]])
        self.tt("dve", tA.ap, Brb, ki, ALU.mult, [Br, c], [tA])
        self.tt("dve", bb[1].ap, bb[1].ap, tA.ap, ALU.add, [bb[1], tA], [bb[1]])
        yield
        for ri in range(2):
            for q4 in range(4):
                d, j0 = q4 // 2, (q4 % 2) * 4
                for jj in range(4):
                    j = j0 + jj
                    self.tt("dve", nat.ap[:, jj], bb[ri].ap[:, d, j].unsqueeze(1).to_broadcast([128, 8, 16]),
                            msk.ap[:, j].unsqueeze(2).to_broadcast([128, 8, 16]), ALU.mult, [bb[ri], msk], [nat])
                yield
                bk = self.bank()
                fw.mm([lambda h, bk=bk, jj=jj: h.transpose(out=bk.ap[:, jj * 128:(jj + 1) * 128],
                                                           in_=nat.ap[:, jj].rearrange("p a b -> p (a b)"), identity=self.ident_f.ap)
                       for jj in range(4)], reads=[nat, self.ident_f], writes=[bk])
                self.cp("act", self.s5w.ap[:, d * 8 + j0:d * 8 + j0 + 4, ri, :], bk.ap.rearrange("p (a b) -> p a b", a=4), [bk], [self.s5w])
        for hf in range(2):
            self.sload(Cn[0].ap[:, :, hf, :], I["s5_c_re"][l].rearrange("d g p n -> (d g p) n").rearrange("(q r) n -> r q n", r=128), Cn[0])
            self.sload(Cn[1].ap[:, :, hf, :], I["s5_c_im"][l].rearrange("d g p n -> (d g p) n").rearrange("(q r) n -> r q n", r=128), Cn[1])
        for ri in range(2):
            for q in range(4):
                d, ut = q // 2, q % 2
                bk = self.bank()
                fw.mm([lambda h, bk=bk, ri=ri, q=q: h.matmul(bk.ap[:, 0:128], lhsT=Cn[ri].ap[:, q].rearrange("p a b -> p (a b)"),
                                                             rhs=self.ident_f.ap, start=True, stop=True)],
                      reads=[Cn[ri], self.ident_f], writes=[bk])
                if ri == 0:
                    self.cp("act", ct2.ap, bk.ap[:, 0:128], [bk], [ct2])
                else:
                    self.act(ct2.ap, bk.ap[:, 0:128], AF.Copy, [bk], [ct2], scale=-1.0)
                j0 = ut * 4
                self.tt("dve", self.s5w.ap[:, d * 8 + j0:d * 8 + j0 + 4, 2 + ri, :].rearrange("p j (g q) -> p j g q", g=8),
                        ct2.ap.rearrange("p (g q) -> p g q", g=8).unsqueeze(1).to_broadcast([128, 4, 8, 16]),
                        msk.ap[:, j0:j0 + 4].unsqueeze(3).to_broadcast([128, 4, 8, 16]), ALU.mult, [ct2, msk], [self.s5w])
                if ri == 0:
                    self.act(ct2.ap, bk.ap[:, 0:128], AF.Copy, [bk], [ct2], scale=-1.0)
                    self.tt("dve", self.s5w.ap[:, d * 8 + j0:d * 8 + j0 + 4, 4, :].rearrange("p j (g q) -> p j g q", g=8),
                            ct2.ap.rearrange("p (g q) -> p g q", g=8).unsqueeze(1).to_broadcast([128, 4, 8, 16]),
                            msk.ap[:, j0:j0 + 4].unsqueeze(3).to_broadcast([128, 4, 8, 16]), ALU.mult, [ct2, msk], [self.s5w])
                yield
        self.sload(self.s5d.ap, I["s5_d"][l].rearrange("(t p) -> p t", p=128), self.s5d)
        yield
        lc = self.lrucols
        for k in range(4):
            self.sload(lc.ap[:, :, k], I["lru_conv_w"][l, k].rearrange("(t p) -> p t", p=128), lc)
        self.sload(lc.ap[:, :, 4], I["lru_conv_b"][l].rearrange("(t p) -> p t", p=128), lc)
        for d in range(2):
            self.sload(lc.ap[:, :, 5 + d], I["lru_b_a"][l, d].rearrange("(t p) -> p t", p=128), lc)
            self.sload(lc.ap[:, :, 7 + d], I["lru_b_x"][l, d].rearrange("(t p) -> p t", p=128), lc)
        for d in range(2):
            self.sload(lam.ap[:, :, d], I["lru_lam"][l, d].rearrange("(t p) -> p t", p=128), lam)
        self.act(xx.ap, lam.ap, AF.Exp, [lam], [xx], scale=-1.0)
        self.ts("dve", pp.ap, xx.ap, -0.25, 1.0 / 3.0, ALU.mult, ALU.add, [xx], [pp])
        self.tt("dve", pp.ap, pp.ap, xx.ap, ALU.mult, [pp, xx], [pp])
        self.ts("dve", pp.ap, pp.ap, -1.0, 0.5, ALU.mult, ALU.add, [pp], [pp])
        self.tt("dve", pp.ap, pp.ap, xx.ap, ALU.mult, [pp, xx], [pp])
        self.ts("dve", pp.ap, pp.ap, -1.0, 1.0, ALU.mult, ALU.add, [pp], [pp])
        self.tt("dve", pp.ap, pp.ap, xx.ap, ALU.mult, [pp, xx], [pp])
        self.ts("dve", lc.ap[:, :, 9:11], pp.ap, -8.0, None, ALU.mult, None, [pp], [lc])
        self.ts("dve", lc.ap[:, :, 11:13], pp.ap, 8.0, None, ALU.mult, None, [pp], [lc])
        self.ts("dve", lc.ap[:, :, 13:15], pp.ap, -16.0, None, ALU.mult, None, [pp], [lc])
        yield
        fw.op("pool", lambda h: h.memset(self.lruw.ap, 0.0), writes=[self.lruw])
        for gi, nm in enumerate(("lru_w_a", "lru_w_x")):
            for d in range(2):
                for t in range(2):
                    for b2 in range(2):
                        idx = gi * 4 + d * 2 + t
                        self.sload(self.lruw.ap[64 * b2:64 * b2 + 64, idx, 64 * b2:64 * b2 + 64], I[nm][l, d, 2 * t + b2], self.lruw, q="pool")
        for d in range(2):
            self.sload(self.lrust0.ap[:, d, :], I["slru"][l, d].rearrange("(t p) -> p t", p=128), self.lrust0)
        self.sload(self.wglu.ap, I["s5_w_glu"][l].rearrange("(k p) n -> p k n", p=128), self.wglu, q="pool")
        yield
        if not self.prep_bg:
            ar.release(m)

    def ffn_prep(self, l):
        I = self.I
        for k in range(3):
            self.sload(self.ffcols.ap[:, :, k], I["ffn_conv_w"][l, k].rearrange("(t p) -> p t", p=128), self.ffcols)
        self.sload(self.ffcols.ap[:, :, 3], I["ffn_conv_b"][l].rearrange("(t p) -> p t", p=128), self.ffcols)

    def group_layer(self, l, g):
        fw, ar, I, O = self.fw, self.ar, self.I, self.O
        if g == 0:
            self.ffn_prep(l)
        self.norm_mod(l, g, self.A1, 0)
        self.chk(f"norm{l}{g}")
        m0 = ar.mark()
        attnT = ar.alloc([128, 4, TG], BF16)
        m1 = ar.mark()
        self.attention(l, g, attnT)
        self.chk(f"attn{l}{g}")
        ar.release(m1)
        s5y = ar.alloc([128, 2, TG], BF16)
        m1 = ar.mark()
        self.s5(l, g, s5y)
        self.chk(f"s5{l}{g}")
        ar.release(m1)
        if g == 0:
            self.store_s5_states(l)
        lruy = ar.alloc([128, 2, TG], BF16)
        m1 = ar.mark()
        self.lru(l, g, lruy)
        self.chk(f"lru{l}{g}")
        ar.release(m1)
        if g == 1 and l + 1 < DEPTH:
            self.prep_bg = True
            self.bg = self.layer_prep_gen(l + 1)
            self.tick()
        if g == 1 and l + 1 == DEPTH:
            self.bg = self.final_norm_gen(0)
            self.tick()
        self.merge(l, g, attnT, s5y, lruy)
        self.drain()
        self.chk(f"merge{l}{g}")
        ar.release(m0)
        self.norm_mod(l, g, self.A2, 3)
        self.ffn(l, g)
        ar.release(m0)

    def attention(self, l, g, attnT):
        fw, ar, I, O = self.fw, self.ar, self.I, self.O
        q_sb = ar.alloc([128, 4, TG], BF16)
        k_sb = ar.alloc([128, 4, TG], BF16)
        v_aug = ar.alloc([128, 8, 8, 66], BF16)
        fw.op("pool", lambda h: h.memset(v_aug.ap[:, :, :, 64:65], 1.0), writes=[v_aug])

        def q_cons(ft, tb, bk):
            self.act(q_sb.ap[:, ft, tb * 512:(tb + 1) * 512], bk.ap, AF.Copy, [bk], [q_sb], scale=0.125)

        def k_cons(ft, tb, bk):
            self.cp("dve", k_sb.ap[:, ft, tb * 512:(tb + 1) * 512], bk.ap, [bk], [k_sb])

        self.proj_fm(self.win_cols(l, 0), 4, self.h_rhs, [self.h], 8, q_cons)
        self.proj_fm(self.win_cols(l, 512), 4, self.h_rhs, [self.h], 8, k_cons)
        self.chk(f"attq{l}{g}")
        for which in ((1, 2) if g == 0 else (2,)):
            for half in range(2):
                slot = self.wslice([(0, 8, 256, I["w_in"][l][:, which * 512 + half * 256: which * 512 + half * 256 + 256])])
                sv = slot.ap.rearrange("p (k n) -> p k n", k=8)
                for tt in range(8):
                    bk = self.bank()
                    fw.mm([lambda h, bk=bk, sv=sv, tt=tt, kc=kc: h.matmul(bk.ap[:, 0:256], lhsT=self.h.ap[:, kc, tt * 128:(tt + 1) * 128], rhs=sv[:, kc, :],
                                                                          start=(kc == 0), stop=(kc == 7)) for kc in range(8)],
                          reads=[slot, self.h], writes=[bk])
                    if which == 2:
                        self.cp("act", v_aug.ap[:, tt, half * 4:half * 4 + 4, 0:64], bk.ap[:, 0:256].rearrange("p (a b) -> p a b", a=4), [bk], [v_aug])
                    if g == 0:
                        s = self.stg_i % 2
                        self.stg_i += 1
                        stg = self.stg[s]
                        self.cp("dve", stg.ap[:, 0:256], bk.ap[:, 0:256], [bk], [stg])
                        dst = O["nk" if which == 1 else "nv"][tt // 2, l, (tt % 2) * 128:(tt % 2) * 128 + 128, half * 256:half * 256 + 256]
                        d = fw.dma("sp", lambda h, stg=stg, dst=dst: h.dma_start(out=dst, in_=stg.ap[:, 0:256]), self.stg_sem[s], reads=[stg])
                        fw.final.append(d)
                    self.chk(f"kv1{l}{g}")
            if which == 1:
                self.chk(f"kvK{l}{g}")

        self.chk(f"attkv{l}{g}")
        pT = [ar.alloc([128, 512], BF16) for _ in range(2)]
        pi = [0]
        atok = ar.alloc([128, 8, 128], BF16)
        rec = ar.alloc([128, 8], F32)

        def transposes(hp, tts):
            bk = self.bank()
            bkb = bk.ap.bitcast(BF16)
            fw.mm([lambda h, bkb=bkb, i=i, tt=tt: h.transpose(out=bkb[:, i * 128:(i + 1) * 128], in_=atok.ap[:, tt, :], identity=self.ident_bf.ap)
                   for i, tt in enumerate(tts)], reads=[atok, self.ident_bf], writes=[bk])
            n = len(tts)
            self.cp("dve", attnT.ap[:, hp, tts[0] * 128:(tts[0] + n) * 128], bkb[:, 0:n * 128], [bk], [attnT])

        if g == 0:
            UP = [(sq, hp, e) for sq in range(4) for hp in range(4) for e in range(2)]

            def issue_score_p(k):
                sq, hp, e = UP[k]
                pb = 64 * e
                sb = self.bank()
                fw.mm([lambda h, sb=sb, kt=kt, pb=pb, sq=sq, hp=hp: h.matmul(sb.ap[:, kt * 256:(kt + 1) * 256],
                                                                            lhsT=k_sb.ap[pb:pb + 64, hp, sq * 256 + kt * 128:sq * 256 + kt * 128 + 128],
                                                                            rhs=q_sb.ap[pb:pb + 64, hp, sq * 256:sq * 256 + 256], start=True, stop=True) for kt in range(2)],
                      reads=[k_sb, q_sb], writes=[sb])
                return sb

            nxt = issue_score_p(0)
            ob = ov = None
            for k, (sq, hp, e) in enumerate(UP):
                hh = 2 * hp + e
                if e == 0:
                    ob = self.bank(hold=True)
                    ov = ob.ap[:, 0:260].rearrange("p (q e c) -> p q e c", q=2, e=2)
                sb = nxt
                p = pT[k % 2]
                self.act(p.ap, sb.ap, AF.Exp, [sb], [p])
                if k + 1 < len(UP):
                    nxt = issue_score_p(k + 1)
                fns = []
                for qt in range(2):
                    for kt in range(2):
                        fns.append(lambda h, qt=qt, kt=kt, e=e, hh=hh, p=p, ov=ov, sq=sq, st_=(e == 0 and qt == 0 and kt == 0): h.matmul(
                            ov[:, qt, e, :], lhsT=p.ap[:, kt * 256 + qt * 128:kt * 256 + qt * 128 + 128], rhs=v_aug.ap[:, 2 * sq + kt, hh, 0:65],
                            start=st_, stop=(e == 1 and qt == 1 and kt == 1)))
                fw.mm(fns, reads=[p, v_aug], writes=[ob])
                if e == 1:
                    fw.op("dve", lambda h, ov=ov: h.reciprocal(out=rec.ap[:, 0:4].rearrange("p (q e) -> p q e", q=2), in_=ov[:, :, :, 64]), reads=[ob], writes=[rec])
                    self.tt("dve", atok.ap[:, 2 * sq:2 * sq + 2, :].rearrange("p q (e c) -> p q e c", e=2), ov[:, :, :, 0:64],
                            rec.ap[:, 0:4].rearrange("p (q e) -> p q e", q=2).unsqueeze(3).to_broadcast([128, 2, 2, 64]), ALU.mult, [ob, rec], [atok])
                    self.unhold(ob)
                    transposes(hp, [2 * sq, 2 * sq + 1])
        else:
            self.na_attention(l, attnT, q_sb, k_sb, v_aug, pT, atok, rec, transposes)

    def na_attention(self, l, attnT, q_sb, k_sb, v_aug, pT, atok, rec, transposes):
        fw, ar, I = self.fw, self.ar, self.I
        kctxT = ar.alloc([128, 4, 512], BF16)
        vctx = ar.alloc([128, 4, 8, 66], BF16)
        fw.op("pool", lambda h: h.memset(vctx.ap[:, :, :, 64:65], 1.0), writes=[vctx])
        cvsem = Buf("cvsem")
        for tt in range(4):
            fw.dma("pool", lambda h, tt=tt: h.dma_start(out=vctx.ap[:, tt, :, 0:64], in_=I["cv"][l][tt * 128:(tt + 1) * 128, :].rearrange("p (a b) -> p a b", a=8)),
                   cvsem, writes=[vctx])
        mk = ar.mark()
        cktok = ar.alloc([128, 4, 512], BF16)
        cksem = Buf("cksem")
        fw.dma("pool", lambda h: h.dma_start(out=cktok.ap, in_=I["ck"][l].rearrange("(t p) f -> p t f", p=128)), cksem, writes=[cktok])
        for hp in range(4):
            bk = self.bank()
            bkb = bk.ap.bitcast(BF16)
            fw.mm([lambda h, bkb=bkb, tt=tt, hp=hp: h.transpose(out=bkb[:, tt * 128:(tt + 1) * 128], in_=cktok.ap[:, tt, hp * 128:(hp + 1) * 128],
                                                               identity=self.ident_bf.ap) for tt in range(4)], reads=[cktok, self.ident_bf], writes=[bk])
            self.cp("dve", kctxT.ap[:, hp, :], bkb[:, 0:512], [bk], [kctxT])
        ar.release(mk)
        LT = ar.alloc([128, 8, 18, 64], BF16)
        mk = ar.mark()
        rp = ar.alloc([128, 2, 32], F32)
        fw.op("pool", lambda h: h.memset(rp.ap, 0.0), writes=[rp])
        for e in range(2):
            self.sload(rp.ap[0:120, e, 0:31], I["rpb"][l].rearrange("h a x -> (h a) x"), rp)
        Rs = ar.alloc([64, 8, 15], F32)
        RE = ar.alloc([64, 8, 18], F32)
        BB = ar.alloc([64, 2, 127], F32)
        msk = ar.alloc([128, 64], F32)
        m2 = ar.alloc([128, 64], F32)
        bk = self.bank()
        fw.mm([lambda h, bk=bk: h.matmul(bk.ap[0:64, 0:120], lhsT=rp.ap[0:120].rearrange("p a b -> p (a b)"),
                                         rhs=self.ident_f.ap[0:120, 0:120], start=True, stop=True)], reads=[rp, self.ident_f], writes=[bk])
        fw.op("pool", lambda h: h.memset(Rs.ap, 0.0), writes=[Rs])
        for e in range(2):
            self.cp("dve", Rs.ap[32 * e:32 * e + 31].rearrange("p a b -> p (a b)"), bk.ap[32 * e:32 * e + 31, 0:120], [bk], [Rs])
        fw.op("pool", lambda h: h.memset(RE.ap, 0.0), writes=[RE])
        for e in range(2):
            self.cp("dve", RE.ap[32 * e:32 * e + 31, :, e + 1:e + 16], Rs.ap[32 * e:32 * e + 31, :, ::-1], [Rs], [RE])
        fw.op("pool", lambda h: h.memset(BB.ap, 0.0), writes=[BB])
        for e in range(2):
            fw.op("pool", lambda h, e=e: h.memset(BB.ap[32 * e:32 * e + 32, e, :], 1.0), writes=[BB])
            fw.op("pool", lambda h, e=e: h.affine_select(out=BB.ap[32 * e:32 * e + 32, e, :], in_=BB.ap[32 * e:32 * e + 32, e, :], pattern=[[1, 127]],
                                                          compare_op=ALU.is_equal, fill=0.0, base=-48, channel_multiplier=-1), reads=[BB], writes=[BB])
        fw.op("pool", lambda h: h.memset(msk.ap, 0.0), writes=[msk])
        fw.op("pool", lambda h: h.memset(m2.ap, 0.0), writes=[m2])
        for hf in range(2):
            sl = slice(64 * hf, 64 * hf + 64)
            fw.op("pool", lambda h, sl=sl: h.affine_select(out=msk.ap[sl], in_=msk.ap[sl], pattern=[[-1, 64]], compare_op=ALU.is_ge, fill=NEG,
                                                            base=8, channel_multiplier=1), reads=[msk], writes=[msk])
            fw.op("pool", lambda h, sl=sl: h.affine_select(out=msk.ap[sl], in_=msk.ap[sl], pattern=[[0, 64]], compare_op=ALU.is_ge, fill=0.0,
                                                            base=47, channel_multiplier=-1), reads=[msk], writes=[msk])
            fw.op("pool", lambda h, sl=sl: h.affine_select(out=m2.ap[sl], in_=m2.ap[sl], pattern=[[1, 64]], compare_op=ALU.is_ge, fill=NEG,
                                                            base=7, channel_multiplier=-1), reads=[m2], writes=[m2])
            fw.op("pool", lambda h, sl=sl: h.affine_select(out=m2.ap[sl], in_=m2.ap[sl], pattern=[[0, 64]], compare_op=ALU.is_ge, fill=0.0,
                                                            base=-16, channel_multiplier=1), reads=[m2], writes=[m2])
        self.tt("pool", msk.ap, msk.ap, m2.ap, ALU.add, [msk, m2], [msk])
        REf = RE.ap.rearrange("p a b -> p (a b)")
        for q0 in range(0, 64, 3):
            nq = min(3, 64 - q0)
            bk = self.bank()
            for i in range(nq):
                qc = q0 + i
                fw.mm([lambda h, bk=bk, i=i, qc=qc, e=e: h.matmul(bk.ap[64 * e:64 * e + 64, i * 144:(i + 1) * 144], lhsT=BB.ap[:, e, 63 - qc:63 - qc + 64], rhs=REf,
                                                                  start=True, stop=True) for e in range(2)], reads=[BB, RE], writes=[bk])
            outv = LT.ap[:, :, :, q0:q0 + nq].rearrange("p h d q -> p q (h d)")
            self.tt("dve", outv, bk.ap[:, 0:nq * 144].rearrange("p (q n) -> p q n", q=nq),
                    msk.ap[:, q0:q0 + nq].unsqueeze(2).to_broadcast([128, nq, 144]), ALU.add, [bk, msk], [LT])
        LTf = LT.ap.rearrange("p h d q -> p (h d q)")
        for hq in range(4):
            self.act(LTf[:, hq * 2304:(hq + 1) * 2304], LTf[:, hq * 2304:(hq + 1) * 2304], AF.Exp, [LT], [LT])
        ar.release(mk)

        U = []
        for hp in range(4):
            for e in range(2):
                for c in range(2):
                    grp = []
                    for mt in range(8):
                        js = [j for j in range(4 * c, 4 * c + 4) if any(na_valid(2 * mt + ee, 2 * j + r) for ee in range(2) for r in range(2))]
                        if js:
                            grp.append(("loc", mt, js[0], js[-1]))
                    for kt in range(4):
                        grp.append(("ctx", kt, 4 * c, 4 * c + 3))
                    for ui, (kind, mt, ja, jb) in enumerate(grp):
                        U.append((hp, e, c, kind, mt, ja, jb, ui == 0, ui == len(grp) - 1))

        def issue_score(k):
            hp, e, c, kind, mt, ja, jb, gfirst, glast = U[k]
            pb = 64 * e
            hh = 2 * hp + e
            nq = 128 * (jb - ja + 1)
            sb = self.bank()
            if kind == "loc":
                d0 = 2 * ja - 2 * mt + 8
                d1 = 2 * jb + 1 - 2 * mt + 8
                assert 0 <= d0 and d1 < 18, (mt, ja, jb)
                ltv = LT.ap[:, hh, d0:d1 + 1, :].rearrange("p a b -> p (a b)")
                fw.mm([lambda h, sb=sb, mt=mt, ja=ja, nq=nq, pb=pb, hp=hp: h.matmul(sb.ap[:, 0:nq], lhsT=k_sb.ap[pb:pb + 64, hp, mt * 128:(mt + 1) * 128],
                                                                                   rhs=q_sb.ap[pb:pb + 64, hp, ja * 128:ja * 128 + nq], start=True, stop=True)],
                      reads=[k_sb, q_sb], writes=[sb])
            else:
                fw.mm([lambda h, sb=sb, mt=mt, ja=ja, nq=nq, pb=pb, hp=hp: h.matmul(sb.ap[:, 0:nq], lhsT=kctxT.ap[pb:pb + 64, hp, mt * 128:(mt + 1) * 128],
                                                                                   rhs=q_sb.ap[pb:pb + 64, hp, ja * 128:ja * 128 + nq], start=True, stop=True)],
                      reads=[kctxT, q_sb], writes=[sb])
            return sb

        nxt = issue_score(0)
        ob = ov = None
        first = True
        for k, (hp, e, c, kind, mt, ja, jb, gfirst, glast) in enumerate(U):
            hh = 2 * hp + e
            nq = 128 * (jb - ja + 1)
            if gfirst:
                ob = self.bank(hold=True)
                ov = ob.ap[:, 0:260].rearrange("p (q c) -> p q c", q=4)
                first = True
            sb = nxt
            p = pT[k % 2]
            self.act(p.ap[:, 0:nq], sb.ap[:, 0:nq], AF.Exp, [sb], [p])
            if k + 1 < len(U):
                nxt = issue_score(k + 1)
            if kind == "loc":
                d0 = 2 * ja - 2 * mt + 8
                d1 = 2 * jb + 1 - 2 * mt + 8
                ltv = LT.ap[:, hh, d0:d1 + 1, :].rearrange("p a b -> p (a b)")
                self.tt("dve", p.ap[:, 0:nq], p.ap[:, 0:nq], ltv, ALU.mult, [p, LT], [p])
            fns = []
            for j in range(ja, jb + 1):
                if kind == "loc":
                    val = [[na_valid(2 * mt + ee, 2 * j + r) for r in range(2)] for ee in range(2)]
                    if not any(val[0]) and not any(val[1]):
                        continue
                    for ee in range(2):
                        for r in range(2):
                            if not val[ee][r]:
                                c0 = (j - ja) * 128 + r * 64
                                fw.op("dve", lambda h, p=p, ee=ee, c0=c0: h.memset(p.ap[64 * ee:64 * ee + 64, c0:c0 + 64], 0.0), reads=[p], writes=[p])
                    rhs = v_aug.ap[:, mt, hh, 0:65]
                else:
                    rhs = vctx.ap[:, mt, hh, 0:65]
                fns.append(lambda h, j=j, ja=ja, p=p, rhs=rhs, st_=first, ov=ov, c=c: h.matmul(ov[:, j - 4 * c, :], lhsT=p.ap[:, (j - ja) * 128:(j - ja + 1) * 128],
                                                                                            rhs=rhs, start=st_, stop=False))
                first = False
            fw.mm(fns, reads=[p, v_aug, vctx], writes=[ob])
            if glast:
                fw.op("dve", lambda h, ov=ov: h.reciprocal(out=rec.ap[:, 0:4], in_=ov[:, :, 64]), reads=[ob], writes=[rec])
                self.tt("dve", atok.ap[:, 4 * c:4 * c + 4, 64 * e:64 * e + 64], ov[:, :, 0:64],
                        rec.ap[:, 0:4].unsqueeze(2).to_broadcast([128, 4, 64]), ALU.mult, [ob, rec], [atok])
                self.unhold(ob)
                if e == 1 and c == 1:
                    transposes(hp, [0, 1, 2, 3])
                    transposes(hp, [4, 5, 6, 7])

    def s5(self, l, g, s5y):
        fw, ar, I, O = self.fw, self.ar, self.I, self.O
        c = self.s5cols
        u_sb = ar.alloc([128, 2, TG], BF16)

        def u_cons(ft, tb, bk):
            self.cp("act", u_sb.ap[:, ft, tb * 512:(tb + 1) * 512], bk.ap, [bk], [u_sb])

        self.proj_fm(self.win_cols(l, 1536), 2, self.h_rhs, [self.h], 8, u_cons)
        Ec = ar.alloc([128, 16, 256], F32)
        Es = ar.alloc([128, 16, 256], F32)
        mk = ar.mark()
        io_i = ar.alloc([128, 256], I32)
        io_f = ar.alloc([128, 256], F32)
        fw.op("pool", lambda h: h.iota(io_i.ap, pattern=[[1, 256]], base=0, channel_multiplier=0), writes=[io_i])
        self.cp("dve", io_f.ap, io_i.ap, [io_i], [io_f])
        yv = ar.alloc([128, 2, 256], F32)
        for q in range(8):
            self.tt("dve", yv.ap, c.ap[:, 2 * q:2 * q + 2, 0].unsqueeze(2).to_broadcast([128, 2, 256]),
                    io_f.ap.unsqueeze(1).to_broadcast([128, 2, 256]), ALU.mult, [c, io_f], [yv])
            yflat = Reg(yv.ap.rearrange("p a b -> p (a b)"), yv.bufs)
            self.sincos(yflat, 512, Ec.ap[:, 2 * q:2 * q + 2, :].rearrange("p a b -> p (a b)"),
                        Es.ap[:, 2 * q:2 * q + 2, :].rearrange("p a b -> p (a b)"), Ec, Es)
        ar.release(mk)
        Kc = ar.alloc([128, 16, 4], F32)
        if g == 0:
            self.ts("dve", Kc.ap[:, :, 3], Es.ap[:, :, 255], -1.0, None, ALU.mult, None, [Es], [Kc])
        if g == 1:
            e255c = Ec.ap[:, :, 255]
            e255s = Es.ap[:, :, 255]
            self.tt("dve", Kc.ap[:, :, 0], c.ap[:, :, 2], e255c, ALU.mult, [c, Ec], [Kc])
            self.tt("dve", Kc.ap[:, :, 3], c.ap[:, :, 3], e255s, ALU.mult, [c, Es], [Kc])
            self.tt("dve", Kc.ap[:, :, 0], Kc.ap[:, :, 0], Kc.ap[:, :, 3], ALU.subtract, [Kc], [Kc])
            self.tt("dve", Kc.ap[:, :, 1], c.ap[:, :, 2], e255s, ALU.mult, [c, Es], [Kc])
            self.tt("dve", Kc.ap[:, :, 3], c.ap[:, :, 3], e255c, ALU.mult, [c, Ec], [Kc])
            self.tt("dve", Kc.ap[:, :, 1], Kc.ap[:, :, 1], Kc.ap[:, :, 3], ALU.add, [Kc], [Kc])
            self.ts("dve", Kc.ap[:, :, 2], Kc.ap[:, :, 1], -1.0, None, ALU.mult, None, [Kc], [Kc])
        ygb = ar.alloc([128, 2, TG], BF16)
        bpr = ar.alloc([128, 512], F32)
        bpi = ar.alloc([128, 512], F32)
        grs = [ar.alloc([128, 512], F32) for _ in range(2)]
        gis = [ar.alloc([128, 512], F32) for _ in range(2)]
        t2 = ar.alloc([128, 512], F32)
        qs = [ar.alloc([128, 512], BF16) for _ in range(4)]
        unit = [0]
        sm = ar.alloc([128, 16], F32)

        def v2(ap):
            return ap.rearrange("p (s t) -> p s t", s=2)

        def seg(ap, s2, rev):
            v = ap[:, s2 * 256:(s2 + 1) * 256]
            return v[:, ::-1] if rev else v

        units = []
        for ut in range(2):
            for d in range(2):
                for jj in range(4):
                    for ti, tb in enumerate([1, 0] if d == 1 else [0, 1]):
                        units.append((ut, d, jj, ti, tb))

        def issue_bu(k):
            ut, d, jj, ti, tb = units[k]
            dj = d * 8 + ut * 4 + jj
            tsl = slice(tb * 512, (tb + 1) * 512)
            br = self.bank()
            bi = self.bank()
            fw.mm([lambda h, br=br, dj=dj, tsl=tsl, ut=ut: h.matmul(br.ap, lhsT=self.s5w.ap[:, dj, 0, :], rhs=u_sb.ap[:, ut, tsl], start=True, stop=True)],
                  reads=[self.s5w, u_sb], writes=[br])
            fw.mm([lambda h, bi=bi, dj=dj, tsl=tsl, ut=ut: h.matmul(bi.ap, lhsT=self.s5w.ap[:, dj, 1, :], rhs=u_sb.ap[:, ut, tsl], start=True, stop=True)],
                  reads=[self.s5w, u_sb], writes=[bi])
            return br, bi

        ybanks = None
        yfirst = None
        if l == 0:
            self.bg = self.mod_gen(0, look=2, s0=8, s1=24, doA=(False, True)) if g == 0 else self.mod_gen(1, look=2)
        tbk = self.bank(hold=True)
        tbk2 = self.bank(hold=True)
        deferred = []
        deferred_mul = []
        prev_last = None
        nxt = issue_bu(0)
        for k, (ut, d, jj, ti, tb) in enumerate(units):
            rev = (d == 1)
            j = ut * 4 + jj
            dj = d * 8 + j
            if d == 0 and jj == 0 and ti == 0:
                ybanks = [self.bank(hold=True), self.bank(hold=True)]
                yfirst = [True, True]
            Ecv = Ec.ap[:, dj, :]
            Esv = Es.ap[:, dj, :]
            if rev:
                Ecv = Ecv[:, ::-1]
                Esv = Esv[:, ::-1]
            Ec2 = Ecv.unsqueeze(1).to_broadcast([128, 2, 256])
            Es2 = Esv.unsqueeze(1).to_broadcast([128, 2, 256])
            rb = c.ap[:, dj, 1:2].to_broadcast([128, 256])
            gr, gi = grs[k % 2], gis[k % 2]
            br, bi = nxt
            self.tt("dve", v2(t2.ap), v2(bi.ap), Es2, ALU.mult, [bi, Es], [t2])
            self.tt("dve", v2(tbk.ap), v2(br.ap), Ec2, ALU.mult, [br, Ec], [tbk])
            self.tt("dve", v2(tbk2.ap), v2(bi.ap), Ec2, ALU.mult, [bi, Ec], [tbk2])
            self.tt("dve", bpr.ap, tbk.ap, t2.ap, ALU.add, [tbk, t2], [bpr])
            self.tt("dve", v2(t2.ap), v2(br.ap), Es2, ALU.mult, [br, Es], [t2])
            self.tt("dve", bpi.ap, tbk2.ap, t2.ap, ALU.subtract, [tbk2, t2], [bpi])
            if k + 1 < len(units):
                nxt = issue_bu(k + 1)
            for fn in deferred:
                fn()
            deferred = []
            segs = [1, 0] if rev else [0, 1]
            if g == 0:
                for (src, dst) in ((bpr, gr), (bpi, gi)):
                    for s2 in segs:
                        fw.op("dve", lambda h, src=src, dst=dst, s2=s2, rev=rev, rb=rb: h.tensor_tensor_scan(
                            out=seg(dst.ap, s2, rev), data0=rb, data1=seg(src.ap, s2, rev), initial=0.0, op0=ALU.mult, op1=ALU.add),
                            reads=[src, c], writes=[dst])
            for sj, s2 in enumerate(segs):
                s = tb * 2 + s2
                si = ti * 2 + sj
                if g == 1:
                    f_r = seg(bpr.ap, s2, rev)[:, 0:1]
                    f_i = seg(bpi.ap, s2, rev)[:, 0:1]
                    if si == 0:
                        hpr, hpi = c.ap[:, dj, 6:7], c.ap[:, dj, 7:8]
                        self.stt(f_r, hpr, c.ap[:, dj, 2:3], f_r, ALU.mult, ALU.add, [c, bpr], [bpr])
                        self.stt(f_i, hpi, c.ap[:, dj, 2:3], f_i, ALU.mult, ALU.add, [c, bpi], [bpi])
                        self.ts("dve", sm.ap[:, 0:1], hpi, c.ap[:, dj, 3:4], None, ALU.mult, None, [c], [sm])
                        self.stt(f_i, hpr, c.ap[:, dj, 3:4], f_i, ALU.mult, ALU.add, [c, bpi], [bpi])
                        self.tt("dve", f_r, f_r, sm.ap[:, 0:1], ALU.subtract, [bpr, sm], [bpr])
                    else:
                        pgr, pgi = prev_last
                        self.stt(f_r, pgr[0], Kc.ap[:, dj, 0:1], f_r, ALU.mult, ALU.add, [pgr[1], Kc, bpr], [bpr])
                        self.stt(f_i, pgi[0], Kc.ap[:, dj, 0:1], f_i, ALU.mult, ALU.add, [pgi[1], Kc, bpi], [bpi])
                        self.stt(f_r, pgi[0], Kc.ap[:, dj, 2:3], f_r, ALU.mult, ALU.add, [pgi[1], Kc, bpr], [bpr])
                        self.stt(f_i, pgr[0], Kc.ap[:, dj, 1:2], f_i, ALU.mult, ALU.add, [pgr[1], Kc, bpi], [bpi])
                if g == 1:
                    for (src, dst) in ((bpr, gr), (bpi, gi)):
                        fw.op("dve", lambda h, src=src, dst=dst, s2=s2, rev=rev, rb=rb: h.tensor_tensor_scan(
                            out=seg(dst.ap, s2, rev), data0=rb, data1=seg(src.ap, s2, rev), initial=0.0, op0=ALU.mult, op1=ALU.add),
                            reads=[src, c], writes=[dst])
                g_r = seg(gr.ap, s2, rev)[:, 255:256]
                g_i = seg(gi.ap, s2, rev)[:, 255:256]
                prev_last = ((g_r, gr), (g_i, gi))
                if g == 0:
                    def state_ops(dj=dj, g_r=g_r, g_i=g_i, gr=gr, gi=gi, s=s, d=d, j=j, sj=sj):
                        e_c = Ec.ap[:, dj, 255:256]
                        e_s = Es.ap[:, dj, 255:256]
                        n_s = Kc.ap[:, dj, 3:4]
                        o_r, o_i = self.outst.ap[:, s, d, j, 0:1], self.outst.ap[:, s, d, j, 1:2]
                        ta, tb_ = sm.ap[:, 8 + 2 * sj:9 + 2 * sj], sm.ap[:, 9 + 2 * sj:10 + 2 * sj]
                        self.act(ta, g_i, AF.Identity, [gi, Kc], [sm], scale=n_s)
                        self.act(tb_, g_i, AF.Identity, [gi, Ec], [sm], scale=e_c)
                        self.act(o_r, g_r, AF.Identity, [gr, Ec, sm], [self.outst], scale=e_c, bias=ta)
                        self.act(o_i, g_r, AF.Identity, [gr, Es, sm], [self.outst], scale=e_s, bias=tb_)
                    deferred.append(state_ops)
            self.tt("pool", v2(qs[0].ap), v2(gr.ap), Ec2, ALU.mult, [gr, Ec], [qs[0]])
            self.tt("pool", v2(qs[1].ap), v2(gi.ap), Es2, ALU.mult, [gi, Es], [qs[1]])
            self.tt("pool", v2(qs[2].ap), v2(gi.ap), Ec2, ALU.mult, [gi, Ec], [qs[2]])
            self.tt("pool", v2(qs[3].ap), v2(gr.ap), Es2, ALU.mult, [gr, Es], [qs[3]])
            self.tick()
            yb = ybanks[tb]
            last = (d == 1 and jj == 3)
            fw.mm([lambda h, yb=yb, dj=dj, st_=yfirst[tb]: h.matmul(yb.ap, lhsT=self.s5w.ap[:, dj, 2, :], rhs=qs[0].ap, start=st_, stop=False),
                   lambda h, yb=yb, dj=dj: h.matmul(yb.ap, lhsT=self.s5w.ap[:, dj, 4, :], rhs=qs[1].ap, start=False, stop=False),
                   lambda h, yb=yb, dj=dj: h.matmul(yb.ap, lhsT=self.s5w.ap[:, dj, 3, :], rhs=qs[2].ap, start=False, stop=False),
                   lambda h, yb=yb, dj=dj, last=last: h.matmul(yb.ap, lhsT=self.s5w.ap[:, dj, 3, :], rhs=qs[3].ap, start=False, stop=last)],
                  reads=[self.s5w] + qs, writes=[yb])
            yfirst[tb] = False
            if d == 1 and jj == 3 and ti == 1:
                for tb2 in range(2):
                    yb = ybanks[tb2]
                    sl = slice(tb2 * 512, (tb2 + 1) * 512)
                    self.stt(t2.ap, u_sb.ap[:, ut, sl], self.s5d.ap[:, ut:ut + 1], yb.ap, ALU.mult, ALU.add, [u_sb, self.s5d, yb], [t2])
                    self.act(ygb.ap[:, ut, sl], t2.ap, AF.Gelu_apprx_tanh, [t2], [ygb])
                    self.unhold(yb)
        for fn in deferred:
            fn()
        self.drain()
        self.unhold(tbk)
        self.unhold(tbk2)
        for ot in range(2):
            for tb in range(2):
                sl = slice(tb * 512, (tb + 1) * 512)
                bk = self.bank()
                fw.mm([lambda h, bk=bk, kc=kc, ot=ot, sl=sl: h.matmul(bk.ap, lhsT=self.wglu.ap[:, kc, ot * 128:(ot + 1) * 128], rhs=ygb.ap[:, kc, sl],
                                                                      start=(kc == 0), stop=(kc == 1)) for kc in range(2)], reads=[self.wglu, ygb], writes=[bk])
                self.act(t2.ap, bk.ap, AF.Sigmoid, [bk], [t2])
                self.tt("dve", s5y.ap[:, ot, sl], ygb.ap[:, ot, sl], t2.ap, ALU.mult, [ygb, t2], [s5y])

    def store_s5_states(self, l):
        fw, ar, O = self.fw, self.ar, self.O
        mk = ar.mark()
        tr = ar.alloc([128, 128], F32)
        bk = self.bank()
        fw.mm([lambda h: h.transpose(out=bk.ap[:, 0:128], in_=self.outst.ap.rearrange("p s d j r -> p (s d j r)"), identity=self.ident_f.ap)],
              reads=[self.outst, self.ident_f], writes=[bk])
        self.cp("dve", tr.ap, bk.ap[:, 0:128], [bk], [tr])
        sem = Buf("s5out")
        for ri, nm in enumerate(("ns5r", "ns5i")):
            for s in range(4):
                for d in range(2):
                    r0 = ((s * 2 + d) * 8) * 2 + ri
                    src = tr.ap[r0:r0 + 15:2, :]
                    dst = O[nm][s, l, d, :].rearrange("(j q) -> j q", q=128)
                    dd = fw.dma("sp", lambda h, src=src, dst=dst: h.dma_start(out=dst, in_=src), sem, reads=[tr])
                    fw.final.append(dd)
        ar.release(mk)

    def lru(self, l, g, lruy):
        fw, ar, I, O = self.fw, self.ar, self.I, self.O
        lc = self.lrucols
        xr = ar.alloc([128, 2, TG], F32)
        gg = ar.alloc([128, 2, TG], F32)
        xc = ar.alloc([128, 2, TG], F32)
        xcb = ar.alloc([128, 2, TG], BF16)

        def xr_cons(ft, tb, bk):
            self.cp("act", xr.ap[:, ft, tb * 512:(tb + 1) * 512], bk.ap, [bk], [xr])

        def xg_cons(ft, tb, bk):
            self.act(gg.ap[:, ft, tb * 512:(tb + 1) * 512], bk.ap, AF.Gelu_apprx_tanh, [bk], [gg])

        self.proj_fm(self.win_cols(l, 1792), 2, self.h_rhs, [self.h], 8, xr_cons)
        self.proj_fm(self.win_cols(l, 2048), 2, self.h_rhs, [self.h], 8, xg_cons)
        nseq, L = (4, 256) if g == 0 else (1, 1024)
        for t in range(2):
            xv = xr.ap[:, t, :].rearrange("p (s q) -> p s q", s=nseq)
            cv = xc.ap[:, t, :].rearrange("p (s q) -> p s q", s=nseq)
            self.ts("dve", cv, xv, lc.ap[:, t, 2:3], lc.ap[:, t, 4:5], ALU.mult, ALU.add, [xr, lc], [xc])
            for k in (0, 1, 3):
                sh = k - 2
                lo, hi = max(0, -sh), L - max(0, sh)
                self.stt(cv[:, :, lo:hi], xv[:, :, lo + sh:hi + sh], lc.ap[:, t, k:k + 1], cv[:, :, lo:hi], ALU.mult, ALU.add, [xr, lc, xc], [xc])
            self.cp("act", xcb.ap[:, t, :], xc.ap[:, t, :], [xc], [xcb])
        r_ = ar.alloc([128, 512], F32)
        i_ = ar.alloc([128, 512], F32)
        th = ar.alloc([128, 512], F32)
        e2 = ar.alloc([128, 512], F32)
        a_sb = ar.alloc([128, TG], F32)
        b_sb = ar.alloc([128, TG], F32)
        hs = [ar.alloc([128, TG], F32) for _ in range(2)]
        for t in range(2):
            for d in range(2):
                rev = (d == 1)
                for tb in range(2):
                    sl = slice(tb * 512, (tb + 1) * 512)
                    pr = self.bank()
                    pi = self.bank()
                    fw.mm([lambda h, pr=pr, d=d, t=t, sl=sl: h.matmul(pr.ap, lhsT=self.lruw.ap[:, 0 * 4 + d * 2 + t, :], rhs=xcb.ap[:, t, sl], start=True, stop=True)],
                          reads=[self.lruw, xcb], writes=[pr])
                    fw.mm([lambda h, pi=pi, d=d, t=t, sl=sl: h.matmul(pi.ap, lhsT=self.lruw.ap[:, 1 * 4 + d * 2 + t, :], rhs=xcb.ap[:, t, sl], start=True, stop=True)],
                          reads=[self.lruw, xcb], writes=[pi])
                    self.act(r_.ap, pr.ap, AF.Sigmoid, [pr, lc], [r_], bias=lc.ap[:, t, 5 + d:6 + d])
                    self.act(i_.ap, pi.ap, AF.Sigmoid, [pi, lc], [i_], bias=lc.ap[:, t, 7 + d:8 + d])
                    self.act(a_sb.ap[:, sl], r_.ap, AF.Exp, [r_, lc], [a_sb], scale=lc.ap[:, t, 9 + d:10 + d])
                    self.act(th.ap, r_.ap, AF.Tanh, [r_, lc], [th], scale=lc.ap[:, t, 11 + d:12 + d])
                    self.act(e2.ap, r_.ap, AF.Exp, [r_, lc], [e2], scale=lc.ap[:, t, 13 + d:14 + d])
                    self.stt(e2.ap, e2.ap, 1.0, th.ap, ALU.add, ALU.mult, [e2, th], [e2])
                    self.act(e2.ap, e2.ap, AF.Sqrt, [e2], [e2])
                    self.tt("dve", i_.ap, i_.ap, xc.ap[:, t, sl], ALU.mult, [i_, xc], [i_])
                    self.tt("dve", b_sb.ap[:, sl], e2.ap, i_.ap, ALU.mult, [e2, i_], [b_sb])
                hd = hs[d]
                for s in range(nseq):
                    sq = slice(s * L, (s + 1) * L)
                    av, bv, hv = a_sb.ap[:, sq], b_sb.ap[:, sq], hd.ap[:, sq]
                    if rev:
                        av, bv, hv = av[:, ::-1], bv[:, ::-1], hv[:, ::-1]
                    init = 0.0 if g == 0 else self.lrust0.ap[:, d, t:t + 1]
                    rd = [a_sb, b_sb] + ([] if g == 0 else [self.lrust0])
                    fw.op("dve", lambda h, av=av, bv=bv, hv=hv, init=init: h.tensor_tensor_scan(out=hv, data0=av, data1=bv, initial=init, op0=ALU.mult, op1=ALU.add),
                          reads=rd, writes=[hd])
                    if g == 0:
                        self.cp("dve", self.lrust.ap[:, s, d, t:t + 1], hv[:, L - 1:L], [hd], [self.lrust])
            self.tt("dve", hs[0].ap, hs[0].ap, hs[1].ap, ALU.add, [hs[0], hs[1]], [hs[0]])
            self.tt("dve", lruy.ap[:, t, :], hs[0].ap, gg.ap[:, t, :], ALU.mult, [hs[0], gg], [lruy])
        if g == 0:
            sem = Buf("lruout")
            for s in range(4):
                for d in range(2):
                    dst = O["nlru"][s, l, d, :].rearrange("(t p) -> p t", p=128)
                    src = self.lrust.ap[:, s, d, :]
                    dd = fw.dma("sp", lambda h, src=src, dst=dst: h.dma_start(out=dst, in_=src, allow_slow_non_contiguous=True), sem, reads=[self.lrust])
                    fw.final.append(dd)

    def merge(self, l, g, attnT, s5y, lruy):
        fw, ar, I = self.fw, self.ar, self.I
        merged = ar.alloc([128, 8, TG], BF16)
        sig = [ar.alloc([128, 512], F32) for _ in range(2)]
        pr = [ar.alloc([128, 512], F32) for _ in range(3)]
        si = [0]

        def br_rhs(kc, sl):
            if kc < 4:
                return attnT.ap[:, kc, sl]
            if kc < 6:
                return s5y.ap[:, kc - 4, sl]
            return lruy.ap[:, kc - 6, sl]

        for sp in range(4):
            c0 = sp * 256
            wb = self.wslice([(0, 4, 256, I["w_br_attn"][l][:, c0:c0 + 256]), (4 * 256, 2, 256, I["w_br_s5"][l][:, c0:c0 + 256]),
                              (6 * 256, 2, 256, I["w_br_lru"][l][:, c0:c0 + 256])])
            wbv = wb.ap.rearrange("p (k n) -> p k n", k=8)
            wg = [self.wslice([(0, 8, 256, I["w_in"][l][:, 2304 + b * 1024 + c0:2304 + b * 1024 + c0 + 256])]) for b in range(3)]
            for t in range(2):
                ft = sp * 2 + t
                for tb in range(2):
                    sl = slice(tb * 512, (tb + 1) * 512)
                    for b, (k0, k1) in enumerate(((0, 4), (4, 6), (6, 8))):
                        gb = self.bank()
                        wgv = wg[b].ap.rearrange("p (k n) -> p k n", k=8)
                        fw.mm([lambda h, gb=gb, wgv=wgv, kc=kc, t=t, tb=tb: h.matmul(gb.ap, lhsT=wgv[:, kc, t * 128:(t + 1) * 128], rhs=self.h_rhs(kc, tb),
                                                                                     start=(kc == 0), stop=(kc == 7)) for kc in range(8)],
                              reads=[wg[b], self.h], writes=[gb])
                        bb = self.bank()
                        fw.mm([lambda h, bb=bb, kc=kc, t=t, sl=sl, k0=k0, k1=k1: h.matmul(bb.ap, lhsT=wbv[:, kc, t * 128:(t + 1) * 128], rhs=br_rhs(kc, sl),
                                                                                         start=(kc == k0), stop=(kc == k1 - 1)) for kc in range(k0, k1)],
                              reads=[wb, attnT, s5y, lruy], writes=[bb])
                        sg = sig[si[0] % 2]
                        si[0] += 1
                        self.act(sg.ap, gb.ap, AF.Sigmoid, [gb], [sg])
                        self.tt("dve", pr[b].ap, bb.ap, sg.ap, ALU.mult, [bb, sg], [pr[b]])
                    self.tt("dve", pr[0].ap, pr[0].ap, pr[1].ap, ALU.add, [pr[0], pr[1]], [pr[0]])
                    self.tt("dve", merged.ap[:, ft, sl], pr[0].ap, pr[2].ap, ALU.add, [pr[0], pr[2]], [merged])
                    self.tick()

        def out_cons(ft, tb, bk):
            sl = slice(tb * 512, (tb + 1) * 512)
            xv = self.x.ap[:, g, ft, sl]
            self.stt(xv, bk.ap, self.mod.ap[:, l, 2 * 8 + ft, g:g + 1], xv, ALU.mult, ALU.add, [bk, self.mod, self.x], [self.x])
            self.tick()

        self.proj_fm(lambda c0, n: [(0, 8, n, I["w_out"][l][:, c0:c0 + n])], 8,
                     lambda kc, tb: merged.ap[:, kc, tb * 512:(tb + 1) * 512], [merged], 8, out_cons)

    def ffn(self, l, g):
        fw, ar, I = self.fw, self.ar, self.I
        gg = ar.alloc([128, 22, TG], BF16)
        a_sb = [ar.alloc([128, TG], F32) for _ in range(2)]
        c_sb = [ar.alloc([128, TG], F32) for _ in range(2)]
        gl = [ar.alloc([128, TG], BF16) for _ in range(2)]
        nseq, L = (4, 256) if g == 0 else (1, 1024)
        fc = self.ffcols
        it = 0
        for sp in range(11):
            wa = self.wslice([(0, 8, 256, I["ffn_w_up"][l][:, sp * 256:sp * 256 + 256])])
            wb = self.wslice([(0, 8, 256, I["ffn_w_up"][l][:, 2816 + sp * 256:2816 + sp * 256 + 256])])
            wav = wa.ap.rearrange("p (k n) -> p k n", k=8)
            wbv = wb.ap.rearrange("p (k n) -> p k n", k=8)
            for t in range(2):
                ft = sp * 2 + t
                a_, c_, g_ = a_sb[it % 2], c_sb[it % 2], gl[it % 2]
                it += 1
                for tb in range(2):
                    bk = self.bank()
                    fw.mm([lambda h, bk=bk, kc=kc, t=t, tb=tb: h.matmul(bk.ap, lhsT=wav[:, kc, t * 128:(t + 1) * 128], rhs=self.h_rhs(kc, tb),
                                                                       start=(kc == 0), stop=(kc == 7)) for kc in range(8)], reads=[wa, self.h], writes=[bk])
                    self.cp("act", a_.ap[:, tb * 512:(tb + 1) * 512], bk.ap, [bk], [a_])
                av = a_.ap.rearrange("p (s q) -> p s q", s=nseq)
                cv = c_.ap.rearrange("p (s q) -> p s q", s=nseq)
                self.act(cv, av, AF.Identity, [a_, fc], [c_], bias=fc.ap[:, ft, 3:4], scale=fc.ap[:, ft, 1:2])
                self.stt(cv[:, :, 1:L], av[:, :, 0:L - 1], fc.ap[:, ft, 0:1], cv[:, :, 1:L], ALU.mult, ALU.add, [a_, fc, c_], [c_])
                self.stt(cv[:, :, 0:L - 1], av[:, :, 1:L], fc.ap[:, ft, 2:3], cv[:, :, 0:L - 1], ALU.mult, ALU.add, [a_, fc, c_], [c_])
                self.act(g_.ap, c_.ap, AF.Gelu_apprx_tanh, [c_], [g_])
                for tb in range(2):
                    sl = slice(tb * 512, (tb + 1) * 512)
                    bk = self.bank()
                    fw.mm([lambda h, bk=bk, kc=kc, t=t, tb=tb: h.matmul(bk.ap, lhsT=wbv[:, kc, t * 128:(t + 1) * 128], rhs=self.h_rhs(kc, tb),
                                                                       start=(kc == 0), stop=(kc == 7)) for kc in range(8)], reads=[wb, self.h], writes=[bk])
                    self.tt("dve", gg.ap[:, ft, sl], bk.ap, g_.ap[:, sl], ALU.mult, [bk, g_], [gg])
                self.tick()
        for ft in range(8):
            w0 = self.wslice([(0, 11, 128, I["ffn_w_down"][l][0:1408, ft * 128:(ft + 1) * 128])])
            w1 = self.wslice([(0, 11, 128, I["ffn_w_down"][l][1408:2816, ft * 128:(ft + 1) * 128])])
            wv = [w0.ap[:, 0:1408].rearrange("p (k n) -> p k n", k=11), w1.ap[:, 0:1408].rearrange("p (k n) -> p k n", k=11)]
            for tb in range(2):
                sl = slice(tb * 512, (tb + 1) * 512)
                bk = self.bank()
                fw.mm([lambda h, bk=bk, kc=kc, sl=sl: h.matmul(bk.ap, lhsT=wv[kc // 11][:, kc % 11, :], rhs=gg.ap[:, kc, sl],
                                                               start=(kc == 0), stop=(kc == 21)) for kc in range(22)], reads=[w0, w1, gg], writes=[bk])
                xv = self.x.ap[:, g, ft, sl]
                self.stt(xv, bk.ap, self.mod.ap[:, l, 5 * 8 + ft, g:g + 1], xv, ALU.mult, ALU.add, [bk, self.mod, self.x], [self.x])
            self.tick()
        self.drain()

    def final_norm_gen(self, g):
        fw, ar, O = self.fw, self.ar, self.O
        rstd = ar.alloc([128, TG], F32)
        yts = [ar.alloc([128, 8, 128], F32) for _ in range(2)]
        self.rms_rstd(g, rstd)
        yield
        dst_t = O["yp" if g == 0 else "ys"]
        for tt in range(8):
            yt = yts[tt % 2]
            tsl = slice(tt * 128, (tt + 1) * 128)
            for kc in range(8):
                self.stt(yt.ap[:, kc, :], self.x.ap[:, g, kc, tsl], self.gcols.ap[:, 4, kc:kc + 1], rstd.ap[:, tsl], ALU.mult, ALU.mult,
                         [self.x, self.gcols, rstd], [yt])
            yield
            for qd in range(4):
                s = self.stg_i % 2
                self.stg_i += 1
                stg = self.stg[s]
                bk = self.bank()
                fw.mm([lambda h, bk=bk, j=j, qd=qd, yt=yt: h.transpose(out=bk.ap[:, j * 128:(j + 1) * 128], in_=yt.ap[:, qd * 2 + j, :],
                                                                       identity=self.ident_f.ap) for j in range(2)], reads=[yt, self.ident_f], writes=[bk])
                self.cp("act" if qd % 2 else "dve", stg.ap, bk.ap[:, 0:256], [bk], [stg])
                dst = dst_t[tt * 128:(tt + 1) * 128, qd * 256:(qd + 1) * 256]
                dd = fw.dma("sp", lambda h, stg=stg, dst=dst: h.dma_start(out=dst, in_=stg.ap), self.stg_sem[s], reads=[stg])
                fw.final.append(dd)
                if qd == 1:
                    yield
            yield

    def final_norm(self, g):
        fw, ar, O = self.fw, self.ar, self.O
        m = ar.mark()
        rstd = ar.alloc([128, TG], F32)
        self.rms_rstd(g, rstd)
        y = ar.alloc([128, 8, TG], F32)
        for kc in range(8):
            self.stt(y.ap[:, kc, :], self.x.ap[:, g, kc, :], self.gcols.ap[:, 4, kc:kc + 1], rstd.ap, ALU.mult, ALU.mult, [self.x, self.gcols, rstd], [y])
        dst_t = O["yp" if g == 0 else "ys"]
        for tt in range(8):
            for qd in range(4):
                s = self.stg_i % 2
                self.stg_i += 1
                stg = self.stg[s]
                bk = self.bank()
                fw.mm([lambda h, bk=bk, j=j, qd=qd, tt=tt: h.transpose(out=bk.ap[:, j * 128:(j + 1) * 128], in_=y.ap[:, qd * 2 + j, tt * 128:(tt + 1) * 128],
                                                                       identity=self.ident_f.ap) for j in range(2)], reads=[y, self.ident_f], writes=[bk])
                self.cp("act" if qd % 2 else "dve", stg.ap, bk.ap[:, 0:256], [bk], [stg])
                dst = dst_t[tt * 128:(tt + 1) * 128, qd * 256:(qd + 1) * 256]
                dd = fw.dma("sp", lambda h, stg=stg, dst=dst: h.dma_start(out=dst, in_=stg.ap), self.stg_sem[s], reads=[stg])
                fw.final.append(dd)
        ar.release(m)


_W_KEYS = ["w_ada", "b_ada", "g_norm1", "g_norm2", "w_in", "rpb", "s5_lam_re", "s5_lam_im", "s5_log_step", "s5_b_re", "s5_b_im",
           "s5_c_re", "s5_c_im", "s5_d", "s5_w_glu", "lru_conv_w", "lru_conv_b", "lru_w_a", "lru_b_a", "lru_w_x", "lru_b_x", "lru_lam",
           "w_br_attn", "w_br_s5", "w_br_lru", "w_out", "ffn_w_up", "ffn_conv_w", "ffn_conv_b", "ffn_w_down", "g_final"]


def make_in_maps(inp):
    f = lambda a: np.ascontiguousarray(np.asarray(a, dtype=np.float32))
    shared = {k: f(inp[k]) for k in _W_KEYS}
    maps = []
    for i in range(NCORES):
        m = dict(shared)
        m["xp"] = f(inp["x_prompt"][4 * i:4 * i + 4]).reshape(TG, D)
        m["xs"] = f(inp["x_sample"][i]).reshape(TG, D)
        m["ck"] = f(inp["cache_k"][i]).reshape(DEPTH, 512, 512)
        m["cv"] = f(inp["cache_v"][i]).reshape(DEPTH, 512, 512)
        m["s5r"] = f(inp["state_s5_re"][i]).reshape(DEPTH, 2, 1024)
        m["s5i"] = f(inp["state_s5_im"][i]).reshape(DEPTH, 2, 1024)
        m["slru"] = f(inp["state_lru"][i]).reshape(DEPTH, 2, 256)
        m["cvec"] = f(np.stack([np.asarray(inp["c_ctx"]), np.asarray(inp["c"])[i]], axis=0))
        maps.append(m)
    return maps


def assemble(results):
    cat = lambda k: np.concatenate([np.asarray(r[k]) for r in results], axis=0)
    y_prompt = cat("yp").reshape(32, 256, D)
    y_sample = cat("ys").reshape(8, 1024, D)
    new_k = cat("nk").reshape(32, DEPTH, 256, 8, 64)
    new_v = cat("nv").reshape(32, DEPTH, 256, 8, 64)
    ns5r = cat("ns5r").reshape(32, DEPTH, 2, 16, 64)
    ns5i = cat("ns5i").reshape(32, DEPTH, 2, 16, 64)
    nlru = cat("nlru").reshape(32, DEPTH, 2, 256)
    return tuple(np.ascontiguousarray(a, dtype=np.float32) for a in (y_prompt, y_sample, new_k, new_v, ns5r, ns5i, nlru))


def kernel(**inputs):
    nc = Builder().build()
    in_maps = make_in_maps(inputs)
    res = run_bass_kernel_spmd(nc, in_maps, core_ids=list(range(NCORES)))
    return assemble(res.results)


def debug_run(inputs, stage, ncores=1, trace=False):
    b = Builder(stage=stage)
    nc = b.build()
    in_maps = make_in_maps(inputs)[:ncores]
    res = run_bass_kernel_spmd(nc, in_maps, core_ids=list(range(ncores)), trace=trace)
    if trace:
        print("EXEC_NS", stage, res.exec_time_ns)
    return b, res.results
```

```python
import math
import types as _types
import numpy as np
from contextlib import ExitStack
import concourse.bass as bass
import concourse.mybir as mybir
from concourse.bass_utils import run_bass_kernel_spmd

F32 = mybir.dt.float32
BF16 = mybir.dt.bfloat16
I32 = mybir.dt.int32
AF = mybir.ActivationFunctionType
ALU = mybir.AluOpType

NCORES = 8
D = 1024
TG = 1024
DEPTH = 2
NEG = -30000.0
EPS = 1e-6
IN_W = 5376
PAGE = 512


class Buf:
    __slots__ = ("name", "lw", "rd", "dsem", "dcnt", "excl")

    def __init__(self, name, excl=False):
        self.name = name
        self.excl = excl
        self.lw = None
        self.rd = []
        self.dsem = None
        self.dcnt = 0


class Reg:
    __slots__ = ("ap", "bufs", "tag")

    def __init__(self, ap, bufs, tag=None):
        self.ap = ap
        self.bufs = bufs
        self.tag = tag

    def __getitem__(self, k):
        return self.ap[k]


def _freeze(f):
    if getattr(f, "__closure__", None) is None:
        return f
    cells = []
    for c in f.__closure__:
        try:
            cells.append(_types.CellType(c.cell_contents))
        except ValueError:
            cells.append(c)
    return _types.FunctionType(f.__code__, f.__globals__, f.__name__, f.__defaults__, tuple(cells))


class Eng:
    def __init__(self, name):
        self.key = name
        self.cnt = 0
        self.seen = {}
        self.prog = []


class FW:
    def __init__(self, nc, stack):
        self.nc = nc
        self.stack = stack
        self.sems = {}
        self.E = {}
        for n in ("pe", "act", "dve", "pool", "sp"):
            self.sems[n] = stack.enter_context(nc.semaphore("s_" + n))
            self.E[n] = Eng(n)
        self.ndsem = 0
        self.dsem_free = []
        self.final = []

    def _expand(self, lst):
        out = []
        for r in lst:
            if isinstance(r, Buf):
                out.append(r)
            else:
                out.extend(r.bufs)
        return out

    def _need(self, eng, dep, same_ok):
        key, val, clock = dep
        if same_ok and key == eng.key:
            return
        if eng.seen.get(key, 0) >= val:
            return
        eng.prog.append(("wait", key, val))
        eng.seen[key] = val
        for k, v in clock.items():
            if eng.seen.get(k, 0) < v:
                eng.seen[k] = v

    def _deps(self, eng, reads, writes):
        for b in reads:
            if b.lw is not None:
                self._need(eng, b.lw, False)
            if b.excl:
                for r in b.rd:
                    self._need(eng, r, True)
        for b in writes:
            if b.lw is not None:
                self._need(eng, b.lw, True)
            for r in b.rd:
                self._need(eng, r, True)

    def _commit(self, dep, reads, writes):
        for b in reads:
            b.rd.append(dep)
        for b in writes:
            b.lw = dep
            b.rd = []

    def op(self, en, fn, reads=(), writes=()):
        eng = self.E[en]
        reads = self._expand(reads)
        writes = self._expand(writes)
        self._deps(eng, reads, writes)
        eng.cnt += 1
        eng.prog.append(("ins", _freeze(fn), eng.key, 1))
        clock = dict(eng.seen)
        clock[eng.key] = eng.cnt
        dep = (eng.key, eng.cnt, clock)
        self._commit(dep, reads, writes)
        return dep

    def mm(self, fns, reads=(), writes=()):
        eng = self.E["pe"]
        reads = self._expand(reads)
        writes = self._expand(writes)
        self._deps(eng, reads, writes)
        for f in fns[:-1]:
            eng.prog.append(("ins", _freeze(f), None, 0))
        eng.cnt += 1
        eng.prog.append(("ins", _freeze(fns[-1]), eng.key, 1))
        clock = dict(eng.seen)
        clock[eng.key] = eng.cnt
        dep = (eng.key, eng.cnt, clock)
        self._commit(dep, reads, writes)
        return dep

    def dma(self, qn, fn, dbuf, reads=(), writes=()):
        eng = self.E[qn]
        reads = self._expand(reads)
        writes = self._expand(writes)
        self._deps(eng, reads, writes)
        if dbuf.dsem is None:
            self.ndsem += 1
            dbuf.dsem = self.stack.enter_context(self.nc.semaphore(f"d{self.ndsem}"))
        if dbuf.dcnt:
            self._need(eng, ("D%d" % id(dbuf), dbuf.dcnt, {}), False)
        dbuf.dcnt += 16
        key = "D%d" % id(dbuf)
        self.sems[key] = dbuf.dsem
        eng.prog.append(("ins", _freeze(fn), key, 16))
        dep = (key, dbuf.dcnt, dict(eng.seen))
        self._commit(dep, reads, writes)
        return dep

    def emit(self):
        nc = self.nc
        sems = self.sems
        for d in self.final:
            self._need(self.E["sp"], d, False)

        def replay(eng, h):
            for it in eng.prog:
                if it[0] == "wait":
                    h.wait_ge(sems[it[1]], it[2])
                else:
                    ins = it[1](h)
                    if it[2] is not None:
                        ins.then_inc(sems[it[2]], it[3])

        with nc.Block() as block:
            @block.sync
            def _(h):
                replay(self.E["sp"], h)

            @block.scalar
            def _(h):
                replay(self.E["act"], h)

            @block.vector
            def _(h):
                replay(self.E["dve"], h)

            @block.gpsimd
            def _(h):
                replay(self.E["pool"], h)

            @block.tensor
            def _(h):
                replay(self.E["pe"], h)


class Arena:
    def __init__(self, fw, nbytes):
        self.fw = fw
        self.nbytes = nbytes
        self.t = fw.stack.enter_context(fw.nc.sbuf_tensor("arena", [128, nbytes // 2], BF16))
        self.pages = [Buf(f"pg{i}") for i in range((nbytes + PAGE - 1) // PAGE)]
        self.top = 0
        self.peak = 0

    def alloc(self, shape, dt, align=64):
        esz = 4 if dt in (F32, I32) else 2
        n = 1
        for s in shape[1:]:
            n *= s
        nb = n * esz
        if nb >= PAGE:
            align = max(align, PAGE)
        off = (self.top + align - 1) // align * align
        assert off + nb <= self.nbytes, f"arena overflow: need {off + nb} have {self.nbytes}"
        self.top = off + nb
        self.peak = max(self.peak, self.top)
        ap = self.t[0:shape[0], off // 2:(off + nb) // 2]
        if esz == 4:
            ap = ap.bitcast(dt)
        if len(shape) > 2:
            names = " ".join(f"d{i}" for i in range(1, len(shape)))
            kw = {f"d{i}": shape[i] for i in range(1, len(shape))}
            ap = ap.rearrange(f"p ({names}) -> p {names}", **kw)
        pages = self.pages[off // PAGE:(off + nb - 1) // PAGE + 1]
        return Reg(ap, pages)

    def mark(self):
        return self.top

    def release(self, m):
        self.top = m


def na_valid(kr, qr):
    w0 = min(max(qr - 4, 0), 8)
    return w0 <= kr < w0 + 8


class _Stop(Exception):
    pass


class Builder:
    def __init__(self, debug=False, stage=None):
        self.debug = debug
        self.stage = stage
        self.dbg_outs = []
        self.nc = bass.Bass("TRN2", target_bir_lowering=False)
        self.stack = ExitStack()
        self.dram = {}

    def din(self, name, shape):
        t = self.nc.dram_tensor(name, list(shape), F32, kind="ExternalInput")
        self.dram[name] = t
        return t.ap()

    def dout(self, name, shape):
        t = self.nc.dram_tensor(name, list(shape), F32, kind="ExternalOutput")
        self.dram[name] = t
        return t.ap()

    def build(self):
        nc = self.nc
        with self.stack as st:
            fw = self.fw = FW(nc, st)
            I = self.I = {}
            O = self.O = {}
            I["xp"] = self.din("xp", [TG, D])
            I["xs"] = self.din("xs", [TG, D])
            I["ck"] = self.din("ck", [DEPTH, 512, 512])
            I["cv"] = self.din("cv", [DEPTH, 512, 512])
            I["s5r"] = self.din("s5r", [DEPTH, 2, 1024])
            I["s5i"] = self.din("s5i", [DEPTH, 2, 1024])
            I["slru"] = self.din("slru", [DEPTH, 2, 256])
            I["cvec"] = self.din("cvec", [2, D])
            wshapes = {
                "w_ada": [2, D, 6 * D], "b_ada": [2, 6 * D], "g_norm1": [2, D], "g_norm2": [2, D],
                "w_in": [2, D, IN_W], "rpb": [2, 8, 15, 31],
                "s5_lam_re": [2, 2, 16, 64], "s5_lam_im": [2, 2, 16, 64], "s5_log_step": [2, 2, 16],
                "s5_b_re": [2, 16, 64, 16], "s5_b_im": [2, 16, 64, 16],
                "s5_c_re": [2, 2, 16, 16, 64], "s5_c_im": [2, 2, 16, 16, 64],
                "s5_d": [2, 256], "s5_w_glu": [2, 256, 256],
                "lru_conv_w": [2, 4, 256], "lru_conv_b": [2, 256],
                "lru_w_a": [2, 2, 4, 64, 64], "lru_b_a": [2, 2, 256],
                "lru_w_x": [2, 2, 4, 64, 64], "lru_b_x": [2, 2, 256], "lru_lam": [2, 2, 256],
                "w_br_attn": [2, 512, D], "w_br_s5": [2, 256, D], "w_br_lru": [2, 256, D],
                "w_out": [2, D, D], "ffn_w_up": [2, D, 5632], "ffn_conv_w": [2, 3, 2816],
                "ffn_conv_b": [2, 2816], "ffn_w_down": [2, 2816, D], "g_final": [D],
            }
            self.wshapes = wshapes
            for k, s in wshapes.items():
                I[k] = self.din(k, s)
            O["yp"] = self.dout("yp", [TG, D])
            O["ys"] = self.dout("ys", [TG, D])
            O["nk"] = self.dout("nk", [4, DEPTH, 256, 512])
            O["nv"] = self.dout("nv", [4, DEPTH, 256, 512])
            O["ns5r"] = self.dout("ns5r", [4, DEPTH, 2, 1024])
            O["ns5i"] = self.dout("ns5i", [4, DEPTH, 2, 1024])
            O["nlru"] = self.dout("nlru", [4, DEPTH, 2, 256])

            self.ar = Arena(fw, 207 * 1024)
            ps_t = st.enter_context(nc.psum_tensor("psum", [128, 8, 512], F32))
            self.banks = [Reg(ps_t[:, i, :], [Buf(f"bank{i}", excl=True)], i) for i in range(8)]
            self.bank_i = 0
            self.held = set()
            try:
                self.setup()
                self.chk("setup")
                for l in range(DEPTH):
                    if l == 0:
                        self.bg = self.mod_gen(0, look=4, s0=0, s1=8, doA=(True, False))
                        self.drain()
                    self.chk(f"prep{l}")
                    for g in range(2):
                        self.group_layer(l, g)
                        self.chk(f"gl{l}{g}")
                self.final_norm(1)
            except _Stop:
                pass
            fw.emit()
        return nc

    def chk(self, name):
        if self.stage == name:
            raise _Stop()

    def dbg(self, name, reg, ap, shape):
        t = self.nc.dram_tensor("dbg_" + name, list(shape), F32, kind="ExternalOutput").ap()
        d = self.fw.dma("sp", lambda h: h.dma_start(out=t, in_=ap), Buf("dbg_" + name), reads=[reg])
        self.fw.final.append(d)
        self.dbg_outs.append("dbg_" + name)

    def bank(self, hold=False):
        while True:
            i = self.bank_i % 8
            self.bank_i += 1
            if i not in self.held:
                break
        if hold:
            self.held.add(i)
        return self.banks[i]

    def unhold(self, bk):
        self.held.discard(bk.tag)

    def col(self, reg, i):
        return reg.ap[:, i:i + 1]

    def setup(self):
        fw, ar, nc, I = self.fw, self.ar, self.nc, self.I
        self.x = ar.alloc([128, 2, 8, TG], F32)
        self.ident_bf = ar.alloc([128, 128], BF16, align=PAGE)
        self.ident_f = ar.alloc([128, 128], F32)
        self.ones_bf = ar.alloc([128, 128], BF16)
        self.epsc = ar.alloc([128, 1], F32)
        self.gcols = ar.alloc([128, 5, 8], F32)
        self.cTb = ar.alloc([128, 8, 2], BF16)
        self.badaT = ar.alloc([128, 2, 48], F32)
        self.mod = ar.alloc([128, 2, 48, 2], F32, align=PAGE)
        self.A1 = ar.alloc([128, 2, 8, 2], F32)
        self.A2 = ar.alloc([128, 2, 8, 2], F32)
        self.s5cols = ar.alloc([128, 16, 8], F32, align=PAGE)
        self.lrucols = ar.alloc([128, 2, 16], F32)
        self.s5d = ar.alloc([128, 2], F32)
        self.ffcols = ar.alloc([128, 22, 4], F32)
        self.lrust0 = ar.alloc([128, 2, 2], F32)
        self.lruw = ar.alloc([128, 8, 128], BF16)
        self.wglu = ar.alloc([128, 2, 256], BF16)
        self.s5w = ar.alloc([128, 16, 5, 128], BF16, align=PAGE)
        self.outst = ar.alloc([128, 4, 2, 8, 2], F32, align=PAGE)
        self.lrust = ar.alloc([128, 4, 2, 2], F32)
        self.slots = [ar.alloc([128, 2048], BF16, align=PAGE) for _ in range(6)]
        self.slot_sem = [Buf(f"slotsem{i}") for i in range(6)]
        self.slot_i = 0
        self.stg = [ar.alloc([128, 256], F32, align=PAGE) for _ in range(2)]
        self.stg_sem = [Buf("stg0"), Buf("stg1")]
        self.stg_i = 0
        self.small_sem = Buf("small")
        self.h = ar.alloc([128, 8, TG], BF16, align=PAGE)
        self.bg = None
        fw.op("pool", lambda h: h.memset(self.epsc.ap, EPS), writes=[self.epsc])
        self.scr_mark = ar.mark()

        idb, idf, ones = self.ident_bf, self.ident_f, self.ones_bf
        fw.op("pool", lambda h: h.memset(idf.ap, 1.0), writes=[idf])
        fw.op("pool", lambda h: h.affine_select(out=idf.ap, in_=idf.ap, pattern=[[-1, 128]], compare_op=ALU.is_equal,
                                                fill=0.0, base=0, channel_multiplier=1), reads=[idf], writes=[idf])
        fw.op("dve", lambda h: h.tensor_copy(out=idb.ap, in_=idf.ap), reads=[idf], writes=[idb])
        fw.op("dve", lambda h: h.memset(ones.ap, 1.0), writes=[ones])

        self.scr_mark = ar.mark()
        self.prep_bg = False
        self.mod_init()
        self.bg = self.layer_prep_gen(0)
        for g, name in enumerate(("xp", "xs")):
            for tt in range(8):
                for qd in range(4):
                    s = self.stg_i % 2
                    self.stg_i += 1
                    stg = self.stg[s]
                    src = I[name][tt * 128:(tt + 1) * 128, qd * 256:(qd + 1) * 256]
                    fw.dma("sp", lambda h, stg=stg, src=src: h.dma_start(out=stg.ap, in_=src), self.stg_sem[s], writes=[stg])
                    bk = self.bank()
                    fw.mm([lambda h, bk=bk, stg=stg, j=j: h.transpose(out=bk.ap[:, j * 128:(j + 1) * 128], in_=stg.ap[:, j * 128:(j + 1) * 128],
                                                                      identity=idf.ap) for j in range(2)], reads=[stg, idf], writes=[bk])
                    dstv = self.x.ap[:, g, qd * 2:qd * 2 + 2, tt * 128:(tt + 1) * 128]
                    srcv = bk.ap[:, 0:256].rearrange("p (j t) -> p j t", j=2)
                    if qd % 2:
                        fw.op("act", lambda h, dstv=dstv, srcv=srcv: h.activation(out=dstv, in_=srcv, func=AF.Copy), reads=[bk], writes=[self.x])
                    else:
                        fw.op("dve", lambda h, dstv=dstv, srcv=srcv: h.tensor_copy(out=dstv, in_=srcv), reads=[bk], writes=[self.x])
                    if g == 0:
                        self.tick()
        self.drain()

    def small_load(self, dst_ap, src_ap, dst_reg):
        return self.sload(dst_ap, src_ap, dst_reg)

    def wslice(self, parts):
        s = self.slot_i % 6
        self.slot_i += 1
        slot = self.slots[s]
        for (off, kc, ncols, src) in parts:
            dst = slot.ap[:, off:off + kc * ncols].rearrange("p (k n) -> p k n", k=kc)
            sv = src.rearrange("(k p) n -> p k n", p=128)
            self.fw.dma("pool", lambda h, dst=dst, sv=sv: h.dma_start(out=dst, in_=sv), self.slot_sem[s], writes=[slot])
        return slot

    def mod_init(self):
        fw, ar, I = self.fw, self.ar, self.I
        m = ar.mark()
        cT = ar.alloc([128, 8, 2], F32)
        for cc in range(2):
            self.small_load(cT.ap[:, :, cc], I["cvec"][cc].rearrange("(k p) -> p k", p=128), cT)
            self.small_load(self.badaT.ap[:, cc, :], I["b_ada"][cc].rearrange("(t p) -> p t", p=128), self.badaT)
            self.small_load(self.gcols.ap[:, cc, :], I["g_norm1"][cc].rearrange("(k p) -> p k", p=128), self.gcols)
            self.small_load(self.gcols.ap[:, 2 + cc, :], I["g_norm2"][cc].rearrange("(k p) -> p k", p=128), self.gcols)
        self.small_load(self.gcols.ap[:, 4, :], I["g_final"].rearrange("(k p) -> p k", p=128), self.gcols)
        fw.op("act", lambda h: h.activation(out=self.cTb.ap, in_=cT.ap, func=AF.Silu), reads=[cT], writes=[self.cTb])
        ar.release(m)

    def mod_gen(self, l, look=2, s0=0, s1=24, doA=(True, True)):
        fw, I = self.fw, self.I
        cTb, badaT = self.cTb, self.badaT
        pend = []
        nxt = s0
        for s in range(s0, s1):
            while nxt < s1 and nxt <= s + look:
                pend.append(self.wslice([(0, 8, 256, I["w_ada"][l][:, nxt * 256:(nxt + 1) * 256])]))
                nxt += 1
            slot = pend.pop(0)
            sv = slot.ap.rearrange("p (k n) -> p k n", k=8)
            bk = self.bank()
            for t in range(2):
                fw.mm([lambda h, bk=bk, sv=sv, t=t, kc=kc: h.matmul(bk.ap[:, t * 2:t * 2 + 2], lhsT=sv[:, kc, t * 128:(t + 1) * 128],
                                                                    rhs=cTb.ap[:, kc, :], start=(kc == 0), stop=(kc == 7)) for kc in range(8)],
                      reads=[slot, cTb], writes=[bk])
            fw.op("dve", lambda h, bk=bk, l=l, s=s: h.tensor_tensor(out=self.mod.ap[:, l, 2 * s:2 * s + 2, :],
                                                                   in0=bk.ap[:, 0:4].rearrange("p (t c) -> p t c", t=2),
                                                                   in1=badaT.ap[:, l, 2 * s:2 * s + 2].unsqueeze(2).to_broadcast([128, 2, 2]),
                                                                   op=ALU.add), reads=[bk, badaT], writes=[self.mod])
            yield
        for ai, (A, comp, gi) in enumerate(((self.A1, 1, 0), (self.A2, 4, 2))):
            if not doA[ai]:
                continue
            fw.op("dve", lambda h, A=A, comp=comp, gi=gi, l=l: h.scalar_tensor_tensor(
                out=A.ap[:, l], in0=self.mod.ap[:, l, comp * 8:comp * 8 + 8, :], scalar=1.0,
                in1=self.gcols.ap[:, gi + l, :].unsqueeze(2).to_broadcast([128, 8, 2]), op0=ALU.add, op1=ALU.mult),
                reads=[self.mod, self.gcols], writes=[A])

    def tick(self):
        if self.bg is not None:
            try:
                next(self.bg)
            except StopIteration:
                self.bg = None

    def drain(self):
        while self.bg is not None:
            self.tick()

    def rms_rstd(self, g, rstd):
        fw, ar = self.fw, self.ar
        m = ar.mark()
        sq = [ar.alloc([128, 512], BF16) for _ in range(2)]
        tmp = ar.alloc([128, 512], F32)
        for tb in range(2):
            bk = self.bank()
            for kc in range(8):
                q = sq[kc % 2]
                fw.op("act", lambda h, q=q, kc=kc, tb=tb: h.activation(out=q.ap, in_=self.x.ap[:, g, kc, tb * 512:(tb + 1) * 512], func=AF.Square),
                      reads=[self.x], writes=[q])
                fw.mm([lambda h, bk=bk, q=q, kc=kc: h.matmul(bk.ap, lhsT=self.ones_bf.ap, rhs=q.ap, start=(kc == 0), stop=(kc == 7))],
                      reads=[q, self.ones_bf], writes=[bk])
            fw.op("act", lambda h, bk=bk: h.activation(out=tmp.ap, in_=bk.ap, func=AF.Sqrt, scale=1.0 / D, bias=self.epsc.ap[:, 0:1]),
                  reads=[bk, self.epsc], writes=[tmp])
            fw.op("dve", lambda h, tb=tb: h.reciprocal(out=rstd.ap[:, tb * 512:(tb + 1) * 512], in_=tmp.ap), reads=[tmp], writes=[rstd])
        ar.release(m)

    def norm_mod(self, l, g, A, shcomp):
        fw, ar = self.fw, self.ar
        m = ar.mark()
        rstd = ar.alloc([128, TG], F32)
        self.rms_rstd(g, rstd)
        tmps = [ar.alloc([128, TG], F32) for _ in range(2)]
        for kc in range(8):
            t = tmps[kc % 2]
            fw.op("dve", lambda h, t=t, kc=kc: h.scalar_tensor_tensor(out=t.ap, in0=self.x.ap[:, g, kc, :], scalar=A.ap[:, l, kc, g:g + 1],
                                                                      in1=rstd.ap, op0=ALU.mult, op1=ALU.mult),
                  reads=[self.x, A, rstd], writes=[t])
            fw.op("act", lambda h, t=t, kc=kc: h.activation(out=self.h.ap[:, kc, :], in_=t.ap, func=AF.Identity,
                                                            bias=self.mod.ap[:, l, shcomp * 8 + kc, g:g + 1], scale=1.0),
                  reads=[t, self.mod], writes=[self.h])
        ar.release(m)

    def proj_fm(self, src_cols_fn, ntiles, rhs, rhs_regs, kcs, consume):
        fw = self.fw
        for sp in range(0, ntiles, 2):
            nt = min(2, ntiles - sp)
            slot = self.wslice(src_cols_fn(sp * 128, nt * 128))
            sv = slot.ap[:, 0:kcs * nt * 128].rearrange("p (k n) -> p k n", k=kcs)
            for t in range(nt):
                for tb in range(2):
                    bk = self.bank()
                    fw.mm([lambda h, bk=bk, sv=sv, t=t, tb=tb, kc=kc: h.matmul(bk.ap, lhsT=sv[:, kc, t * 128:(t + 1) * 128], rhs=rhs(kc, tb),
                                                                                start=(kc == 0), stop=(kc == kcs - 1)) for kc in range(kcs)],
                          reads=[slot] + rhs_regs, writes=[bk])
                    consume(sp + t, tb, bk)

    def win_cols(self, l, base):
        return lambda c0, n: [(0, 8, n, self.I["w_in"][l][:, base + c0:base + c0 + n])]

    def h_rhs(self, kc, tb):
        return self.h.ap[:, kc, tb * 512:(tb + 1) * 512]

    def tt(self, en, out, in0, in1, op, R, W):
        self.fw.op(en, lambda h: h.tensor_tensor(out=out, in0=in0, in1=in1, op=op), reads=R, writes=W)

    def ts(self, en, out, in0, s1, s2, op0, op1, R, W):
        if s2 is None:
            self.fw.op(en, lambda h: h.tensor_scalar(out=out, in0=in0, scalar1=s1, scalar2=None, op0=op0), reads=R, writes=W)
        else:
            self.fw.op(en, lambda h: h.tensor_scalar(out=out, in0=in0, scalar1=s1, scalar2=s2, op0=op0, op1=op1), reads=R, writes=W)

    def stt(self, out, in0, sc, in1, op0, op1, R, W, en="dve"):
        self.fw.op(en, lambda h: h.scalar_tensor_tensor(out=out, in0=in0, scalar=sc, in1=in1, op0=op0, op1=op1), reads=R, writes=W)

    def act(self, out, in_, func, R, W, bias=None, scale=None):
        kw = {}
        if bias is not None:
            kw["bias"] = bias
        if scale is not None:
            kw["scale"] = scale
        self.fw.op("act", lambda h: h.activation(out=out, in_=in_, func=func, **kw), reads=R, writes=W)

    def cp(self, en, out, in_, R, W):
        if en == "act":
            self.fw.op("act", lambda h: h.activation(out=out, in_=in_, func=AF.Copy), reads=R, writes=W)
        else:
            self.fw.op(en, lambda h: h.tensor_copy(out=out, in_=in_), reads=R, writes=W)

    def sincos(self, y, n, cos_out, sin_out, Wc, Ws):
        ar = self.ar
        MAGIC = 12582912.0
        m = ar.mark()
        kf = ar.alloc([128, n], F32)
        fc = ar.alloc([128, n], F32)
        self.ts("dve", kf.ap, y.ap, 0.25, MAGIC, ALU.add, ALU.add, [y], [kf])
        self.ts("dve", kf.ap, kf.ap, MAGIC, None, ALU.subtract, None, [kf], [kf])
        self.stt(fc.ap, y.ap, 0.25, kf.ap, ALU.add, ALU.subtract, [y, kf], [fc])
        self.act(cos_out, fc.ap, AF.Sin, [fc], [Wc], scale=2.0 * math.pi)
        self.ts("dve", kf.ap, y.ap, MAGIC, None, ALU.add, None, [y], [kf])
        self.ts("dve", kf.ap, kf.ap, MAGIC, None, ALU.subtract, None, [kf], [kf])
        self.tt("dve", y.ap, y.ap, kf.ap, ALU.subtract, [y, kf], [y])
        self.act(sin_out, y.ap, AF.Sin, [y], [Ws], scale=2.0 * math.pi)
        ar.release(m)

    _ssem_i = 0

    def sload(self, dst_ap, src_ap, dst_reg, q="pool"):
        if not hasattr(self, "ssems"):
            self.ssems = [Buf(f"ss{i}") for i in range(8)]
        s = self.ssems[Builder._ssem_i % 8]
        Builder._ssem_i += 1
        if q == "sp":
            return self.fw.dma("sp", lambda h: h.dma_start(out=dst_ap, in_=src_ap, allow_slow_non_contiguous=True), s, writes=[dst_reg])
        return self.fw.dma("pool", lambda h: h.dma_start(out=dst_ap, in_=src_ap, allow_slow_non_contiguous=True), s, writes=[dst_reg])

    def layer_prep_gen(self, l):
        fw, ar, I = self.fw, self.ar, self.I
        m = ar.mark()
        c = self.s5cols
        lam_r = ar.alloc([128, 16], F32)
        lam_i = ar.alloc([128, 16], F32)
        stp = ar.alloc([128, 16], F32)
        t1 = ar.alloc([128, 16], F32)
        t2 = ar.alloc([128, 16], F32)
        t3 = ar.alloc([128, 16], F32)
        yv = ar.alloc([128, 16], F32)
        cs = ar.alloc([128, 16], F32)
        sn = ar.alloc([128, 16], F32)
        nat = ar.alloc([128, 4, 8, 16], F32)
        Cn = [ar.alloc([128, 4, 2, 64], F32) for _ in range(2)]
        ct2 = ar.alloc([128, 128], F32)
        lam = ar.alloc([128, 2, 2], F32)
        xx = ar.alloc([128, 2, 2], F32)
        pp = ar.alloc([128, 2, 2], F32)
        msk = ar.alloc([128, 8, 8], F32)
        Br = ar.alloc([128, 8, 16], F32)
        Bi = ar.alloc([128, 8, 16], F32)
        bb = [ar.alloc([128, 2, 8, 16], F32) for _ in range(2)]
        tA = ar.alloc([128, 2, 8, 16], F32)
        self.sload(lam_r.ap, I["s5_lam_re"][l].rearrange("d (j gl) n -> (gl n) (d j)", gl=2), lam_r)
        self.sload(lam_i.ap, I["s5_lam_im"][l].rearrange("d (j gl) n -> (gl n) (d j)", gl=2), lam_i)
        for gl in range(2):
            src = bass.AP(I["s5_log_step"].tensor, l * 32 + gl, [[0, 64], [2, 16]])
            self.sload(stp.ap[64 * gl:64 * gl + 64, :], src, stp)
        self.sload(c.ap[:, :, 6], I["s5r"][l].rearrange("d (j q) -> q (d j)", q=128), c)
        self.sload(c.ap[:, :, 7], I["s5i"][l].rearrange("d (j q) -> q (d j)", q=128), c)
        yield
        self.act(stp.ap, stp.ap, AF.Exp, [stp], [stp])
        self.tt("dve", t1.ap, lam_r.ap, stp.ap, ALU.mult, [lam_r, stp], [t1])
        self.tt("dve", t2.ap, lam_i.ap, stp.ap, ALU.mult, [lam_i, stp], [t2])
        self.act(c.ap[:, :, 1], t1.ap, AF.Exp, [t1], [c])
        self.ts("dve", yv.ap, t2.ap, 1.0 / (2.0 * math.pi), None, ALU.mult, None, [t2], [yv])
        self.sincos(yv, 16, cs.ap, sn.ap, cs, sn)
        self.cp("dve", c.ap[:, :, 0], yv.ap, [yv], [c])
        self.tt("dve", c.ap[:, :, 2], c.ap[:, :, 1], cs.ap, ALU.mult, [c, cs], [c])
        self.tt("dve", c.ap[:, :, 3], c.ap[:, :, 1], sn.ap, ALU.mult, [c, sn], [c])
        yield
        self.ts("dve", t1.ap, c.ap[:, :, 2], -1.0, None, ALU.add, None, [c], [t1])
        self.tt("dve", t2.ap, lam_r.ap, lam_r.ap, ALU.mult, [lam_r], [t2])
        self.tt("dve", t3.ap, lam_i.ap, lam_i.ap, ALU.mult, [lam_i], [t3])
        self.tt("dve", t2.ap, t2.ap, t3.ap, ALU.add, [t2, t3], [t2])
        fw.op("dve", lambda h: h.reciprocal(out=t2.ap, in_=t2.ap), reads=[t2], writes=[t2])
        self.tt("dve", t3.ap, t1.ap, lam_r.ap, ALU.mult, [t1, lam_r], [t3])
        self.tt("dve", yv.ap, c.ap[:, :, 3], lam_i.ap, ALU.mult, [c, lam_i], [yv])
        self.tt("dve", t3.ap, t3.ap, yv.ap, ALU.add, [t3, yv], [t3])
        self.tt("dve", c.ap[:, :, 4], t3.ap, t2.ap, ALU.mult, [t3, t2], [c])
        self.tt("dve", t3.ap, c.ap[:, :, 3], lam_r.ap, ALU.mult, [c, lam_r], [t3])
        self.tt("dve", yv.ap, t1.ap, lam_i.ap, ALU.mult, [t1, lam_i], [yv])
        self.tt("dve", t3.ap, t3.ap, yv.ap, ALU.subtract, [t3, yv], [t3])
        self.tt("dve", c.ap[:, :, 5], t3.ap, t2.ap, ALU.mult, [t3, t2], [c])

        yield
        fw.op("pool", lambda h: h.memset(msk.ap, 1.0), writes=[msk])
        for gl in range(2):
            fw.op("pool", lambda h, gl=gl: h.affine_select(out=msk.ap[64 * gl:64 * gl + 64], in_=msk.ap[64 * gl:64 * gl + 64],
                                                           pattern=[[0, 2], [-2, 4], [1, 8]], compare_op=ALU.is_equal, fill=0.0,
                                                           base=-gl, channel_multiplier=0), reads=[msk], writes=[msk])
        self.sload(Br.ap, I["s5_b_re"][l].rearrange("(j gl) n p -> (gl n) j p", gl=2), Br)
        self.sload(Bi.ap, I["s5_b_im"][l].rearrange("(j gl) n p -> (gl n) j p", gl=2), Bi)
        kr = c.ap[:, :, 4].rearrange("p (d j) -> p d j", d=2).unsqueeze(3).to_broadcast([128, 2, 8, 16])
        ki = c.ap[:, :, 5].rearrange("p (d j) -> p d j", d=2).unsqueeze(3).to_broadcast([128, 2, 8, 16])
        Brb = Br.ap.unsqueeze(1).to_broadcast([128, 2, 8, 16])
        Bib = Bi.ap.unsqueeze(1).to_broadcast([128, 2, 8, 16])
        self.tt("dve", bb[0].ap, Brb, kr, ALU.mult, [Br, c], [bb[0]])
        self.tt("dve", tA.ap, Bib, ki, ALU.mult, [Bi, c], [tA])
        self.tt("dve", bb[0].ap, bb[0].ap, tA.ap, ALU.subtract, [bb[0], tA], [bb[0]])
        self.tt("dve", bb[1].ap, Bib, kr, ALU.mult, [Bi, c], [bb[1]])
        self.tt("dve", tA.ap, Brb, ki, ALU.mult, [Br, c], [tA])
        self.tt("dve", bb[1].ap, bb[1].ap, tA.ap, ALU.add, [bb[1], tA], [bb[1]])
        yield
        for ri in range(2):
            for q4 in range(4):
                d, j0 = q4 // 2, (q4 % 2) * 4
                for jj in range(4):
                    j = j0 + jj
                    self.tt("dve", nat.ap[:, jj], bb[ri].ap[:, d, j].unsqueeze(1).to_broadcast([128, 8, 16]),
                            msk.ap[:, j].unsqueeze(2).to_broadcast([128, 8, 16]), ALU.mult, [bb[ri], msk], [nat])
                yield
                bk = self.bank()
                fw.mm([lambda h, bk=bk, jj=jj: h.transpose(out=bk.ap[:, jj * 128:(jj + 1) * 128],
                                                           in_=nat.ap[:, jj].rearrange("p a b -> p (a b)"), identity=self.ident_f.ap)
                       for jj in range(4)], reads=[nat, self.ident_f], writes=[bk])
                self.cp("act", self.s5w.ap[:, d * 8 + j0:d * 8 + j0 + 4, ri, :], bk.ap.rearrange("p (a b) -> p a b", a=4), [bk], [self.s5w])
        for hf in range(2):
            self.sload(Cn[0].ap[:, :, hf, :], I["s5_c_re"][l].rearrange("d g p n -> (d g p) n").rearrange("(q r) n -> r q n", r=128), Cn[0])
            self.sload(Cn[1].ap[:, :, hf, :], I["s5_c_im"][l].rearrange("d g p n -> (d g p) n").rearrange("(q r) n -> r q n", r=128), Cn[1])
        for ri in range(2):
            for q in range(4):
                d, ut = q // 2, q % 2
                bk = self.bank()
                fw.mm([lambda h, bk=bk, ri=ri, q=q: h.matmul(bk.ap[:, 0:128], lhsT=Cn[ri].ap[:, q].rearrange("p a b -> p (a b)"),
                                                             rhs=self.ident_f.ap, start=True, stop=True)],
                      reads=[Cn[ri], self.ident_f], writes=[bk])
                if ri == 0:
                    self.cp("act", ct2.ap, bk.ap[:, 0:128], [bk], [ct2])
                else:
                    self.act(ct2.ap, bk.ap[:, 0:128], AF.Copy, [bk], [ct2], scale=-1.0)
                j0 = ut * 4
                self.tt("dve", self.s5w.ap[:, d * 8 + j0:d * 8 + j0 + 4, 2 + ri, :].rearrange("p j (g q) -> p j g q", g=8),
                        ct2.ap.rearrange("p (g q) -> p g q", g=8).unsqueeze(1).to_broadcast([128, 4, 8, 16]),
                        msk.ap[:, j0:j0 + 4].unsqueeze(3).to_broadcast([128, 4, 8, 16]), ALU.mult, [ct2, msk], [self.s5w])
                if ri == 0:
                    self.act(ct2.ap, bk.ap[:, 0:128], AF.Copy, [bk], [ct2], scale=-1.0)
                    self.tt("dve", self.s5w.ap[:, d * 8 + j0:d * 8 + j0 + 4, 4, :].rearrange("p j (g q) -> p j g q", g=8),
                            ct2.ap.rearrange("p (g q) -> p g q", g=8).unsqueeze(1).to_broadcast([128, 4, 8, 16]),
                            msk.ap[:, j0:j0 + 4].unsqueeze(3).to_broadcast([128, 4, 8, 16]), ALU.mult, [ct2, msk], [self.s5w])
                yield
        self.sload(self.s5d.ap, I["s5_d"][l].rearrange("(t p) -> p t", p=128), self.s5d)
        yield
        lc = self.lrucols
        for k in range(4):
            self.sload(lc.ap[:, :, k], I["lru_conv_w"][l, k].rearrange("(t p) -> p t", p=128), lc)
        self.sload(lc.ap[:, :, 4], I["lru_conv_b"][l].rearrange("(t p) -> p t", p=128), lc)
        for d in range(2):
            self.sload(lc.ap[:, :, 5 + d], I["lru_b_a"][l, d].rearrange("(t p) -> p t", p=128), lc)
            self.sload(lc.ap[:, :, 7 + d], I["lru_b_x"][l, d].rearrange("(t p) -> p t", p=128), lc)
        for d in range(2):
            self.sload(lam.ap[:, :, d], I["lru_lam"][l, d].rearrange("(t p) -> p t", p=128), lam)
        self.act(xx.ap, lam.ap, AF.Exp, [lam], [xx], scale=-1.0)
        self.ts("dve", pp.ap, xx.ap, -0.25, 1.0 / 3.0, ALU.mult, ALU.add, [xx], [pp])
        self.tt("dve", pp.ap, pp.ap, xx.ap, ALU.mult, [pp, xx], [pp])
        self.ts("dve", pp.ap, pp.ap, -1.0, 0.5, ALU.mult, ALU.add, [pp], [pp])
        self.tt("dve", pp.ap, pp.ap, xx.ap, ALU.mult, [pp, xx], [pp])
        self.ts("dve", pp.ap, pp.ap, -1.0, 1.0, ALU.mult, ALU.add, [pp], [pp])
        self.tt("dve", pp.ap, pp.ap, xx.ap, ALU.mult, [pp, xx], [pp])
        self.ts("dve", lc.ap[:, :, 9:11], pp.ap, -8.0, None, ALU.mult, None, [pp], [lc])
        self.ts("dve", lc.ap[:, :, 11:13], pp.ap, 8.0, None, ALU.mult, None, [pp], [lc])
        self.ts("dve", lc.ap[:, :, 13:15], pp.ap, -16.0, None, ALU.mult, None, [pp], [lc])
        yield
        fw.op("pool", lambda h: h.memset(self.lruw.ap, 0.0), writes=[self.lruw])
        for gi, nm in enumerate(("lru_w_a", "lru_w_x")):
            for d in range(2):
                for t in range(2):
                    for b2 in range(2):
                        idx = gi * 4 + d * 2 + t
                        self.sload(self.lruw.ap[64 * b2:64 * b2 + 64, idx, 64 * b2:64 * b2 + 64], I[nm][l, d, 2 * t + b2], self.lruw, q="pool")
        for d in range(2):
            self.sload(self.lrust0.ap[:, d, :], I["slru"][l, d].rearrange("(t p) -> p t", p=128), self.lrust0)
        self.sload(self.wglu.ap, I["s5_w_glu"][l].rearrange("(k p) n -> p k n", p=128), self.wglu, q="pool")
        yield
        if not self.prep_bg:
            ar.release(m)

    def ffn_prep(self, l):
        I = self.I
        for k in range(3):
            self.sload(self.ffcols.ap[:, :, k], I["ffn_conv_w"][l, k].rearrange("(t p) -> p t", p=128), self.ffcols)
        self.sload(self.ffcols.ap[:, :, 3], I["ffn_conv_b"][l].rearrange("(t p) -> p t", p=128), self.ffcols)

    def group_layer(self, l, g):
        fw, ar, I, O = self.fw, self.ar, self.I, self.O
        if g == 0:
            self.ffn_prep(l)
        self.norm_mod(l, g, self.A1, 0)
        self.chk(f"norm{l}{g}")
        m0 = ar.mark()
        attnT = ar.alloc([128, 4, TG], BF16)
        m1 = ar.mark()
        self.attention(l, g, attnT)
        self.chk(f"attn{l}{g}")
        ar.release(m1)
        s5y = ar.alloc([128, 2, TG], BF16)
        m1 = ar.mark()
        self.s5(l, g, s5y)
        self.chk(f"s5{l}{g}")
        ar.release(m1)
        if g == 0:
            self.store_s5_states(l)
        lruy = ar.alloc([128, 2, TG], BF16)
        m1 = ar.mark()
        self.lru(l, g, lruy)
        self.chk(f"lru{l}{g}")
        ar.release(m1)
        if g == 1 and l + 1 < DEPTH:
            self.prep_bg = True
            self.bg = self.layer_prep_gen(l + 1)
            self.tick()
        if g == 1 and l + 1 == DEPTH:
            self.bg = self.final_norm_gen(0)
            self.tick()
        self.merge(l, g, attnT, s5y, lruy)
        self.drain()
        self.chk(f"merge{l}{g}")
        ar.release(m0)
        self.norm_mod(l, g, self.A2, 3)
        self.ffn(l, g)
        ar.release(m0)

    def attention(self, l, g, attnT):
        fw, ar, I, O = self.fw, self.ar, self.I, self.O
        q_sb = ar.alloc([128, 4, TG], BF16)
        k_sb = ar.alloc([128, 4, TG], BF16)
        v_aug = ar.alloc([128, 8, 8, 66], BF16)
        fw.op("pool", lambda h: h.memset(v_aug.ap[:, :, :, 64:65], 1.0), writes=[v_aug])

        def q_cons(ft, tb, bk):
            self.act(q_sb.ap[:, ft, tb * 512:(tb + 1) * 512], bk.ap, AF.Copy, [bk], [q_sb], scale=0.125)

        def k_cons(ft, tb, bk):
            self.cp("dve", k_sb.ap[:, ft, tb * 512:(tb + 1) * 512], bk.ap, [bk], [k_sb])

        self.proj_fm(self.win_cols(l, 0), 4, self.h_rhs, [self.h], 8, q_cons)
        self.proj_fm(self.win_cols(l, 512), 4, self.h_rhs, [self.h], 8, k_cons)
        self.chk(f"attq{l}{g}")
        for which in ((1, 2) if g == 0 else (2,)):
            for half in range(2):
                slot = self.wslice([(0, 8, 256, I["w_in"][l][:, which * 512 + half * 256: which * 512 + half * 256 + 256])])
                sv = slot.ap.rearrange("p (k n) -> p k n", k=8)
                for tt in range(8):
                    bk = self.bank()
                    fw.mm([lambda h, bk=bk, sv=sv, tt=tt, kc=kc: h.matmul(bk.ap[:, 0:256], lhsT=self.h.ap[:, kc, tt * 128:(tt + 1) * 128], rhs=sv[:, kc, :],
                                                                          start=(kc == 0), stop=(kc == 7)) for kc in range(8)],
                          reads=[slot, self.h], writes=[bk])
                    if which == 2:
                        self.cp("act", v_aug.ap[:, tt, half * 4:half * 4 + 4, 0:64], bk.ap[:, 0:256].rearrange("p (a b) -> p a b", a=4), [bk], [v_aug])
                    if g == 0:
                        s = self.stg_i % 2
                        self.stg_i += 1
                        stg = self.stg[s]
                        self.cp("dve", stg.ap[:, 0:256], bk.ap[:, 0:256], [bk], [stg])
                        dst = O["nk" if which == 1 else "nv"][tt // 2, l, (tt % 2) * 128:(tt % 2) * 128 + 128, half * 256:half * 256 + 256]
                        d = fw.dma("sp", lambda h, stg=stg, dst=dst: h.dma_start(out=dst, in_=stg.ap[:, 0:256]), self.stg_sem[s], reads=[stg])
                        fw.final.append(d)
                    self.chk(f"kv1{l}{g}")
            if which == 1:
                self.chk(f"kvK{l}{g}")

        self.chk(f"attkv{l}{g}")
        pT = [ar.alloc([128, 512], BF16) for _ in range(2)]
        pi = [0]
        atok = ar.alloc([128, 8, 128], BF16)
        rec = ar.alloc([128, 8], F32)

        def transposes(hp, tts):
            bk = self.bank()
            bkb = bk.ap.bitcast(BF16)
            fw.mm([lambda h, bkb=bkb, i=i, tt=tt: h.transpose(out=bkb[:, i * 128:(i + 1) * 128], in_=atok.ap[:, tt, :], identity=self.ident_bf.ap)
                   for i, tt in enumerate(tts)], reads=[atok, self.ident_bf], writes=[bk])
            n = len(tts)
            self.cp("dve", attnT.ap[:, hp, tts[0] * 128:(tts[0] + n) * 128], bkb[:, 0:n * 128], [bk], [attnT])

        if g == 0:
            UP = [(sq, hp, e) for sq in range(4) for hp in range(4) for e in range(2)]

            def issue_score_p(k):
                sq, hp, e = UP[k]
                pb = 64 * e
                sb = self.bank()
                fw.mm([lambda h, sb=sb, kt=kt, pb=pb, sq=sq, hp=hp: h.matmul(sb.ap[:, kt * 256:(kt + 1) * 256],
                                                                            lhsT=k_sb.ap[pb:pb + 64, hp, sq * 256 + kt * 128:sq * 256 + kt * 128 + 128],
                                                                            rhs=q_sb.ap[pb:pb + 64, hp, sq * 256:sq * 256 + 256], start=True, stop=True) for kt in range(2)],
                      reads=[k_sb, q_sb], writes=[sb])
                return sb

            nxt = issue_score_p(0)
            ob = ov = None
            for k, (sq, hp, e) in enumerate(UP):
                hh = 2 * hp + e
                if e == 0:
                    ob = self.bank(hold=True)
                    ov = ob.ap[:, 0:260].rearrange("p (q e c) -> p q e c", q=2, e=2)
                sb = nxt
                p = pT[k % 2]
                self.act(p.ap, sb.ap, AF.Exp, [sb], [p])
                if k + 1 < len(UP):
                    nxt = issue_score_p(k + 1)
                fns = []
                for qt in range(2):
                    for kt in range(2):
                        fns.append(lambda h, qt=qt, kt=kt, e=e, hh=hh, p=p, ov=ov, sq=sq, st_=(e == 0 and qt == 0 and kt == 0): h.matmul(
                            ov[:, qt, e, :], lhsT=p.ap[:, kt * 256 + qt * 128:kt * 256 + qt * 128 + 128], rhs=v_aug.ap[:, 2 * sq + kt, hh, 0:65],
                            start=st_, stop=(e == 1 and qt == 1 and kt == 1)))
                fw.mm(fns, reads=[p, v_aug], writes=[ob])
                if e == 1:
                    fw.op("dve", lambda h, ov=ov: h.reciprocal(out=rec.ap[:, 0:4].rearrange("p (q e) -> p q e", q=2), in_=ov[:, :, :, 64]), reads=[ob], writes=[rec])
                    self.tt("dve", atok.ap[:, 2 * sq:2 * sq + 2, :].rearrange("p q (e c) -> p q e c", e=2), ov[:, :, :, 0:64],
                            rec.ap[:, 0:4].rearrange("p (q e) -> p q e", q=2).unsqueeze(3).to_broadcast([128, 2, 2, 64]), ALU.mult, [ob, rec], [atok])
                    self.unhold(ob)
                    transposes(hp, [2 * sq, 2 * sq + 1])
        else:
            self.na_attention(l, attnT, q_sb, k_sb, v_aug, pT, atok, rec, transposes)

    def na_attention(self, l, attnT, q_sb, k_sb, v_aug, pT, atok, rec, transposes):
        fw, ar, I = self.fw, self.ar, self.I
        kctxT = ar.alloc([128, 4, 512], BF16)
        vctx = ar.alloc([128, 4, 8, 66], BF16)
        fw.op("pool", lambda h: h.memset(vctx.ap[:, :, :, 64:65], 1.0), writes=[vctx])
        cvsem = Buf("cvsem")
        for tt in range(4):
            fw.dma("pool", lambda h, tt=tt: h.dma_start(out=vctx.ap[:, tt, :, 0:64], in_=I["cv"][l][tt * 128:(tt + 1) * 128, :].rearrange("p (a b) -> p a b", a=8)),
                   cvsem, writes=[vctx])
        mk = ar.mark()
        cktok = ar.alloc([128, 4, 512], BF16)
        cksem = Buf("cksem")
        fw.dma("pool", lambda h: h.dma_start(out=cktok.ap, in_=I["ck"][l].rearrange("(t p) f -> p t f", p=128)), cksem, writes=[cktok])
        for hp in range(4):
            bk = self.bank()
            bkb = bk.ap.bitcast(BF16)
            fw.mm([lambda h, bkb=bkb, tt=tt, hp=hp: h.transpose(out=bkb[:, tt * 128:(tt + 1) * 128], in_=cktok.ap[:, tt, hp * 128:(hp + 1) * 128],
                                                               identity=self.ident_bf.ap) for tt in range(4)], reads=[cktok, self.ident_bf], writes=[bk])
            self.cp("dve", kctxT.ap[:, hp, :], bkb[:, 0:512], [bk], [kctxT])
        ar.release(mk)
        LT = ar.alloc([128, 8, 18, 64], BF16)
        mk = ar.mark()
        rp = ar.alloc([128, 2, 32], F32)
        fw.op("pool", lambda h: h.memset(rp.ap, 0.0), writes=[rp])
        for e in range(2):
            self.sload(rp.ap[0:120, e, 0:31], I["rpb"][l].rearrange("h a x -> (h a) x"), rp)
        Rs = ar.alloc([64, 8, 15], F32)
        RE = ar.alloc([64, 8, 18], F32)
        BB = ar.alloc([64, 2, 127], F32)
        msk = ar.alloc([128, 64], F32)
        m2 = ar.alloc([128, 64], F32)
        bk = self.bank()
        fw.mm([lambda h, bk=bk: h.matmul(bk.ap[0:64, 0:120], lhsT=rp.ap[0:120].rearrange("p a b -> p (a b)"),
                                         rhs=self.ident_f.ap[0:120, 0:120], start=True, stop=True)], reads=[rp, self.ident_f], writes=[bk])
        fw.op("pool", lambda h: h.memset(Rs.ap, 0.0), writes=[Rs])
        for e in range(2):
            self.cp("dve", Rs.ap[32 * e:32 * e + 31].rearrange("p a b -> p (a b)"), bk.ap[32 * e:32 * e + 31, 0:120], [bk], [Rs])
        fw.op("pool", lambda h: h.memset(RE.ap, 0.0), writes=[RE])
        for e in range(2):
            self.cp("dve", RE.ap[32 * e:32 * e + 31, :, e + 1:e + 16], Rs.ap[32 * e:32 * e + 31, :, ::-1], [Rs], [RE])
        fw.op("pool", lambda h: h.memset(BB.ap, 0.0), writes=[BB])
        for e in range(2):
            fw.op("pool", lambda h, e=e: h.memset(BB.ap[32 * e:32 * e + 32, e, :], 1.0), writes=[BB])
            fw.op("pool", lambda h, e=e: h.affine_select(out=BB.ap[32 * e:32 * e + 32, e, :], in_=BB.ap[32 * e:32 * e + 32, e, :], pattern=[[1, 127]],
                                                          compare_op=ALU.is_equal, fill=0.0, base=-48, channel_multiplier=-1), reads=[BB], writes=[BB])
        fw.op("pool", lambda h: h.memset(msk.ap, 0.0), writes=[msk])
        fw.op("pool", lambda h: h.memset(m2.ap, 0.0), writes=[m2])
        for hf in range(2):
            sl = slice(64 * hf, 64 * hf + 64)
            fw.op("pool", lambda h, sl=sl: h.affine_select(out=msk.ap[sl], in_=msk.ap[sl], pattern=[[-1, 64]], compare_op=ALU.is_ge, fill=NEG,
                                                            base=8, channel_multiplier=1), reads=[msk], writes=[msk])
            fw.op("pool", lambda h, sl=sl: h.affine_select(out=msk.ap[sl], in_=msk.ap[sl], pattern=[[0, 64]], compare_op=ALU.is_ge, fill=0.0,
                                                            base=47, channel_multiplier=-1), reads=[msk], writes=[msk])
            fw.op("pool", lambda h, sl=sl: h.affine_select(out=m2.ap[sl], in_=m2.ap[sl], pattern=[[1, 64]], compare_op=ALU.is_ge, fill=NEG,
                                                            base=7, channel_multiplier=-1), reads=[m2], writes=[m2])
            fw.op("pool", lambda h, sl=sl: h.affine_select(out=m2.ap[sl], in_=m2.ap[sl], pattern=[[0, 64]], compare_op=ALU.is_ge, fill=0.0,
                                                            base=-16, channel_multiplier=1), reads=[m2], writes=[m2])
        self.tt("pool", msk.ap, msk.ap, m2.ap, ALU.add, [msk, m2], [msk])
        REf = RE.ap.rearrange("p a b -> p (a b)")
        for q0 in range(0, 64, 3):
            nq = min(3, 64 - q0)
            bk = self.bank()
            for i in range(nq):
                qc = q0 + i
                fw.mm([lambda h, bk=bk, i=i, qc=qc, e=e: h.matmul(bk.ap[64 * e:64 * e + 64, i * 144:(i + 1) * 144], lhsT=BB.ap[:, e, 63 - qc:63 - qc + 64], rhs=REf,
                                                                  start=True, stop=True) for e in range(2)], reads=[BB, RE], writes=[bk])
            outv = LT.ap[:, :, :, q0:q0 + nq].rearrange("p h d q -> p q (h d)")
            self.tt("dve", outv, bk.ap[:, 0:nq * 144].rearrange("p (q n) -> p q n", q=nq),
                    msk.ap[:, q0:q0 + nq].unsqueeze(2).to_broadcast([128, nq, 144]), ALU.add, [bk, msk], [LT])
        LTf = LT.ap.rearrange("p h d q -> p (h d q)")
        for hq in range(4):
            self.act(LTf[:, hq * 2304:(hq + 1) * 2304], LTf[:, hq * 2304:(hq + 1) * 2304], AF.Exp, [LT], [LT])
        ar.release(mk)

        U = []
        for hp in range(4):
            for e in range(2):
                for c in range(2):
                    grp = []
                    for mt in range(8):
                        js = [j for j in range(4 * c, 4 * c + 4) if any(na_valid(2 * mt + ee, 2 * j + r) for ee in range(2) for r in range(2))]
                        if js:
                            grp.append(("loc", mt, js[0], js[-1]))
                    for kt in range(4):
                        grp.append(("ctx", kt, 4 * c, 4 * c + 3))
                    for ui, (kind, mt, ja, jb) in enumerate(grp):
                        U.append((hp, e, c, kind, mt, ja, jb, ui == 0, ui == len(grp) - 1))

        def issue_score(k):
            hp, e, c, kind, mt, ja, jb, gfirst, glast = U[k]
            pb = 64 * e
            hh = 2 * hp + e
            nq = 128 * (jb - ja + 1)
            sb = self.bank()
            if kind == "loc":
                d0 = 2 * ja - 2 * mt + 8
                d1 = 2 * jb + 1 - 2 * mt + 8
                assert 0 <= d0 and d1 < 18, (mt, ja, jb)
                ltv = LT.ap[:, hh, d0:d1 + 1, :].rearrange("p a b -> p (a b)")
                fw.mm([lambda h, sb=sb, mt=mt, ja=ja, nq=nq, pb=pb, hp=hp: h.matmul(sb.ap[:, 0:nq], lhsT=k_sb.ap[pb:pb + 64, hp, mt * 128:(mt + 1) * 128],
                                                                                   rhs=q_sb.ap[pb:pb + 64, hp, ja * 128:ja * 128 + nq], start=True, stop=True)],
                      reads=[k_sb, q_sb], writes=[sb])
            else:
                fw.mm([lambda h, sb=sb, mt=mt, ja=ja, nq=nq, pb=pb, hp=hp: h.matmul(sb.ap[:, 0:nq], lhsT=kctxT.ap[pb:pb + 64, hp, mt * 128:(mt + 1) * 128],
                                                                                   rhs=q_sb.ap[pb:pb + 64, hp, ja * 128:ja * 128 + nq], start=True, stop=True)],
                      reads=[kctxT, q_sb], writes=[sb])
            return sb

        nxt = issue_score(0)
        ob = ov = None
        first = True
        for k, (hp, e, c, kind, mt, ja, jb, gfirst, glast) in enumerate(U):
            hh = 2 * hp + e
            nq = 128 * (jb - ja + 1)
            if gfirst:
                ob = self.bank(hold=True)
                ov = ob.ap[:, 0:260].rearrange("p (q c) -> p q c", q=4)
                first = True
            sb = nxt
            p = pT[k % 2]
            self.act(p.ap[:, 0:nq], sb.ap[:, 0:nq], AF.Exp, [sb], [p])
            if k + 1 < len(U):
                nxt = issue_score(k + 1)
            if kind == "loc":
                d0 = 2 * ja - 2 * mt + 8
                d1 = 2 * jb + 1 - 2 * mt + 8
                ltv = LT.ap[:, hh, d0:d1 + 1, :].rearrange("p a b -> p (a b)")
                self.tt("dve", p.ap[:, 0:nq], p.ap[:, 0:nq], ltv, ALU.mult, [p, LT], [p])
            fns = []
            for j in range(ja, jb + 1):
                if kind == "loc":
                    val = [[na_valid(2 * mt + ee, 2 * j + r) for r in range(2)] for ee in range(2)]
                    if not any(val[0]) and not any(val[1]):
                        continue
                    for ee in range(2):
                        for r in range(2):
                            if not val[ee][r]:
                                c0 = (j - ja) * 128 + r * 64
                                fw.op("dve", lambda h, p=p, ee=ee, c0=c0: h.memset(p.ap[64 * ee:64 * ee + 64, c0:c0 + 64], 0.0), reads=[p], writes=[p])
                    rhs = v_aug.ap[:, mt, hh, 0:65]
                else:
                    rhs = vctx.ap[:, mt, hh, 0:65]
                fns.append(lambda h, j=j, ja=ja, p=p, rhs=rhs, st_=first, ov=ov, c=c: h.matmul(ov[:, j - 4 * c, :], lhsT=p.ap[:, (j - ja) * 128:(j - ja + 1) * 128],
                                                                                            rhs=rhs, start=st_, stop=False))
                first = False
            fw.mm(fns, reads=[p, v_aug, vctx], writes=[ob])
            if glast:
                fw.op("dve", lambda h, ov=ov: h.reciprocal(out=rec.ap[:, 0:4], in_=ov[:, :, 64]), reads=[ob], writes=[rec])
                self.tt("dve", atok.ap[:, 4 * c:4 * c + 4, 64 * e:64 * e + 64], ov[:, :, 0:64],
                        rec.ap[:, 0:4].unsqueeze(2).to_broadcast([128, 4, 64]), ALU.mult, [ob, rec], [atok])
                self.unhold(ob)
                if e == 1 and c == 1:
                    transposes(hp, [0, 1, 2, 3])
                    transposes(hp, [4, 5, 6, 7])

    def s5(self, l, g, s5y):
        fw, ar, I, O = self.fw, self.ar, self.I, self.O
        c = self.s5cols
        u_sb = ar.alloc([128, 2, TG], BF16)

        def u_cons(ft, tb, bk):
            self.cp("act", u_sb.ap[:, ft, tb * 512:(tb + 1) * 512], bk.ap, [bk], [u_sb])

        self.proj_fm(self.win_cols(l, 1536), 2, self.h_rhs, [self.h], 8, u_cons)
        Ec = ar.alloc([128, 16, 256], F32)
        Es = ar.alloc([128, 16, 256], F32)
        mk = ar.mark()
        io_i = ar.alloc([128, 256], I32)
        io_f = ar.alloc([128, 256], F32)
        fw.op("pool", lambda h: h.iota(io_i.ap, pattern=[[1, 256]], base=0, channel_multiplier=0), writes=[io_i])
        self.cp("dve", io_f.ap, io_i.ap, [io_i], [io_f])
        yv = ar.alloc([128, 2, 256], F32)
        for q in range(8):
            self.tt("dve", yv.ap, c.ap[:, 2 * q:2 * q + 2, 0].unsqueeze(2).to_broadcast([128, 2, 256]),
                    io_f.ap.unsqueeze(1).to_broadcast([128, 2, 256]), ALU.mult, [c, io_f], [yv])
            yflat = Reg(yv.ap.rearrange("p a b -> p (a b)"), yv.bufs)
            self.sincos(yflat, 512, Ec.ap[:, 2 * q:2 * q + 2, :].rearrange("p a b -> p (a b)"),
                        Es.ap[:, 2 * q:2 * q + 2, :].rearrange("p a b -> p (a b)"), Ec, Es)
        ar.release(mk)
        Kc = ar.alloc([128, 16, 4], F32)
        if g == 0:
            self.ts("dve", Kc.ap[:, :, 3], Es.ap[:, :, 255], -1.0, None, ALU.mult, None, [Es], [Kc])
        if g == 1:
            e255c = Ec.ap[:, :, 255]
            e255s = Es.ap[:, :, 255]
            self.tt("dve", Kc.ap[:, :, 0], c.ap[:, :, 2], e255c, ALU.mult, [c, Ec], [Kc])
            self.tt("dve", Kc.ap[:, :, 3], c.ap[:, :, 3], e255s, ALU.mult, [c, Es], [Kc])
            self.tt("dve", Kc.ap[:, :, 0], Kc.ap[:, :, 0], Kc.ap[:, :, 3], ALU.subtract, [Kc], [Kc])
            self.tt("dve", Kc.ap[:, :, 1], c.ap[:, :, 2], e255s, ALU.mult, [c, Es], [Kc])
            self.tt("dve", Kc.ap[:, :, 3], c.ap[:, :, 3], e255c, ALU.mult, [c, Ec], [Kc])
            self.tt("dve", Kc.ap[:, :, 1], Kc.ap[:, :, 1], Kc.ap[:, :, 3], ALU.add, [Kc], [Kc])
            self.ts("dve", Kc.ap[:, :, 2], Kc.ap[:, :, 1], -1.0, None, ALU.mult, None, [Kc], [Kc])
        ygb = ar.alloc([128, 2, TG], BF16)
        bpr = ar.alloc([128, 512], F32)
        bpi = ar.alloc([128, 512], F32)
        grs = [ar.alloc([128, 512], F32) for _ in range(2)]
        gis = [ar.alloc([128, 512], F32) for _ in range(2)]
        t2 = ar.alloc([128, 512], F32)
        qs = [ar.alloc([128, 512], BF16) for _ in range(4)]
        unit = [0]
        sm = ar.alloc([128, 16], F32)

        def v2(ap):
            return ap.rearrange("p (s t) -> p s t", s=2)

        def seg(ap, s2, rev):
            v = ap[:, s2 * 256:(s2 + 1) * 256]
            return v[:, ::-1] if rev else v

        units = []
        for ut in range(2):
            for d in range(2):
                for jj in range(4):
                    for ti, tb in enumerate([1, 0] if d == 1 else [0, 1]):
                        units.append((ut, d, jj, ti, tb))

        def issue_bu(k):
            ut, d, jj, ti, tb = units[k]
            dj = d * 8 + ut * 4 + jj
            tsl = slice(tb * 512, (tb + 1) * 512)
            br = self.bank()
            bi = self.bank()
            fw.mm([lambda h, br=br, dj=dj, tsl=tsl, ut=ut: h.matmul(br.ap, lhsT=self.s5w.ap[:, dj, 0, :], rhs=u_sb.ap[:, ut, tsl], start=True, stop=True)],
                  reads=[self.s5w, u_sb], writes=[br])
            fw.mm([lambda h, bi=bi, dj=dj, tsl=tsl, ut=ut: h.matmul(bi.ap, lhsT=self.s5w.ap[:, dj, 1, :], rhs=u_sb.ap[:, ut, tsl], start=True, stop=True)],
                  reads=[self.s5w, u_sb], writes=[bi])
            return br, bi

        ybanks = None
        yfirst = None
        if l == 0:
            self.bg = self.mod_gen(0, look=2, s0=8, s1=24, doA=(False, True)) if g == 0 else self.mod_gen(1, look=2)
        tbk = self.bank(hold=True)
        tbk2 = self.bank(hold=True)
        deferred = []
        deferred_mul = []
        prev_last = None
        nxt = issue_bu(0)
        for k, (ut, d, jj, ti, tb) in enumerate(units):
            rev = (d == 1)
            j = ut * 4 + jj
            dj = d * 8 + j
            if d == 0 and jj == 0 and ti == 0:
                ybanks = [self.bank(hold=True), self.bank(hold=True)]
                yfirst = [True, True]
            Ecv = Ec.ap[:, dj, :]
            Esv = Es.ap[:, dj, :]
            if rev:
                Ecv = Ecv[:, ::-1]
                Esv = Esv[:, ::-1]
            Ec2 = Ecv.unsqueeze(1).to_broadcast([128, 2, 256])
            Es2 = Esv.unsqueeze(1).to_broadcast([128, 2, 256])
            rb = c.ap[:, dj, 1:2].to_broadcast([128, 256])
            gr, gi = grs[k % 2], gis[k % 2]
            br, bi = nxt
            self.tt("dve", v2(t2.ap), v2(bi.ap), Es2, ALU.mult, [bi, Es], [t2])
            self.tt("dve", v2(tbk.ap), v2(br.ap), Ec2, ALU.mult, [br, Ec], [tbk])
            self.tt("dve", v2(tbk2.ap), v2(bi.ap), Ec2, ALU.mult, [bi, Ec], [tbk2])
            self.tt("dve", bpr.ap, tbk.ap, t2.ap, ALU.add, [tbk, t2], [bpr])
            self.tt("dve", v2(t2.ap), v2(br.ap), Es2, ALU.mult, [br, Es], [t2])
            self.tt("dve", bpi.ap, tbk2.ap, t2.ap, ALU.subtract, [tbk2, t2], [bpi])
            if k + 1 < len(units):
                nxt = issue_bu(k + 1)
            for fn in deferred:
                fn()
            deferred = []
            segs = [1, 0] if rev else [0, 1]
            if g == 0:
                for (src, dst) in ((bpr, gr), (bpi, gi)):
                    for s2 in segs:
                        fw.op("dve", lambda h, src=src, dst=dst, s2=s2, rev=rev, rb=rb: h.tensor_tensor_scan(
                            out=seg(dst.ap, s2, rev), data0=rb, data1=seg(src.ap, s2, rev), initial=0.0, op0=ALU.mult, op1=ALU.add),
                            reads=[src, c], writes=[dst])
            for sj, s2 in enumerate(segs):
                s = tb * 2 + s2
                si = ti * 2 + sj
                if g == 1:
                    f_r = seg(bpr.ap, s2, rev)[:, 0:1]
                    f_i = seg(bpi.ap, s2, rev)[:, 0:1]
                    if si == 0:
                        hpr, hpi = c.ap[:, dj, 6:7], c.ap[:, dj, 7:8]
                        self.stt(f_r, hpr, c.ap[:, dj, 2:3], f_r, ALU.mult, ALU.add, [c, bpr], [bpr])
                        self.stt(f_i, hpi, c.ap[:, dj, 2:3], f_i, ALU.mult, ALU.add, [c, bpi], [bpi])
                        self.ts("dve", sm.ap[:, 0:1], hpi, c.ap[:, dj, 3:4], None, ALU.mult, None, [c], [sm])
                        self.stt(f_i, hpr, c.ap[:, dj, 3:4], f_i, ALU.mult, ALU.add, [c, bpi], [bpi])
                        self.tt("dve", f_r, f_r, sm.ap[:, 0:1], ALU.subtract, [bpr, sm], [bpr])
                    else:
                        pgr, pgi = prev_last
                        self.stt(f_r, pgr[0], Kc.ap[:, dj, 0:1], f_r, ALU.mult, ALU.add, [pgr[1], Kc, bpr], [bpr])
                        self.stt(f_i, pgi[0], Kc.ap[:, dj, 0:1], f_i, ALU.mult, ALU.add, [pgi[1], Kc, bpi], [bpi])
                        self.stt(f_r, pgi[0], Kc.ap[:, dj, 2:3], f_r, ALU.mult, ALU.add, [pgi[1], Kc, bpr], [bpr])
                        self.stt(f_i, pgr[0], Kc.ap[:, dj, 1:2], f_i, ALU.mult, ALU.add, [pgr[1], Kc, bpi], [bpi])
                if g == 1:
                    for (src, dst) in ((bpr, gr), (bpi, gi)):
                        fw.op("dve", lambda h, src=src, dst=dst, s2=s2, rev=rev, rb=rb: h.tensor_tensor_scan(
                            out=seg(dst.ap, s2, rev), data0=rb, data1=seg(src.ap, s2, rev), initial=0.0, op0=ALU.mult, op1=ALU.add),
                            reads=[src, c], writes=[dst])
                g_r = seg(gr.ap, s2, rev)[:, 255:256]
                g_i = seg(gi.ap, s2, rev)[:, 255:256]
                prev_last = ((g_r, gr), (g_i, gi))
                if g == 0:
                    def state_ops(dj=dj, g_r=g_r, g_i=g_i, gr=gr, gi=gi, s=s, d=d, j=j, sj=sj):
                        e_c = Ec.ap[:, dj, 255:256]
                        e_s = Es.ap[:, dj, 255:256]
                        n_s = Kc.ap[:, dj, 3:4]
                        o_r, o_i = self.outst.ap[:, s, d, j, 0:1], self.outst.ap[:, s, d, j, 1:2]
                        ta, tb_ = sm.ap[:, 8 + 2 * sj:9 + 2 * sj], sm.ap[:, 9 + 2 * sj:10 + 2 * sj]
                        self.act(ta, g_i, AF.Identity, [gi, Kc], [sm], scale=n_s)
                        self.act(tb_, g_i, AF.Identity, [gi, Ec], [sm], scale=e_c)
                        self.act(o_r, g_r, AF.Identity, [gr, Ec, sm], [self.outst], scale=e_c, bias=ta)
                        self.act(o_i, g_r, AF.Identity, [gr, Es, sm], [self.outst], scale=e_s, bias=tb_)
                    deferred.append(state_ops)
            self.tt("pool", v2(qs[0].ap), v2(gr.ap), Ec2, ALU.mult, [gr, Ec], [qs[0]])
            self.tt("pool", v2(qs[1].ap), v2(gi.ap), Es2, ALU.mult, [gi, Es], [qs[1]])
            self.tt("pool", v2(qs[2].ap), v2(gi.ap), Ec2, ALU.mult, [gi, Ec], [qs[2]])
            self.tt("pool", v2(qs[3].ap), v2(gr.ap), Es2, ALU.mult, [gr, Es], [qs[3]])
            self.tick()
            yb = ybanks[tb]
            last = (d == 1 and jj == 3)
            fw.mm([lambda h, yb=yb, dj=dj, st_=yfirst[tb]: h.matmul(yb.ap, lhsT=self.s5w.ap[:, dj, 2, :], rhs=qs[0].ap, start=st_, stop=False),
                   lambda h, yb=yb, dj=dj: h.matmul(yb.ap, lhsT=self.s5w.ap[:, dj, 4, :], rhs=qs[1].ap, start=False, stop=False),
                   lambda h, yb=yb, dj=dj: h.matmul(yb.ap, lhsT=self.s5w.ap[:, dj, 3, :], rhs=qs[2].ap, start=False, stop=False),
                   lambda h, yb=yb, dj=dj, last=last: h.matmul(yb.ap, lhsT=self.s5w.ap[:, dj, 3, :], rhs=qs[3].ap, start=False, stop=last)],
                  reads=[self.s5w] + qs, writes=[yb])
            yfirst[tb] = False
            if d == 1 and jj == 3 and ti == 1:
                for tb2 in range(2):
                    yb = ybanks[tb2]
                    sl = slice(tb2 * 512, (tb2 + 1) * 512)
                    self.stt(t2.ap, u_sb.ap[:, ut, sl], self.s5d.ap[:, ut:ut + 1], yb.ap, ALU.mult, ALU.add, [u_sb, self.s5d, yb], [t2])
                    self.act(ygb.ap[:, ut, sl], t2.ap, AF.Gelu_apprx_tanh, [t2], [ygb])
                    self.unhold(yb)
        for fn in deferred:
            fn()
        self.drain()
        self.unhold(tbk)
        self.unhold(tbk2)
        for ot in range(2):
            for tb in range(2):
                sl = slice(tb * 512, (tb + 1) * 512)
                bk = self.bank()
                fw.mm([lambda h, bk=bk, kc=kc, ot=ot, sl=sl: h.matmul(bk.ap, lhsT=self.wglu.ap[:, kc, ot * 128:(ot + 1) * 128], rhs=ygb.ap[:, kc, sl],
                                                                      start=(kc == 0), stop=(kc == 1)) for kc in range(2)], reads=[self.wglu, ygb], writes=[bk])
                self.act(t2.ap, bk.ap, AF.Sigmoid, [bk], [t2])
                self.tt("dve", s5y.ap[:, ot, sl], ygb.ap[:, ot, sl], t2.ap, ALU.mult, [ygb, t2], [s5y])

    def store_s5_states(self, l):
        fw, ar, O = self.fw, self.ar, self.O
        mk = ar.mark()
        tr = ar.alloc([128, 128], F32)
        bk = self.bank()
        fw.mm([lambda h: h.transpose(out=bk.ap[:, 0:128], in_=self.outst.ap.rearrange("p s d j r -> p (s d j r)"), identity=self.ident_f.ap)],
              reads=[self.outst, self.ident_f], writes=[bk])
        self.cp("dve", tr.ap, bk.ap[:, 0:128], [bk], [tr])
        sem = Buf("s5out")
        for ri, nm in enumerate(("ns5r", "ns5i")):
            for s in range(4):
                for d in range(2):
                    r0 = ((s * 2 + d) * 8) * 2 + ri
                    src = tr.ap[r0:r0 + 15:2, :]
                    dst = O[nm][s, l, d, :].rearrange("(j q) -> j q", q=128)
                    dd = fw.dma("sp", lambda h, src=src, dst=dst: h.dma_start(out=dst, in_=src), sem, reads=[tr])
                    fw.final.append(dd)
        ar.release(mk)

    def lru(self, l, g, lruy):
        fw, ar, I, O = self.fw, self.ar, self.I, self.O
        lc = self.lrucols
        xr = ar.alloc([128, 2, TG], F32)
        gg = ar.alloc([128, 2, TG], F32)
        xc = ar.alloc([128, 2, TG], F32)
        xcb = ar.alloc([128, 2, TG], BF16)

        def xr_cons(ft, tb, bk):
            self.cp("act", xr.ap[:, ft, tb * 512:(tb + 1) * 512], bk.ap, [bk], [xr])

        def xg_cons(ft, tb, bk):
            self.act(gg.ap[:, ft, tb * 512:(tb + 1) * 512], bk.ap, AF.Gelu_apprx_tanh, [bk], [gg])

        self.proj_fm(self.win_cols(l, 1792), 2, self.h_rhs, [self.h], 8, xr_cons)
        self.proj_fm(self.win_cols(l, 2048), 2, self.h_rhs, [self.h], 8, xg_cons)
        nseq, L = (4, 256) if g == 0 else (1, 1024)
        for t in range(2):
            xv = xr.ap[:, t, :].rearrange("p (s q) -> p s q", s=nseq)
            cv = xc.ap[:, t, :].rearrange("p (s q) -> p s q", s=nseq)
            self.ts("dve", cv, xv, lc.ap[:, t, 2:3], lc.ap[:, t, 4:5], ALU.mult, ALU.add, [xr, lc], [xc])
            for k in (0, 1, 3):
                sh = k - 2
                lo, hi = max(0, -sh), L - max(0, sh)
                self.stt(cv[:, :, lo:hi], xv[:, :, lo + sh:hi + sh], lc.ap[:, t, k:k + 1], cv[:, :, lo:hi], ALU.mult, ALU.add, [xr, lc, xc], [xc])
            self.cp("act", xcb.ap[:, t, :], xc.ap[:, t, :], [xc], [xcb])
        r_ = ar.alloc([128, TG], F32)
        i_ = ar.alloc([128, TG], F32)
        e2 = ar.alloc([128, TG], F32)
        a_sb = ar.alloc([128, TG], F32)
        b_sb = ar.alloc([128, TG], F32)
        hs = [ar.alloc([128, TG], F32) for _ in range(2)]
        th = hs[1]
        for t in range(2):
            for d in range(2):
                rev = (d == 1)
                for tb in range(2):
                    sl = slice(tb * 512, (tb + 1) * 512)
                    pr = self.bank()
                    pi = self.bank()
                    fw.mm([lambda h, pr=pr, d=d, t=t, sl=sl: h.matmul(pr.ap, lhsT=self.lruw.ap[:, 0 * 4 + d * 2 + t, :], rhs=xcb.ap[:, t, sl], start=True, stop=True)],
                          reads=[self.lruw, xcb], writes=[pr])
                    fw.mm([lambda h, pi=pi, d=d, t=t, sl=sl: h.matmul(pi.ap, lhsT=self.lruw.ap[:, 1 * 4 + d * 2 + t, :], rhs=xcb.ap[:, t, sl], start=True, stop=True)],
                          reads=[self.lruw, xcb], writes=[pi])
                    self.act(r_.ap[:, sl], pr.ap, AF.Sigmoid, [pr, lc], [r_], bias=lc.ap[:, t, 5 + d:6 + d])
                    self.act(i_.ap[:, sl], pi.ap, AF.Sigmoid, [pi, lc], [i_], bias=lc.ap[:, t, 7 + d:8 + d])
                self.act(a_sb.ap, r_.ap, AF.Exp, [r_, lc], [a_sb], scale=lc.ap[:, t, 9 + d:10 + d])
                self.act(e2.ap, r_.ap, AF.Exp, [r_, lc], [e2], scale=lc.ap[:, t, 13 + d:14 + d])
                self.act(th.ap, r_.ap, AF.Tanh, [r_, lc], [th], scale=lc.ap[:, t, 11 + d:12 + d])
                self.tt("dve", i_.ap, i_.ap, xc.ap[:, t, :], ALU.mult, [i_, xc], [i_])
                self.stt(e2.ap, e2.ap, 1.0, th.ap, ALU.add, ALU.mult, [e2, th], [e2])
                self.act(e2.ap, e2.ap, AF.Sqrt, [e2], [e2])
                self.tt("dve", b_sb.ap, e2.ap, i_.ap, ALU.mult, [e2, i_], [b_sb])
                hd = hs[d]
                for s in range(nseq):
                    sq = slice(s * L, (s + 1) * L)
                    av, bv, hv = a_sb.ap[:, sq], b_sb.ap[:, sq], hd.ap[:, sq]
                    if rev:
                        av, bv, hv = av[:, ::-1], bv[:, ::-1], hv[:, ::-1]
                    init = 0.0 if g == 0 else self.lrust0.ap[:, d, t:t + 1]
                    rd = [a_sb, b_sb] + ([] if g == 0 else [self.lrust0])
                    fw.op("dve", lambda h, av=av, bv=bv, hv=hv, init=init: h.tensor_tensor_scan(out=hv, data0=av, data1=bv, initial=init, op0=ALU.mult, op1=ALU.add),
                          reads=rd, writes=[hd])
                    if g == 0:
                        self.cp("dve", self.lrust.ap[:, s, d, t:t + 1], hv[:, L - 1:L], [hd], [self.lrust])
            self.tt("dve", hs[0].ap, hs[0].ap, hs[1].ap, ALU.add, [hs[0], hs[1]], [hs[0]])
            self.tt("dve", lruy.ap[:, t, :], hs[0].ap, gg.ap[:, t, :], ALU.mult, [hs[0], gg], [lruy])
        if g == 0:
            sem = Buf("lruout")
            for s in range(4):
                for d in range(2):
                    dst = O["nlru"][s, l, d, :].rearrange("(t p) -> p t", p=128)
                    src = self.lrust.ap[:, s, d, :]
                    dd = fw.dma("sp", lambda h, src=src, dst=dst: h.dma_start(out=dst, in_=src, allow_slow_non_contiguous=True), sem, reads=[self.lrust])
                    fw.final.append(dd)

    def merge(self, l, g, attnT, s5y, lruy):
        fw, ar, I = self.fw, self.ar, self.I
        merged = ar.alloc([128, 8, TG], BF16)
        sig = [ar.alloc([128, 512], F32) for _ in range(2)]
        pr = [ar.alloc([128, 512], F32) for _ in range(3)]
        si = [0]

        def br_rhs(kc, sl):
            if kc < 4:
                return attnT.ap[:, kc, sl]
            if kc < 6:
                return s5y.ap[:, kc - 4, sl]
            return lruy.ap[:, kc - 6, sl]

        for sp in range(4):
            c0 = sp * 256
            wb = self.wslice([(0, 4, 256, I["w_br_attn"][l][:, c0:c0 + 256]), (4 * 256, 2, 256, I["w_br_s5"][l][:, c0:c0 + 256]),
                              (6 * 256, 2, 256, I["w_br_lru"][l][:, c0:c0 + 256])])
            wbv = wb.ap.rearrange("p (k n) -> p k n", k=8)
            wg = [self.wslice([(0, 8, 256, I["w_in"][l][:, 2304 + b * 1024 + c0:2304 + b * 1024 + c0 + 256])]) for b in range(3)]
            for t in range(2):
                ft = sp * 2 + t
                for tb in range(2):
                    sl = slice(tb * 512, (tb + 1) * 512)
                    for b, (k0, k1) in enumerate(((0, 4), (4, 6), (6, 8))):
                        gb = self.bank()
                        wgv = wg[b].ap.rearrange("p (k n) -> p k n", k=8)
                        fw.mm([lambda h, gb=gb, wgv=wgv, kc=kc, t=t, tb=tb: h.matmul(gb.ap, lhsT=wgv[:, kc, t * 128:(t + 1) * 128], rhs=self.h_rhs(kc, tb),
                                                                                     start=(kc == 0), stop=(kc == 7)) for kc in range(8)],
                              reads=[wg[b], self.h], writes=[gb])
                        bb = self.bank()
                        fw.mm([lambda h, bb=bb, kc=kc, t=t, sl=sl, k0=k0, k1=k1: h.matmul(bb.ap, lhsT=wbv[:, kc, t * 128:(t + 1) * 128], rhs=br_rhs(kc, sl),
                                                                                         start=(kc == k0), stop=(kc == k1 - 1)) for kc in range(k0, k1)],
                              reads=[wb, attnT, s5y, lruy], writes=[bb])
                        sg = sig[si[0] % 2]
                        si[0] += 1
                        self.act(sg.ap, gb.ap, AF.Sigmoid, [gb], [sg])
                        self.tt("dve", pr[b].ap, bb.ap, sg.ap, ALU.mult, [bb, sg], [pr[b]])
                    self.tt("dve", pr[0].ap, pr[0].ap, pr[1].ap, ALU.add, [pr[0], pr[1]], [pr[0]])
                    self.tt("dve", merged.ap[:, ft, sl], pr[0].ap, pr[2].ap, ALU.add, [pr[0], pr[2]], [merged])
                    self.tick()

        def out_cons(ft, tb, bk):
            sl = slice(tb * 512, (tb + 1) * 512)
            xv = self.x.ap[:, g, ft, sl]
            self.stt(xv, bk.ap, self.mod.ap[:, l, 2 * 8 + ft, g:g + 1], xv, ALU.mult, ALU.add, [bk, self.mod, self.x], [self.x])
            self.tick()

        self.proj_fm(lambda c0, n: [(0, 8, n, I["w_out"][l][:, c0:c0 + n])], 8,
                     lambda kc, tb: merged.ap[:, kc, tb * 512:(tb + 1) * 512], [merged], 8, out_cons)

    def ffn(self, l, g):
        fw, ar, I = self.fw, self.ar, self.I
        gg = ar.alloc([128, 22, TG], BF16)
        a_sb = [ar.alloc([128, TG], F32) for _ in range(2)]
        c_sb = [ar.alloc([128, TG], F32) for _ in range(2)]
        gl = [ar.alloc([128, TG], BF16) for _ in range(2)]
        nseq, L = (4, 256) if g == 0 else (1, 1024)
        fc = self.ffcols
        it = 0
        for sp in range(11):
            wa = self.wslice([(0, 8, 256, I["ffn_w_up"][l][:, sp * 256:sp * 256 + 256])])
            wb = self.wslice([(0, 8, 256, I["ffn_w_up"][l][:, 2816 + sp * 256:2816 + sp * 256 + 256])])
            wav = wa.ap.rearrange("p (k n) -> p k n", k=8)
            wbv = wb.ap.rearrange("p (k n) -> p k n", k=8)
            for t in range(2):
                ft = sp * 2 + t
                a_, c_, g_ = a_sb[it % 2], c_sb[it % 2], gl[it % 2]
                it += 1
                for tb in range(2):
                    bk = self.bank()
                    fw.mm([lambda h, bk=bk, kc=kc, t=t, tb=tb: h.matmul(bk.ap, lhsT=wav[:, kc, t * 128:(t + 1) * 128], rhs=self.h_rhs(kc, tb),
                                                                       start=(kc == 0), stop=(kc == 7)) for kc in range(8)], reads=[wa, self.h], writes=[bk])
                    self.cp("act", a_.ap[:, tb * 512:(tb + 1) * 512], bk.ap, [bk], [a_])
                av = a_.ap.rearrange("p (s q) -> p s q", s=nseq)
                cv = c_.ap.rearrange("p (s q) -> p s q", s=nseq)
                self.act(cv, av, AF.Identity, [a_, fc], [c_], bias=fc.ap[:, ft, 3:4], scale=fc.ap[:, ft, 1:2])
                self.stt(cv[:, :, 1:L], av[:, :, 0:L - 1], fc.ap[:, ft, 0:1], cv[:, :, 1:L], ALU.mult, ALU.add, [a_, fc, c_], [c_])
                self.stt(cv[:, :, 0:L - 1], av[:, :, 1:L], fc.ap[:, ft, 2:3], cv[:, :, 0:L - 1], ALU.mult, ALU.add, [a_, fc, c_], [c_])
                self.act(g_.ap, c_.ap, AF.Gelu_apprx_tanh, [c_], [g_])
                for tb in range(2):
                    sl = slice(tb * 512, (tb + 1) * 512)
                    bk = self.bank()
                    fw.mm([lambda h, bk=bk, kc=kc, t=t, tb=tb: h.matmul(bk.ap, lhsT=wbv[:, kc, t * 128:(t + 1) * 128], rhs=self.h_rhs(kc, tb),
                                                                       start=(kc == 0), stop=(kc == 7)) for kc in range(8)], reads=[wb, self.h], writes=[bk])
                    self.tt("dve", gg.ap[:, ft, sl], bk.ap, g_.ap[:, sl], ALU.mult, [bk, g_], [gg])
                self.tick()
        for ft in range(8):
            w0 = self.wslice([(0, 11, 128, I["ffn_w_down"][l][0:1408, ft * 128:(ft + 1) * 128])])
            w1 = self.wslice([(0, 11, 128, I["ffn_w_down"][l][1408:2816, ft * 128:(ft + 1) * 128])])
            wv = [w0.ap[:, 0:1408].rearrange("p (k n) -> p k n", k=11), w1.ap[:, 0:1408].rearrange("p (k n) -> p k n", k=11)]
            for tb in range(2):
                sl = slice(tb * 512, (tb + 1) * 512)
                bk = self.bank()
                fw.mm([lambda h, bk=bk, kc=kc, sl=sl: h.matmul(bk.ap, lhsT=wv[kc // 11][:, kc % 11, :], rhs=gg.ap[:, kc, sl],
                                                               start=(kc == 0), stop=(kc == 21)) for kc in range(22)], reads=[w0, w1, gg], writes=[bk])
                xv = self.x.ap[:, g, ft, sl]
                self.stt(xv, bk.ap, self.mod.ap[:, l, 5 * 8 + ft, g:g + 1], xv, ALU.mult, ALU.add, [bk, self.mod, self.x], [self.x])
            self.tick()
        self.drain()

    def final_norm_gen(self, g):
        fw, ar, O = self.fw, self.ar, self.O
        rstd = ar.alloc([128, TG], F32)
        yts = [ar.alloc([128, 8, 128], F32) for _ in range(2)]
        self.rms_rstd(g, rstd)
        yield
        dst_t = O["yp" if g == 0 else "ys"]
        for tt in range(8):
            yt = yts[tt % 2]
            tsl = slice(tt * 128, (tt + 1) * 128)
            for kc in range(8):
                self.stt(yt.ap[:, kc, :], self.x.ap[:, g, kc, tsl], self.gcols.ap[:, 4, kc:kc + 1], rstd.ap[:, tsl], ALU.mult, ALU.mult,
                         [self.x, self.gcols, rstd], [yt])
            yield
            for qd in range(4):
                s = self.stg_i % 2
                self.stg_i += 1
                stg = self.stg[s]
                bk = self.bank()
                fw.mm([lambda h, bk=bk, j=j, qd=qd, yt=yt: h.transpose(out=bk.ap[:, j * 128:(j + 1) * 128], in_=yt.ap[:, qd * 2 + j, :],
                                                                       identity=self.ident_f.ap) for j in range(2)], reads=[yt, self.ident_f], writes=[bk])
                self.cp("act" if qd % 2 else "dve", stg.ap, bk.ap[:, 0:256], [bk], [stg])
                dst = dst_t[tt * 128:(tt + 1) * 128, qd * 256:(qd + 1) * 256]
                dd = fw.dma("sp", lambda h, stg=stg, dst=dst: h.dma_start(out=dst, in_=stg.ap), self.stg_sem[s], reads=[stg])
                fw.final.append(dd)
                if qd == 1:
                    yield
            yield

    def final_norm(self, g):
        fw, ar, O = self.fw, self.ar, self.O
        m = ar.mark()
        rstd = ar.alloc([128, TG], F32)
        self.rms_rstd(g, rstd)
        y = ar.alloc([128, 8, TG], F32)
        for kc in range(8):
            self.stt(y.ap[:, kc, :], self.x.ap[:, g, kc, :], self.gcols.ap[:, 4, kc:kc + 1], rstd.ap, ALU.mult, ALU.mult, [self.x, self.gcols, rstd], [y])
        dst_t = O["yp" if g == 0 else "ys"]
        for tt in range(8):
            for qd in range(4):
                s = self.stg_i % 2
                self.stg_i += 1
                stg = self.stg[s]
                bk = self.bank()
                fw.mm([lambda h, bk=bk, j=j, qd=qd, tt=tt: h.transpose(out=bk.ap[:, j * 128:(j + 1) * 128], in_=y.ap[:, qd * 2 + j, tt * 128:(tt + 1) * 128],
                                                                       identity=self.ident_f.ap) for j in range(2)], reads=[y, self.ident_f], writes=[bk])
                self.cp("act" if qd % 2 else "dve", stg.ap, bk.ap[:, 0:256], [bk], [stg])
                dst = dst_t[tt * 128:(tt + 1) * 128, qd * 256:(qd + 1) * 256]
                dd = fw.dma("sp", lambda h, stg=stg, dst=dst: h.dma_start(out=dst, in_=stg.ap), self.stg_sem[s], reads=[stg])
                fw.final.append(dd)
        ar.release(m)


_W_KEYS = ["w_ada", "b_ada", "g_norm1", "g_norm2", "w_in", "rpb", "s5_lam_re", "s5_lam_im", "s5_log_step", "s5_b_re", "s5_b_im",
           "s5_c_re", "s5_c_im", "s5_d", "s5_w_glu", "lru_conv_w", "lru_conv_b", "lru_w_a", "lru_b_a", "lru_w_x", "lru_b_x", "lru_lam",
           "w_br_attn", "w_br_s5", "w_br_lru", "w_out", "ffn_w_up", "ffn_conv_w", "ffn_conv_b", "ffn_w_down", "g_final"]


def make_in_maps(inp):
    f = lambda a: np.ascontiguousarray(np.asarray(a, dtype=np.float32))
    shared = {k: f(inp[k]) for k in _W_KEYS}
    maps = []
    for i in range(NCORES):
        m = dict(shared)
        m["xp"] = f(inp["x_prompt"][4 * i:4 * i + 4]).reshape(TG, D)
        m["xs"] = f(inp["x_sample"][i]).reshape(TG, D)
        m["ck"] = f(inp["cache_k"][i]).reshape(DEPTH, 512, 512)
        m["cv"] = f(inp["cache_v"][i]).reshape(DEPTH, 512, 512)
        m["s5r"] = f(inp["state_s5_re"][i]).reshape(DEPTH, 2, 1024)
        m["s5i"] = f(inp["state_s5_im"][i]).reshape(DEPTH, 2, 1024)
        m["slru"] = f(inp["state_lru"][i]).reshape(DEPTH, 2, 256)
        m["cvec"] = f(np.stack([np.asarray(inp["c_ctx"]), np.asarray(inp["c"])[i]], axis=0))
        maps.append(m)
    return maps


def assemble(results):
    cat = lambda k: np.concatenate([np.asarray(r[k]) for r in results], axis=0)
    y_prompt = cat("yp").reshape(32, 256, D)
    y_sample = cat("ys").reshape(8, 1024, D)
    new_k = cat("nk").reshape(32, DEPTH, 256, 8, 64)
    new_v = cat("nv").reshape(32, DEPTH, 256, 8, 64)
    ns5r = cat("ns5r").reshape(32, DEPTH, 2, 16, 64)
    ns5i = cat("ns5i").reshape(32, DEPTH, 2, 16, 64)
    nlru = cat("nlru").reshape(32, DEPTH, 2, 256)
    return tuple(np.ascontiguousarray(a, dtype=np.float32) for a in (y_prompt, y_sample, new_k, new_v, ns5r, ns5i, nlru))


def kernel(**inputs):
    nc = Builder().build()
    in_maps = make_in_maps(inputs)
    res = run_bass_kernel_spmd(nc, in_maps, core_ids=list(range(NCORES)))
    return assemble(res.results)


def debug_run(inputs, stage, ncores=1, trace=False):
    b = Builder(stage=stage)
    nc = b.build()
    in_maps = make_in_maps(inputs)[:ncores]
    res = run_bass_kernel_spmd(nc, in_maps, core_ids=list(range(ncores)), trace=trace)
    if trace:
        print("EXEC_NS", stage, res.exec_time_ns)
    return b, res.results
```
